# Optimizing a Trainium2 kernel written in Bass

```python
import jax
import jax.numpy as jnp
from jax import lax
import numpy as np

D_MODEL = 1024
BATCH = 4
SEQ = 8192
DEPTH = 2

N_GROUPS = 4
D_GROUP = D_MODEL // N_GROUPS
HEAD_DIM = 64
N_HEADS = D_GROUP // HEAD_DIM
CHUNK = 64
CONF_KERNEL = 31
SHORT_CONV = 4
ROPE_BASE = 10000.0
D_FF = -(-8 * D_MODEL // (3 * 256)) * 256
IN_SIZES = (D_GROUP, D_GROUP, D_GROUP, D_GROUP,
            D_GROUP, D_GROUP,
            D_GROUP, D_GROUP, D_GROUP, D_GROUP,
            N_HEADS, N_HEADS,
            D_GROUP, D_GROUP, D_GROUP, D_GROUP)
D_IN = 14 * D_GROUP + 2 * N_HEADS

kernel_name = 'hybrid_parallel_head_groups_trunk'


def rms_norm(x, w, eps=1e-6):
    xf = x.astype(jnp.float32)
    y = xf * lax.rsqrt(jnp.mean(xf * xf, axis=-1, keepdims=True) + eps)
    return (y * w.astype(jnp.float32)).astype(x.dtype)


def layer_norm(x, w, b, eps=1e-5):
    mu = jnp.mean(x, axis=-1, keepdims=True)
    var = jnp.mean(jnp.square(x - mu), axis=-1, keepdims=True)
    return (x - mu) * lax.rsqrt(var + eps) * w.astype(jnp.float32) + b.astype(jnp.float32)


def head_rms_norm(o, w, eps=1e-6):
    y = o * lax.rsqrt(jnp.mean(o * o, axis=-1, keepdims=True) + eps) * w.astype(jnp.float32)
    return y.reshape(o.shape[0], o.shape[1], -1)


def head_group_norm(o, w, eps=1e-5):
    mu = jnp.mean(o, axis=-1, keepdims=True)
    var = jnp.mean(jnp.square(o - mu), axis=-1, keepdims=True)
    y = (o - mu) * lax.rsqrt(var + eps) * w.astype(jnp.float32).reshape(N_HEADS, HEAD_DIM)
    return y.reshape(o.shape[0], o.shape[1], -1)


def l2_normalize(x, eps=1e-6):
    return x * lax.rsqrt(jnp.sum(x * x, axis=-1, keepdims=True) + eps)


def split_heads(t):
    b, s, _ = t.shape
    return t.reshape(b, s, -1, HEAD_DIM).transpose(0, 2, 1, 3)


def to_chunks(t):
    b, h, s = t.shape[:3]
    t = t.reshape(b, h, s // CHUNK, CHUNK, *t.shape[3:])
    return jnp.moveaxis(t, 2, 0)


def from_chunks(t):
    n, b, h, c, dv = t.shape
    return jnp.moveaxis(t, 0, 2).reshape(b, h, n * c, dv).transpose(0, 2, 1, 3)


def causal_depthwise_conv(x, w):
    k, c = w.shape
    return lax.conv_general_dilated(
        x, w.astype(x.dtype)[:, None, :], window_strides=(1,), padding=[(k - 1, 0)],
        dimension_numbers=('NWC', 'WIO', 'NWC'), feature_group_count=c)


def apply_rotary(x, pos):
    half = HEAD_DIM // 2
    inv = ROPE_BASE ** (-jnp.arange(half, dtype=jnp.float32) / half)
    ang = pos[:, None] * inv[None, :]
    cos, sin = jnp.cos(ang), jnp.sin(ang)
    x1, x2 = x[..., :half], x[..., half:]
    return jnp.concatenate([x1 * cos - x2 * sin, x1 * sin + x2 * cos], axis=-1)


def retention_mixer(q, k, v):
    q, k, v = split_heads(q), split_heads(k), split_heads(v)
    pos = jnp.arange(q.shape[2], dtype=jnp.float32)
    q = apply_rotary(q, pos)
    k = apply_rotary(k, pos) * HEAD_DIM ** -0.5
    log_gamma = jnp.log1p(-(2.0 ** (-5.0 - jnp.arange(N_HEADS, dtype=jnp.float32))))
    idx = jnp.arange(CHUNK, dtype=jnp.float32)
    diff = idx[:, None] - idx[None, :]
    decay_mask = jnp.where(diff >= 0, jnp.exp(jnp.maximum(diff, 0.0) * log_gamma[:, None, None]), 0.0)
    q_decay = jnp.exp((idx + 1.0) * log_gamma[:, None])[..., None]
    k_decay = jnp.exp((CHUNK - 1.0 - idx) * log_gamma[:, None])[..., None]
    chunk_decay = jnp.exp(CHUNK * log_gamma)[:, None, None]

    def step(state, inp):
        qc, kc, vc = inp
        scores = jnp.einsum('bhid,bhjd->bhij', qc, kc) * decay_mask
        out = (jnp.einsum('bhij,bhjv->bhiv', scores, vc)
               + jnp.einsum('bhid,bhdv->bhiv', qc * q_decay, state))
        state = state * chunk_decay + jnp.einsum('bhjd,bhjv->bhdv', kc * k_decay, vc)
        return state, out

    s0 = jnp.zeros((q.shape[0], N_HEADS, HEAD_DIM, HEAD_DIM), jnp.float32)
    _, out = lax.scan(step, s0, (to_chunks(q), to_chunks(k), to_chunks(v)))
    return from_chunks(out)


def conformer_conv_mixer(a, gate, conv_w, conv_b, ln_w, ln_b):
    glu = a * jax.nn.sigmoid(gate)
    y = causal_depthwise_conv(glu, conv_w) + conv_b.astype(jnp.float32)
    return jax.nn.silu(layer_norm(y, ln_w, ln_b))


def gated_deltanet_mixer(q, k, v, beta_logit, a_logit, A_log, dt_bias):
    q = l2_normalize(split_heads(q)) * HEAD_DIM ** -0.5
    k = l2_normalize(split_heads(k))
    v = split_heads(v)
    beta = jax.nn.sigmoid(beta_logit).transpose(0, 2, 1)
    g = (-jnp.exp(A_log.astype(jnp.float32))
         * jax.nn.softplus(a_logit + dt_bias.astype(jnp.float32))).transpose(0, 2, 1)
    idx = jnp.arange(CHUNK)
    incl = idx[:, None] >= idx[None, :]
    strict = idx[:, None] > idx[None, :]

    def step(state, inp):
        qc, kc, vc, bc, gc = inp
        G = jnp.cumsum(gc, axis=-1)
        L = jnp.exp(jnp.where(incl, G[..., :, None] - G[..., None, :], -jnp.inf))
        kb = kc * bc[..., None]
        A = jnp.where(strict, jnp.einsum('bhid,bhjd->bhij', kb, kc) * L, 0.0)
        rhs = jnp.concatenate([vc * bc[..., None], kb * jnp.exp(G)[..., None]], axis=-1)
        sol = lax.linalg.triangular_solve(A, rhs, left_side=True, lower=True, unit_diagonal=True)
        u, w = sol[..., :HEAD_DIM], sol[..., HEAD_DIM:]
        v_new = u - jnp.einsum('bhik,bhkv->bhiv', w, state)
        attn = jnp.einsum('bhid,bhjd->bhij', qc, kc) * L
        out = (jnp.einsum('bhid,bhdv->bhiv', qc * jnp.exp(G)[..., None], state)
               + jnp.einsum('bhij,bhjv->bhiv', attn, v_new))
        g_last = G[..., -1:]
        state = (state * jnp.exp(g_last)[..., None]
                 + jnp.einsum('bhjd,bhjv->bhdv', kc * jnp.exp(g_last - G)[..., None], v_new))
        return state, out

    s0 = jnp.zeros((q.shape[0], N_HEADS, HEAD_DIM, HEAD_DIM), jnp.float32)
    _, out = lax.scan(step, s0, (to_chunks(q), to_chunks(k), to_chunks(v), to_chunks(beta), to_chunks(g)))
    return from_chunks(out)


def hgrn2_mixer(q, f_logit, i, lb):
    lb = lb.astype(jnp.float32)
    log_f = jnp.logaddexp(jnp.log(lb), jnp.log1p(-lb) + jax.nn.log_sigmoid(f_logit))
    k = -jnp.expm1(log_f)
    q, k, v, gf = split_heads(q), split_heads(k), split_heads(i), split_heads(log_f)
    idx = jnp.arange(CHUNK)
    incl = (idx[:, None] >= idx[None, :])[:, :, None]

    def step(state, inp):
        qc, kc, vc, gc = inp
        G = jnp.cumsum(gc, axis=2)
        dec = jnp.exp(jnp.where(incl, G[:, :, :, None, :] - G[:, :, None, :, :], -jnp.inf))
        scores = jnp.einsum('bhid,bhijd,bhjd->bhij', qc, dec, kc)
        out = (jnp.einsum('bhij,bhjv->bhiv', scores, vc)
               + jnp.einsum('bhid,bhdv->bhiv', qc * jnp.exp(G), state))
        g_last = G[:, :, -1:, :]
        state = (state * jnp.exp(g_last[:, :, 0, :])[..., None]
                 + jnp.einsum('bhjd,bhjv->bhdv', kc * jnp.exp(g_last - G), vc))
        return state, out

    s0 = jnp.zeros((q.shape[0], N_HEADS, HEAD_DIM, HEAD_DIM), jnp.float32)
    _, out = lax.scan(step, s0, (to_chunks(q), to_chunks(k), to_chunks(v), to_chunks(gf)))
    return from_chunks(out)


def setup_inputs(seed: int = 0) -> dict:
    key = jax.random.key(seed)
    ks = jax.random.split(key, 20)
    f32 = jnp.float32
    nrm = lambda k, shape, scale: jax.random.normal(k, shape, f32) * scale
    gain = lambda k, shape: 1.0 + 0.02 * jax.random.normal(k, shape, f32)
    dt = jnp.exp(jax.random.uniform(ks[10], (DEPTH, N_HEADS), f32, minval=np.log(1e-3), maxval=np.log(1e-1)))
    return {
        'x': jax.random.normal(ks[0], (BATCH, SEQ, D_MODEL), f32),
        'norm_mix_w': gain(ks[1], (DEPTH, D_MODEL)),
        'w_in': nrm(ks[2], (DEPTH, D_MODEL, D_IN), D_MODEL ** -0.5),
        'ret_norm_w': gain(ks[3], (DEPTH, D_GROUP)),
        'conf_conv_w': nrm(ks[4], (DEPTH, CONF_KERNEL, D_GROUP), CONF_KERNEL ** -0.5),
        'conf_conv_b': nrm(ks[5], (DEPTH, D_GROUP), 0.02),
        'conf_ln_w': gain(ks[6], (DEPTH, D_GROUP)),
        'conf_ln_b': nrm(ks[7], (DEPTH, D_GROUP), 0.02),
        'gdn_conv_w': nrm(ks[8], (DEPTH, SHORT_CONV, 3 * D_GROUP), SHORT_CONV ** -0.5),
        'gdn_A_log': jnp.log(jax.random.uniform(ks[9], (DEPTH, N_HEADS), f32, minval=1.0, maxval=16.0)),
        'gdn_dt_bias': dt + jnp.log(-jnp.expm1(-dt)),
        'gdn_norm_w': gain(ks[11], (DEPTH, HEAD_DIM)),
        'hgrn_lb_logits': nrm(ks[12], (DEPTH, D_GROUP), 0.1),
        'hgrn_norm_w': gain(ks[13], (DEPTH, HEAD_DIM)),
        'w_out': nrm(ks[14], (DEPTH, D_MODEL, D_MODEL), D_MODEL ** -0.5),
        'norm_ffn_w': gain(ks[15], (DEPTH, D_MODEL)),
        'ffn_w_gate': nrm(ks[16], (DEPTH, D_MODEL, D_FF), D_MODEL ** -0.5),
        'ffn_w_up': nrm(ks[17], (DEPTH, D_MODEL, D_FF), D_MODEL ** -0.5),
        'ffn_w_down': nrm(ks[18], (DEPTH, D_FF, D_MODEL), D_FF ** -0.5),
        'final_norm_w': gain(ks[19], (D_MODEL,)),
    }


def reference(x, norm_mix_w, w_in, ret_norm_w, conf_conv_w, conf_conv_b, conf_ln_w, conf_ln_b,
              gdn_conv_w, gdn_A_log, gdn_dt_bias, gdn_norm_w, hgrn_lb_logits, hgrn_norm_w,
              w_out, norm_ffn_w, ffn_w_gate, ffn_w_up, ffn_w_down, final_norm_w):
    b, t = x.shape[0], x.shape[1]
    offsets = [int(o) for o in np.cumsum(IN_SIZES)[:-1]]
    lb_all = jnp.cumsum(jax.nn.softmax(hgrn_lb_logits.astype(jnp.float32), axis=0), axis=0)
    lb_all = lb_all - lb_all[0:1]
    for l in range(DEPTH):
        h = rms_norm(x, norm_mix_w[l])
        proj = (h @ w_in[l]).astype(jnp.float32)
        (rq, rk, rv, rg, ca, cg, gq, gk, gv, gg, gb, ga, hq, hf, hi, hgate) = jnp.split(proj, offsets, axis=-1)
        ret = head_group_norm(retention_mixer(rq, rk, rv), ret_norm_w[l]) * jax.nn.silu(rg)
        conf = conformer_conv_mixer(ca, cg, conf_conv_w[l], conf_conv_b[l], conf_ln_w[l], conf_ln_b[l])
        qkv = jax.nn.silu(causal_depthwise_conv(jnp.concatenate([gq, gk, gv], axis=-1), gdn_conv_w[l]))
        gq, gk, gv = jnp.split(qkv, [D_GROUP, 2 * D_GROUP], axis=-1)
        gdn = head_rms_norm(gated_deltanet_mixer(gq, gk, gv, gb, ga, gdn_A_log[l], gdn_dt_bias[l]),
                            gdn_norm_w[l]) * jax.nn.silu(gg)
        hg = head_rms_norm(hgrn2_mixer(hq, hf, hi, lb_all[l]), hgrn_norm_w[l]) * jax.nn.silu(hgate)
        mix = jnp.concatenate([ret, conf, gdn, hg], axis=-1).astype(x.dtype)
        x = x + mix @ w_out[l]
        h = rms_norm(x, norm_ffn_w[l])
        x = x + (jax.nn.silu(h @ ffn_w_gate[l]) * (h @ ffn_w_up[l])) @ ffn_w_down[l]
    return rms_norm(x, final_norm_w)
```

```python
from contextlib import ExitStack
import numpy as np
import ml_dtypes
import concourse.bass as bass
import concourse.mybir as mybir
from concourse.bass_utils import run_bass_kernel_spmd

F32 = mybir.dt.float32
BF16 = mybir.dt.bfloat16
AF = mybir.ActivationFunctionType
ALU = mybir.AluOpType
AX = mybir.AxisListType

D = 1024
DIN = 3592
DFF = 2816
NFT = DFF // 128
NEG = -30000.0


class Buf:
    __slots__ = ("name", "w", "r", "excl")

    def __init__(self, name="", excl=False):
        self.name = name
        self.w = None
        self.r = {}
        self.excl = excl


class K:
    NDSEM = 40

    def __init__(self, nc):
        self.nc = nc
        self.eng = {"pe": nc.tensor, "act": nc.scalar, "dve": nc.vector,
                    "pool": nc.gpsimd, "sp": nc.sync}
        self.sem = {e: nc.alloc_semaphore(name="s_" + e) for e in ("pe", "act", "dve", "pool")}
        self.cnt = {e: 0 for e in self.sem}
        self.dsem = [nc.alloc_semaphore(name="d%d" % i) for i in range(self.NDSEM)]
        self.dcnt = [0] * self.NDSEM
        self.dnext = 0
        self.waited = {e: {} for e in self.eng}
        self.nwaits = 0
        self.ninst = 0

    def _wait(self, eng, ev):
        if ev is None:
            return
        if ev[0] == "e":
            key, val = ("e", ev[1]), ev[2]
            if ev[1] == "pe" and eng == "pe":
                return
            sem = self.sem[ev[1]]
        else:
            key, val = ("d", ev[1]), ev[2]
            sem = self.dsem[ev[1]]
        if self.waited[eng].get(key, 0) >= val:
            return
        self.eng[eng].wait_ge(sem, val)
        self.waited[eng][key] = val
        self.nwaits += 1

    def _deps(self, eng, R, W):
        for b in R:
            self._wait(eng, b.w)
            if b.excl:
                for ev in b.r.values():
                    if not (ev[0] == "e" and ev[1] == eng):
                        self._wait(eng, ev)
        for b in W:
            self._wait(eng, b.w)
            for ev in b.r.values():
                self._wait(eng, ev)

    def _post(self, ev, R, W):
        for b in R:
            b.r[(ev[0], ev[1])] = ev
        for b in W:
            b.w = ev
            b.r = {}

    def op(self, eng, fn, R=(), W=()):
        self._deps(eng, R, W)
        inst = fn(self.eng[eng])
        self.cnt[eng] += 1
        inst.then_inc(self.sem[eng], 1)
        self._post(("e", eng, self.cnt[eng]), R, W)
        self.ninst += 1
        return inst

    def dma(self, out, in_, R=(), W=(), q="sp", **kw):
        idx = self.dnext
        self.dnext = (self.dnext + 1) % self.NDSEM
        if self.dcnt[idx] > 0:
            self._wait(q, ("d", idx, self.dcnt[idx]))
        self._deps(q, R, W)
        inst = self.eng[q].dma_start(out=out, in_=in_, **kw)
        self.dcnt[idx] += 16
        inst.then_inc(self.dsem[idx], 16)
        ev = ("d", idx, self.dcnt[idx])
        self._post(ev, R, W)
        self.ninst += 1
        return ev

    def barrier(self):
        for e in self.eng:
            for e2 in self.sem:
                if self.cnt[e2] > 0:
                    self._wait(e, ("e", e2, self.cnt[e2])) if not (e == e2 == "pe") else None
            for i in range(self.NDSEM):
                if self.dcnt[i] > 0:
                    self._wait(e, ("d", i, self.dcnt[i]))


def host_consts(T):
    c = {}
    c["c_identf"] = np.eye(128, dtype=np.float32)
    j = np.arange(128)[:, None]
    i = np.arange(128)[None, :]
    same = (j // 64) == (i // 64)
    c["c_cmask"] = (j <= i).astype(np.float32)
    c["c_ublk"] = ((j <= i) & same).astype(np.float32)
    c["c_oblk"] = same.astype(np.float32)
    cind = np.zeros((128, 2), np.float32)
    cind[:64, 0] = 1.0
    cind[64:, 1] = 1.0
    c["c_cind"] = cind
    sel = np.zeros((4, 4, 128), np.float32)
    for h in range(4):
        sel[h, h, :] = 1.0
    c["c_sel"] = sel.reshape(4, 512)
    c["c_negA"] = np.where(same & (j > i), 0.0, NEG).astype(np.float32)
    c["c_negT"] = np.where(same & (i >= j), 0.0, NEG).astype(np.float32)
    c["c_ones64"] = np.ones((64, 64), np.float32)
    c["c_ones4"] = np.ones((4, 128), np.float32)
    c["c_negA4"] = np.tile(c["c_negA"], (1, 4))
    c["c_negT4"] = np.tile(c["c_negT"], (1, 4))
    t = np.arange(T, dtype=np.float64)
    inv = 10000.0 ** (-np.arange(32, dtype=np.float64) / 32.0)
    ang = (t[:, None].astype(np.float32) * inv[None, :].astype(np.float32)).astype(np.float64)
    cos, sin = np.cos(ang), np.sin(ang)
    gam = 1.0 - 2.0 ** (-5.0 - np.arange(4, dtype=np.float64))
    pos = (np.arange(T) % 128 + 1).astype(np.float64)
    dq = gam[None, :] ** pos[:, None]
    dk = gam[None, :] ** (-pos[:, None]) * 0.125
    rope = np.zeros((T, 4, 4, 32), np.float64)
    rope[:, 0] = cos[:, None, :] * dq[:, :, None]
    rope[:, 1] = sin[:, None, :] * dq[:, :, None]
    rope[:, 2] = cos[:, None, :] * dk[:, :, None]
    rope[:, 3] = sin[:, None, :] * dk[:, :, None]
    c["c_rope"] = rope.reshape(T, 512).astype(np.float32)
    gC = np.zeros((64, 4, 64), np.float64)
    gC[:] = (gam ** 128.0)[None, :, None]
    c["c_gC"] = gC.reshape(64, 256).astype(np.float32)
    return c


CONST_SHAPES = {"c_identf": [128, 128], "c_cmask": [128, 128], "c_ublk": [128, 128],
                "c_oblk": [128, 128], "c_cind": [128, 2], "c_sel": [4, 512],
                "c_negA": [128, 128], "c_negT": [128, 128], "c_ones64": [64, 64],
                "c_ones4": [4, 128], "c_negA4": [128, 512], "c_negT4": [128, 512],
                "c_gC": [64, 256]}

PARAM_SHAPES = {
    "norm_mix_w": [2, 1024], "w_in": [2, 1024, DIN], "ret_norm_w": [2, 256],
    "conf_conv_w": [2, 31, 256], "conf_conv_b": [2, 256], "conf_ln_w": [2, 256],
    "conf_ln_b": [2, 256], "gdn_conv_w": [2, 4, 768], "gdn_A_log": [2, 4],
    "gdn_dt_bias": [2, 4], "gdn_norm_w": [2, 64], "hgrn_lb_logits": [2, 256],
    "hgrn_norm_w": [2, 64], "w_out": [2, 1024, 1024], "norm_ffn_w": [2, 1024],
    "ffn_w_gate": [2, 1024, DFF], "ffn_w_up": [2, 1024, DFF], "ffn_w_down": [2, DFF, 1024],
    "final_norm_w": [1024],
}


class StopBuild(Exception):
    pass


def build(T, NL=2, dbg=False, skip=(), stop=99):
    nc = bass.Bass("TRN2", target_bir_lowering=False)

    def ckpt(n):
        if n >= stop:
            raise StopBuild()

    NT = T // 128
    k = K(nc)
    dr = {}
    dr["x"] = nc.dram_tensor("x", [T, D], F32, kind="ExternalInput").ap()
    for n, s in PARAM_SHAPES.items():
        dr[n] = nc.dram_tensor(n, s, F32, kind="ExternalInput").ap()
    for n, s in CONST_SHAPES.items():
        dr[n] = nc.dram_tensor(n, s, F32, kind="ExternalInput").ap()
    dr["c_rope"] = nc.dram_tensor("c_rope", [T, 512], F32, kind="ExternalInput").ap()
    out = nc.dram_tensor("out", [T, D], F32, kind="ExternalOutput").ap()
    xs = nc.dram_tensor("xs", [T, D], F32, kind="Internal").ap()
    if dbg:
        dmix = nc.dram_tensor("dmix", [NL, T, D], F32, kind="ExternalOutput").ap()
    xsb = [Buf("xs%d" % i) for i in range(NT)]

    es_top = ExitStack()

    uniq = {"n": 0}

    def sb(es, name, shape, dt=F32):
        uniq["n"] += 1
        t = es.enter_context(nc.sbuf_tensor("%s_%d" % (name, uniq["n"]), shape, dt))
        return t.ap(), Buf(name)

    def mm(out_, lhsT, rhs, R, W, start=True, stop=True):
        k.op("pe", lambda e: e.matmul(out_, lhsT=lhsT, rhs=rhs, start=start, stop=stop), R=R, W=W)

    def tr(out_, in_, ident, R, W):
        k.op("pe", lambda e: e.transpose(out=out_, in_=in_, identity=ident), R=R, W=W)

    def act(out_, in_, func, R, W, bias=None, scale=None, accum=None):
        kw = {}
        if bias is not None:
            kw["bias"] = bias
        if scale is not None:
            kw["scale"] = scale
        if accum is not None:
            kw["accum_out"] = accum
        k.op("act", lambda e: e.activation(out=out_, in_=in_, func=func, **kw), R=R, W=W)

    def silu(out_, in_, R, W, scr, b_scr):
        act(scr, in_, AF.Sigmoid, R, [b_scr])
        tt(out_, in_, scr, ALU.mult, list(R) + [b_scr], W)

    def tt(out_, a, b, op, R, W, eng="dve"):
        k.op(eng, lambda e: e.tensor_tensor(out=out_, in0=a, in1=b, op=op), R=R, W=W)

    def ts(out_, a, s1, op0, R, W, s2=None, op1=None, eng="dve"):
        if op1 is None:
            k.op(eng, lambda e: e.tensor_scalar(out=out_, in0=a, scalar1=s1, scalar2=None, op0=op0), R=R, W=W)
        else:
            k.op(eng, lambda e: e.tensor_scalar(out=out_, in0=a, scalar1=s1, scalar2=None, op0=op0), R=R, W=W)
            k.op(eng, lambda e: e.tensor_scalar(out=out_, in0=out_, scalar1=s2, scalar2=None, op0=op1), R=list(R) + list(W), W=W)

    def stt(out_, in0, scalar, in1, op0, op1, R, W):
        k.op("dve", lambda e: e.scalar_tensor_tensor(out=out_, in0=in0, scalar=scalar, in1=in1, op0=op0, op1=op1), R=R, W=W)

    def cp(out_, in_, R, W, eng="dve"):
        if eng == "act":
            k.op("act", lambda e: e.copy(out=out_, in_=in_), R=R, W=W)
        else:
            k.op(eng, lambda e: e.tensor_copy(out=out_, in_=in_), R=R, W=W)

    def red(out_, in_, R, W):
        k.op("dve", lambda e: e.tensor_reduce(out=out_, in_=in_, axis=AX.X, op=ALU.add), R=R, W=W)

    def rcp(out_, in_, R, W):
        k.op("dve", lambda e: e.reciprocal(out=out_, in_=in_), R=R, W=W)

    def mset(ap, val, W, eng="pool"):
        k.op(eng, lambda e: e.memset(ap, val), W=W)

    def v3(ap, inner):
        return ap.rearrange("p (a b) -> p a b", b=inner)

    banks = []
    for i in range(8):
        t = es_top.enter_context(nc.psum_tensor("bank%d" % i, [128, 512], F32))
        banks.append((t.ap(), Buf("bank%d" % i, excl=True)))
    freeq = list(range(8))
    bank_idx = {id(b[1]): i for i, b in enumerate(banks)}

    def psum():
        assert freeq, "no free PSUM bank"
        return banks[freeq.pop(0)]

    def galloc():
        n = 0
        while not freeq:
            n += 1
            assert n < 100000, "PSUM deadlock"
            yield
        return banks[freeq.pop(0)]

    def pfree(*bufs):
        for b in bufs:
            i = bank_idx[id(b)]
            assert i not in freeq
            freeq.append(i)

    def run_interleaved(gens):
        gens = [g() for g in gens]
        while gens:
            for g in list(gens):
                try:
                    next(g)
                except StopIteration:
                    gens.remove(g)

    stg = [sb(es_top, "stg%d" % i, [128, 1024]) for i in range(2)]
    stg_c = stg[0]
    identf, b_identf = sb(es_top, "identf", [128, 128])
    identb, b_identb = sb(es_top, "identb", [128, 128], BF16)
    cmask, b_cmask = sb(es_top, "cmask", [128, 128])
    ublk, b_ublk = sb(es_top, "ublk", [128, 128])
    oblk, b_oblk = sb(es_top, "oblk", [128, 128])
    cind, b_cind = sb(es_top, "cind", [128, 2])
    sel, b_sel = sb(es_top, "sel", [4, 512])
    negAb, b_negAb = sb(es_top, "negAb", [128, 512], BF16)
    negTb, b_negTb = sb(es_top, "negTb", [128, 512], BF16)
    ones4, b_ones4 = sb(es_top, "ones4", [4, 128])
    ones64, b_ones64 = sb(es_top, "ones64", [64, 64])
    gC, b_gC = sb(es_top, "gC", [64, 256])
    CONSTS = []
    k.dma(identf, dr["c_identf"], W=[b_identf])
    k.dma(cmask, dr["c_cmask"], W=[b_cmask])
    k.dma(ublk, dr["c_ublk"], W=[b_ublk])
    k.dma(oblk, dr["c_oblk"], W=[b_oblk])
    k.dma(cind, dr["c_cind"], W=[b_cind])
    k.dma(sel, dr["c_sel"], W=[b_sel])
    k.dma(ones4, dr["c_ones4"], W=[b_ones4])
    k.dma(ones64, dr["c_ones64"], W=[b_ones64])
    k.dma(gC, dr["c_gC"], W=[b_gC])
    cp(identb, identf, [b_identf], [b_identb])
    for (dst_, b_dst_, nm_) in [(negAb, b_negAb, "c_negA4"), (negTb, b_negTb, "c_negT4")]:
        s_ap, s_b = stg_c
        k.dma(s_ap[:, 0:512], dr[nm_], W=[s_b])
        cp(dst_, s_ap[:, 0:512], [s_b], [b_dst_])
    sel3 = v3(sel, 128)
    try:
        ckpt(1)
    except StopBuild:
        k.barrier()
        return nc, k

    xt, b_xt = sb(es_top, "xt", [128, D])
    xo, b_xo = sb(es_top, "xo", [128, D])
    xn, b_xn = sb(es_top, "xn", [128, D], BF16)
    hT, b_hT = sb(es_top, "hT", [128, 8, 128], BF16)
    ss, b_ss = sb(es_top, "ss", [128, 1])
    nw, b_nw = sb(es_top, "nw", [128, 8])
    stg_i = {"i": 0}

    def load_w(dst, b_dst, src, KC, N, scale):
        for kc in range(KC):
            for n0 in range(0, N, 1024):
                n1 = min(N, n0 + 1024)
                s_ap, s_b = stg[stg_i["i"] % 2]
                eng = "dve" if stg_i["i"] % 2 == 0 else "pool"
                stg_i["i"] += 1
                k.dma(s_ap[:, :n1 - n0], src[kc * 128:(kc + 1) * 128, n0:n1], W=[s_b])
                if scale is not None:
                    ts(dst[:, kc, n0:n1], s_ap[:, :n1 - n0], scale[:, kc:kc + 1], ALU.mult,
                       [s_b, b_nw], [b_dst], eng=eng)
                else:
                    cp(dst[:, kc, n0:n1], s_ap[:, :n1 - n0], [s_b], [b_dst], eng=eng)

    def norm_to_hT(src_xt):
        act(xn, src_xt, AF.Square, [b_xt], [b_xn, b_ss], accum=ss)
        act(ss, ss, AF.Sqrt, [b_ss], [b_ss], bias=1e-6, scale=1.0 / D)
        rcp(ss, ss, [b_ss], [b_ss])
        ts(xn, src_xt, ss, ALU.mult, [b_xt, b_ss], [b_xn])
        pb, b_pb = psum()
        pbv = v3(pb.bitcast(BF16), 128)
        for kc in range(8):
            tr(pbv[:, kc, :], xn[:, kc * 128:(kc + 1) * 128], identb, [b_xn, b_identb], [b_pb])
        cp(hT, pbv, [b_pb], [b_hT], eng="act")
        pfree(b_pb)

    def pass_a(l):
        es = ExitStack()
        Wi, b_Wi = sb(es, "Wi", [128, 8, DIN], BF16)
        Wo, b_Wo = sb(es, "Wo", [128, 8, D], BF16)
        Dc, b_Dc = sb(es, "Dc", [128, 62, 128], BF16)
        ccw, b_ccw = sb(es, "ccw", [128, 2, 31])
        ccb, b_ccb = sb(es, "ccb", [128, 2])
        gcw, b_gcw = sb(es, "gcw", [64, 12, 4])
        rnw, b_rnw = sb(es, "rnw", [128, 256])
        clnw, b_clnw = sb(es, "clnw", [128, 256])
        clnb, b_clnb = sb(es, "clnb", [128, 256])
        gnw, b_gnw = sb(es, "gnw", [128, 64])
        hnw, b_hnw = sb(es, "hnw", [128, 64])
        negA, b_negA = sb(es, "negA", [128, 4])
        dtb, b_dtb = sb(es, "dtb", [128, 4])
        lbl, b_lbl = sb(es, "lbl", [128, 2, 256])
        lbt, b_lbt = sb(es, "lbt", [128, 256])
        oml, b_oml = sb(es, "oml", [128, 256])
        with nc.allow_non_contiguous_dma(reason="tiny param loads"):
            k.dma(nw, dr["norm_mix_w"][l].rearrange("(c p) -> p c", p=128), W=[b_nw])
            for ct in range(2):
                k.dma(ccw[:, ct, :], dr["conf_conv_w"][l][:, ct * 128:(ct + 1) * 128].rearrange("k c -> c k"), W=[b_ccw])
            k.dma(ccb, dr["conf_conv_b"][l].rearrange("(ct c) -> c ct", c=128), W=[b_ccb])
            for s_ in range(12):
                k.dma(gcw[:, s_, :], dr["gdn_conv_w"][l][:, s_ * 64:(s_ + 1) * 64].rearrange("k d -> d k"), W=[b_gcw])
            k.dma(rnw, dr["ret_norm_w"][l].partition_broadcast(128), W=[b_rnw])
            k.dma(clnw, dr["conf_ln_w"][l].partition_broadcast(128), W=[b_clnw])
            k.dma(clnb, dr["conf_ln_b"][l].partition_broadcast(128), W=[b_clnb])
            k.dma(gnw, dr["gdn_norm_w"][l].partition_broadcast(128), W=[b_gnw])
            k.dma(hnw, dr["hgrn_norm_w"][l].partition_broadcast(128), W=[b_hnw])
            k.dma(negA, dr["gdn_A_log"][l].partition_broadcast(128), W=[b_negA])
            k.dma(dtb, dr["gdn_dt_bias"][l].partition_broadcast(128), W=[b_dtb])
            k.dma(lbl[:, 0, :], dr["hgrn_lb_logits"][0].partition_broadcast(128), W=[b_lbl])
            k.dma(lbl[:, 1, :], dr["hgrn_lb_logits"][1].partition_broadcast(128), W=[b_lbl])
        ckpt(2)
        act(negA, negA, AF.Exp, [b_negA], [b_negA])
        ts(negA, negA, -1.0, ALU.mult, [b_negA], [b_negA])
        if l == 0:
            mset(lbt, 0.0, [b_lbt])
            mset(oml, 1.0, [b_oml])
        else:
            act(lbl, lbl, AF.Exp, [b_lbl], [b_lbl])
            tt(oml, lbl[:, 0, :], lbl[:, 1, :], ALU.add, [b_lbl], [b_oml])
            rcp(oml, oml, [b_oml], [b_oml])
            tt(lbt, lbl[:, 1, :], oml, ALU.mult, [b_lbl, b_oml], [b_lbt])
            ts(oml, lbt, -1.0, ALU.mult, [b_lbt], [b_oml], s2=1.0, op1=ALU.add)
        for kk in range(31):
            for ct in range(2):
                ts(Dc[:, kk * 2 + ct, :], identf, ccw[:, ct, kk:kk + 1], ALU.mult, [b_identf, b_ccw], [b_Dc])
        ckpt(3)
        load_w(Wi, b_Wi, dr["w_in"][l], 8, DIN, nw)
        load_w(Wo, b_Wo, dr["w_out"][l], 8, D, None)

        ckpt(4)
        rope_t, b_rope = sb(es, "rope_t", [128, 512])
        rqk, b_rqk = sb(es, "rqk", [128, 512])
        ctsg, b_ctsg = sb(es, "ctsg", [128, 256])
        glt, b_glt = sb(es, "glt", [128, 256], BF16)
        gpt, b_gpt = sb(es, "gpt", [128, 768], BF16)
        Gblk, b_Gblk = sb(es, "Gblk", [4, 512])
        nGblk, b_nGblk = sb(es, "nGblk", [4, 512])
        hqf, b_hqf = sb(es, "hqf", [128, 256])
        t1, b_t1 = sb(es, "t1", [128, 4, 32])
        t2, b_t2 = sb(es, "t2", [128, 4, 32])
        qrt, b_qrt = sb(es, "qrt", [128, 4, 64], BF16)
        krt, b_krt = sb(es, "krt", [128, 4, 64], BF16)
        qkT, b_qkT = sb(es, "qkT", [64, 8, 128], BF16)
        PTr, b_PTr = sb(es, "PTr", [128, 4, 128], BF16)
        rvb, b_rvb = sb(es, "rvb", [128, 256], BF16)
        Sr32, b_Sr32 = sb(es, "Sr32", [64, 256])
        Srb, b_Srb = sb(es, "Srb", [64, 4, 64], BF16)
        silr, b_silr = sb(es, "silr", [128, 256])
        silg, b_silg = sb(es, "silg", [128, 256])
        silh, b_silh = sb(es, "silh", [128, 256])
        hsq, b_hsq = sb(es, "hsq", [128, 256])
        hy, b_hy = sb(es, "hy", [128, 256])
        s1, b_s1 = sb(es, "s1", [128, 4])
        s2, b_s2 = sb(es, "s2", [128, 4])
        mean, b_mean = sb(es, "mean", [128, 4])
        msq, b_msq = sb(es, "msq", [128, 4])
        var, b_var = sb(es, "var", [128, 4])
        hf, b_hf = sb(es, "hf", [128, 256])
        logf, b_logf = sb(es, "logf", [128, 256])
        hkk, b_hkk = sb(es, "hkk", [128, 256])
        Gsb, b_Gsb = sb(es, "Gsb", [128, 256])
        eG, b_eG = sb(es, "eG", [128, 256])
        enG, b_enG = sb(es, "enG", [128, 256])
        hD, b_hD = sb(es, "hD", [128, 256])
        qh, b_qh = sb(es, "qh", [128, 256], BF16)
        kht, b_kht = sb(es, "kht", [128, 256], BF16)
        khat, b_khat = sb(es, "khat", [128, 256], BF16)
        hvb, b_hvb = sb(es, "hvb", [128, 256], BF16)
        hq = [sb(es, "hq%d" % c, [64, 4, 128], BF16) for c in range(2)]
        hkT, b_hkT = sb(es, "hkT", [64, 4, 128], BF16)
        PTh, b_PTh = sb(es, "PTh", [128, 4, 128], BF16)
        Sh32, b_Sh32 = sb(es, "Sh32", [64, 4, 64])
        Shb = [sb(es, "Shb%d" % c, [64, 4, 64], BF16) for c in range(3)]
        eGl, b_eGl = sb(es, "eGl", [64, 8])
        gl, b_gl = sb(es, "gl", [128, 2, 158], BF16)
        yT, b_yT = sb(es, "yT", [128, 2, 128])
        yc, b_yc = sb(es, "yc", [128, 256])
        c1, b_c1 = sb(es, "c1", [128, 1])
        c2, b_c2 = sb(es, "c2", [128, 1])
        c3, b_c3 = sb(es, "c3", [128, 1])
        gp, b_gp = sb(es, "gp", [64, 12, 131], BF16)
        gx, b_gx = sb(es, "gx", [64, 12, 128])
        gsq, b_gsq = sb(es, "gsq", [64, 4, 128])
        grs, b_grs = sb(es, "grs", [64, 4, 128])
        gqT, b_gqT = sb(es, "gqT", [64, 4, 128], BF16)
        gkT, b_gkT = sb(es, "gkT", [64, 4, 128], BF16)
        gvT, b_gvT = sb(es, "gvT", [64, 4, 128], BF16)
        grk, b_grk = sb(es, "grk", [128, 4, 64], BF16)
        gkh, b_gkh = sb(es, "gkh", [128, 4, 64], BF16)
        gbv, b_gbv = sb(es, "gbv", [128, 4, 64], BF16)
        EA, b_EA = sb(es, "EA", [128, 4, 128])
        ET, b_ET = sb(es, "ET", [128, 4, 128])
        eGr, b_eGr = sb(es, "eGr", [64, 4, 128])
        gq = [sb(es, "gq%d" % c, [64, 4, 128], BF16) for c in range(2)]
        gw = [sb(es, "gw%d" % c, [64, 4, 128], BF16) for c in range(2)]
        Pm, b_Pm = sb(es, "Pm", [128, 4, 128], BF16)
        PmT, b_PmT = sb(es, "PmT", [128, 4, 128], BF16)
        Wt, b_Wt = sb(es, "Wt", [128, 4, 128], BF16)
        attT, b_attT = sb(es, "attT", [128, 4, 128], BF16)
        usb, b_usb = sb(es, "usb", [128, 256])
        vn, b_vn = sb(es, "vn", [128, 256], BF16)
        Sg32, b_Sg32 = sb(es, "Sg32", [64, 4, 64])
        Sgb = [sb(es, "Sgb%d" % c, [64, 4, 64], BF16) for c in range(3)]
        beta, b_beta = sb(es, "beta", [128, 4])
        gba, b_gba = sb(es, "gba", [128, 8])
        lnb, b_lnb = sb(es, "lnb", [128, 4])
        gsp, b_gsp = sb(es, "gsp", [128, 4])
        gg_, b_gg = sb(es, "gg_", [128, 4])
        Gs, b_Gs = sb(es, "Gs", [128, 4])
        nGs, b_nGs = sb(es, "nGs", [128, 4])
        Bs, b_Bs = sb(es, "Bs", [128, 4])
        eB, b_eB = sb(es, "eB", [128, 4])
        eDl, b_eDl = sb(es, "eDl", [128, 4])
        GTs, b_GTs = sb(es, "GTs", [4, 128])
        mix, b_mix = sb(es, "mix", [128, D], BF16)
        mT, b_mT = sb(es, "mT", [128, 8, 128], BF16)
        mixf, b_mixf = (xo, b_xo) if dbg else (None, None)

        mset(Sr32, 0.0, [b_Sr32]); mset(Srb, 0.0, [b_Srb])
        mset(Sh32, 0.0, [b_Sh32]); mset(Sg32, 0.0, [b_Sg32])
        for c in range(3):
            mset(Shb[c][0], 0.0, [Shb[c][1]]); mset(Sgb[c][0], 0.0, [Sgb[c][1]])
        for c in range(2):
            mset(hq[c][0], 0.0, [hq[c][1]]); mset(gq[c][0], 0.0, [gq[c][1]]); mset(gw[c][0], 0.0, [gw[c][1]])
        mset(gl, 0.0, [b_gl]); mset(gp, 0.0, [b_gp])
        mset(mix, 0.0, [b_mix])

        ckpt(5)

        def head_norm(o, b_o, sil, b_sil, w3, b_w, center, eps, dst3):
            o3 = v3(o, 64)
            red(s1, o3, [b_o], [b_s1])
            act(hsq, o, AF.Square, [b_o], [b_hsq])
            red(s2, v3(hsq, 64), [b_hsq], [b_s2])
            if center:
                ts(mean, s1, 1.0 / 64, ALU.mult, [b_s1], [b_mean])
                tt(msq, mean, mean, ALU.mult, [b_mean], [b_msq])
                stt(var, s2, 1.0 / 64, msq, ALU.mult, ALU.subtract, [b_s2, b_msq], [b_var])
            else:
                ts(var, s2, 1.0 / 64, ALU.mult, [b_s2], [b_var])
            act(var, var, AF.Sqrt, [b_var], [b_var], bias=eps)
            rcp(var, var, [b_var], [b_var])
            hy3 = v3(hy, 64)
            if center:
                tt(hy3, o3, mean.unsqueeze(2).broadcast_to([128, 4, 64]), ALU.subtract, [b_o, b_mean], [b_hy])
                tt(hy3, hy3, var.unsqueeze(2).broadcast_to([128, 4, 64]), ALU.mult, [b_hy, b_var], [b_hy])
            else:
                tt(hy3, o3, var.unsqueeze(2).broadcast_to([128, 4, 64]), ALU.mult, [b_o, b_var], [b_hy])
            tt(hy3, hy3, w3, ALU.mult, [b_hy, b_w], [b_hy])
            tt(dst3, hy3, v3(sil, 64), ALU.mult, [b_hy, b_sil], [b_mix])

        for i in range(NT):
            pv, mid, end = (2 * i) % 3, (2 * i + 1) % 3, (2 * i + 2) % 3
            r0, r1 = slice(0, 64), slice(64, 128)
            src = dr["x"] if l == 0 else xs
            k.dma(xt, src[i * 128:(i + 1) * 128, :], R=([] if l == 0 else [xsb[i]]), W=[b_xt])
            k.dma(rope_t, dr["c_rope"][i * 128:(i + 1) * 128, :], W=[b_rope])
            ckpt(6)
            norm_to_hT(xt)
            ckpt(7)
            P = []
            for (c0, c1_) in [(0, 512), (512, 1024), (2304, 2568), (2568, 3080), (3080, 3592)]:
                pg, b_pg = psum()
                for kc in range(8):
                    mm(pg[:, :c1_ - c0], hT[:, kc, :], Wi[:, kc, c0:c1_], [b_hT, b_Wi], [b_pg], kc == 0, kc == 7)
                P.append((pg, b_pg))
                if len(P) == 1:
                    cp(rqk, pg, [b_pg], [b_rqk], eng="act")
                if len(P) == 4:
                    cp(hqf, pg[:, 0:256], [b_pg], [b_hqf])
                    act(hf, pg[:, 256:512], AF.Sigmoid, [b_pg], [b_hf])
                if len(P) == 2:
                    silu(silr, pg[:, 256:512], [b_pg], [b_silr], hsq, b_hsq)
                    cp(rvb, pg[:, 0:256], [b_pg], [b_rvb])
                if len(P) == 3:
                    silu(silg, pg[:, 0:256], [b_pg], [b_silg], hsq, b_hsq)
                    cp(gba, pg[:, 256:264], [b_pg], [b_gba])
                if len(P) == 5:
                    silu(silh, pg[:, 256:512], [b_pg], [b_silh], hsq, b_hsq)
                    cp(hvb, pg[:, 0:256], [b_pg], [b_hvb])
                pfree(b_pg)
            pg, b_pg = psum()
            for kc in range(8):
                mm(pg, hT[:, kc, :], Wi[:, kc, 1024:1536], [b_hT, b_Wi], [b_pg], kc == 0, kc == 7)
            act(ctsg, pg[:, 256:512], AF.Sigmoid, [b_pg], [b_ctsg])
            tt(glt, pg[:, 0:256], ctsg, ALU.mult, [b_pg, b_ctsg], [b_glt])
            pfree(b_pg)
            pg, b_pg = psum()
            for kc in range(8):
                mm(pg, hT[:, kc, :], Wi[:, kc, 1536:2048], [b_hT, b_Wi], [b_pg], kc == 0, kc == 7)
            cp(gpt[:, 0:512], pg, [b_pg], [b_gpt], eng="act")
            pfree(b_pg)
            pg, b_pg = psum()
            for kc in range(8):
                mm(pg[:, 0:256], hT[:, kc, :], Wi[:, kc, 2048:2304], [b_hT, b_Wi], [b_pg], kc == 0, kc == 7)
            cp(gpt[:, 512:768], pg[:, 0:256], [b_pg], [b_gpt])
            pfree(b_pg)
            ckpt(8)

            def g_ret():
                rt4 = rope_t.rearrange("p (a h d) -> p a h d", a=4, h=4)
                for (off, ci, si, dst, b_dst) in [(0, 0, 1, qrt, b_qrt), (256, 2, 3, krt, b_krt)]:
                    X = v3(rqk[:, off:off + 256], 64)
                    x1, x2 = X[:, :, 0:32], X[:, :, 32:64]
                    cs, sn = rt4[:, ci], rt4[:, si]
                    tt(t1, x1, cs, ALU.mult, [b_rqk, b_rope], [b_t1])
                    tt(t2, x2, sn, ALU.mult, [b_rqk, b_rope], [b_t2])
                    tt(dst[:, :, 0:32], t1, t2, ALU.subtract, [b_t1, b_t2], [b_dst])
                    tt(t1, x1, sn, ALU.mult, [b_rqk, b_rope], [b_t1])
                    tt(t2, x2, cs, ALU.mult, [b_rqk, b_rope], [b_t2])
                    tt(dst[:, :, 32:64], t1, t2, ALU.add, [b_t1, b_t2], [b_dst])
                pb, b_pb = yield from galloc()
                pbv = v3(pb.bitcast(BF16), 128)
                for h in range(4):
                    tr(pbv[0:64, h, :], qrt[:, h, :], identb, [b_qrt, b_identb], [b_pb])
                    tr(pbv[0:64, 4 + h, :], krt[:, h, :], identb, [b_krt, b_identb], [b_pb])
                cp(qkT, pbv[0:64], [b_pb], [b_qkT], eng="act")
                pfree(b_pb)
                yield
                ps, b_ps = yield from galloc()
                ps3 = v3(ps, 128)
                for h in range(4):
                    mm(ps3[:, h, :], qkT[:, 4 + h, :], qkT[:, h, :], [b_qkT], [b_ps])
                tt(PTr, ps3, cmask.unsqueeze(1).broadcast_to([128, 4, 128]), ALU.mult, [b_ps, b_cmask], [b_PTr])
                pfree(b_ps)
                yield
                po, b_po = yield from galloc()
                for h in range(4):
                    hs = slice(h * 64, (h + 1) * 64)
                    mm(po[:, hs], PTr[:, h, :], rvb[:, hs], [b_PTr, b_rvb], [b_po], True, False)
                    mm(po[:, hs], qkT[:, h, :], Srb[:, h, :], [b_qkT, b_Srb], [b_po], False, True)
                pd, b_pd = yield from galloc()
                for h in range(4):
                    hs = slice(h * 64, (h + 1) * 64)
                    mm(pd[0:64, hs], krt[:, h, :], rvb[:, hs], [b_krt, b_rvb], [b_pd])
                tt(Sr32, Sr32, pd[0:64, 0:256], ALU.add, [b_Sr32, b_pd], [b_Sr32])
                tt(Sr32, Sr32, gC, ALU.mult, [b_Sr32, b_gC], [b_Sr32])
                cp(Srb, v3(Sr32, 64), [b_Sr32], [b_Srb], eng="pool")
                pfree(b_pd)
                yield
                head_norm(po[:, 0:256], b_po, silr, b_silr, v3(rnw, 64), b_rnw, True, 1e-5, v3(mix[:, 0:256], 64))
                pfree(b_po)

            def g_hgrn():
                tt(hf, hf, oml, ALU.mult, [b_hf, b_oml], [b_hf])
                tt(hf, hf, lbt, ALU.add, [b_hf, b_lbt], [b_hf])
                act(logf, hf, AF.Ln, [b_hf], [b_logf])
                ts(hkk, hf, -1.0, ALU.mult, [b_hf], [b_hkk], s2=1.0, op1=ALU.add)
                pG, b_pG = yield from galloc()
                mm(pG[:, 0:256], ublk, logf, [b_ublk, b_logf], [b_pG])
                mm(pG[:, 256:512], oblk, logf, [b_oblk, b_logf], [b_pG])
                pL, b_pL = yield from galloc()
                for h in range(4):
                    mm(pL[0:64, h * 2:(h + 1) * 2], logf[:, h * 64:(h + 1) * 64], cind, [b_logf, b_cind], [b_pL])
                act(eGl, pL[0:64, 0:8], AF.Exp, [b_pL], [b_eGl])
                pfree(b_pL)
                cp(Gsb, pG[:, 0:256], [b_pG], [b_Gsb], eng="act")
                act(eG, pG[:, 0:256], AF.Exp, [b_pG], [b_eG])
                tt(qh, hqf, eG, ALU.mult, [b_hqf, b_eG], [b_qh])
                act(enG, pG[:, 0:256], AF.Exp, [b_pG], [b_enG], scale=-1.0)
                tt(kht, hkk, enG, ALU.mult, [b_hkk, b_enG], [b_kht])
                tt(hD, pG[:, 256:512], Gsb, ALU.subtract, [b_pG, b_Gsb], [b_hD])
                pfree(b_pG)
                yield
                act(hD, hD, AF.Exp, [b_hD], [b_hD])
                tt(khat, hkk, hD, ALU.mult, [b_hkk, b_hD], [b_khat])
                pb, b_pb = yield from galloc()
                pbv = v3(pb.bitcast(BF16), 128)
                for h in range(4):
                    tr(pbv[0:64, h, :], qh[:, h * 64:(h + 1) * 64], identb, [b_qh, b_identb], [b_pb])
                    tr(pbv[0:64, 4 + h, :], kht[:, h * 64:(h + 1) * 64], identb, [b_kht, b_identb], [b_pb])
                cp(hq[0][0][:, :, 0:64], pbv[0:64, 0:4, 0:64], [b_pb], [hq[0][1]], eng="act")
                cp(hq[1][0][:, :, 64:128], pbv[0:64, 0:4, 64:128], [b_pb], [hq[1][1]], eng="act")
                cp(hkT, pbv[0:64, 4:8, :], [b_pb], [b_hkT])
                pfree(b_pb)
                yield
                ps, b_ps = yield from galloc()
                ps3 = v3(ps, 128)
                for h in range(4):
                    for c in range(2):
                        cs_ = slice(c * 64, (c + 1) * 64)
                        mm(ps3[:, h, cs_], hkT[:, h, :], hq[c][0][:, h, cs_], [b_hkT, hq[c][1]], [b_ps])
                tt(PTh, ps3, ublk.unsqueeze(1).broadcast_to([128, 4, 128]), ALU.mult, [b_ps, b_ublk], [b_PTh])
                pfree(b_ps)
                yield
                for c, (rr, sdst) in enumerate([(r0, mid), (r1, end)]):
                    pd, b_pd = yield from galloc()
                    for h in range(4):
                        hs = slice(h * 64, (h + 1) * 64)
                        mm(pd[0:64, hs], khat[rr, hs], hvb[rr, hs], [b_khat, b_hvb], [b_pd])
                    for h in range(4):
                        hs = slice(h * 64, (h + 1) * 64)
                        stt(Sh32[:, h, :], Sh32[:, h, :], eGl[:, h * 2 + c:h * 2 + c + 1], pd[0:64, hs],
                            ALU.mult, ALU.add, [b_Sh32, b_eGl, b_pd], [b_Sh32])
                    cp(Shb[sdst][0], Sh32, [b_Sh32], [Shb[sdst][1]], eng="pool")
                    pfree(b_pd)
                    yield
                po, b_po = yield from galloc()
                for h in range(4):
                    hs = slice(h * 64, (h + 1) * 64)
                    mm(po[:, hs], PTh[:, h, :], hvb[:, hs], [b_PTh, b_hvb], [b_po], True, False)
                    mm(po[:, hs], hq[0][0][:, h, :], Shb[pv][0][:, h, :], [hq[0][1], Shb[pv][1]], [b_po], False, False)
                    mm(po[:, hs], hq[1][0][:, h, :], Shb[mid][0][:, h, :], [hq[1][1], Shb[mid][1]], [b_po], False, True)
                head_norm(po[:, 0:256], b_po, silh, b_silh, hnw.unsqueeze(1).broadcast_to([128, 4, 64]), b_hnw,
                          False, 1e-6, v3(mix[:, 768:1024], 64))
                pfree(b_po)

            def g_conf():
                pc, b_pc = yield from galloc()
                pc3 = v3(pc.bitcast(BF16)[:, 0:256], 128)
                for ct in range(2):
                    tr(pc3[:, ct, :], glt[:, ct * 128:(ct + 1) * 128], identb, [b_glt, b_identb], [b_pc])
                cp(gl[:, :, 30:158], pc3, [b_pc], [b_gl], eng="act")
                pfree(b_pc)
                yield
                ckpt(21)
                py, b_py = yield from galloc()
                py3 = v3(py, 128)
                for ct in range(2):
                    for kk in range(31):
                        mm(py3[:, ct, :], Dc[:, kk * 2 + ct, :], gl[:, ct, kk:kk + 128], [b_Dc, b_gl], [b_py],
                           kk == 0, kk == 30)
                ckpt(22)
                for ct in range(2):
                    act(yT[:, ct, :], py3[:, ct, :], AF.Identity, [b_py, b_ccb], [b_yT], bias=ccb[:, ct:ct + 1])
                pfree(b_py)
                yield
                cp(gl[:, :, 0:30], gl[:, :, 128:158], [b_gl], [b_gl], eng="pool")
                ckpt(24)
                pt, b_pt = yield from galloc()
                for ct in range(2):
                    tr(pt[:, ct * 128:(ct + 1) * 128], yT[:, ct, :], identf, [b_yT, b_identf], [b_pt])
                ckpt(25)
                red(c1, pt[:, 0:256].unsqueeze(1), [b_pt], [b_c1])
                act(yc, pt[:, 0:256], AF.Square, [b_pt], [b_yc])
                red(c2, yc.unsqueeze(1), [b_yc], [b_c2])
                ckpt(27)
                ts(c1, c1, 1.0 / 256, ALU.mult, [b_c1], [b_c1])
                tt(c3, c1, c1, ALU.mult, [b_c1], [b_c3])
                stt(c2, c2, 1.0 / 256, c3, ALU.mult, ALU.subtract, [b_c2, b_c3], [b_c2])
                ckpt(28)
                act(c2, c2, AF.Sqrt, [b_c2], [b_c2], bias=1e-5)
                rcp(c2, c2, [b_c2], [b_c2])
                ckpt(29)
                ts(yc, pt[:, 0:256], c1, ALU.subtract, [b_pt, b_c1, b_c2], [b_yc], s2=c2, op1=ALU.mult)
                pfree(b_pt)
                ckpt(30)
                tt(yc, yc, clnw, ALU.mult, [b_yc, b_clnw], [b_yc])
                tt(yc, yc, clnb, ALU.add, [b_yc, b_clnb], [b_yc])
                silu(mix[:, 256:512], yc, [b_yc], [b_mix], hsq, b_hsq)

            def g_gdn():
                pb1, b_pb1 = yield from galloc()
                pb2, b_pb2 = yield from galloc()
                v1 = v3(pb1.bitcast(BF16), 128)
                v2 = v3(pb2.bitcast(BF16)[:, 0:512], 128)
                for s in range(12):
                    dst_ = v1[0:64, s, :] if s < 8 else v2[0:64, s - 8, :]
                    tr(dst_, gpt[:, s * 64:(s + 1) * 64], identb, [b_gpt, b_identb], [b_pb1 if s < 8 else b_pb2])
                cp(gp[:, 0:8, 3:131], v1[0:64], [b_pb1], [b_gp], eng="act")
                cp(gp[:, 8:12, 3:131], v2[0:64], [b_pb2], [b_gp])
                pfree(b_pb1, b_pb2)
                yield
                for s in range(12):
                    ts(gx[:, s, :], gp[:, s, 0:128], gcw[:, s, 0:1], ALU.mult, [b_gp, b_gcw], [b_gx])
                    for kk in range(1, 4):
                        stt(gx[:, s, :], gp[:, s, kk:kk + 128], gcw[:, s, kk:kk + 1], gx[:, s, :],
                            ALU.mult, ALU.add, [b_gp, b_gcw, b_gx], [b_gx])
                    if s % 4 == 3:
                        yield
                for b in range(3):
                    silu(gx[:, b * 4:(b + 1) * 4, :], gx[:, b * 4:(b + 1) * 4, :], [b_gx], [b_gx], gsq, b_gsq)
                yield
                cp(gp[:, :, 0:3], gp[:, :, 128:131], [b_gp], [b_gp], eng="pool")
                for b in range(2):
                    xb = gx[:, b * 4:(b + 1) * 4, :]
                    tt(gsq, xb, xb, ALU.mult, [b_gx], [b_gsq])
                    pn, b_pn = yield from galloc()
                    pn3 = v3(pn, 128)
                    for h in range(4):
                        mm(pn3[0:64, h, :], ones64, gsq[:, h, :], [b_ones64, b_gsq], [b_pn])
                    act(grs, pn3[0:64], AF.Sqrt, [b_pn], [b_grs], bias=1e-6)
                    pfree(b_pn)
                    rcp(grs, grs, [b_grs], [b_grs])
                    if b == 0:
                        stt(gqT, xb, 0.125, grs, ALU.mult, ALU.mult, [b_gx, b_grs], [b_gqT])
                    else:
                        tt(gkT, xb, grs, ALU.mult, [b_gx, b_grs], [b_gkT])
                cp(gvT, gx[:, 8:12, :], [b_gx], [b_gvT], eng="act")
                act(beta, gba[:, 0:4], AF.Sigmoid, [b_gba], [b_beta])
                act(lnb, beta, AF.Ln, [b_beta], [b_lnb])
                tt(gsp, gba[:, 4:8], dtb, ALU.add, [b_gba, b_dtb], [b_gsp])
                act(gsp, gsp, AF.Exp, [b_gsp], [b_gsp])
                act(gsp, gsp, AF.Ln, [b_gsp], [b_gsp], bias=1.0)
                tt(gg_, gsp, negA, ALU.mult, [b_gsp, b_negA], [b_gg])
                pq, b_pq = yield from galloc()
                mm(pq[:, 0:4], ublk, gg_, [b_ublk, b_gg], [b_pq])
                mm(pq[:, 4:8], oblk, gg_, [b_oblk, b_gg], [b_pq])
                mm(pq[0:4, 128:256], gg_, ublk, [b_gg, b_ublk], [b_pq])
                cp(Gs, pq[:, 0:4], [b_pq], [b_Gs])
                ts(nGs, Gs, -1.0, ALU.mult, [b_Gs], [b_nGs])
                tt(Bs, Gs, lnb, ALU.add, [b_Gs, b_lnb], [b_Bs])
                act(eB, Bs, AF.Exp, [b_Bs], [b_eB])
                tt(eDl, pq[:, 4:8], Gs, ALU.subtract, [b_pq, b_Gs], [b_eDl])
                act(eDl, eDl, AF.Exp, [b_eDl], [b_eDl])
                cp(GTs, pq[0:4, 128:256], [b_pq], [b_GTs])
                pfree(b_pq)
                yield
                pb, b_pb = yield from galloc()
                pbv = v3(pb.bitcast(BF16)[:, 0:512], 64)
                for h in range(4):
                    tr(pbv[:, h, :], gkT[:, h, :], identb[0:64, 0:64], [b_gkT, b_identb], [b_pb])
                    tr(pbv[:, 4 + h, :], gvT[:, h, :], identb[0:64, 0:64], [b_gvT, b_identb], [b_pb])
                tt(grk, pbv[:, 0:4, :], eB.unsqueeze(2).broadcast_to([128, 4, 64]), ALU.mult, [b_pb, b_eB], [b_grk])
                tt(gkh, pbv[:, 0:4, :], eDl.unsqueeze(2).broadcast_to([128, 4, 64]), ALU.mult, [b_pb, b_eDl], [b_gkh])
                tt(gbv, pbv[:, 4:8, :], beta.unsqueeze(2).broadcast_to([128, 4, 64]), ALU.mult, [b_pb, b_beta], [b_gbv])
                pfree(b_pb)
                yield
                tt(v3(Gblk, 128), GTs.unsqueeze(1).broadcast_to([4, 4, 128]), sel3, ALU.mult, [b_GTs, b_sel], [b_Gblk])
                ts(nGblk, Gblk, -1.0, ALU.mult, [b_Gblk], [b_nGblk])
                pea, b_pea = yield from galloc()
                pea3 = v3(pea, 128)
                mm(pea, ones4, nGblk, [b_ones4, b_nGblk], [b_pea], True, False)
                mm(pea, identb, negAb, [b_identb, b_negAb], [b_pea], False, True)
                for h in range(4):
                    act(EA[:, h, :], pea3[:, h, :], AF.Exp, [b_pea, b_Bs], [b_EA], bias=Bs[:, h:h + 1])
                pfree(b_pea)
                pet, b_pet = yield from galloc()
                pet3 = v3(pet, 128)
                mm(pet, ones4, Gblk, [b_ones4, b_Gblk], [b_pet], True, False)
                mm(pet, identb, negTb, [b_identb, b_negTb], [b_pet], False, True)
                for h in range(4):
                    act(ET[:, h, :], pet3[:, h, :], AF.Exp, [b_pet, b_nGs], [b_ET], bias=nGs[:, h:h + 1])
                pfree(b_pet)
                peg, b_peg = yield from galloc()
                peg3 = v3(peg, 128)
                mm(peg[0:64, :], ones4[:, 0:64], Gblk, [b_ones4, b_Gblk], [b_peg])
                act(eGr, peg3[0:64], AF.Exp, [b_peg], [b_eGr])
                pfree(b_peg)
                yield
                tt(gq[0][0][:, :, 0:64], gqT[:, :, 0:64], eGr[:, :, 0:64], ALU.mult, [b_gqT, b_eGr], [gq[0][1]])
                tt(gq[1][0][:, :, 64:128], gqT[:, :, 64:128], eGr[:, :, 64:128], ALU.mult, [b_gqT, b_eGr], [gq[1][1]])
                pkk, b_pkk = yield from galloc()
                pkq, b_pkq = yield from galloc()
                for h in range(4):
                    mm(v3(pkk, 128)[:, h, :], gkT[:, h, :], gkT[:, h, :], [b_gkT], [b_pkk])
                    mm(v3(pkq, 128)[:, h, :], gkT[:, h, :], gqT[:, h, :], [b_gkT, b_gqT], [b_pkq])
                stt(Pm, v3(pkk, 128), -1.0, EA, ALU.mult, ALU.mult, [b_pkk, b_EA], [b_Pm])
                tt(attT, v3(pkq, 128), ET, ALU.mult, [b_pkq, b_ET], [b_attT])
                pfree(b_pkk, b_pkq)
                yield
                pb, b_pb = yield from galloc()
                pbv = v3(pb.bitcast(BF16)[:, 0:512], 128)
                for h in range(4):
                    tr(pbv[:, h, :], Pm[:, h, :], identb, [b_Pm, b_identb], [b_pb])
                cp(PmT, pbv, [b_pb], [b_PmT], eng="act")
                tt(Wt, pbv, identb.unsqueeze(1).broadcast_to([128, 4, 128]), ALU.add, [b_pb, b_identb], [b_Wt])
                pfree(b_pb)
                yield
                for m in range(5):
                    pa1, b_pa1 = yield from galloc()
                    for h in range(4):
                        mm(v3(pa1, 128)[:, h, :], PmT[:, h, :], Pm[:, h, :], [b_PmT, b_Pm], [b_pa1])
                    if m < 4:
                        pa2, b_pa2 = yield from galloc()
                        for h in range(4):
                            mm(v3(pa2, 128)[:, h, :], Pm[:, h, :], PmT[:, h, :], [b_PmT, b_Pm], [b_pa2])
                    cp(Pm, v3(pa1, 128), [b_pa1], [b_Pm], eng="act")
                    pfree(b_pa1)
                    if m < 4:
                        cp(PmT, v3(pa2, 128), [b_pa2], [b_PmT])
                        pfree(b_pa2)
                    yield
                    pa3, b_pa3 = yield from galloc()
                    for h in range(4):
                        mm(v3(pa3, 128)[:, h, :], Pm[:, h, :], Wt[:, h, :], [b_Pm, b_Wt], [b_pa3])
                    tt(Wt, Wt, v3(pa3, 128), ALU.add, [b_Wt, b_pa3], [b_Wt])
                    pfree(b_pa3)
                    yield
                pu, b_pu = yield from galloc()
                pw, b_pw = yield from galloc()
                for h in range(4):
                    hs = slice(h * 64, (h + 1) * 64)
                    mm(pu[:, hs], Wt[:, h, :], gbv[:, h, :], [b_Wt, b_gbv], [b_pu])
                    mm(v3(pw, 128)[0:64, h, :], grk[:, h, :], Wt[:, h, :], [b_grk, b_Wt], [b_pw])
                cp(usb, pu[:, 0:256], [b_pu], [b_usb])
                cp(gw[0][0][:, :, 0:64], v3(pw, 128)[0:64, :, 0:64], [b_pw], [gw[0][1]], eng="act")
                cp(gw[1][0][:, :, 64:128], v3(pw, 128)[0:64, :, 64:128], [b_pw], [gw[1][1]], eng="act")
                pfree(b_pu, b_pw)
                yield
                for c, (rr, ssrc, sdst) in enumerate([(r0, pv, mid), (r1, mid, end)]):
                    pa, b_pa = yield from galloc()
                    for h in range(4):
                        hs = slice(h * 64, (h + 1) * 64)
                        mm(pa[:, hs], gw[c][0][:, h, :], Sgb[ssrc][0][:, h, :], [gw[c][1], Sgb[ssrc][1]], [b_pa])
                    tt(vn[rr, :], usb[rr, :], pa[rr, 0:256], ALU.subtract, [b_usb, b_pa], [b_vn])
                    pfree(b_pa)
                    pd, b_pd = yield from galloc()
                    for h in range(4):
                        hs = slice(h * 64, (h + 1) * 64)
                        mm(pd[0:64, hs], gkh[rr, h, :], vn[rr, hs], [b_gkh, b_vn], [b_pd])
                    col = 63 + 64 * c
                    for h in range(4):
                        hs = slice(h * 64, (h + 1) * 64)
                        stt(Sg32[:, h, :], Sg32[:, h, :], eGr[:, h, col:col + 1], pd[0:64, hs],
                            ALU.mult, ALU.add, [b_Sg32, b_eGr, b_pd], [b_Sg32])
                    cp(Sgb[sdst][0], Sg32, [b_Sg32], [Sgb[sdst][1]], eng="pool")
                    pfree(b_pd)
                    yield
                po, b_po = yield from galloc()
                for h in range(4):
                    hs = slice(h * 64, (h + 1) * 64)
                    mm(po[:, hs], attT[:, h, :], vn[:, hs], [b_attT, b_vn], [b_po], True, False)
                    mm(po[:, hs], gq[0][0][:, h, :], Sgb[pv][0][:, h, :], [gq[0][1], Sgb[pv][1]], [b_po], False, False)
                    mm(po[:, hs], gq[1][0][:, h, :], Sgb[mid][0][:, h, :], [gq[1][1], Sgb[mid][1]], [b_po], False, True)
                head_norm(po[:, 0:256], b_po, silg, b_silg, gnw.unsqueeze(1).broadcast_to([128, 4, 64]), b_gnw,
                          False, 1e-6, v3(mix[:, 512:768], 64))
                pfree(b_po)

            gl_ = [("gdn", g_gdn), ("hgrn", g_hgrn), ("conf", g_conf), ("ret", g_ret)]
            run_interleaved([g for n_, g in gl_ if n_ not in skip])
            if True:
                if dbg:
                    cp(mixf, mix, [b_mix], [b_mixf], eng="pool")
                    k.dma(dmix[l, i * 128:(i + 1) * 128, :], mixf, R=[b_mixf])
                pb, b_pb = psum()
                pbv = v3(pb.bitcast(BF16), 128)
                for kc in range(8):
                    tr(pbv[:, kc, :], mix[:, kc * 128:(kc + 1) * 128], identb, [b_mix, b_identb], [b_pb])
                cp(mT, pbv, [b_pb], [b_mT], eng="act")
                pfree(b_pb)
                for n in range(2):
                    pso, b_pso = psum()
                    ns = slice(n * 512, (n + 1) * 512)
                    for kc in range(8):
                        mm(pso, mT[:, kc, :], Wo[:, kc, ns], [b_mT, b_Wo], [b_pso], kc == 0, kc == 7)
                    tt(xo[:, ns], xt[:, ns], pso, ALU.add, [b_xt, b_pso], [b_xo])
                    pfree(b_pso)
                k.dma(xs[i * 128:(i + 1) * 128, :], xo, R=[b_xo], W=[xsb[i]])
        k.barrier()
        es.close()

    def pass_b(l, last):
        es = ExitStack()
        Wg, b_Wg = sb(es, "Wg", [128, 8, DFF], BF16)
        Wu, b_Wu = sb(es, "Wu", [128, 8, DFF], BF16)
        Wd, b_Wd = sb(es, "Wd", [128, NFT, D], BF16)
        aT, b_aT = sb(es, "aT", [128, NFT, 128], BF16)
        fsg, b_fsg = sb(es, "fsg", [128, 512])
        fsc, b_fsc = sb(es, "fsc", [128, 512])
        fnw, b_fnw = sb(es, "fnw", [128, D])
        with nc.allow_non_contiguous_dma(reason="tiny param loads"):
            k.dma(nw, dr["norm_ffn_w"][l].rearrange("(c p) -> p c", p=128), W=[b_nw])
            k.dma(fnw, dr["final_norm_w"].partition_broadcast(128), W=[b_fnw])
        load_w(Wg, b_Wg, dr["ffn_w_gate"][l], 8, DFF, nw)
        load_w(Wu, b_Wu, dr["ffn_w_up"][l], 8, DFF, nw)
        load_w(Wd, b_Wd, dr["ffn_w_down"][l], NFT, D, None)
        for i in range(NT):
            k.dma(xt, xs[i * 128:(i + 1) * 128, :], R=[xsb[i]], W=[b_xt])
            norm_to_hT(xt)
            for grp in range(6):
                nf = 4 if grp < 5 else 2
                psg, b_psg = psum()
                psu, b_psu = psum()
                for j in range(nf):
                    ft = grp * 4 + j
                    fs = slice(ft * 128, (ft + 1) * 128)
                    for kc in range(8):
                        mm(psg[:, j * 128:(j + 1) * 128], Wg[:, kc, fs], hT[:, kc, :], [b_Wg, b_hT], [b_psg], kc == 0, kc == 7)
                    for kc in range(8):
                        mm(psu[:, j * 128:(j + 1) * 128], Wu[:, kc, fs], hT[:, kc, :], [b_Wu, b_hT], [b_psu], kc == 0, kc == 7)
                silu(fsg[:, :nf * 128], psg[:, :nf * 128], [b_psg], [b_fsg], fsc[:, :nf * 128], b_fsc)
                tt(aT[:, grp * 4:grp * 4 + nf, :], v3(psu[:, :nf * 128], 128), v3(fsg[:, :nf * 128], 128), ALU.mult,
                   [b_psu, b_fsg], [b_aT])
                pfree(b_psg, b_psu)
            for n in range(2):
                pso, b_pso = psum()
                ns = slice(n * 512, (n + 1) * 512)
                for ft in range(NFT):
                    mm(pso, aT[:, ft, :], Wd[:, ft, ns], [b_aT, b_Wd], [b_pso], ft == 0, ft == NFT - 1)
                tt(xo[:, ns], xt[:, ns], pso, ALU.add, [b_xt, b_pso], [b_xo])
                pfree(b_pso)
            if last:
                act(xn, xo, AF.Square, [b_xo], [b_xn, b_ss], accum=ss)
                act(ss, ss, AF.Sqrt, [b_ss], [b_ss], bias=1e-6, scale=1.0 / D)
                rcp(ss, ss, [b_ss], [b_ss])
                ts(xo, xo, ss, ALU.mult, [b_xo, b_ss], [b_xo])
                tt(xo, xo, fnw, ALU.mult, [b_xo, b_fnw], [b_xo])
                k.dma(out[i * 128:(i + 1) * 128, :], xo, R=[b_xo])
            else:
                k.dma(xs[i * 128:(i + 1) * 128, :], xo, R=[b_xo], W=[xsb[i]])
        k.barrier()
        es.close()

    for l in range(NL):
        try:
            pass_a(l)
        except StopBuild:
            k.barrier()
            return nc, k
        if "b" in skip:
            for i in range(NT):
                k.dma(xt, xs[i * 128:(i + 1) * 128, :], R=[xsb[i]], W=[b_xt])
                k.dma(out[i * 128:(i + 1) * 128, :], xt, R=[b_xt])
        else:
            pass_b(l, l == NL - 1)
    k.barrier()
    es_top.close()
    return nc, k


_CACHE = {}


def kernel(**inputs):
    x = np.asarray(inputs["x"], dtype=np.float32)
    B, T, _ = x.shape
    key = T
    if key not in _CACHE:
        _CACHE[key] = (build(T)[0], host_consts(T))
    nc, consts = _CACHE[key]
    params = {n: np.ascontiguousarray(np.asarray(inputs[n], dtype=np.float32)) for n in PARAM_SHAPES}
    in_maps = []
    for c in range(8):
        m = {"x": np.ascontiguousarray(x[c % B])}
        m.update(params)
        m.update(consts)
        in_maps.append(m)
    res = run_bass_kernel_spmd(nc, in_maps, core_ids=list(range(8)))
    outs = [np.asarray(res.results[c]["out"], dtype=np.float32) for c in range(B)]
    return np.stack(outs, axis=0)
```

```python
from contextlib import ExitStack
import numpy as np
import ml_dtypes
import concourse.bass as bass
import concourse.mybir as mybir
from concourse.bass_utils import run_bass_kernel_spmd

F32 = mybir.dt.float32
BF16 = mybir.dt.bfloat16
AF = mybir.ActivationFunctionType
ALU = mybir.AluOpType
AX = mybir.AxisListType

D = 1024
DIN = 3592
DFF = 2816
NFT = DFF // 128
NEG = -30000.0


class Buf:
    __slots__ = ("name", "w", "r", "excl")

    def __init__(self, name="", excl=False):
        self.name = name
        self.w = None
        self.r = {}
        self.excl = excl


class K:
    NDSEM = 40

    def __init__(self, nc):
        self.nc = nc
        self.eng = {"pe": nc.tensor, "act": nc.scalar, "dve": nc.vector,
                    "pool": nc.gpsimd, "sp": nc.sync}
        self.sem = {e: nc.alloc_semaphore(name="s_" + e) for e in ("pe", "act", "dve", "pool")}
        self.cnt = {e: 0 for e in self.sem}
        self.dsem = [nc.alloc_semaphore(name="d%d" % i) for i in range(self.NDSEM)]
        self.dcnt = [0] * self.NDSEM
        self.dnext = 0
        self.waited = {e: {} for e in self.eng}
        self.nwaits = 0
        self.ninst = 0
        self.rec = None

    def replay(self, item):
        if item[0] == "op":
            self.op(*item[1:])
        else:
            self.dma(*item[1:3], **item[3])

    def _wait(self, eng, ev):
        if ev is None:
            return
        if ev[0] == "e":
            key, val = ("e", ev[1]), ev[2]
            if ev[1] == "pe" and eng == "pe":
                return
            sem = self.sem[ev[1]]
        else:
            key, val = ("d", ev[1]), ev[2]
            sem = self.dsem[ev[1]]
        if self.waited[eng].get(key, 0) >= val:
            return
        self.eng[eng].wait_ge(sem, val)
        self.waited[eng][key] = val
        self.nwaits += 1

    def _deps(self, eng, R, W):
        for b in R:
            self._wait(eng, b.w)
            if b.excl:
                for ev in b.r.values():
                    if not (ev[0] == "e" and ev[1] == eng):
                        self._wait(eng, ev)
        for b in W:
            self._wait(eng, b.w)
            for ev in b.r.values():
                self._wait(eng, ev)

    def _post(self, ev, R, W):
        for b in R:
            b.r[(ev[0], ev[1])] = ev
        for b in W:
            b.w = ev
            b.r = {}

    def op(self, eng, fn, R=(), W=()):
        if self.rec is not None:
            self.rec.append(("op", eng, fn, tuple(R), tuple(W)))
            return None
        self._deps(eng, R, W)
        inst = fn(self.eng[eng])
        self.cnt[eng] += 1
        inst.then_inc(self.sem[eng], 1)
        self._post(("e", eng, self.cnt[eng]), R, W)
        self.ninst += 1
        return inst

    def dma(self, out, in_, R=(), W=(), q="sp", **kw):
        if self.rec is not None:
            self.rec.append(("dma", out, in_, dict(R=tuple(R), W=tuple(W), q=q, **kw)))
            return None
        idx = self.dnext
        self.dnext = (self.dnext + 1) % self.NDSEM
        if self.dcnt[idx] > 0:
            self._wait(q, ("d", idx, self.dcnt[idx]))
        self._deps(q, R, W)
        inst = self.eng[q].dma_start(out=out, in_=in_, **kw)
        self.dcnt[idx] += 16
        inst.then_inc(self.dsem[idx], 16)
        ev = ("d", idx, self.dcnt[idx])
        self._post(ev, R, W)
        self.ninst += 1
        return ev

    def barrier(self):
        for e in self.eng:
            for e2 in self.sem:
                if self.cnt[e2] > 0:
                    self._wait(e, ("e", e2, self.cnt[e2])) if not (e == e2 == "pe") else None
            for i in range(self.NDSEM):
                if self.dcnt[i] > 0:
                    self._wait(e, ("d", i, self.dcnt[i]))


def host_consts(T):
    c = {}
    c["c_identf"] = np.eye(128, dtype=np.float32)
    j = np.arange(128)[:, None]
    i = np.arange(128)[None, :]
    same = (j // 64) == (i // 64)
    c["c_cmask"] = (j <= i).astype(np.float32)
    c["c_ublk"] = ((j <= i) & same).astype(np.float32)
    c["c_oblk"] = same.astype(np.float32)
    cind = np.zeros((128, 2), np.float32)
    cind[:64, 0] = 1.0
    cind[64:, 1] = 1.0
    c["c_cind"] = cind
    sel = np.zeros((4, 4, 128), np.float32)
    for h in range(4):
        sel[h, h, :] = 1.0
    c["c_sel"] = sel.reshape(4, 512)
    c["c_negA"] = np.where(same & (j > i), 0.0, NEG).astype(np.float32)
    c["c_negT"] = np.where(same & (i >= j), 0.0, NEG).astype(np.float32)
    c["c_ones64"] = np.ones((64, 64), np.float32)
    c["c_ones4"] = np.ones((4, 128), np.float32)
    c["c_negA4"] = np.tile(c["c_negA"], (1, 4))
    c["c_negT4"] = np.tile(c["c_negT"], (1, 4))
    t = np.arange(T, dtype=np.float64)
    inv = 10000.0 ** (-np.arange(32, dtype=np.float64) / 32.0)
    ang = (t[:, None].astype(np.float32) * inv[None, :].astype(np.float32)).astype(np.float64)
    cos, sin = np.cos(ang), np.sin(ang)
    gam = 1.0 - 2.0 ** (-5.0 - np.arange(4, dtype=np.float64))
    pos = (np.arange(T) % 128 + 1).astype(np.float64)
    dq = gam[None, :] ** pos[:, None]
    dk = gam[None, :] ** (-pos[:, None]) * 0.125
    rope = np.zeros((T, 4, 4, 32), np.float64)
    rope[:, 0] = cos[:, None, :] * dq[:, :, None]
    rope[:, 1] = sin[:, None, :] * dq[:, :, None]
    rope[:, 2] = cos[:, None, :] * dk[:, :, None]
    rope[:, 3] = sin[:, None, :] * dk[:, :, None]
    c["c_rope"] = rope.reshape(T, 512).astype(np.float32)
    gC = np.zeros((64, 4, 64), np.float64)
    gC[:] = (gam ** 128.0)[None, :, None]
    c["c_gC"] = gC.reshape(64, 256).astype(np.float32)
    return c


CONST_SHAPES = {"c_identf": [128, 128], "c_cmask": [128, 128], "c_ublk": [128, 128],
                "c_oblk": [128, 128], "c_cind": [128, 2], "c_sel": [4, 512],
                "c_negA": [128, 128], "c_negT": [128, 128], "c_ones64": [64, 64],
                "c_ones4": [4, 128], "c_negA4": [128, 512], "c_negT4": [128, 512],
                "c_gC": [64, 256]}

PARAM_SHAPES = {
    "norm_mix_w": [2, 1024], "w_in": [2, 1024, DIN], "ret_norm_w": [2, 256],
    "conf_conv_w": [2, 31, 256], "conf_conv_b": [2, 256], "conf_ln_w": [2, 256],
    "conf_ln_b": [2, 256], "gdn_conv_w": [2, 4, 768], "gdn_A_log": [2, 4],
    "gdn_dt_bias": [2, 4], "gdn_norm_w": [2, 64], "hgrn_lb_logits": [2, 256],
    "hgrn_norm_w": [2, 64], "w_out": [2, 1024, 1024], "norm_ffn_w": [2, 1024],
    "ffn_w_gate": [2, 1024, DFF], "ffn_w_up": [2, 1024, DFF], "ffn_w_down": [2, DFF, 1024],
    "final_norm_w": [1024],
}


class StopBuild(Exception):
    pass


def build(T, NL=2, dbg=False, skip=(), stop=99):
    nc = bass.Bass("TRN2", target_bir_lowering=False)

    def ckpt(n):
        if n >= stop:
            raise StopBuild()

    NT = T // 128
    k = K(nc)
    dr = {}
    dr["x"] = nc.dram_tensor("x", [T, D], F32, kind="ExternalInput").ap()
    for n, s in PARAM_SHAPES.items():
        dr[n] = nc.dram_tensor(n, s, F32, kind="ExternalInput").ap()
    for n, s in CONST_SHAPES.items():
        dr[n] = nc.dram_tensor(n, s, F32, kind="ExternalInput").ap()
    dr["c_rope"] = nc.dram_tensor("c_rope", [T, 512], F32, kind="ExternalInput").ap()
    out = nc.dram_tensor("out", [T, D], F32, kind="ExternalOutput").ap()
    xs = nc.dram_tensor("xs", [T, D], F32, kind="Internal").ap()
    if dbg:
        dmix = nc.dram_tensor("dmix", [NL, T, D], F32, kind="ExternalOutput").ap()
    xsb = [Buf("xs%d" % i) for i in range(NT)]

    es_top = ExitStack()

    uniq = {"n": 0}

    def sb(es, name, shape, dt=F32):
        uniq["n"] += 1
        t = es.enter_context(nc.sbuf_tensor("%s_%d" % (name, uniq["n"]), shape, dt))
        return t.ap(), Buf(name)

    def mm(out_, lhsT, rhs, R, W, start=True, stop=True):
        k.op("pe", lambda e: e.matmul(out_, lhsT=lhsT, rhs=rhs, start=start, stop=stop), R=R, W=W)

    def tr(out_, in_, ident, R, W):
        k.op("pe", lambda e: e.transpose(out=out_, in_=in_, identity=ident), R=R, W=W)

    def act(out_, in_, func, R, W, bias=None, scale=None, accum=None):
        kw = {}
        if bias is not None:
            kw["bias"] = bias
        if scale is not None:
            kw["scale"] = scale
        if accum is not None:
            kw["accum_out"] = accum
        k.op("act", lambda e: e.activation(out=out_, in_=in_, func=func, **kw), R=R, W=W)

    def silu(out_, in_, R, W, scr, b_scr):
        act(scr, in_, AF.Sigmoid, R, [b_scr])
        tt(out_, in_, scr, ALU.mult, list(R) + [b_scr], W)

    def tt(out_, a, b, op, R, W, eng="dve"):
        k.op(eng, lambda e: e.tensor_tensor(out=out_, in0=a, in1=b, op=op), R=R, W=W)

    def ts(out_, a, s1, op0, R, W, s2=None, op1=None, eng="dve"):
        if op1 is None:
            k.op(eng, lambda e: e.tensor_scalar(out=out_, in0=a, scalar1=s1, scalar2=None, op0=op0), R=R, W=W)
        else:
            k.op(eng, lambda e: e.tensor_scalar(out=out_, in0=a, scalar1=s1, scalar2=None, op0=op0), R=R, W=W)
            k.op(eng, lambda e: e.tensor_scalar(out=out_, in0=out_, scalar1=s2, scalar2=None, op0=op1), R=list(R) + list(W), W=W)

    def stt(out_, in0, scalar, in1, op0, op1, R, W):
        k.op("dve", lambda e: e.scalar_tensor_tensor(out=out_, in0=in0, scalar=scalar, in1=in1, op0=op0, op1=op1), R=R, W=W)

    def cp(out_, in_, R, W, eng="dve"):
        if eng == "act":
            k.op("act", lambda e: e.copy(out=out_, in_=in_), R=R, W=W)
        else:
            k.op(eng, lambda e: e.tensor_copy(out=out_, in_=in_), R=R, W=W)

    def red(out_, in_, R, W):
        k.op("dve", lambda e: e.tensor_reduce(out=out_, in_=in_, axis=AX.X, op=ALU.add), R=R, W=W)

    def rcp(out_, in_, R, W):
        k.op("dve", lambda e: e.reciprocal(out=out_, in_=in_), R=R, W=W)

    def mset(ap, val, W, eng="pool"):
        k.op(eng, lambda e: e.memset(ap, val), W=W)

    def v3(ap, inner):
        return ap.rearrange("p (a b) -> p a b", b=inner)

    banks = []
    for i in range(8):
        t = es_top.enter_context(nc.psum_tensor("bank%d" % i, [128, 512], F32))
        banks.append((t.ap(), Buf("bank%d" % i, excl=True)))
    freeq = list(range(8))
    bank_idx = {id(b[1]): i for i, b in enumerate(banks)}
    cur = {"free": freeq}

    def psum():
        assert cur["free"], "no free PSUM bank"
        return banks[cur["free"].pop(0)]

    def galloc():
        if False:
            yield
        return psum()

    def pfree(*bufs):
        for b in bufs:
            i = bank_idx[id(b)]
            assert i not in cur["free"]
            cur["free"].append(i)

    def run_interleaved(gens):
        lists = []
        for g, bset in gens:
            cur["free"] = list(bset)
            k.rec = []
            for _ in g():
                pass
            lists.append(k.rec)
            k.rec = None
            assert sorted(cur["free"]) == sorted(bset), "mixer leaked PSUM banks"
        cur["free"] = freeq
        pos = [0] * len(lists)
        while True:
            best, bf = -1, 2.0
            for j, lst in enumerate(lists):
                if pos[j] < len(lst):
                    f = pos[j] / len(lst)
                    if f < bf:
                        best, bf = j, f
            if best < 0:
                break
            k.replay(lists[best][pos[best]])
            pos[best] += 1

    stg = [sb(es_top, "stg%d" % i, [128, 1024]) for i in range(2)]
    stg_c = stg[0]
    identf, b_identf = sb(es_top, "identf", [128, 128])
    identb, b_identb = sb(es_top, "identb", [128, 128], BF16)
    cmask, b_cmask = sb(es_top, "cmask", [128, 128])
    ublk, b_ublk = sb(es_top, "ublk", [128, 128])
    oblk, b_oblk = sb(es_top, "oblk", [128, 128])
    cind, b_cind = sb(es_top, "cind", [128, 2])
    sel, b_sel = sb(es_top, "sel", [4, 512])
    negAb, b_negAb = sb(es_top, "negAb", [128, 512], BF16)
    negTb, b_negTb = sb(es_top, "negTb", [128, 512], BF16)
    ones4, b_ones4 = sb(es_top, "ones4", [4, 128])
    ones64, b_ones64 = sb(es_top, "ones64", [64, 64])
    gC, b_gC = sb(es_top, "gC", [64, 256])
    CONSTS = []
    k.dma(identf, dr["c_identf"], W=[b_identf])
    k.dma(cmask, dr["c_cmask"], W=[b_cmask])
    k.dma(ublk, dr["c_ublk"], W=[b_ublk])
    k.dma(oblk, dr["c_oblk"], W=[b_oblk])
    k.dma(cind, dr["c_cind"], W=[b_cind])
    k.dma(sel, dr["c_sel"], W=[b_sel])
    k.dma(ones4, dr["c_ones4"], W=[b_ones4])
    k.dma(ones64, dr["c_ones64"], W=[b_ones64])
    k.dma(gC, dr["c_gC"], W=[b_gC])
    cp(identb, identf, [b_identf], [b_identb])
    for (dst_, b_dst_, nm_) in [(negAb, b_negAb, "c_negA4"), (negTb, b_negTb, "c_negT4")]:
        s_ap, s_b = stg_c
        k.dma(s_ap[:, 0:512], dr[nm_], W=[s_b])
        cp(dst_, s_ap[:, 0:512], [s_b], [b_dst_])
    sel3 = v3(sel, 128)
    try:
        ckpt(1)
    except StopBuild:
        k.barrier()
        return nc, k

    xt, b_xt = sb(es_top, "xt", [128, D])
    xo, b_xo = sb(es_top, "xo", [128, D])
    xn, b_xn = sb(es_top, "xn", [128, D], BF16)
    hT, b_hT = sb(es_top, "hT", [128, 8, 128], BF16)
    ss, b_ss = sb(es_top, "ss", [128, 1])
    nw, b_nw = sb(es_top, "nw", [128, 8])
    stg_i = {"i": 0}

    def load_w(dst, b_dst, src, KC, N, scale):
        for kc in range(KC):
            for n0 in range(0, N, 1024):
                n1 = min(N, n0 + 1024)
                s_ap, s_b = stg[stg_i["i"] % 2]
                eng = "dve" if stg_i["i"] % 2 == 0 else "pool"
                stg_i["i"] += 1
                k.dma(s_ap[:, :n1 - n0], src[kc * 128:(kc + 1) * 128, n0:n1], W=[s_b])
                if scale is not None:
                    ts(dst[:, kc, n0:n1], s_ap[:, :n1 - n0], scale[:, kc:kc + 1], ALU.mult,
                       [s_b, b_nw], [b_dst], eng=eng)
                else:
                    cp(dst[:, kc, n0:n1], s_ap[:, :n1 - n0], [s_b], [b_dst], eng=eng)

    def norm_to_hT(src_xt):
        act(xn, src_xt, AF.Square, [b_xt], [b_xn, b_ss], accum=ss)
        act(ss, ss, AF.Sqrt, [b_ss], [b_ss], bias=1e-6, scale=1.0 / D)
        rcp(ss, ss, [b_ss], [b_ss])
        ts(xn, src_xt, ss, ALU.mult, [b_xt, b_ss], [b_xn])
        pb, b_pb = psum()
        pbv = v3(pb.bitcast(BF16), 128)
        for kc in range(8):
            tr(pbv[:, kc, :], xn[:, kc * 128:(kc + 1) * 128], identb, [b_xn, b_identb], [b_pb])
        cp(hT, pbv, [b_pb], [b_hT], eng="act")
        pfree(b_pb)

    def pass_a(l):
        es = ExitStack()
        Wi, b_Wi = sb(es, "Wi", [128, 8, DIN], BF16)
        Wo, b_Wo = sb(es, "Wo", [128, 8, D], BF16)
        Dc, b_Dc = sb(es, "Dc", [128, 62, 128], BF16)
        ccw, b_ccw = sb(es, "ccw", [128, 2, 31])
        ccb, b_ccb = sb(es, "ccb", [128, 2])
        gcw, b_gcw = sb(es, "gcw", [64, 12, 4])
        rnw, b_rnw = sb(es, "rnw", [128, 256])
        clnw, b_clnw = sb(es, "clnw", [128, 256])
        clnb, b_clnb = sb(es, "clnb", [128, 256])
        gnw, b_gnw = sb(es, "gnw", [128, 64])
        hnw, b_hnw = sb(es, "hnw", [128, 64])
        negA, b_negA = sb(es, "negA", [128, 4])
        dtb, b_dtb = sb(es, "dtb", [128, 4])
        lbl, b_lbl = sb(es, "lbl", [128, 2, 256])
        lbt, b_lbt = sb(es, "lbt", [128, 256])
        oml, b_oml = sb(es, "oml", [128, 256])
        with nc.allow_non_contiguous_dma(reason="tiny param loads"):
            k.dma(nw, dr["norm_mix_w"][l].rearrange("(c p) -> p c", p=128), W=[b_nw])
            for ct in range(2):
                k.dma(ccw[:, ct, :], dr["conf_conv_w"][l][:, ct * 128:(ct + 1) * 128].rearrange("k c -> c k"), W=[b_ccw])
            k.dma(ccb, dr["conf_conv_b"][l].rearrange("(ct c) -> c ct", c=128), W=[b_ccb])
            for s_ in range(12):
                k.dma(gcw[:, s_, :], dr["gdn_conv_w"][l][:, s_ * 64:(s_ + 1) * 64].rearrange("k d -> d k"), W=[b_gcw])
            k.dma(rnw, dr["ret_norm_w"][l].partition_broadcast(128), W=[b_rnw])
            k.dma(clnw, dr["conf_ln_w"][l].partition_broadcast(128), W=[b_clnw])
            k.dma(clnb, dr["conf_ln_b"][l].partition_broadcast(128), W=[b_clnb])
            k.dma(gnw, dr["gdn_norm_w"][l].partition_broadcast(128), W=[b_gnw])
            k.dma(hnw, dr["hgrn_norm_w"][l].partition_broadcast(128), W=[b_hnw])
            k.dma(negA, dr["gdn_A_log"][l].partition_broadcast(128), W=[b_negA])
            k.dma(dtb, dr["gdn_dt_bias"][l].partition_broadcast(128), W=[b_dtb])
            k.dma(lbl[:, 0, :], dr["hgrn_lb_logits"][0].partition_broadcast(128), W=[b_lbl])
            k.dma(lbl[:, 1, :], dr["hgrn_lb_logits"][1].partition_broadcast(128), W=[b_lbl])
        ckpt(2)
        act(negA, negA, AF.Exp, [b_negA], [b_negA])
        ts(negA, negA, -1.0, ALU.mult, [b_negA], [b_negA])
        if l == 0:
            mset(lbt, 0.0, [b_lbt])
            mset(oml, 1.0, [b_oml])
        else:
            act(lbl, lbl, AF.Exp, [b_lbl], [b_lbl])
            tt(oml, lbl[:, 0, :], lbl[:, 1, :], ALU.add, [b_lbl], [b_oml])
            rcp(oml, oml, [b_oml], [b_oml])
            tt(lbt, lbl[:, 1, :], oml, ALU.mult, [b_lbl, b_oml], [b_lbt])
            ts(oml, lbt, -1.0, ALU.mult, [b_lbt], [b_oml], s2=1.0, op1=ALU.add)
        for kk in range(31):
            for ct in range(2):
                ts(Dc[:, kk * 2 + ct, :], identf, ccw[:, ct, kk:kk + 1], ALU.mult, [b_identf, b_ccw], [b_Dc])
        ckpt(3)
        load_w(Wi, b_Wi, dr["w_in"][l], 8, DIN, nw)
        load_w(Wo, b_Wo, dr["w_out"][l], 8, D, None)

        ckpt(4)
        rope_t, b_rope = sb(es, "rope_t", [128, 512])
        rqk, b_rqk = sb(es, "rqk", [128, 512])
        ctsg, b_ctsg = sb(es, "ctsg", [128, 256])
        glt, b_glt = sb(es, "glt", [128, 256], BF16)
        gpt, b_gpt = sb(es, "gpt", [128, 768], BF16)
        Gblk, b_Gblk = sb(es, "Gblk", [4, 512])
        nGblk, b_nGblk = sb(es, "nGblk", [4, 512])
        hqf, b_hqf = sb(es, "hqf", [128, 256])
        t1, b_t1 = sb(es, "t1", [128, 4, 32])
        t2, b_t2 = sb(es, "t2", [128, 4, 32])
        qrt, b_qrt = sb(es, "qrt", [128, 4, 64], BF16)
        krt, b_krt = sb(es, "krt", [128, 4, 64], BF16)
        qkT, b_qkT = sb(es, "qkT", [64, 8, 128], BF16)
        PTr, b_PTr = sb(es, "PTr", [128, 4, 128], BF16)
        rvb, b_rvb = sb(es, "rvb", [128, 256], BF16)
        Sr32, b_Sr32 = sb(es, "Sr32", [64, 256])
        Srb, b_Srb = sb(es, "Srb", [64, 4, 64], BF16)
        silr, b_silr = sb(es, "silr", [128, 256])
        silg, b_silg = sb(es, "silg", [128, 256])
        silh, b_silh = sb(es, "silh", [128, 256])
        HN = {}
        for nm_ in ("ret", "hgrn", "gdn"):
            HN[nm_] = dict(hsq=sb(es, "hsq", [128, 256]), hy=sb(es, "hy", [128, 256]), s1=sb(es, "s1", [128, 4]),
                           s2=sb(es, "s2", [128, 4]), mean=sb(es, "mean", [128, 4]), msq=sb(es, "msq", [128, 4]),
                           var=sb(es, "var", [128, 4]))
        hsq, b_hsq = HN["ret"]["hsq"]
        hf, b_hf = sb(es, "hf", [128, 256])
        logf, b_logf = sb(es, "logf", [128, 256])
        hkk, b_hkk = sb(es, "hkk", [128, 256])
        Gsb, b_Gsb = sb(es, "Gsb", [128, 256])
        eG, b_eG = sb(es, "eG", [128, 256])
        enG, b_enG = eG, b_eG
        hD, b_hD = Gsb, b_Gsb
        qh, b_qh = sb(es, "qh", [128, 256], BF16)
        kht, b_kht = sb(es, "kht", [128, 256], BF16)
        khat, b_khat = sb(es, "khat", [128, 256], BF16)
        hvb, b_hvb = sb(es, "hvb", [128, 256], BF16)
        hq = [sb(es, "hq%d" % c, [64, 4, 128], BF16) for c in range(2)]
        hkT, b_hkT = sb(es, "hkT", [64, 4, 128], BF16)
        PTh, b_PTh = sb(es, "PTh", [128, 4, 128], BF16)
        Sh32, b_Sh32 = sb(es, "Sh32", [64, 4, 64])
        Shb = [sb(es, "Shb%d" % c, [64, 4, 64], BF16) for c in range(3)]
        eGl, b_eGl = sb(es, "eGl", [64, 8])
        gl, b_gl = sb(es, "gl", [128, 2, 158], BF16)
        yT, b_yT = sb(es, "yT", [128, 2, 128])
        yc, b_yc = sb(es, "yc", [128, 256])
        c1, b_c1 = sb(es, "c1", [128, 1])
        c2, b_c2 = sb(es, "c2", [128, 1])
        c3, b_c3 = sb(es, "c3", [128, 1])
        gp, b_gp = sb(es, "gp", [64, 12, 131], BF16)
        gx, b_gx = sb(es, "gx", [64, 12, 128])
        gsq, b_gsq = sb(es, "gsq", [64, 4, 128])
        grs, b_grs = sb(es, "grs", [64, 4, 128])
        gqT, b_gqT = sb(es, "gqT", [64, 4, 128], BF16)
        gkT, b_gkT = sb(es, "gkT", [64, 4, 128], BF16)
        gvT, b_gvT = sb(es, "gvT", [64, 4, 128], BF16)
        grk, b_grk = sb(es, "grk", [128, 4, 64], BF16)
        gkh, b_gkh = sb(es, "gkh", [128, 4, 64], BF16)
        gbv, b_gbv = sb(es, "gbv", [128, 4, 64], BF16)
        EA, b_EA = sb(es, "EA", [128, 4, 128], BF16)
        ET, b_ET = sb(es, "ET", [128, 4, 128], BF16)
        eGr, b_eGr = sb(es, "eGr", [64, 4, 128])
        gq = [sb(es, "gq%d" % c, [64, 4, 128], BF16) for c in range(2)]
        gw = [sb(es, "gw%d" % c, [64, 4, 128], BF16) for c in range(2)]
        Pm, b_Pm = sb(es, "Pm", [128, 4, 128], BF16)
        PmT, b_PmT = sb(es, "PmT", [128, 4, 128], BF16)
        Wt, b_Wt = sb(es, "Wt", [128, 4, 128], BF16)
        attT, b_attT = sb(es, "attT", [128, 4, 128], BF16)
        usb, b_usb = sb(es, "usb", [128, 256])
        vn, b_vn = sb(es, "vn", [128, 256], BF16)
        Sg32, b_Sg32 = sb(es, "Sg32", [64, 4, 64])
        Sgb = [sb(es, "Sgb%d" % c, [64, 4, 64], BF16) for c in range(3)]
        beta, b_beta = sb(es, "beta", [128, 4])
        gba, b_gba = sb(es, "gba", [128, 8])
        lnb, b_lnb = sb(es, "lnb", [128, 4])
        gsp, b_gsp = sb(es, "gsp", [128, 4])
        gg_, b_gg = sb(es, "gg_", [128, 4])
        Gs, b_Gs = sb(es, "Gs", [128, 4])
        nGs, b_nGs = sb(es, "nGs", [128, 4])
        Bs, b_Bs = sb(es, "Bs", [128, 4])
        eB, b_eB = sb(es, "eB", [128, 4])
        eDl, b_eDl = sb(es, "eDl", [128, 4])
        GTs, b_GTs = sb(es, "GTs", [4, 128])
        mix, b_mix = sb(es, "mix", [128, D], BF16)
        mT, b_mT = sb(es, "mT", [128, 8, 128], BF16)
        mixf, b_mixf = (xo, b_xo) if dbg else (None, None)

        mset(Sr32, 0.0, [b_Sr32]); mset(Srb, 0.0, [b_Srb])
        mset(Sh32, 0.0, [b_Sh32]); mset(Sg32, 0.0, [b_Sg32])
        for c in range(3):
            mset(Shb[c][0], 0.0, [Shb[c][1]]); mset(Sgb[c][0], 0.0, [Sgb[c][1]])
        for c in range(2):
            mset(hq[c][0], 0.0, [hq[c][1]]); mset(gq[c][0], 0.0, [gq[c][1]]); mset(gw[c][0], 0.0, [gw[c][1]])
        mset(gl, 0.0, [b_gl]); mset(gp, 0.0, [b_gp])
        mset(mix, 0.0, [b_mix])

        ckpt(5)

        def head_norm(o, b_o, sil, b_sil, w3, b_w, center, eps, dst3, who):
            t_ = HN[who]
            (hsq, b_hsq), (hy, b_hy), (s1, b_s1), (s2, b_s2) = t_["hsq"], t_["hy"], t_["s1"], t_["s2"]
            (mean, b_mean), (msq, b_msq), (var, b_var) = t_["mean"], t_["msq"], t_["var"]
            o3 = v3(o, 64)
            red(s1, o3, [b_o], [b_s1])
            act(hsq, o, AF.Square, [b_o], [b_hsq])
            red(s2, v3(hsq, 64), [b_hsq], [b_s2])
            if center:
                ts(mean, s1, 1.0 / 64, ALU.mult, [b_s1], [b_mean])
                tt(msq, mean, mean, ALU.mult, [b_mean], [b_msq])
                stt(var, s2, 1.0 / 64, msq, ALU.mult, ALU.subtract, [b_s2, b_msq], [b_var])
            else:
                ts(var, s2, 1.0 / 64, ALU.mult, [b_s2], [b_var])
            act(var, var, AF.Sqrt, [b_var], [b_var], bias=eps)
            rcp(var, var, [b_var], [b_var])
            hy3 = v3(hy, 64)
            if center:
                tt(hy3, o3, mean.unsqueeze(2).broadcast_to([128, 4, 64]), ALU.subtract, [b_o, b_mean], [b_hy])
                tt(hy3, hy3, var.unsqueeze(2).broadcast_to([128, 4, 64]), ALU.mult, [b_hy, b_var], [b_hy])
            else:
                tt(hy3, o3, var.unsqueeze(2).broadcast_to([128, 4, 64]), ALU.mult, [b_o, b_var], [b_hy])
            tt(hy3, hy3, w3, ALU.mult, [b_hy, b_w], [b_hy])
            tt(dst3, hy3, v3(sil, 64), ALU.mult, [b_hy, b_sil], [b_mix])

        for i in range(NT):
            pv, mid, end = (2 * i) % 3, (2 * i + 1) % 3, (2 * i + 2) % 3
            r0, r1 = slice(0, 64), slice(64, 128)
            src = dr["x"] if l == 0 else xs
            k.dma(xt, src[i * 128:(i + 1) * 128, :], R=([] if l == 0 else [xsb[i]]), W=[b_xt])
            k.dma(rope_t, dr["c_rope"][i * 128:(i + 1) * 128, :], W=[b_rope])
            ckpt(6)
            norm_to_hT(xt)
            ckpt(7)
            P = []
            for (c0, c1_) in [(0, 512), (512, 1024), (2304, 2568), (2568, 3080), (3080, 3592)]:
                pg, b_pg = psum()
                for kc in range(8):
                    mm(pg[:, :c1_ - c0], hT[:, kc, :], Wi[:, kc, c0:c1_], [b_hT, b_Wi], [b_pg], kc == 0, kc == 7)
                P.append((pg, b_pg))
                if len(P) == 1:
                    cp(rqk, pg, [b_pg], [b_rqk], eng="act")
                if len(P) == 4:
                    cp(hqf, pg[:, 0:256], [b_pg], [b_hqf])
                    act(hf, pg[:, 256:512], AF.Sigmoid, [b_pg], [b_hf])
                if len(P) == 2:
                    silu(silr, pg[:, 256:512], [b_pg], [b_silr], hsq, b_hsq)
                    cp(rvb, pg[:, 0:256], [b_pg], [b_rvb])
                if len(P) == 3:
                    silu(silg, pg[:, 0:256], [b_pg], [b_silg], hsq, b_hsq)
                    cp(gba, pg[:, 256:264], [b_pg], [b_gba])
                if len(P) == 5:
                    silu(silh, pg[:, 256:512], [b_pg], [b_silh], hsq, b_hsq)
                    cp(hvb, pg[:, 0:256], [b_pg], [b_hvb])
                pfree(b_pg)
            pg, b_pg = psum()
            for kc in range(8):
                mm(pg, hT[:, kc, :], Wi[:, kc, 1024:1536], [b_hT, b_Wi], [b_pg], kc == 0, kc == 7)
            act(ctsg, pg[:, 256:512], AF.Sigmoid, [b_pg], [b_ctsg])
            tt(glt, pg[:, 0:256], ctsg, ALU.mult, [b_pg, b_ctsg], [b_glt])
            pfree(b_pg)
            pg, b_pg = psum()
            for kc in range(8):
                mm(pg, hT[:, kc, :], Wi[:, kc, 1536:2048], [b_hT, b_Wi], [b_pg], kc == 0, kc == 7)
            cp(gpt[:, 0:512], pg, [b_pg], [b_gpt], eng="act")
            pfree(b_pg)
            pg, b_pg = psum()
            for kc in range(8):
                mm(pg[:, 0:256], hT[:, kc, :], Wi[:, kc, 2048:2304], [b_hT, b_Wi], [b_pg], kc == 0, kc == 7)
            cp(gpt[:, 512:768], pg[:, 0:256], [b_pg], [b_gpt])
            pfree(b_pg)
            ckpt(8)

            def g_ret():
                rt4 = rope_t.rearrange("p (a h d) -> p a h d", a=4, h=4)
                for (off, ci, si, dst, b_dst) in [(0, 0, 1, qrt, b_qrt), (256, 2, 3, krt, b_krt)]:
                    X = v3(rqk[:, off:off + 256], 64)
                    x1, x2 = X[:, :, 0:32], X[:, :, 32:64]
                    cs, sn = rt4[:, ci], rt4[:, si]
                    tt(t1, x1, cs, ALU.mult, [b_rqk, b_rope], [b_t1])
                    tt(t2, x2, sn, ALU.mult, [b_rqk, b_rope], [b_t2])
                    tt(dst[:, :, 0:32], t1, t2, ALU.subtract, [b_t1, b_t2], [b_dst])
                    tt(t1, x1, sn, ALU.mult, [b_rqk, b_rope], [b_t1])
                    tt(t2, x2, cs, ALU.mult, [b_rqk, b_rope], [b_t2])
                    tt(dst[:, :, 32:64], t1, t2, ALU.add, [b_t1, b_t2], [b_dst])
                pb, b_pb = yield from galloc()
                pbv = v3(pb.bitcast(BF16), 128)
                for h in range(4):
                    tr(pbv[0:64, h, :], qrt[:, h, :], identb, [b_qrt, b_identb], [b_pb])
                    tr(pbv[0:64, 4 + h, :], krt[:, h, :], identb, [b_krt, b_identb], [b_pb])
                cp(qkT, pbv[0:64], [b_pb], [b_qkT], eng="act")
                pfree(b_pb)
                yield
                ps, b_ps = yield from galloc()
                ps3 = v3(ps, 128)
                for h in range(4):
                    mm(ps3[:, h, :], qkT[:, 4 + h, :], qkT[:, h, :], [b_qkT], [b_ps])
                tt(PTr, ps3, cmask.unsqueeze(1).broadcast_to([128, 4, 128]), ALU.mult, [b_ps, b_cmask], [b_PTr])
                pfree(b_ps)
                yield
                po, b_po = yield from galloc()
                for h in range(4):
                    hs = slice(h * 64, (h + 1) * 64)
                    mm(po[:, hs], PTr[:, h, :], rvb[:, hs], [b_PTr, b_rvb], [b_po], True, False)
                    mm(po[:, hs], qkT[:, h, :], Srb[:, h, :], [b_qkT, b_Srb], [b_po], False, True)
                pd, b_pd = yield from galloc()
                for h in range(4):
                    hs = slice(h * 64, (h + 1) * 64)
                    mm(pd[0:64, hs], krt[:, h, :], rvb[:, hs], [b_krt, b_rvb], [b_pd])
                tt(Sr32, Sr32, pd[0:64, 0:256], ALU.add, [b_Sr32, b_pd], [b_Sr32])
                tt(Sr32, Sr32, gC, ALU.mult, [b_Sr32, b_gC], [b_Sr32])
                cp(Srb, v3(Sr32, 64), [b_Sr32], [b_Srb], eng="pool")
                pfree(b_pd)
                yield
                head_norm(po[:, 0:256], b_po, silr, b_silr, v3(rnw, 64), b_rnw, True, 1e-5, v3(mix[:, 0:256], 64), "ret")
                pfree(b_po)

            def g_hgrn():
                tt(hf, hf, oml, ALU.mult, [b_hf, b_oml], [b_hf])
                tt(hf, hf, lbt, ALU.add, [b_hf, b_lbt], [b_hf])
                act(logf, hf, AF.Ln, [b_hf], [b_logf])
                ts(hkk, hf, -1.0, ALU.mult, [b_hf], [b_hkk], s2=1.0, op1=ALU.add)
                pG, b_pG = yield from galloc()
                mm(pG[:, 0:256], ublk, logf, [b_ublk, b_logf], [b_pG])
                mm(pG[:, 256:512], oblk, logf, [b_oblk, b_logf], [b_pG])
                pL, b_pL = yield from galloc()
                for h in range(4):
                    mm(pL[0:64, h * 2:(h + 1) * 2], logf[:, h * 64:(h + 1) * 64], cind, [b_logf, b_cind], [b_pL])
                act(eGl, pL[0:64, 0:8], AF.Exp, [b_pL], [b_eGl])
                pfree(b_pL)
                cp(Gsb, pG[:, 0:256], [b_pG], [b_Gsb], eng="act")
                act(eG, pG[:, 0:256], AF.Exp, [b_pG], [b_eG])
                tt(qh, hqf, eG, ALU.mult, [b_hqf, b_eG], [b_qh])
                act(enG, pG[:, 0:256], AF.Exp, [b_pG], [b_enG], scale=-1.0)
                tt(kht, hkk, enG, ALU.mult, [b_hkk, b_enG], [b_kht])
                tt(hD, pG[:, 256:512], Gsb, ALU.subtract, [b_pG, b_Gsb], [b_hD])
                pfree(b_pG)
                yield
                act(hD, hD, AF.Exp, [b_hD], [b_hD])
                tt(khat, hkk, hD, ALU.mult, [b_hkk, b_hD], [b_khat])
                pb, b_pb = yield from galloc()
                pbv = v3(pb.bitcast(BF16), 128)
                for h in range(4):
                    tr(pbv[0:64, h, :], qh[:, h * 64:(h + 1) * 64], identb, [b_qh, b_identb], [b_pb])
                    tr(pbv[0:64, 4 + h, :], kht[:, h * 64:(h + 1) * 64], identb, [b_kht, b_identb], [b_pb])
                cp(hq[0][0][:, :, 0:64], pbv[0:64, 0:4, 0:64], [b_pb], [hq[0][1]], eng="act")
                cp(hq[1][0][:, :, 64:128], pbv[0:64, 0:4, 64:128], [b_pb], [hq[1][1]], eng="act")
                cp(hkT, pbv[0:64, 4:8, :], [b_pb], [b_hkT])
                pfree(b_pb)
                yield
                ps, b_ps = yield from galloc()
                ps3 = v3(ps, 128)
                for h in range(4):
                    for c in range(2):
                        cs_ = slice(c * 64, (c + 1) * 64)
                        mm(ps3[:, h, cs_], hkT[:, h, :], hq[c][0][:, h, cs_], [b_hkT, hq[c][1]], [b_ps])
                tt(PTh, ps3, ublk.unsqueeze(1).broadcast_to([128, 4, 128]), ALU.mult, [b_ps, b_ublk], [b_PTh])
                pfree(b_ps)
                yield
                for c, (rr, sdst) in enumerate([(r0, mid), (r1, end)]):
                    pd, b_pd = yield from galloc()
                    for h in range(4):
                        hs = slice(h * 64, (h + 1) * 64)
                        mm(pd[0:64, hs], khat[rr, hs], hvb[rr, hs], [b_khat, b_hvb], [b_pd])
                    for h in range(4):
                        hs = slice(h * 64, (h + 1) * 64)
                        stt(Sh32[:, h, :], Sh32[:, h, :], eGl[:, h * 2 + c:h * 2 + c + 1], pd[0:64, hs],
                            ALU.mult, ALU.add, [b_Sh32, b_eGl, b_pd], [b_Sh32])
                    cp(Shb[sdst][0], Sh32, [b_Sh32], [Shb[sdst][1]], eng="pool")
                    pfree(b_pd)
                    yield
                po, b_po = yield from galloc()
                for h in range(4):
                    hs = slice(h * 64, (h + 1) * 64)
                    mm(po[:, hs], PTh[:, h, :], hvb[:, hs], [b_PTh, b_hvb], [b_po], True, False)
                    mm(po[:, hs], hq[0][0][:, h, :], Shb[pv][0][:, h, :], [hq[0][1], Shb[pv][1]], [b_po], False, False)
                    mm(po[:, hs], hq[1][0][:, h, :], Shb[mid][0][:, h, :], [hq[1][1], Shb[mid][1]], [b_po], False, True)
                head_norm(po[:, 0:256], b_po, silh, b_silh, hnw.unsqueeze(1).broadcast_to([128, 4, 64]), b_hnw,
                          False, 1e-6, v3(mix[:, 768:1024], 64), "hgrn")
                pfree(b_po)

            def g_conf():
                pc, b_pc = yield from galloc()
                pc3 = v3(pc.bitcast(BF16)[:, 0:256], 128)
                for ct in range(2):
                    tr(pc3[:, ct, :], glt[:, ct * 128:(ct + 1) * 128], identb, [b_glt, b_identb], [b_pc])
                cp(gl[:, :, 30:158], pc3, [b_pc], [b_gl], eng="act")
                pfree(b_pc)
                yield
                ckpt(21)
                py, b_py = yield from galloc()
                py3 = v3(py, 128)
                for ct in range(2):
                    for kk in range(31):
                        mm(py3[:, ct, :], Dc[:, kk * 2 + ct, :], gl[:, ct, kk:kk + 128], [b_Dc, b_gl], [b_py],
                           kk == 0, kk == 30)
                ckpt(22)
                for ct in range(2):
                    act(yT[:, ct, :], py3[:, ct, :], AF.Identity, [b_py, b_ccb], [b_yT], bias=ccb[:, ct:ct + 1])
                pfree(b_py)
                yield
                cp(gl[:, :, 0:30], gl[:, :, 128:158], [b_gl], [b_gl], eng="pool")
                ckpt(24)
                pt, b_pt = yield from galloc()
                for ct in range(2):
                    tr(pt[:, ct * 128:(ct + 1) * 128], yT[:, ct, :], identf, [b_yT, b_identf], [b_pt])
                ckpt(25)
                red(c1, pt[:, 0:256].unsqueeze(1), [b_pt], [b_c1])
                act(yc, pt[:, 0:256], AF.Square, [b_pt], [b_yc])
                red(c2, yc.unsqueeze(1), [b_yc], [b_c2])
                ckpt(27)
                ts(c1, c1, 1.0 / 256, ALU.mult, [b_c1], [b_c1])
                tt(c3, c1, c1, ALU.mult, [b_c1], [b_c3])
                stt(c2, c2, 1.0 / 256, c3, ALU.mult, ALU.subtract, [b_c2, b_c3], [b_c2])
                ckpt(28)
                act(c2, c2, AF.Sqrt, [b_c2], [b_c2], bias=1e-5)
                rcp(c2, c2, [b_c2], [b_c2])
                ckpt(29)
                ts(yc, pt[:, 0:256], c1, ALU.subtract, [b_pt, b_c1, b_c2], [b_yc], s2=c2, op1=ALU.mult)
                pfree(b_pt)
                ckpt(30)
                tt(yc, yc, clnw, ALU.mult, [b_yc, b_clnw], [b_yc])
                tt(yc, yc, clnb, ALU.add, [b_yc, b_clnb], [b_yc])
                silu(mix[:, 256:512], yc, [b_yc], [b_mix], ctsg, b_ctsg)

            def g_gdn():
                pb1, b_pb1 = yield from galloc()
                pb2, b_pb2 = yield from galloc()
                v1 = v3(pb1.bitcast(BF16), 128)
                v2 = v3(pb2.bitcast(BF16)[:, 0:512], 128)
                for s in range(12):
                    dst_ = v1[0:64, s, :] if s < 8 else v2[0:64, s - 8, :]
                    tr(dst_, gpt[:, s * 64:(s + 1) * 64], identb, [b_gpt, b_identb], [b_pb1 if s < 8 else b_pb2])
                cp(gp[:, 0:8, 3:131], v1[0:64], [b_pb1], [b_gp], eng="act")
                cp(gp[:, 8:12, 3:131], v2[0:64], [b_pb2], [b_gp])
                pfree(b_pb1, b_pb2)
                yield
                for s in range(12):
                    ts(gx[:, s, :], gp[:, s, 0:128], gcw[:, s, 0:1], ALU.mult, [b_gp, b_gcw], [b_gx])
                    for kk in range(1, 4):
                        stt(gx[:, s, :], gp[:, s, kk:kk + 128], gcw[:, s, kk:kk + 1], gx[:, s, :],
                            ALU.mult, ALU.add, [b_gp, b_gcw, b_gx], [b_gx])
                    if s % 4 == 3:
                        yield
                for b in range(3):
                    silu(gx[:, b * 4:(b + 1) * 4, :], gx[:, b * 4:(b + 1) * 4, :], [b_gx], [b_gx], gsq, b_gsq)
                yield
                cp(gp[:, :, 0:3], gp[:, :, 128:131], [b_gp], [b_gp], eng="pool")
                for b in range(2):
                    xb = gx[:, b * 4:(b + 1) * 4, :]
                    tt(gsq, xb, xb, ALU.mult, [b_gx], [b_gsq])
                    pn, b_pn = yield from galloc()
                    pn3 = v3(pn, 128)
                    for h in range(4):
                        mm(pn3[0:64, h, :], ones64, gsq[:, h, :], [b_ones64, b_gsq], [b_pn])
                    act(grs, pn3[0:64], AF.Sqrt, [b_pn], [b_grs], bias=1e-6)
                    pfree(b_pn)
                    rcp(grs, grs, [b_grs], [b_grs])
                    if b == 0:
                        stt(gqT, xb, 0.125, grs, ALU.mult, ALU.mult, [b_gx, b_grs], [b_gqT])
                    else:
                        tt(gkT, xb, grs, ALU.mult, [b_gx, b_grs], [b_gkT])
                cp(gvT, gx[:, 8:12, :], [b_gx], [b_gvT], eng="act")
                act(beta, gba[:, 0:4], AF.Sigmoid, [b_gba], [b_beta])
                act(lnb, beta, AF.Ln, [b_beta], [b_lnb])
                tt(gsp, gba[:, 4:8], dtb, ALU.add, [b_gba, b_dtb], [b_gsp])
                act(gsp, gsp, AF.Exp, [b_gsp], [b_gsp])
                act(gsp, gsp, AF.Ln, [b_gsp], [b_gsp], bias=1.0)
                tt(gg_, gsp, negA, ALU.mult, [b_gsp, b_negA], [b_gg])
                pq, b_pq = yield from galloc()
                mm(pq[:, 0:4], ublk, gg_, [b_ublk, b_gg], [b_pq])
                mm(pq[:, 4:8], oblk, gg_, [b_oblk, b_gg], [b_pq])
                mm(pq[0:4, 128:256], gg_, ublk, [b_gg, b_ublk], [b_pq])
                cp(Gs, pq[:, 0:4], [b_pq], [b_Gs])
                ts(nGs, Gs, -1.0, ALU.mult, [b_Gs], [b_nGs])
                tt(Bs, Gs, lnb, ALU.add, [b_Gs, b_lnb], [b_Bs])
                act(eB, Bs, AF.Exp, [b_Bs], [b_eB])
                tt(eDl, pq[:, 4:8], Gs, ALU.subtract, [b_pq, b_Gs], [b_eDl])
                act(eDl, eDl, AF.Exp, [b_eDl], [b_eDl])
                cp(GTs, pq[0:4, 128:256], [b_pq], [b_GTs])
                pfree(b_pq)
                yield
                pb, b_pb = yield from galloc()
                pbv = v3(pb.bitcast(BF16)[:, 0:512], 64)
                for h in range(4):
                    tr(pbv[:, h, :], gkT[:, h, :], identb[0:64, 0:64], [b_gkT, b_identb], [b_pb])
                    tr(pbv[:, 4 + h, :], gvT[:, h, :], identb[0:64, 0:64], [b_gvT, b_identb], [b_pb])
                tt(grk, pbv[:, 0:4, :], eB.unsqueeze(2).broadcast_to([128, 4, 64]), ALU.mult, [b_pb, b_eB], [b_grk])
                tt(gkh, pbv[:, 0:4, :], eDl.unsqueeze(2).broadcast_to([128, 4, 64]), ALU.mult, [b_pb, b_eDl], [b_gkh])
                tt(gbv, pbv[:, 4:8, :], beta.unsqueeze(2).broadcast_to([128, 4, 64]), ALU.mult, [b_pb, b_beta], [b_gbv])
                pfree(b_pb)
                yield
                tt(v3(Gblk, 128), GTs.unsqueeze(1).broadcast_to([4, 4, 128]), sel3, ALU.mult, [b_GTs, b_sel], [b_Gblk])
                ts(nGblk, Gblk, -1.0, ALU.mult, [b_Gblk], [b_nGblk])
                pea, b_pea = yield from galloc()
                pea3 = v3(pea, 128)
                mm(pea, ones4, nGblk, [b_ones4, b_nGblk], [b_pea], True, False)
                mm(pea, identb, negAb, [b_identb, b_negAb], [b_pea], False, True)
                for h in range(4):
                    act(EA[:, h, :], pea3[:, h, :], AF.Exp, [b_pea, b_Bs], [b_EA], bias=Bs[:, h:h + 1])
                pfree(b_pea)
                pet, b_pet = yield from galloc()
                pet3 = v3(pet, 128)
                mm(pet, ones4, Gblk, [b_ones4, b_Gblk], [b_pet], True, False)
                mm(pet, identb, negTb, [b_identb, b_negTb], [b_pet], False, True)
                for h in range(4):
                    act(ET[:, h, :], pet3[:, h, :], AF.Exp, [b_pet, b_nGs], [b_ET], bias=nGs[:, h:h + 1])
                pfree(b_pet)
                peg, b_peg = yield from galloc()
                peg3 = v3(peg, 128)
                mm(peg[0:64, :], ones4[:, 0:64], Gblk, [b_ones4, b_Gblk], [b_peg])
                act(eGr, peg3[0:64], AF.Exp, [b_peg], [b_eGr])
                pfree(b_peg)
                yield
                tt(gq[0][0][:, :, 0:64], gqT[:, :, 0:64], eGr[:, :, 0:64], ALU.mult, [b_gqT, b_eGr], [gq[0][1]])
                tt(gq[1][0][:, :, 64:128], gqT[:, :, 64:128], eGr[:, :, 64:128], ALU.mult, [b_gqT, b_eGr], [gq[1][1]])
                pkk, b_pkk = yield from galloc()
                pkq, b_pkq = yield from galloc()
                for h in range(4):
                    mm(v3(pkk, 128)[:, h, :], gkT[:, h, :], gkT[:, h, :], [b_gkT], [b_pkk])
                    mm(v3(pkq, 128)[:, h, :], gkT[:, h, :], gqT[:, h, :], [b_gkT, b_gqT], [b_pkq])
                stt(Pm, v3(pkk, 128), -1.0, EA, ALU.mult, ALU.mult, [b_pkk, b_EA], [b_Pm])
                tt(attT, v3(pkq, 128), ET, ALU.mult, [b_pkq, b_ET], [b_attT])
                pfree(b_pkk, b_pkq)
                yield
                pb, b_pb = yield from galloc()
                pbv = v3(pb.bitcast(BF16)[:, 0:512], 128)
                for h in range(4):
                    tr(pbv[:, h, :], Pm[:, h, :], identb, [b_Pm, b_identb], [b_pb])
                cp(PmT, pbv, [b_pb], [b_PmT], eng="act")
                tt(Wt, pbv, identb.unsqueeze(1).broadcast_to([128, 4, 128]), ALU.add, [b_pb, b_identb], [b_Wt])
                pfree(b_pb)
                yield
                for m in range(5):
                    pa1, b_pa1 = yield from galloc()
                    for h in range(4):
                        mm(v3(pa1, 128)[:, h, :], PmT[:, h, :], Pm[:, h, :], [b_PmT, b_Pm], [b_pa1])
                    if m < 4:
                        pa2, b_pa2 = yield from galloc()
                        for h in range(4):
                            mm(v3(pa2, 128)[:, h, :], Pm[:, h, :], PmT[:, h, :], [b_PmT, b_Pm], [b_pa2])
                    cp(Pm, v3(pa1, 128), [b_pa1], [b_Pm], eng="act")
                    pfree(b_pa1)
                    if m < 4:
                        cp(PmT, v3(pa2, 128), [b_pa2], [b_PmT])
                        pfree(b_pa2)
                    yield
                    pa3, b_pa3 = yield from galloc()
                    for h in range(4):
                        mm(v3(pa3, 128)[:, h, :], Pm[:, h, :], Wt[:, h, :], [b_Pm, b_Wt], [b_pa3])
                    tt(Wt, Wt, v3(pa3, 128), ALU.add, [b_Wt, b_pa3], [b_Wt])
                    pfree(b_pa3)
                    yield
                pu, b_pu = yield from galloc()
                pw, b_pw = yield from galloc()
                for h in range(4):
                    hs = slice(h * 64, (h + 1) * 64)
                    mm(pu[:, hs], Wt[:, h, :], gbv[:, h, :], [b_Wt, b_gbv], [b_pu])
                    mm(v3(pw, 128)[0:64, h, :], grk[:, h, :], Wt[:, h, :], [b_grk, b_Wt], [b_pw])
                cp(usb, pu[:, 0:256], [b_pu], [b_usb])
                cp(gw[0][0][:, :, 0:64], v3(pw, 128)[0:64, :, 0:64], [b_pw], [gw[0][1]], eng="act")
                cp(gw[1][0][:, :, 64:128], v3(pw, 128)[0:64, :, 64:128], [b_pw], [gw[1][1]], eng="act")
                pfree(b_pu, b_pw)
                yield
                for c, (rr, ssrc, sdst) in enumerate([(r0, pv, mid), (r1, mid, end)]):
                    pa, b_pa = yield from galloc()
                    for h in range(4):
                        hs = slice(h * 64, (h + 1) * 64)
                        mm(pa[:, hs], gw[c][0][:, h, :], Sgb[ssrc][0][:, h, :], [gw[c][1], Sgb[ssrc][1]], [b_pa])
                    tt(vn[rr, :], usb[rr, :], pa[rr, 0:256], ALU.subtract, [b_usb, b_pa], [b_vn])
                    pfree(b_pa)
                    pd, b_pd = yield from galloc()
                    for h in range(4):
                        hs = slice(h * 64, (h + 1) * 64)
                        mm(pd[0:64, hs], gkh[rr, h, :], vn[rr, hs], [b_gkh, b_vn], [b_pd])
                    col = 63 + 64 * c
                    for h in range(4):
                        hs = slice(h * 64, (h + 1) * 64)
                        stt(Sg32[:, h, :], Sg32[:, h, :], eGr[:, h, col:col + 1], pd[0:64, hs],
                            ALU.mult, ALU.add, [b_Sg32, b_eGr, b_pd], [b_Sg32])
                    cp(Sgb[sdst][0], Sg32, [b_Sg32], [Sgb[sdst][1]], eng="pool")
                    pfree(b_pd)
                    yield
                po, b_po = yield from galloc()
                for h in range(4):
                    hs = slice(h * 64, (h + 1) * 64)
                    mm(po[:, hs], attT[:, h, :], vn[:, hs], [b_attT, b_vn], [b_po], True, False)
                    mm(po[:, hs], gq[0][0][:, h, :], Sgb[pv][0][:, h, :], [gq[0][1], Sgb[pv][1]], [b_po], False, False)
                    mm(po[:, hs], gq[1][0][:, h, :], Sgb[mid][0][:, h, :], [gq[1][1], Sgb[mid][1]], [b_po], False, True)
                head_norm(po[:, 0:256], b_po, silg, b_silg, gnw.unsqueeze(1).broadcast_to([128, 4, 64]), b_gnw,
                          False, 1e-6, v3(mix[:, 512:768], 64), "gdn")
                pfree(b_po)

            gl_ = [("gdn", g_gdn, [0, 1, 2]), ("hgrn", g_hgrn, [3, 4]), ("conf", g_conf, [5]), ("ret", g_ret, [6, 7])]
            assert len(freeq) == 8
            run_interleaved([(g, bs) for n_, g, bs in gl_ if n_ not in skip])
            if True:
                if dbg:
                    cp(mixf, mix, [b_mix], [b_mixf], eng="pool")
                    k.dma(dmix[l, i * 128:(i + 1) * 128, :], mixf, R=[b_mixf])
                pb, b_pb = psum()
                pbv = v3(pb.bitcast(BF16), 128)
                for kc in range(8):
                    tr(pbv[:, kc, :], mix[:, kc * 128:(kc + 1) * 128], identb, [b_mix, b_identb], [b_pb])
                cp(mT, pbv, [b_pb], [b_mT], eng="act")
                pfree(b_pb)
                for n in range(2):
                    pso, b_pso = psum()
                    ns = slice(n * 512, (n + 1) * 512)
                    for kc in range(8):
                        mm(pso, mT[:, kc, :], Wo[:, kc, ns], [b_mT, b_Wo], [b_pso], kc == 0, kc == 7)
                    tt(xo[:, ns], xt[:, ns], pso, ALU.add, [b_xt, b_pso], [b_xo])
                    pfree(b_pso)
                k.dma(xs[i * 128:(i + 1) * 128, :], xo, R=[b_xo], W=[xsb[i]])
        k.barrier()
        es.close()

    def pass_b(l, last):
        es = ExitStack()
        Wg, b_Wg = sb(es, "Wg", [128, 8, DFF], BF16)
        Wu, b_Wu = sb(es, "Wu", [128, 8, DFF], BF16)
        Wd, b_Wd = sb(es, "Wd", [128, NFT, D], BF16)
        aT, b_aT = sb(es, "aT", [128, NFT, 128], BF16)
        fsg, b_fsg = sb(es, "fsg", [128, 512])
        fsc, b_fsc = sb(es, "fsc", [128, 512])
        fnw, b_fnw = sb(es, "fnw", [128, D])
        with nc.allow_non_contiguous_dma(reason="tiny param loads"):
            k.dma(nw, dr["norm_ffn_w"][l].rearrange("(c p) -> p c", p=128), W=[b_nw])
            k.dma(fnw, dr["final_norm_w"].partition_broadcast(128), W=[b_fnw])
        load_w(Wg, b_Wg, dr["ffn_w_gate"][l], 8, DFF, nw)
        load_w(Wu, b_Wu, dr["ffn_w_up"][l], 8, DFF, nw)
        load_w(Wd, b_Wd, dr["ffn_w_down"][l], NFT, D, None)
        for i in range(NT):
            k.dma(xt, xs[i * 128:(i + 1) * 128, :], R=[xsb[i]], W=[b_xt])
            norm_to_hT(xt)
            for grp in range(6):
                nf = 4 if grp < 5 else 2
                psg, b_psg = psum()
                psu, b_psu = psum()
                for j in range(nf):
                    ft = grp * 4 + j
                    fs = slice(ft * 128, (ft + 1) * 128)
                    for kc in range(8):
                        mm(psg[:, j * 128:(j + 1) * 128], Wg[:, kc, fs], hT[:, kc, :], [b_Wg, b_hT], [b_psg], kc == 0, kc == 7)
                    for kc in range(8):
                        mm(psu[:, j * 128:(j + 1) * 128], Wu[:, kc, fs], hT[:, kc, :], [b_Wu, b_hT], [b_psu], kc == 0, kc == 7)
                silu(fsg[:, :nf * 128], psg[:, :nf * 128], [b_psg], [b_fsg], fsc[:, :nf * 128], b_fsc)
                tt(aT[:, grp * 4:grp * 4 + nf, :], v3(psu[:, :nf * 128], 128), v3(fsg[:, :nf * 128], 128), ALU.mult,
                   [b_psu, b_fsg], [b_aT])
                pfree(b_psg, b_psu)
            for n in range(2):
                pso, b_pso = psum()
                ns = slice(n * 512, (n + 1) * 512)
                for ft in range(NFT):
                    mm(pso, aT[:, ft, :], Wd[:, ft, ns], [b_aT, b_Wd], [b_pso], ft == 0, ft == NFT - 1)
                tt(xo[:, ns], xt[:, ns], pso, ALU.add, [b_xt, b_pso], [b_xo])
                pfree(b_pso)
            if last:
                act(xn, xo, AF.Square, [b_xo], [b_xn, b_ss], accum=ss)
                act(ss, ss, AF.Sqrt, [b_ss], [b_ss], bias=1e-6, scale=1.0 / D)
                rcp(ss, ss, [b_ss], [b_ss])
                ts(xo, xo, ss, ALU.mult, [b_xo, b_ss], [b_xo])
                tt(xo, xo, fnw, ALU.mult, [b_xo, b_fnw], [b_xo])
                k.dma(out[i * 128:(i + 1) * 128, :], xo, R=[b_xo])
            else:
                k.dma(xs[i * 128:(i + 1) * 128, :], xo, R=[b_xo], W=[xsb[i]])
        k.barrier()
        es.close()

    for l in range(NL):
        try:
            pass_a(l)
        except StopBuild:
            k.barrier()
            return nc, k
        if "b" in skip:
            for i in range(NT):
                k.dma(xt, xs[i * 128:(i + 1) * 128, :], R=[xsb[i]], W=[b_xt])
                k.dma(out[i * 128:(i + 1) * 128, :], xt, R=[b_xt])
        else:
            pass_b(l, l == NL - 1)
    k.barrier()
    es_top.close()
    return nc, k


_CACHE = {}


def kernel(**inputs):
    x = np.asarray(inputs["x"], dtype=np.float32)
    B, T, _ = x.shape
    key = T
    if key not in _CACHE:
        _CACHE[key] = (build(T)[0], host_consts(T))
    nc, consts = _CACHE[key]
    params = {n: np.ascontiguousarray(np.asarray(inputs[n], dtype=np.float32)) for n in PARAM_SHAPES}
    in_maps = []
    for c in range(8):
        m = {"x": np.ascontiguousarray(x[c % B])}
        m.update(params)
        m.update(consts)
        in_maps.append(m)
    res = run_bass_kernel_spmd(nc, in_maps, core_ids=list(range(8)))
    outs = [np.asarray(res.results[c]["out"], dtype=np.float32) for c in range(B)]
    return np.stack(outs, axis=0)
```

```python
from contextlib import ExitStack
import numpy as np
import ml_dtypes
import concourse.bass as bass
import concourse.mybir as mybir
from concourse.bass_utils import run_bass_kernel_spmd

F32 = mybir.dt.float32
BF16 = mybir.dt.bfloat16
AF = mybir.ActivationFunctionType
ALU = mybir.AluOpType
AX = mybir.AxisListType

D = 1024
DIN = 3592
DFF = 2816
NFT = DFF // 128
NEG = -30000.0


class Buf:
    __slots__ = ("name", "w", "r", "excl")

    def __init__(self, name="", excl=False):
        self.name = name
        self.w = None
        self.r = {}
        self.excl = excl


class K:
    NDSEM = 40

    def __init__(self, nc):
        self.nc = nc
        self.eng = {"pe": nc.tensor, "act": nc.scalar, "dve": nc.vector,
                    "pool": nc.gpsimd, "sp": nc.sync}
        self.sem = {e: nc.alloc_semaphore(name="s_" + e) for e in ("pe", "act", "dve", "pool")}
        self.cnt = {e: 0 for e in self.sem}
        self.dsem = [nc.alloc_semaphore(name="d%d" % i) for i in range(self.NDSEM)]
        self.dcnt = [0] * self.NDSEM
        self.dnext = 0
        self.waited = {e: {} for e in self.eng}
        self.nwaits = 0
        self.ninst = 0
        self.rec = None

    def replay(self, item):
        if item[0] == "op":
            self.op(*item[1:])
        else:
            self.dma(*item[1:3], **item[3])

    def _wait(self, eng, ev):
        if ev is None:
            return
        if ev[0] == "e":
            key, val = ("e", ev[1]), ev[2]
            if ev[1] == "pe" and eng == "pe":
                return
            sem = self.sem[ev[1]]
        else:
            key, val = ("d", ev[1]), ev[2]
            sem = self.dsem[ev[1]]
        if self.waited[eng].get(key, 0) >= val:
            return
        self.eng[eng].wait_ge(sem, val)
        self.waited[eng][key] = val
        self.nwaits += 1

    def _deps(self, eng, R, W):
        for b in R:
            self._wait(eng, b.w)
            if b.excl:
                for ev in b.r.values():
                    if not (ev[0] == "e" and ev[1] == eng):
                        self._wait(eng, ev)
        for b in W:
            self._wait(eng, b.w)
            for ev in b.r.values():
                self._wait(eng, ev)

    def _post(self, ev, R, W):
        for b in R:
            b.r[(ev[0], ev[1])] = ev
        for b in W:
            b.w = ev
            b.r = {}

    def op(self, eng, fn, R=(), W=()):
        if self.rec is not None:
            self.rec.append(("op", eng, fn, tuple(R), tuple(W)))
            return None
        self._deps(eng, R, W)
        inst = fn(self.eng[eng])
        self.cnt[eng] += 1
        inst.then_inc(self.sem[eng], 1)
        self._post(("e", eng, self.cnt[eng]), R, W)
        self.ninst += 1
        return inst

    def dma(self, out, in_, R=(), W=(), q="sp", **kw):
        if self.rec is not None:
            self.rec.append(("dma", out, in_, dict(R=tuple(R), W=tuple(W), q=q, **kw)))
            return None
        idx = self.dnext
        self.dnext = (self.dnext + 1) % self.NDSEM
        if self.dcnt[idx] > 0:
            self._wait(q, ("d", idx, self.dcnt[idx]))
        self._deps(q, R, W)
        inst = self.eng[q].dma_start(out=out, in_=in_, **kw)
        self.dcnt[idx] += 16
        inst.then_inc(self.dsem[idx], 16)
        ev = ("d", idx, self.dcnt[idx])
        self._post(ev, R, W)
        self.ninst += 1
        return ev

    def barrier(self):
        for e in self.eng:
            for e2 in self.sem:
                if self.cnt[e2] > 0:
                    self._wait(e, ("e", e2, self.cnt[e2])) if not (e == e2 == "pe") else None
            for i in range(self.NDSEM):
                if self.dcnt[i] > 0:
                    self._wait(e, ("d", i, self.dcnt[i]))


def host_consts(T):
    c = {}
    c["c_identf"] = np.eye(128, dtype=np.float32)
    j = np.arange(128)[:, None]
    i = np.arange(128)[None, :]
    same = (j // 64) == (i // 64)
    c["c_cmask"] = (j <= i).astype(np.float32)
    c["c_ublk"] = ((j <= i) & same).astype(np.float32)
    c["c_oblk"] = same.astype(np.float32)
    cind = np.zeros((128, 2), np.float32)
    cind[:64, 0] = 1.0
    cind[64:, 1] = 1.0
    c["c_cind"] = cind
    sel = np.zeros((4, 4, 128), np.float32)
    for h in range(4):
        sel[h, h, :] = 1.0
    c["c_sel"] = sel.reshape(4, 512)
    c["c_negA"] = np.where(same & (j > i), 0.0, NEG).astype(np.float32)
    c["c_negT"] = np.where(same & (i >= j), 0.0, NEG).astype(np.float32)
    c["c_ones64"] = np.ones((64, 64), np.float32)
    c["c_ones4"] = np.ones((4, 128), np.float32)
    c["c_negA4"] = np.tile(c["c_negA"], (1, 4))
    c["c_negT4"] = np.tile(c["c_negT"], (1, 4))
    t = np.arange(T, dtype=np.float64)
    inv = 10000.0 ** (-np.arange(32, dtype=np.float64) / 32.0)
    ang = (t[:, None].astype(np.float32) * inv[None, :].astype(np.float32)).astype(np.float64)
    cos, sin = np.cos(ang), np.sin(ang)
    gam = 1.0 - 2.0 ** (-5.0 - np.arange(4, dtype=np.float64))
    pos = (np.arange(T) % 128 + 1).astype(np.float64)
    dq = gam[None, :] ** pos[:, None]
    dk = gam[None, :] ** (-pos[:, None]) * 0.125
    rope = np.zeros((T, 4, 4, 32), np.float64)
    rope[:, 0] = cos[:, None, :] * dq[:, :, None]
    rope[:, 1] = sin[:, None, :] * dq[:, :, None]
    rope[:, 2] = cos[:, None, :] * dk[:, :, None]
    rope[:, 3] = sin[:, None, :] * dk[:, :, None]
    c["c_rope"] = rope.reshape(T, 512).astype(np.float32)
    gC = np.zeros((64, 4, 64), np.float64)
    gC[:] = (gam ** 128.0)[None, :, None]
    c["c_gC"] = gC.reshape(64, 256).astype(np.float32)
    return c


CONST_SHAPES = {"c_identf": [128, 128], "c_cmask": [128, 128], "c_ublk": [128, 128],
                "c_oblk": [128, 128], "c_cind": [128, 2], "c_sel": [4, 512],
                "c_negA": [128, 128], "c_negT": [128, 128], "c_ones64": [64, 64],
                "c_ones4": [4, 128], "c_negA4": [128, 512], "c_negT4": [128, 512],
                "c_gC": [64, 256]}

PARAM_SHAPES = {
    "norm_mix_w": [2, 1024], "w_in": [2, 1024, DIN], "ret_norm_w": [2, 256],
    "conf_conv_w": [2, 31, 256], "conf_conv_b": [2, 256], "conf_ln_w": [2, 256],
    "conf_ln_b": [2, 256], "gdn_conv_w": [2, 4, 768], "gdn_A_log": [2, 4],
    "gdn_dt_bias": [2, 4], "gdn_norm_w": [2, 64], "hgrn_lb_logits": [2, 256],
    "hgrn_norm_w": [2, 64], "w_out": [2, 1024, 1024], "norm_ffn_w": [2, 1024],
    "ffn_w_gate": [2, 1024, DFF], "ffn_w_up": [2, 1024, DFF], "ffn_w_down": [2, DFF, 1024],
    "final_norm_w": [1024],
}


class StopBuild(Exception):
    pass


def build(T, NL=2, dbg=False, skip=(), stop=99):
    nc = bass.Bass("TRN2", target_bir_lowering=False)

    def ckpt(n):
        if n >= stop:
            raise StopBuild()

    NT = T // 128
    k = K(nc)
    dr = {}
    dr["x"] = nc.dram_tensor("x", [T, D], F32, kind="ExternalInput").ap()
    for n, s in PARAM_SHAPES.items():
        dr[n] = nc.dram_tensor(n, s, F32, kind="ExternalInput").ap()
    for n, s in CONST_SHAPES.items():
        dr[n] = nc.dram_tensor(n, s, F32, kind="ExternalInput").ap()
    dr["c_rope"] = nc.dram_tensor("c_rope", [T, 512], F32, kind="ExternalInput").ap()
    out = nc.dram_tensor("out", [T, D], F32, kind="ExternalOutput").ap()
    xs = nc.dram_tensor("xs", [T, D], F32, kind="Internal").ap()
    if dbg:
        dmix = nc.dram_tensor("dmix", [NL, T, D], F32, kind="ExternalOutput").ap()
    xsb = [Buf("xs%d" % i) for i in range(NT)]

    es_top = ExitStack()

    uniq = {"n": 0}

    def sb(es, name, shape, dt=F32):
        uniq["n"] += 1
        t = es.enter_context(nc.sbuf_tensor("%s_%d" % (name, uniq["n"]), shape, dt))
        return t.ap(), Buf(name)

    def mm(out_, lhsT, rhs, R, W, start=True, stop=True):
        k.op("pe", lambda e: e.matmul(out_, lhsT=lhsT, rhs=rhs, start=start, stop=stop), R=R, W=W)

    def tr(out_, in_, ident, R, W):
        k.op("pe", lambda e: e.transpose(out=out_, in_=in_, identity=ident), R=R, W=W)

    def act(out_, in_, func, R, W, bias=None, scale=None, accum=None):
        kw = {}
        if bias is not None:
            kw["bias"] = bias
        if scale is not None:
            kw["scale"] = scale
        if accum is not None:
            kw["accum_out"] = accum
        k.op("act", lambda e: e.activation(out=out_, in_=in_, func=func, **kw), R=R, W=W)

    def silu(out_, in_, R, W, scr, b_scr):
        act(scr, in_, AF.Sigmoid, R, [b_scr])
        tt(out_, in_, scr, ALU.mult, list(R) + [b_scr], W)

    def tt(out_, a, b, op, R, W, eng="dve"):
        k.op(eng, lambda e: e.tensor_tensor(out=out_, in0=a, in1=b, op=op), R=R, W=W)

    def ts(out_, a, s1, op0, R, W, s2=None, op1=None, eng="dve"):
        if op1 is None:
            k.op(eng, lambda e: e.tensor_scalar(out=out_, in0=a, scalar1=s1, scalar2=None, op0=op0), R=R, W=W)
        else:
            k.op(eng, lambda e: e.tensor_scalar(out=out_, in0=a, scalar1=s1, scalar2=None, op0=op0), R=R, W=W)
            k.op(eng, lambda e: e.tensor_scalar(out=out_, in0=out_, scalar1=s2, scalar2=None, op0=op1), R=list(R) + list(W), W=W)

    def stt(out_, in0, scalar, in1, op0, op1, R, W):
        k.op("dve", lambda e: e.scalar_tensor_tensor(out=out_, in0=in0, scalar=scalar, in1=in1, op0=op0, op1=op1), R=R, W=W)

    def cp(out_, in_, R, W, eng="dve"):
        if eng == "act":
            k.op("act", lambda e: e.copy(out=out_, in_=in_), R=R, W=W)
        else:
            k.op(eng, lambda e: e.tensor_copy(out=out_, in_=in_), R=R, W=W)

    def red(out_, in_, R, W):
        k.op("dve", lambda e: e.tensor_reduce(out=out_, in_=in_, axis=AX.X, op=ALU.add), R=R, W=W)

    def rcp(out_, in_, R, W):
        k.op("dve", lambda e: e.reciprocal(out=out_, in_=in_), R=R, W=W)

    def mset(ap, val, W, eng="pool"):
        k.op(eng, lambda e: e.memset(ap, val), W=W)

    def v3(ap, inner):
        return ap.rearrange("p (a b) -> p a b", b=inner)

    banks = []
    for i in range(8):
        t = es_top.enter_context(nc.psum_tensor("bank%d" % i, [128, 512], F32))
        banks.append((t.ap(), Buf("bank%d" % i, excl=True)))
    freeq = list(range(8))
    bank_idx = {id(b[1]): i for i, b in enumerate(banks)}
    cur = {"free": freeq}

    def psum():
        assert cur["free"], "no free PSUM bank"
        return banks[cur["free"].pop(0)]

    def galloc():
        if False:
            yield
        return psum()

    def pfree(*bufs):
        for b in bufs:
            i = bank_idx[id(b)]
            assert i not in cur["free"]
            cur["free"].append(i)

    def run_interleaved(gens):
        lists = []
        for g, bset in gens:
            cur["free"] = list(bset)
            k.rec = []
            for _ in g():
                pass
            lists.append(k.rec)
            k.rec = None
            assert sorted(cur["free"]) == sorted(bset), "mixer leaked PSUM banks"
        cur["free"] = freeq
        pos = [0] * len(lists)
        while True:
            best, bf = -1, 2.0
            for j, lst in enumerate(lists):
                if pos[j] < len(lst):
                    f = pos[j] / len(lst)
                    if f < bf:
                        best, bf = j, f
            if best < 0:
                break
            k.replay(lists[best][pos[best]])
            pos[best] += 1

    stg = [sb(es_top, "stg%d" % i, [128, 1024]) for i in range(2)]
    stg_c = stg[0]
    identf, b_identf = sb(es_top, "identf", [128, 128])
    identb, b_identb = sb(es_top, "identb", [128, 128], BF16)
    cmask, b_cmask = sb(es_top, "cmask", [128, 128])
    ublk, b_ublk = sb(es_top, "ublk", [128, 128])
    oblk, b_oblk = sb(es_top, "oblk", [128, 128])
    cind, b_cind = sb(es_top, "cind", [128, 2])
    sel, b_sel = sb(es_top, "sel", [4, 512])
    negAb, b_negAb = sb(es_top, "negAb", [128, 512], BF16)
    negTb, b_negTb = sb(es_top, "negTb", [128, 512], BF16)
    ones4, b_ones4 = sb(es_top, "ones4", [4, 128])
    ones64, b_ones64 = sb(es_top, "ones64", [64, 64])
    gC, b_gC = sb(es_top, "gC", [64, 256])
    CONSTS = []
    k.dma(identf, dr["c_identf"], W=[b_identf])
    k.dma(cmask, dr["c_cmask"], W=[b_cmask])
    k.dma(ublk, dr["c_ublk"], W=[b_ublk])
    k.dma(oblk, dr["c_oblk"], W=[b_oblk])
    k.dma(cind, dr["c_cind"], W=[b_cind])
    k.dma(sel, dr["c_sel"], W=[b_sel])
    k.dma(ones4, dr["c_ones4"], W=[b_ones4])
    k.dma(ones64, dr["c_ones64"], W=[b_ones64])
    k.dma(gC, dr["c_gC"], W=[b_gC])
    cp(identb, identf, [b_identf], [b_identb])
    for (dst_, b_dst_, nm_) in [(negAb, b_negAb, "c_negA4"), (negTb, b_negTb, "c_negT4")]:
        s_ap, s_b = stg_c
        k.dma(s_ap[:, 0:512], dr[nm_], W=[s_b])
        cp(dst_, s_ap[:, 0:512], [s_b], [b_dst_])
    sel3 = v3(sel, 128)
    try:
        ckpt(1)
    except StopBuild:
        k.barrier()
        return nc, k

    xt, b_xt = sb(es_top, "xt", [128, D])
    xo, b_xo = sb(es_top, "xo", [128, D])
    xn, b_xn = sb(es_top, "xn", [128, D], BF16)
    hT, b_hT = sb(es_top, "hT", [128, 8, 128], BF16)
    ss, b_ss = sb(es_top, "ss", [128, 1])
    nw, b_nw = sb(es_top, "nw", [128, 8])
    stg_i = {"i": 0}

    def load_w(dst, b_dst, src, KC, N, scale):
        for kc in range(KC):
            for n0 in range(0, N, 1024):
                n1 = min(N, n0 + 1024)
                s_ap, s_b = stg[stg_i["i"] % 2]
                eng = "dve" if stg_i["i"] % 2 == 0 else "pool"
                stg_i["i"] += 1
                k.dma(s_ap[:, :n1 - n0], src[kc * 128:(kc + 1) * 128, n0:n1], W=[s_b])
                if scale is not None:
                    ts(dst[:, kc, n0:n1], s_ap[:, :n1 - n0], scale[:, kc:kc + 1], ALU.mult,
                       [s_b, b_nw], [b_dst], eng=eng)
                else:
                    cp(dst[:, kc, n0:n1], s_ap[:, :n1 - n0], [s_b], [b_dst], eng=eng)

    def norm_to_hT(src_xt):
        act(xn, src_xt, AF.Square, [b_xt], [b_xn, b_ss], accum=ss)
        act(ss, ss, AF.Sqrt, [b_ss], [b_ss], bias=1e-6, scale=1.0 / D)
        rcp(ss, ss, [b_ss], [b_ss])
        ts(xn, src_xt, ss, ALU.mult, [b_xt, b_ss], [b_xn])
        pb, b_pb = psum()
        pbv = v3(pb.bitcast(BF16), 128)
        for kc in range(8):
            tr(pbv[:, kc, :], xn[:, kc * 128:(kc + 1) * 128], identb, [b_xn, b_identb], [b_pb])
        cp(hT, pbv, [b_pb], [b_hT], eng="act")
        pfree(b_pb)

    def pass_a(l):
        es = ExitStack()
        Wi, b_Wi = sb(es, "Wi", [128, 8, DIN], BF16)
        Wo, b_Wo = sb(es, "Wo", [128, 8, D], BF16)
        Dc, b_Dc = sb(es, "Dc", [128, 62, 128], BF16)
        ccw, b_ccw = sb(es, "ccw", [128, 2, 31])
        ccb, b_ccb = sb(es, "ccb", [128, 2])
        gcw, b_gcw = sb(es, "gcw", [64, 12, 4])
        rnw, b_rnw = sb(es, "rnw", [128, 256])
        clnw, b_clnw = sb(es, "clnw", [128, 256])
        clnb, b_clnb = sb(es, "clnb", [128, 256])
        gnw, b_gnw = sb(es, "gnw", [128, 64])
        hnw, b_hnw = sb(es, "hnw", [128, 64])
        negA, b_negA = sb(es, "negA", [128, 4])
        dtb, b_dtb = sb(es, "dtb", [128, 4])
        lbl, b_lbl = sb(es, "lbl", [128, 2, 256])
        lbt, b_lbt = sb(es, "lbt", [128, 256])
        oml, b_oml = sb(es, "oml", [128, 256])
        with nc.allow_non_contiguous_dma(reason="tiny param loads"):
            k.dma(nw, dr["norm_mix_w"][l].rearrange("(c p) -> p c", p=128), W=[b_nw])
            for ct in range(2):
                k.dma(ccw[:, ct, :], dr["conf_conv_w"][l][:, ct * 128:(ct + 1) * 128].rearrange("k c -> c k"), W=[b_ccw])
            k.dma(ccb, dr["conf_conv_b"][l].rearrange("(ct c) -> c ct", c=128), W=[b_ccb])
            for s_ in range(12):
                k.dma(gcw[:, s_, :], dr["gdn_conv_w"][l][:, s_ * 64:(s_ + 1) * 64].rearrange("k d -> d k"), W=[b_gcw])
            k.dma(rnw, dr["ret_norm_w"][l].partition_broadcast(128), W=[b_rnw])
            k.dma(clnw, dr["conf_ln_w"][l].partition_broadcast(128), W=[b_clnw])
            k.dma(clnb, dr["conf_ln_b"][l].partition_broadcast(128), W=[b_clnb])
            k.dma(gnw, dr["gdn_norm_w"][l].partition_broadcast(128), W=[b_gnw])
            k.dma(hnw, dr["hgrn_norm_w"][l].partition_broadcast(128), W=[b_hnw])
            k.dma(negA, dr["gdn_A_log"][l].partition_broadcast(128), W=[b_negA])
            k.dma(dtb, dr["gdn_dt_bias"][l].partition_broadcast(128), W=[b_dtb])
            k.dma(lbl[:, 0, :], dr["hgrn_lb_logits"][0].partition_broadcast(128), W=[b_lbl])
            k.dma(lbl[:, 1, :], dr["hgrn_lb_logits"][1].partition_broadcast(128), W=[b_lbl])
        ckpt(2)
        act(negA, negA, AF.Exp, [b_negA], [b_negA])
        ts(negA, negA, -1.0, ALU.mult, [b_negA], [b_negA])
        if l == 0:
            mset(lbt, 0.0, [b_lbt])
            mset(oml, 1.0, [b_oml])
        else:
            act(lbl, lbl, AF.Exp, [b_lbl], [b_lbl])
            tt(oml, lbl[:, 0, :], lbl[:, 1, :], ALU.add, [b_lbl], [b_oml])
            rcp(oml, oml, [b_oml], [b_oml])
            tt(lbt, lbl[:, 1, :], oml, ALU.mult, [b_lbl, b_oml], [b_lbt])
            ts(oml, lbt, -1.0, ALU.mult, [b_lbt], [b_oml], s2=1.0, op1=ALU.add)
        for kk in range(31):
            for ct in range(2):
                ts(Dc[:, kk * 2 + ct, :], identf, ccw[:, ct, kk:kk + 1], ALU.mult, [b_identf, b_ccw], [b_Dc])
        ckpt(3)
        load_w(Wi, b_Wi, dr["w_in"][l], 8, DIN, nw)
        load_w(Wo, b_Wo, dr["w_out"][l], 8, D, None)

        ckpt(4)
        rope_t, b_rope = sb(es, "rope_t", [128, 512])
        rqk, b_rqk = sb(es, "rqk", [128, 512])
        ctsg, b_ctsg = sb(es, "ctsg", [128, 256])
        glt, b_glt = sb(es, "glt", [128, 256], BF16)
        gpt, b_gpt = sb(es, "gpt", [128, 768], BF16)
        Gblk, b_Gblk = sb(es, "Gblk", [4, 512])
        nGblk, b_nGblk = sb(es, "nGblk", [4, 512])
        hqf, b_hqf = sb(es, "hqf", [128, 256])
        t1, b_t1 = sb(es, "t1", [128, 4, 32])
        t2, b_t2 = sb(es, "t2", [128, 4, 32])
        qrt, b_qrt = sb(es, "qrt", [128, 4, 64], BF16)
        krt, b_krt = sb(es, "krt", [128, 4, 64], BF16)
        qkT, b_qkT = sb(es, "qkT", [64, 8, 128], BF16)
        PTr, b_PTr = sb(es, "PTr", [128, 4, 128], BF16)
        rvb, b_rvb = sb(es, "rvb", [128, 256], BF16)
        Sr32, b_Sr32 = sb(es, "Sr32", [64, 256])
        Srb, b_Srb = sb(es, "Srb", [64, 4, 64], BF16)
        silr, b_silr = sb(es, "silr", [128, 256])
        silg, b_silg = sb(es, "silg", [128, 256])
        silh, b_silh = sb(es, "silh", [128, 256])
        HN = {}
        for nm_ in ("ret", "hgrn", "gdn"):
            HN[nm_] = dict(hsq=sb(es, "hsq", [128, 256]), hy=sb(es, "hy", [128, 256]), s1=sb(es, "s1", [128, 4]),
                           s2=sb(es, "s2", [128, 4]), mean=sb(es, "mean", [128, 4]), msq=sb(es, "msq", [128, 4]),
                           var=sb(es, "var", [128, 4]))
        hsq, b_hsq = HN["ret"]["hsq"]
        hf, b_hf = sb(es, "hf", [128, 256])
        logf, b_logf = sb(es, "logf", [128, 256])
        hkk, b_hkk = sb(es, "hkk", [128, 256])
        Gsb, b_Gsb = sb(es, "Gsb", [128, 256])
        eG, b_eG = sb(es, "eG", [128, 256])
        enG, b_enG = eG, b_eG
        hD, b_hD = Gsb, b_Gsb
        qh, b_qh = sb(es, "qh", [128, 256], BF16)
        kht, b_kht = sb(es, "kht", [128, 256], BF16)
        khat, b_khat = sb(es, "khat", [128, 256], BF16)
        hvb, b_hvb = sb(es, "hvb", [128, 256], BF16)
        hq = [sb(es, "hq%d" % c, [64, 4, 128], BF16) for c in range(2)]
        hkT, b_hkT = sb(es, "hkT", [64, 4, 128], BF16)
        PTh, b_PTh = sb(es, "PTh", [128, 4, 128], BF16)
        Sh32, b_Sh32 = sb(es, "Sh32", [64, 4, 64])
        Shb = [sb(es, "Shb%d" % c, [64, 4, 64], BF16) for c in range(3)]
        eGl, b_eGl = sb(es, "eGl", [64, 8])
        gl, b_gl = sb(es, "gl", [128, 2, 158], BF16)
        yT, b_yT = sb(es, "yT", [128, 2, 128])
        yc, b_yc = sb(es, "yc", [128, 256])
        c1, b_c1 = sb(es, "c1", [128, 1])
        c2, b_c2 = sb(es, "c2", [128, 1])
        c3, b_c3 = sb(es, "c3", [128, 1])
        gp, b_gp = sb(es, "gp", [64, 12, 131], BF16)
        gx, b_gx = sb(es, "gx", [64, 12, 128])
        gsq, b_gsq = sb(es, "gsq", [64, 4, 128])
        grs, b_grs = sb(es, "grs", [64, 4, 128])
        gqT, b_gqT = sb(es, "gqT", [64, 4, 128], BF16)
        gkT, b_gkT = sb(es, "gkT", [64, 4, 128], BF16)
        gvT, b_gvT = sb(es, "gvT", [64, 4, 128], BF16)
        grk, b_grk = sb(es, "grk", [128, 4, 64], BF16)
        gkh, b_gkh = sb(es, "gkh", [128, 4, 64], BF16)
        gbv, b_gbv = sb(es, "gbv", [128, 4, 64], BF16)
        EA, b_EA = sb(es, "EA", [128, 4, 128], BF16)
        ET, b_ET = sb(es, "ET", [128, 4, 128], BF16)
        eGr, b_eGr = sb(es, "eGr", [64, 4, 128])
        gq = [sb(es, "gq%d" % c, [64, 4, 128], BF16) for c in range(2)]
        gw = [sb(es, "gw%d" % c, [64, 4, 128], BF16) for c in range(2)]
        Pm, b_Pm = sb(es, "Pm", [128, 4, 128], BF16)
        PmT, b_PmT = sb(es, "PmT", [128, 4, 128], BF16)
        Wt, b_Wt = sb(es, "Wt", [128, 4, 128], BF16)
        attT, b_attT = sb(es, "attT", [128, 4, 128], BF16)
        usb, b_usb = sb(es, "usb", [128, 256])
        vn, b_vn = sb(es, "vn", [128, 256], BF16)
        Sg32, b_Sg32 = sb(es, "Sg32", [64, 4, 64])
        Sgb = [sb(es, "Sgb%d" % c, [64, 4, 64], BF16) for c in range(3)]
        beta, b_beta = sb(es, "beta", [128, 4])
        gba, b_gba = sb(es, "gba", [128, 8])
        lnb, b_lnb = sb(es, "lnb", [128, 4])
        gsp, b_gsp = sb(es, "gsp", [128, 4])
        gg_, b_gg = sb(es, "gg_", [128, 4])
        Gs, b_Gs = sb(es, "Gs", [128, 4])
        nGs, b_nGs = sb(es, "nGs", [128, 4])
        Bs, b_Bs = sb(es, "Bs", [128, 4])
        eB, b_eB = sb(es, "eB", [128, 4])
        eDl, b_eDl = sb(es, "eDl", [128, 4])
        GTs, b_GTs = sb(es, "GTs", [4, 128])
        mix, b_mix = sb(es, "mix", [128, D], BF16)
        mT, b_mT = sb(es, "mT", [128, 8, 128], BF16)
        mixf, b_mixf = (xo, b_xo) if dbg else (None, None)

        mset(Sr32, 0.0, [b_Sr32]); mset(Srb, 0.0, [b_Srb])
        mset(Sh32, 0.0, [b_Sh32]); mset(Sg32, 0.0, [b_Sg32])
        for c in range(3):
            mset(Shb[c][0], 0.0, [Shb[c][1]]); mset(Sgb[c][0], 0.0, [Sgb[c][1]])
        for c in range(2):
            mset(hq[c][0], 0.0, [hq[c][1]]); mset(gq[c][0], 0.0, [gq[c][1]]); mset(gw[c][0], 0.0, [gw[c][1]])
        mset(gl, 0.0, [b_gl]); mset(gp, 0.0, [b_gp])
        mset(mix, 0.0, [b_mix])

        ckpt(5)

        def head_norm(o, b_o, sil, b_sil, w3, b_w, center, eps, dst3, who):
            t_ = HN[who]
            (hsq, b_hsq), (hy, b_hy), (s1, b_s1), (s2, b_s2) = t_["hsq"], t_["hy"], t_["s1"], t_["s2"]
            (mean, b_mean), (msq, b_msq), (var, b_var) = t_["mean"], t_["msq"], t_["var"]
            o3 = v3(o, 64)
            red(s1, o3, [b_o], [b_s1])
            act(hsq, o, AF.Square, [b_o], [b_hsq])
            red(s2, v3(hsq, 64), [b_hsq], [b_s2])
            if center:
                ts(mean, s1, 1.0 / 64, ALU.mult, [b_s1], [b_mean])
                tt(msq, mean, mean, ALU.mult, [b_mean], [b_msq])
                stt(var, s2, 1.0 / 64, msq, ALU.mult, ALU.subtract, [b_s2, b_msq], [b_var])
            else:
                ts(var, s2, 1.0 / 64, ALU.mult, [b_s2], [b_var])
            act(var, var, AF.Sqrt, [b_var], [b_var], bias=eps)
            rcp(var, var, [b_var], [b_var])
            hy3 = v3(hy, 64)
            if center:
                tt(hy3, o3, mean.unsqueeze(2).broadcast_to([128, 4, 64]), ALU.subtract, [b_o, b_mean], [b_hy])
                tt(hy3, hy3, var.unsqueeze(2).broadcast_to([128, 4, 64]), ALU.mult, [b_hy, b_var], [b_hy])
            else:
                tt(hy3, o3, var.unsqueeze(2).broadcast_to([128, 4, 64]), ALU.mult, [b_o, b_var], [b_hy])
            tt(hy3, hy3, w3, ALU.mult, [b_hy, b_w], [b_hy])
            tt(dst3, hy3, v3(sil, 64), ALU.mult, [b_hy, b_sil], [b_mix])

        for i in range(NT):
            pv, mid, end = (2 * i) % 3, (2 * i + 1) % 3, (2 * i + 2) % 3
            r0, r1 = slice(0, 64), slice(64, 128)
            src = dr["x"] if l == 0 else xs
            k.dma(xt, src[i * 128:(i + 1) * 128, :], R=([] if l == 0 else [xsb[i]]), W=[b_xt])
            k.dma(rope_t, dr["c_rope"][i * 128:(i + 1) * 128, :], W=[b_rope])
            ckpt(6)
            norm_to_hT(xt)
            ckpt(7)
            ckpt(8)

            def g_ret():
                pg, b_pg = yield from galloc()
                for kc in range(8):
                    mm(pg[:, 0:512], hT[:, kc, :], Wi[:, kc, 0:512], [b_hT, b_Wi], [b_pg], kc == 0, kc == 7)
                cp(rqk, pg, [b_pg], [b_rqk], eng="act")
                pfree(b_pg)
                pg, b_pg = yield from galloc()
                for kc in range(8):
                    mm(pg[:, 0:512], hT[:, kc, :], Wi[:, kc, 512:1024], [b_hT, b_Wi], [b_pg], kc == 0, kc == 7)
                silu(silr, pg[:, 256:512], [b_pg], [b_silr], HN["ret"]["hsq"][0], HN["ret"]["hsq"][1])
                cp(rvb, pg[:, 0:256], [b_pg], [b_rvb])
                pfree(b_pg)
                rt4 = rope_t.rearrange("p (a h d) -> p a h d", a=4, h=4)
                for (off, ci, si, dst, b_dst) in [(0, 0, 1, qrt, b_qrt), (256, 2, 3, krt, b_krt)]:
                    X = v3(rqk[:, off:off + 256], 64)
                    x1, x2 = X[:, :, 0:32], X[:, :, 32:64]
                    cs, sn = rt4[:, ci], rt4[:, si]
                    tt(t1, x1, cs, ALU.mult, [b_rqk, b_rope], [b_t1])
                    tt(t2, x2, sn, ALU.mult, [b_rqk, b_rope], [b_t2])
                    tt(dst[:, :, 0:32], t1, t2, ALU.subtract, [b_t1, b_t2], [b_dst])
                    tt(t1, x1, sn, ALU.mult, [b_rqk, b_rope], [b_t1])
                    tt(t2, x2, cs, ALU.mult, [b_rqk, b_rope], [b_t2])
                    tt(dst[:, :, 32:64], t1, t2, ALU.add, [b_t1, b_t2], [b_dst])
                pb, b_pb = yield from galloc()
                pbv = v3(pb.bitcast(BF16), 128)
                for h in range(4):
                    tr(pbv[0:64, h, :], qrt[:, h, :], identb, [b_qrt, b_identb], [b_pb])
                    tr(pbv[0:64, 4 + h, :], krt[:, h, :], identb, [b_krt, b_identb], [b_pb])
                cp(qkT, pbv[0:64], [b_pb], [b_qkT], eng="act")
                pfree(b_pb)
                yield
                ps, b_ps = yield from galloc()
                ps3 = v3(ps, 128)
                for h in range(4):
                    mm(ps3[:, h, :], qkT[:, 4 + h, :], qkT[:, h, :], [b_qkT], [b_ps])
                tt(PTr, ps3, cmask.unsqueeze(1).broadcast_to([128, 4, 128]), ALU.mult, [b_ps, b_cmask], [b_PTr])
                pfree(b_ps)
                yield
                po, b_po = yield from galloc()
                for h in range(4):
                    hs = slice(h * 64, (h + 1) * 64)
                    mm(po[:, hs], PTr[:, h, :], rvb[:, hs], [b_PTr, b_rvb], [b_po], True, False)
                    mm(po[:, hs], qkT[:, h, :], Srb[:, h, :], [b_qkT, b_Srb], [b_po], False, True)
                pd, b_pd = yield from galloc()
                for h in range(4):
                    hs = slice(h * 64, (h + 1) * 64)
                    mm(pd[0:64, hs], krt[:, h, :], rvb[:, hs], [b_krt, b_rvb], [b_pd])
                tt(Sr32, Sr32, pd[0:64, 0:256], ALU.add, [b_Sr32, b_pd], [b_Sr32])
                tt(Sr32, Sr32, gC, ALU.mult, [b_Sr32, b_gC], [b_Sr32])
                cp(Srb, v3(Sr32, 64), [b_Sr32], [b_Srb], eng="pool")
                pfree(b_pd)
                yield
                head_norm(po[:, 0:256], b_po, silr, b_silr, v3(rnw, 64), b_rnw, True, 1e-5, v3(mix[:, 0:256], 64), "ret")
                pfree(b_po)

            def g_hgrn():
                pg, b_pg = yield from galloc()
                for kc in range(8):
                    mm(pg[:, 0:512], hT[:, kc, :], Wi[:, kc, 2568:3080], [b_hT, b_Wi], [b_pg], kc == 0, kc == 7)
                cp(hqf, pg[:, 0:256], [b_pg], [b_hqf])
                act(hf, pg[:, 256:512], AF.Sigmoid, [b_pg], [b_hf])
                pfree(b_pg)
                pg, b_pg = yield from galloc()
                for kc in range(8):
                    mm(pg[:, 0:512], hT[:, kc, :], Wi[:, kc, 3080:3592], [b_hT, b_Wi], [b_pg], kc == 0, kc == 7)
                silu(silh, pg[:, 256:512], [b_pg], [b_silh], HN["hgrn"]["hsq"][0], HN["hgrn"]["hsq"][1])
                cp(hvb, pg[:, 0:256], [b_pg], [b_hvb])
                pfree(b_pg)
                tt(hf, hf, oml, ALU.mult, [b_hf, b_oml], [b_hf])
                tt(hf, hf, lbt, ALU.add, [b_hf, b_lbt], [b_hf])
                act(logf, hf, AF.Ln, [b_hf], [b_logf])
                ts(hkk, hf, -1.0, ALU.mult, [b_hf], [b_hkk], s2=1.0, op1=ALU.add)
                pG, b_pG = yield from galloc()
                mm(pG[:, 0:256], ublk, logf, [b_ublk, b_logf], [b_pG])
                mm(pG[:, 256:512], oblk, logf, [b_oblk, b_logf], [b_pG])
                pL, b_pL = yield from galloc()
                for h in range(4):
                    mm(pL[0:64, h * 2:(h + 1) * 2], logf[:, h * 64:(h + 1) * 64], cind, [b_logf, b_cind], [b_pL])
                act(eGl, pL[0:64, 0:8], AF.Exp, [b_pL], [b_eGl])
                pfree(b_pL)
                cp(Gsb, pG[:, 0:256], [b_pG], [b_Gsb], eng="act")
                act(eG, pG[:, 0:256], AF.Exp, [b_pG], [b_eG])
                tt(qh, hqf, eG, ALU.mult, [b_hqf, b_eG], [b_qh])
                act(enG, pG[:, 0:256], AF.Exp, [b_pG], [b_enG], scale=-1.0)
                tt(kht, hkk, enG, ALU.mult, [b_hkk, b_enG], [b_kht])
                tt(hD, pG[:, 256:512], Gsb, ALU.subtract, [b_pG, b_Gsb], [b_hD])
                pfree(b_pG)
                yield
                act(hD, hD, AF.Exp, [b_hD], [b_hD])
                tt(khat, hkk, hD, ALU.mult, [b_hkk, b_hD], [b_khat])
                pb, b_pb = yield from galloc()
                pbv = v3(pb.bitcast(BF16), 128)
                for h in range(4):
                    tr(pbv[0:64, h, :], qh[:, h * 64:(h + 1) * 64], identb, [b_qh, b_identb], [b_pb])
                    tr(pbv[0:64, 4 + h, :], kht[:, h * 64:(h + 1) * 64], identb, [b_kht, b_identb], [b_pb])
                cp(hq[0][0][:, :, 0:64], pbv[0:64, 0:4, 0:64], [b_pb], [hq[0][1]], eng="act")
                cp(hq[1][0][:, :, 64:128], pbv[0:64, 0:4, 64:128], [b_pb], [hq[1][1]], eng="act")
                cp(hkT, pbv[0:64, 4:8, :], [b_pb], [b_hkT])
                pfree(b_pb)
                yield
                ps, b_ps = yield from galloc()
                ps3 = v3(ps, 128)
                for h in range(4):
                    for c in range(2):
                        cs_ = slice(c * 64, (c + 1) * 64)
                        mm(ps3[:, h, cs_], hkT[:, h, :], hq[c][0][:, h, cs_], [b_hkT, hq[c][1]], [b_ps])
                tt(PTh, ps3, ublk.unsqueeze(1).broadcast_to([128, 4, 128]), ALU.mult, [b_ps, b_ublk], [b_PTh])
                pfree(b_ps)
                yield
                for c, (rr, sdst) in enumerate([(r0, mid), (r1, end)]):
                    pd, b_pd = yield from galloc()
                    for h in range(4):
                        hs = slice(h * 64, (h + 1) * 64)
                        mm(pd[0:64, hs], khat[rr, hs], hvb[rr, hs], [b_khat, b_hvb], [b_pd])
                    for h in range(4):
                        hs = slice(h * 64, (h + 1) * 64)
                        stt(Sh32[:, h, :], Sh32[:, h, :], eGl[:, h * 2 + c:h * 2 + c + 1], pd[0:64, hs],
                            ALU.mult, ALU.add, [b_Sh32, b_eGl, b_pd], [b_Sh32])
                    cp(Shb[sdst][0], Sh32, [b_Sh32], [Shb[sdst][1]], eng="pool")
                    pfree(b_pd)
                    yield
                po, b_po = yield from galloc()
                for h in range(4):
                    hs = slice(h * 64, (h + 1) * 64)
                    mm(po[:, hs], PTh[:, h, :], hvb[:, hs], [b_PTh, b_hvb], [b_po], True, False)
                    mm(po[:, hs], hq[0][0][:, h, :], Shb[pv][0][:, h, :], [hq[0][1], Shb[pv][1]], [b_po], False, False)
                    mm(po[:, hs], hq[1][0][:, h, :], Shb[mid][0][:, h, :], [hq[1][1], Shb[mid][1]], [b_po], False, True)
                head_norm(po[:, 0:256], b_po, silh, b_silh, hnw.unsqueeze(1).broadcast_to([128, 4, 64]), b_hnw,
                          False, 1e-6, v3(mix[:, 768:1024], 64), "hgrn")
                pfree(b_po)

            def g_conf():
                pg, b_pg = yield from galloc()
                for kc in range(8):
                    mm(pg[:, 0:512], hT[:, kc, :], Wi[:, kc, 1024:1536], [b_hT, b_Wi], [b_pg], kc == 0, kc == 7)
                act(ctsg, pg[:, 256:512], AF.Sigmoid, [b_pg], [b_ctsg])
                tt(glt, pg[:, 0:256], ctsg, ALU.mult, [b_pg, b_ctsg], [b_glt])
                pfree(b_pg)
                pc, b_pc = yield from galloc()
                pc3 = v3(pc.bitcast(BF16)[:, 0:256], 128)
                for ct in range(2):
                    tr(pc3[:, ct, :], glt[:, ct * 128:(ct + 1) * 128], identb, [b_glt, b_identb], [b_pc])
                cp(gl[:, :, 30:158], pc3, [b_pc], [b_gl], eng="act")
                pfree(b_pc)
                yield
                ckpt(21)
                py, b_py = yield from galloc()
                py3 = v3(py, 128)
                for ct in range(2):
                    for kk in range(31):
                        mm(py3[:, ct, :], Dc[:, kk * 2 + ct, :], gl[:, ct, kk:kk + 128], [b_Dc, b_gl], [b_py],
                           kk == 0, kk == 30)
                ckpt(22)
                for ct in range(2):
                    act(yT[:, ct, :], py3[:, ct, :], AF.Identity, [b_py, b_ccb], [b_yT], bias=ccb[:, ct:ct + 1])
                pfree(b_py)
                yield
                cp(gl[:, :, 0:30], gl[:, :, 128:158], [b_gl], [b_gl], eng="pool")
                ckpt(24)
                pt, b_pt = yield from galloc()
                for ct in range(2):
                    tr(pt[:, ct * 128:(ct + 1) * 128], yT[:, ct, :], identf, [b_yT, b_identf], [b_pt])
                ckpt(25)
                red(c1, pt[:, 0:256].unsqueeze(1), [b_pt], [b_c1])
                act(yc, pt[:, 0:256], AF.Square, [b_pt], [b_yc])
                red(c2, yc.unsqueeze(1), [b_yc], [b_c2])
                ckpt(27)
                ts(c1, c1, 1.0 / 256, ALU.mult, [b_c1], [b_c1])
                tt(c3, c1, c1, ALU.mult, [b_c1], [b_c3])
                stt(c2, c2, 1.0 / 256, c3, ALU.mult, ALU.subtract, [b_c2, b_c3], [b_c2])
                ckpt(28)
                act(c2, c2, AF.Sqrt, [b_c2], [b_c2], bias=1e-5)
                rcp(c2, c2, [b_c2], [b_c2])
                ckpt(29)
                ts(yc, pt[:, 0:256], c1, ALU.subtract, [b_pt, b_c1, b_c2], [b_yc], s2=c2, op1=ALU.mult)
                pfree(b_pt)
                ckpt(30)
                tt(yc, yc, clnw, ALU.mult, [b_yc, b_clnw], [b_yc])
                tt(yc, yc, clnb, ALU.add, [b_yc, b_clnb], [b_yc])
                silu(mix[:, 256:512], yc, [b_yc], [b_mix], ctsg, b_ctsg)

            def g_gdn():
                pg, b_pg = yield from galloc()
                for kc in range(8):
                    mm(pg[:, 0:264], hT[:, kc, :], Wi[:, kc, 2304:2568], [b_hT, b_Wi], [b_pg], kc == 0, kc == 7)
                silu(silg, pg[:, 0:256], [b_pg], [b_silg], HN["gdn"]["hsq"][0], HN["gdn"]["hsq"][1])
                cp(gba, pg[:, 256:264], [b_pg], [b_gba])
                pfree(b_pg)
                pg, b_pg = yield from galloc()
                for kc in range(8):
                    mm(pg[:, 0:512], hT[:, kc, :], Wi[:, kc, 1536:2048], [b_hT, b_Wi], [b_pg], kc == 0, kc == 7)
                cp(gpt[:, 0:512], pg, [b_pg], [b_gpt], eng="act")
                pfree(b_pg)
                pg, b_pg = yield from galloc()
                for kc in range(8):
                    mm(pg[:, 0:256], hT[:, kc, :], Wi[:, kc, 2048:2304], [b_hT, b_Wi], [b_pg], kc == 0, kc == 7)
                cp(gpt[:, 512:768], pg[:, 0:256], [b_pg], [b_gpt])
                pfree(b_pg)
                pb1, b_pb1 = yield from galloc()
                pb2, b_pb2 = yield from galloc()
                v1 = v3(pb1.bitcast(BF16), 128)
                v2 = v3(pb2.bitcast(BF16)[:, 0:512], 128)
                for s in range(12):
                    dst_ = v1[0:64, s, :] if s < 8 else v2[0:64, s - 8, :]
                    tr(dst_, gpt[:, s * 64:(s + 1) * 64], identb, [b_gpt, b_identb], [b_pb1 if s < 8 else b_pb2])
                cp(gp[:, 0:8, 3:131], v1[0:64], [b_pb1], [b_gp], eng="act")
                cp(gp[:, 8:12, 3:131], v2[0:64], [b_pb2], [b_gp])
                pfree(b_pb1, b_pb2)
                yield
                for s in range(12):
                    ts(gx[:, s, :], gp[:, s, 0:128], gcw[:, s, 0:1], ALU.mult, [b_gp, b_gcw], [b_gx])
                    for kk in range(1, 4):
                        stt(gx[:, s, :], gp[:, s, kk:kk + 128], gcw[:, s, kk:kk + 1], gx[:, s, :],
                            ALU.mult, ALU.add, [b_gp, b_gcw, b_gx], [b_gx])
                    if s % 4 == 3:
                        yield
                for b in range(3):
                    silu(gx[:, b * 4:(b + 1) * 4, :], gx[:, b * 4:(b + 1) * 4, :], [b_gx], [b_gx], gsq, b_gsq)
                yield
                cp(gp[:, :, 0:3], gp[:, :, 128:131], [b_gp], [b_gp], eng="pool")
                for b in range(2):
                    xb = gx[:, b * 4:(b + 1) * 4, :]
                    tt(gsq, xb, xb, ALU.mult, [b_gx], [b_gsq])
                    pn, b_pn = yield from galloc()
                    pn3 = v3(pn, 128)
                    for h in range(4):
                        mm(pn3[0:64, h, :], ones64, gsq[:, h, :], [b_ones64, b_gsq], [b_pn])
                    act(grs, pn3[0:64], AF.Sqrt, [b_pn], [b_grs], bias=1e-6)
                    pfree(b_pn)
                    rcp(grs, grs, [b_grs], [b_grs])
                    if b == 0:
                        stt(gqT, xb, 0.125, grs, ALU.mult, ALU.mult, [b_gx, b_grs], [b_gqT])
                    else:
                        tt(gkT, xb, grs, ALU.mult, [b_gx, b_grs], [b_gkT])
                cp(gvT, gx[:, 8:12, :], [b_gx], [b_gvT], eng="act")
                act(beta, gba[:, 0:4], AF.Sigmoid, [b_gba], [b_beta])
                act(lnb, beta, AF.Ln, [b_beta], [b_lnb])
                tt(gsp, gba[:, 4:8], dtb, ALU.add, [b_gba, b_dtb], [b_gsp])
                act(gsp, gsp, AF.Exp, [b_gsp], [b_gsp])
                act(gsp, gsp, AF.Ln, [b_gsp], [b_gsp], bias=1.0)
                tt(gg_, gsp, negA, ALU.mult, [b_gsp, b_negA], [b_gg])
                pq, b_pq = yield from galloc()
                mm(pq[:, 0:4], ublk, gg_, [b_ublk, b_gg], [b_pq])
                mm(pq[:, 4:8], oblk, gg_, [b_oblk, b_gg], [b_pq])
                mm(pq[0:4, 128:256], gg_, ublk, [b_gg, b_ublk], [b_pq])
                cp(Gs, pq[:, 0:4], [b_pq], [b_Gs])
                ts(nGs, Gs, -1.0, ALU.mult, [b_Gs], [b_nGs])
                tt(Bs, Gs, lnb, ALU.add, [b_Gs, b_lnb], [b_Bs])
                act(eB, Bs, AF.Exp, [b_Bs], [b_eB])
                tt(eDl, pq[:, 4:8], Gs, ALU.subtract, [b_pq, b_Gs], [b_eDl])
                act(eDl, eDl, AF.Exp, [b_eDl], [b_eDl])
                cp(GTs, pq[0:4, 128:256], [b_pq], [b_GTs])
                pfree(b_pq)
                yield
                pb, b_pb = yield from galloc()
                pbv = v3(pb.bitcast(BF16)[:, 0:512], 64)
                for h in range(4):
                    tr(pbv[:, h, :], gkT[:, h, :], identb[0:64, 0:64], [b_gkT, b_identb], [b_pb])
                    tr(pbv[:, 4 + h, :], gvT[:, h, :], identb[0:64, 0:64], [b_gvT, b_identb], [b_pb])
                tt(grk, pbv[:, 0:4, :], eB.unsqueeze(2).broadcast_to([128, 4, 64]), ALU.mult, [b_pb, b_eB], [b_grk])
                tt(gkh, pbv[:, 0:4, :], eDl.unsqueeze(2).broadcast_to([128, 4, 64]), ALU.mult, [b_pb, b_eDl], [b_gkh])
                tt(gbv, pbv[:, 4:8, :], beta.unsqueeze(2).broadcast_to([128, 4, 64]), ALU.mult, [b_pb, b_beta], [b_gbv])
                pfree(b_pb)
                yield
                tt(v3(Gblk, 128), GTs.unsqueeze(1).broadcast_to([4, 4, 128]), sel3, ALU.mult, [b_GTs, b_sel], [b_Gblk])
                ts(nGblk, Gblk, -1.0, ALU.mult, [b_Gblk], [b_nGblk])
                pea, b_pea = yield from galloc()
                pea3 = v3(pea, 128)
                mm(pea, ones4, nGblk, [b_ones4, b_nGblk], [b_pea], True, False)
                mm(pea, identb, negAb, [b_identb, b_negAb], [b_pea], False, True)
                for h in range(4):
                    act(EA[:, h, :], pea3[:, h, :], AF.Exp, [b_pea, b_Bs], [b_EA], bias=Bs[:, h:h + 1])
                pfree(b_pea)
                pet, b_pet = yield from galloc()
                pet3 = v3(pet, 128)
                mm(pet, ones4, Gblk, [b_ones4, b_Gblk], [b_pet], True, False)
                mm(pet, identb, negTb, [b_identb, b_negTb], [b_pet], False, True)
                for h in range(4):
                    act(ET[:, h, :], pet3[:, h, :], AF.Exp, [b_pet, b_nGs], [b_ET], bias=nGs[:, h:h + 1])
                pfree(b_pet)
                peg, b_peg = yield from galloc()
                peg3 = v3(peg, 128)
                mm(peg[0:64, :], ones4[:, 0:64], Gblk, [b_ones4, b_Gblk], [b_peg])
                act(eGr, peg3[0:64], AF.Exp, [b_peg], [b_eGr])
                pfree(b_peg)
                yield
                tt(gq[0][0][:, :, 0:64], gqT[:, :, 0:64], eGr[:, :, 0:64], ALU.mult, [b_gqT, b_eGr], [gq[0][1]])
                tt(gq[1][0][:, :, 64:128], gqT[:, :, 64:128], eGr[:, :, 64:128], ALU.mult, [b_gqT, b_eGr], [gq[1][1]])
                pkk, b_pkk = yield from galloc()
                pkq, b_pkq = yield from galloc()
                for h in range(4):
                    mm(v3(pkk, 128)[:, h, :], gkT[:, h, :], gkT[:, h, :], [b_gkT], [b_pkk])
                    mm(v3(pkq, 128)[:, h, :], gkT[:, h, :], gqT[:, h, :], [b_gkT, b_gqT], [b_pkq])
                stt(Pm, v3(pkk, 128), -1.0, EA, ALU.mult, ALU.mult, [b_pkk, b_EA], [b_Pm])
                tt(attT, v3(pkq, 128), ET, ALU.mult, [b_pkq, b_ET], [b_attT])
                pfree(b_pkk, b_pkq)
                yield
                pb, b_pb = yield from galloc()
                pbv = v3(pb.bitcast(BF16)[:, 0:512], 128)
                for h in range(4):
                    tr(pbv[:, h, :], Pm[:, h, :], identb, [b_Pm, b_identb], [b_pb])
                cp(PmT, pbv, [b_pb], [b_PmT], eng="act")
                tt(Wt, pbv, identb.unsqueeze(1).broadcast_to([128, 4, 128]), ALU.add, [b_pb, b_identb], [b_Wt])
                pfree(b_pb)
                yield
                for m in range(5):
                    pa1, b_pa1 = yield from galloc()
                    for h in range(4):
                        mm(v3(pa1, 128)[:, h, :], PmT[:, h, :], Pm[:, h, :], [b_PmT, b_Pm], [b_pa1])
                    if m < 4:
                        pa2, b_pa2 = yield from galloc()
                        for h in range(4):
                            mm(v3(pa2, 128)[:, h, :], Pm[:, h, :], PmT[:, h, :], [b_PmT, b_Pm], [b_pa2])
                    cp(Pm, v3(pa1, 128), [b_pa1], [b_Pm], eng="act")
                    pfree(b_pa1)
                    if m < 4:
                        cp(PmT, v3(pa2, 128), [b_pa2], [b_PmT])
                        pfree(b_pa2)
                    yield
                    pa3, b_pa3 = yield from galloc()
                    for h in range(4):
                        mm(v3(pa3, 128)[:, h, :], Pm[:, h, :], Wt[:, h, :], [b_Pm, b_Wt], [b_pa3])
                    tt(Wt, Wt, v3(pa3, 128), ALU.add, [b_Wt, b_pa3], [b_Wt])
                    pfree(b_pa3)
                    yield
                pu, b_pu = yield from galloc()
                pw, b_pw = yield from galloc()
                for h in range(4):
                    hs = slice(h * 64, (h + 1) * 64)
                    mm(pu[:, hs], Wt[:, h, :], gbv[:, h, :], [b_Wt, b_gbv], [b_pu])
                    mm(v3(pw, 128)[0:64, h, :], grk[:, h, :], Wt[:, h, :], [b_grk, b_Wt], [b_pw])
                cp(usb, pu[:, 0:256], [b_pu], [b_usb])
                cp(gw[0][0][:, :, 0:64], v3(pw, 128)[0:64, :, 0:64], [b_pw], [gw[0][1]], eng="act")
                cp(gw[1][0][:, :, 64:128], v3(pw, 128)[0:64, :, 64:128], [b_pw], [gw[1][1]], eng="act")
                pfree(b_pu, b_pw)
                yield
                for c, (rr, ssrc, sdst) in enumerate([(r0, pv, mid), (r1, mid, end)]):
                    pa, b_pa = yield from galloc()
                    for h in range(4):
                        hs = slice(h * 64, (h + 1) * 64)
                        mm(pa[:, hs], gw[c][0][:, h, :], Sgb[ssrc][0][:, h, :], [gw[c][1], Sgb[ssrc][1]], [b_pa])
                    tt(vn[rr, :], usb[rr, :], pa[rr, 0:256], ALU.subtract, [b_usb, b_pa], [b_vn])
                    pfree(b_pa)
                    pd, b_pd = yield from galloc()
                    for h in range(4):
                        hs = slice(h * 64, (h + 1) * 64)
                        mm(pd[0:64, hs], gkh[rr, h, :], vn[rr, hs], [b_gkh, b_vn], [b_pd])
                    col = 63 + 64 * c
                    for h in range(4):
                        hs = slice(h * 64, (h + 1) * 64)
                        stt(Sg32[:, h, :], Sg32[:, h, :], eGr[:, h, col:col + 1], pd[0:64, hs],
                            ALU.mult, ALU.add, [b_Sg32, b_eGr, b_pd], [b_Sg32])
                    cp(Sgb[sdst][0], Sg32, [b_Sg32], [Sgb[sdst][1]], eng="pool")
                    pfree(b_pd)
                    yield
                po, b_po = yield from galloc()
                for h in range(4):
                    hs = slice(h * 64, (h + 1) * 64)
                    mm(po[:, hs], attT[:, h, :], vn[:, hs], [b_attT, b_vn], [b_po], True, False)
                    mm(po[:, hs], gq[0][0][:, h, :], Sgb[pv][0][:, h, :], [gq[0][1], Sgb[pv][1]], [b_po], False, False)
                    mm(po[:, hs], gq[1][0][:, h, :], Sgb[mid][0][:, h, :], [gq[1][1], Sgb[mid][1]], [b_po], False, True)
                head_norm(po[:, 0:256], b_po, silg, b_silg, gnw.unsqueeze(1).broadcast_to([128, 4, 64]), b_gnw,
                          False, 1e-6, v3(mix[:, 512:768], 64), "gdn")
                pfree(b_po)

            gl_ = [("gdn", g_gdn, [0, 1, 2]), ("hgrn", g_hgrn, [3, 4]), ("conf", g_conf, [5]), ("ret", g_ret, [6, 7])]
            assert len(freeq) == 8
            run_interleaved([(g, bs) for n_, g, bs in gl_ if n_ not in skip])
            if True:
                if dbg:
                    cp(mixf, mix, [b_mix], [b_mixf], eng="pool")
                    k.dma(dmix[l, i * 128:(i + 1) * 128, :], mixf, R=[b_mixf])
                pb, b_pb = psum()
                pbv = v3(pb.bitcast(BF16), 128)
                for kc in range(8):
                    tr(pbv[:, kc, :], mix[:, kc * 128:(kc + 1) * 128], identb, [b_mix, b_identb], [b_pb])
                cp(mT, pbv, [b_pb], [b_mT], eng="act")
                pfree(b_pb)
                for n in range(2):
                    pso, b_pso = psum()
                    ns = slice(n * 512, (n + 1) * 512)
                    for kc in range(8):
                        mm(pso, mT[:, kc, :], Wo[:, kc, ns], [b_mT, b_Wo], [b_pso], kc == 0, kc == 7)
                    tt(xo[:, ns], xt[:, ns], pso, ALU.add, [b_xt, b_pso], [b_xo])
                    pfree(b_pso)
                k.dma(xs[i * 128:(i + 1) * 128, :], xo, R=[b_xo], W=[xsb[i]])
        k.barrier()
        es.close()

    def pass_b(l, last):
        es = ExitStack()
        Wg, b_Wg = sb(es, "Wg", [128, 8, DFF], BF16)
        Wu, b_Wu = sb(es, "Wu", [128, 8, DFF], BF16)
        Wd, b_Wd = sb(es, "Wd", [128, NFT, D], BF16)
        aT, b_aT = sb(es, "aT", [128, NFT, 128], BF16)
        fsg, b_fsg = sb(es, "fsg", [128, 512])
        fsc, b_fsc = sb(es, "fsc", [128, 512])
        fnw, b_fnw = sb(es, "fnw", [128, D])
        with nc.allow_non_contiguous_dma(reason="tiny param loads"):
            k.dma(nw, dr["norm_ffn_w"][l].rearrange("(c p) -> p c", p=128), W=[b_nw])
            k.dma(fnw, dr["final_norm_w"].partition_broadcast(128), W=[b_fnw])
        load_w(Wg, b_Wg, dr["ffn_w_gate"][l], 8, DFF, nw)
        load_w(Wu, b_Wu, dr["ffn_w_up"][l], 8, DFF, nw)
        load_w(Wd, b_Wd, dr["ffn_w_down"][l], NFT, D, None)
        for i in range(NT):
            k.dma(xt, xs[i * 128:(i + 1) * 128, :], R=[xsb[i]], W=[b_xt])
            norm_to_hT(xt)
            for grp in range(6):
                nf = 4 if grp < 5 else 2
                psg, b_psg = psum()
                psu, b_psu = psum()
                for j in range(nf):
                    ft = grp * 4 + j
                    fs = slice(ft * 128, (ft + 1) * 128)
                    for kc in range(8):
                        mm(psg[:, j * 128:(j + 1) * 128], Wg[:, kc, fs], hT[:, kc, :], [b_Wg, b_hT], [b_psg], kc == 0, kc == 7)
                    for kc in range(8):
                        mm(psu[:, j * 128:(j + 1) * 128], Wu[:, kc, fs], hT[:, kc, :], [b_Wu, b_hT], [b_psu], kc == 0, kc == 7)
                silu(fsg[:, :nf * 128], psg[:, :nf * 128], [b_psg], [b_fsg], fsc[:, :nf * 128], b_fsc)
                tt(aT[:, grp * 4:grp * 4 + nf, :], v3(psu[:, :nf * 128], 128), v3(fsg[:, :nf * 128], 128), ALU.mult,
                   [b_psu, b_fsg], [b_aT])
                pfree(b_psg, b_psu)
            for n in range(2):
                pso, b_pso = psum()
                ns = slice(n * 512, (n + 1) * 512)
                for ft in range(NFT):
                    mm(pso, aT[:, ft, :], Wd[:, ft, ns], [b_aT, b_Wd], [b_pso], ft == 0, ft == NFT - 1)
                tt(xo[:, ns], xt[:, ns], pso, ALU.add, [b_xt, b_pso], [b_xo])
                pfree(b_pso)
            if last:
                act(xn, xo, AF.Square, [b_xo], [b_xn, b_ss], accum=ss)
                act(ss, ss, AF.Sqrt, [b_ss], [b_ss], bias=1e-6, scale=1.0 / D)
                rcp(ss, ss, [b_ss], [b_ss])
                ts(xo, xo, ss, ALU.mult, [b_xo, b_ss], [b_xo])
                tt(xo, xo, fnw, ALU.mult, [b_xo, b_fnw], [b_xo])
                k.dma(out[i * 128:(i + 1) * 128, :], xo, R=[b_xo])
            else:
                k.dma(xs[i * 128:(i + 1) * 128, :], xo, R=[b_xo], W=[xsb[i]])
        k.barrier()
        es.close()

    for l in range(NL):
        try:
            pass_a(l)
        except StopBuild:
            k.barrier()
            return nc, k
        if "b" in skip:
            for i in range(NT):
                k.dma(xt, xs[i * 128:(i + 1) * 128, :], R=[xsb[i]], W=[b_xt])
                k.dma(out[i * 128:(i + 1) * 128, :], xt, R=[b_xt])
        else:
            pass_b(l, l == NL - 1)
    k.barrier()
    es_top.close()
    return nc, k


_CACHE = {}


def kernel(**inputs):
    x = np.asarray(inputs["x"], dtype=np.float32)
    B, T, _ = x.shape
    key = T
    if key not in _CACHE:
        _CACHE[key] = (build(T)[0], host_consts(T))
    nc, consts = _CACHE[key]
    params = {n: np.ascontiguousarray(np.asarray(inputs[n], dtype=np.float32)) for n in PARAM_SHAPES}
    in_maps = []
    for c in range(8):
        m = {"x": np.ascontiguousarray(x[c % B])}
        m.update(params)
        m.update(consts)
        in_maps.append(m)
    res = run_bass_kernel_spmd(nc, in_maps, core_ids=list(range(8)))
    outs = [np.asarray(res.results[c]["out"], dtype=np.float32) for c in range(B)]
    return np.stack(outs, axis=0)
```

```python
from contextlib import ExitStack
import numpy as np
import ml_dtypes
import concourse.bass as bass
import concourse.mybir as mybir
from concourse.bass_utils import run_bass_kernel_spmd

F32 = mybir.dt.float32
BF16 = mybir.dt.bfloat16
AF = mybir.ActivationFunctionType
ALU = mybir.AluOpType
AX = mybir.AxisListType

D = 1024
DIN = 3592
DFF = 2816
NFT = DFF // 128
NEG = -30000.0


class Buf:
    __slots__ = ("name", "w", "r", "excl")

    def __init__(self, name="", excl=False):
        self.name = name
        self.w = None
        self.r = {}
        self.excl = excl


class K:
    NDSEM = 40

    def __init__(self, nc):
        self.nc = nc
        self.eng = {"pe": nc.tensor, "act": nc.scalar, "dve": nc.vector,
                    "pool": nc.gpsimd, "sp": nc.sync}
        self.sem = {e: nc.alloc_semaphore(name="s_" + e) for e in ("pe", "act", "dve", "pool")}
        self.cnt = {e: 0 for e in self.sem}
        self.dsem = [nc.alloc_semaphore(name="d%d" % i) for i in range(self.NDSEM)]
        self.dcnt = [0] * self.NDSEM
        self.dnext = 0
        self.waited = {e: {} for e in self.eng}
        self.nwaits = 0
        self.ninst = 0
        self.rec = None

    def replay(self, item):
        if item[0] == "op":
            self.op(*item[1:])
        else:
            self.dma(*item[1:3], **item[3])

    def _wait(self, eng, ev):
        if ev is None:
            return
        if ev[0] == "e":
            key, val = ("e", ev[1]), ev[2]
            if ev[1] == "pe" and eng == "pe":
                return
            sem = self.sem[ev[1]]
        else:
            key, val = ("d", ev[1]), ev[2]
            sem = self.dsem[ev[1]]
        if self.waited[eng].get(key, 0) >= val:
            return
        self.eng[eng].wait_ge(sem, val)
        self.waited[eng][key] = val
        self.nwaits += 1

    def _deps(self, eng, R, W):
        for b in R:
            self._wait(eng, b.w)
            if b.excl:
                for ev in b.r.values():
                    if not (ev[0] == "e" and ev[1] == eng):
                        self._wait(eng, ev)
        for b in W:
            self._wait(eng, b.w)
            for ev in b.r.values():
                self._wait(eng, ev)

    def _post(self, ev, R, W):
        for b in R:
            b.r[(ev[0], ev[1])] = ev
        for b in W:
            b.w = ev
            b.r = {}

    def op(self, eng, fn, R=(), W=()):
        if self.rec is not None:
            self.rec.append(("op", eng, fn, tuple(R), tuple(W)))
            return None
        self._deps(eng, R, W)
        inst = fn(self.eng[eng])
        self.cnt[eng] += 1
        inst.then_inc(self.sem[eng], 1)
        self._post(("e", eng, self.cnt[eng]), R, W)
        self.ninst += 1
        return inst

    def dma(self, out, in_, R=(), W=(), q="sp", **kw):
        if self.rec is not None:
            self.rec.append(("dma", out, in_, dict(R=tuple(R), W=tuple(W), q=q, **kw)))
            return None
        idx = self.dnext
        self.dnext = (self.dnext + 1) % self.NDSEM
        if self.dcnt[idx] > 0:
            self._wait(q, ("d", idx, self.dcnt[idx]))
        self._deps(q, R, W)
        inst = self.eng[q].dma_start(out=out, in_=in_, **kw)
        self.dcnt[idx] += 16
        inst.then_inc(self.dsem[idx], 16)
        ev = ("d", idx, self.dcnt[idx])
        self._post(ev, R, W)
        self.ninst += 1
        return ev

    def barrier(self):
        for e in self.eng:
            for e2 in self.sem:
                if self.cnt[e2] > 0:
                    self._wait(e, ("e", e2, self.cnt[e2])) if not (e == e2 == "pe") else None
            for i in range(self.NDSEM):
                if self.dcnt[i] > 0:
                    self._wait(e, ("d", i, self.dcnt[i]))


def host_consts(T):
    c = {}
    c["c_identf"] = np.eye(128, dtype=np.float32)
    j = np.arange(128)[:, None]
    i = np.arange(128)[None, :]
    same = (j // 64) == (i // 64)
    c["c_cmask"] = (j <= i).astype(np.float32)
    c["c_ublk"] = ((j <= i) & same).astype(np.float32)
    c["c_oblk"] = same.astype(np.float32)
    cind = np.zeros((128, 2), np.float32)
    cind[:64, 0] = 1.0
    cind[64:, 1] = 1.0
    c["c_cind"] = cind
    sel = np.zeros((4, 4, 128), np.float32)
    for h in range(4):
        sel[h, h, :] = 1.0
    c["c_sel"] = sel.reshape(4, 512)
    c["c_negA"] = np.where(same & (j > i), 0.0, NEG).astype(np.float32)
    c["c_negT"] = np.where(same & (i >= j), 0.0, NEG).astype(np.float32)
    c["c_ones64"] = np.ones((64, 64), np.float32)
    c["c_ones4"] = np.ones((4, 128), np.float32)
    c["c_negA4"] = np.tile(c["c_negA"], (1, 4))
    c["c_negT4"] = np.tile(c["c_negT"], (1, 4))
    t = np.arange(T, dtype=np.float64)
    inv = 10000.0 ** (-np.arange(32, dtype=np.float64) / 32.0)
    ang = (t[:, None].astype(np.float32) * inv[None, :].astype(np.float32)).astype(np.float64)
    cos, sin = np.cos(ang), np.sin(ang)
    gam = 1.0 - 2.0 ** (-5.0 - np.arange(4, dtype=np.float64))
    pos = (np.arange(T) % 128 + 1).astype(np.float64)
    dq = gam[None, :] ** pos[:, None]
    dk = gam[None, :] ** (-pos[:, None]) * 0.125
    rope = np.zeros((T, 4, 4, 32), np.float64)
    rope[:, 0] = cos[:, None, :] * dq[:, :, None]
    rope[:, 1] = sin[:, None, :] * dq[:, :, None]
    rope[:, 2] = cos[:, None, :] * dk[:, :, None]
    rope[:, 3] = sin[:, None, :] * dk[:, :, None]
    c["c_rope"] = rope.reshape(T, 512).astype(np.float32)
    gC = np.zeros((64, 4, 64), np.float64)
    gC[:] = (gam ** 128.0)[None, :, None]
    c["c_gC"] = gC.reshape(64, 256).astype(np.float32)
    return c


CONST_SHAPES = {"c_identf": [128, 128], "c_cmask": [128, 128], "c_ublk": [128, 128],
                "c_oblk": [128, 128], "c_cind": [128, 2], "c_sel": [4, 512],
                "c_negA": [128, 128], "c_negT": [128, 128], "c_ones64": [64, 64],
                "c_ones4": [4, 128], "c_negA4": [128, 512], "c_negT4": [128, 512],
                "c_gC": [64, 256]}

PARAM_SHAPES = {
    "norm_mix_w": [2, 1024], "w_in": [2, 1024, DIN], "ret_norm_w": [2, 256],
    "conf_conv_w": [2, 31, 256], "conf_conv_b": [2, 256], "conf_ln_w": [2, 256],
    "conf_ln_b": [2, 256], "gdn_conv_w": [2, 4, 768], "gdn_A_log": [2, 4],
    "gdn_dt_bias": [2, 4], "gdn_norm_w": [2, 64], "hgrn_lb_logits": [2, 256],
    "hgrn_norm_w": [2, 64], "w_out": [2, 1024, 1024], "norm_ffn_w": [2, 1024],
    "ffn_w_gate": [2, 1024, DFF], "ffn_w_up": [2, 1024, DFF], "ffn_w_down": [2, DFF, 1024],
    "final_norm_w": [1024],
}


class StopBuild(Exception):
    pass


def build(T, NL=2, dbg=False, skip=(), stop=99):
    nc = bass.Bass("TRN2", target_bir_lowering=False)

    def ckpt(n):
        if n >= stop:
            raise StopBuild()

    NT = T // 128
    k = K(nc)
    dr = {}
    dr["x"] = nc.dram_tensor("x", [T, D], F32, kind="ExternalInput").ap()
    for n, s in PARAM_SHAPES.items():
        dr[n] = nc.dram_tensor(n, s, F32, kind="ExternalInput").ap()
    for n, s in CONST_SHAPES.items():
        dr[n] = nc.dram_tensor(n, s, F32, kind="ExternalInput").ap()
    dr["c_rope"] = nc.dram_tensor("c_rope", [T, 512], F32, kind="ExternalInput").ap()
    out = nc.dram_tensor("out", [T, D], F32, kind="ExternalOutput").ap()
    xs = nc.dram_tensor("xs", [T, D], F32, kind="Internal").ap()
    if dbg:
        dmix = nc.dram_tensor("dmix", [NL, T, D], F32, kind="ExternalOutput").ap()
    xsb = [Buf("xs%d" % i) for i in range(NT)]

    es_top = ExitStack()

    uniq = {"n": 0}

    def sb(es, name, shape, dt=F32):
        uniq["n"] += 1
        t = es.enter_context(nc.sbuf_tensor("%s_%d" % (name, uniq["n"]), shape, dt))
        return t.ap(), Buf(name)

    def mm(out_, lhsT, rhs, R, W, start=True, stop=True):
        k.op("pe", lambda e: e.matmul(out_, lhsT=lhsT, rhs=rhs, start=start, stop=stop), R=R, W=W)

    def tr(out_, in_, ident, R, W):
        k.op("pe", lambda e: e.transpose(out=out_, in_=in_, identity=ident), R=R, W=W)

    def act(out_, in_, func, R, W, bias=None, scale=None, accum=None):
        kw = {}
        if bias is not None:
            kw["bias"] = bias
        if scale is not None:
            kw["scale"] = scale
        if accum is not None:
            kw["accum_out"] = accum
        k.op("act", lambda e: e.activation(out=out_, in_=in_, func=func, **kw), R=R, W=W)

    def silu(out_, in_, R, W, scr, b_scr):
        act(scr, in_, AF.Sigmoid, R, [b_scr])
        tt(out_, in_, scr, ALU.mult, list(R) + [b_scr], W)

    def tt(out_, a, b, op, R, W, eng="dve"):
        k.op(eng, lambda e: e.tensor_tensor(out=out_, in0=a, in1=b, op=op), R=R, W=W)

    def ts(out_, a, s1, op0, R, W, s2=None, op1=None, eng="dve"):
        if op1 is None:
            k.op(eng, lambda e: e.tensor_scalar(out=out_, in0=a, scalar1=s1, scalar2=None, op0=op0), R=R, W=W)
        else:
            k.op(eng, lambda e: e.tensor_scalar(out=out_, in0=a, scalar1=s1, scalar2=None, op0=op0), R=R, W=W)
            k.op(eng, lambda e: e.tensor_scalar(out=out_, in0=out_, scalar1=s2, scalar2=None, op0=op1), R=list(R) + list(W), W=W)

    def stt(out_, in0, scalar, in1, op0, op1, R, W):
        k.op("dve", lambda e: e.scalar_tensor_tensor(out=out_, in0=in0, scalar=scalar, in1=in1, op0=op0, op1=op1), R=R, W=W)

    def cp(out_, in_, R, W, eng="dve"):
        if eng == "act":
            k.op("act", lambda e: e.copy(out=out_, in_=in_), R=R, W=W)
        else:
            k.op(eng, lambda e: e.tensor_copy(out=out_, in_=in_), R=R, W=W)

    def red(out_, in_, R, W):
        k.op("dve", lambda e: e.tensor_reduce(out=out_, in_=in_, axis=AX.X, op=ALU.add), R=R, W=W)

    def rcp(out_, in_, R, W):
        k.op("dve", lambda e: e.reciprocal(out=out_, in_=in_), R=R, W=W)

    def mset(ap, val, W, eng="pool"):
        k.op(eng, lambda e: e.memset(ap, val), W=W)

    def v3(ap, inner):
        return ap.rearrange("p (a b) -> p a b", b=inner)

    banks = []
    for i in range(8):
        t = es_top.enter_context(nc.psum_tensor("bank%d" % i, [128, 512], F32))
        banks.append((t.ap(), Buf("bank%d" % i, excl=True)))
    freeq = list(range(8))
    bank_idx = {id(b[1]): i for i, b in enumerate(banks)}
    cur = {"free": freeq}

    def psum():
        assert cur["free"], "no free PSUM bank"
        return banks[cur["free"].pop(0)]

    def galloc():
        if False:
            yield
        return psum()

    def pfree(*bufs):
        for b in bufs:
            i = bank_idx[id(b)]
            assert i not in cur["free"]
            cur["free"].append(i)

    def run_interleaved(gens):
        lists = []
        for g, bset in gens:
            cur["free"] = list(bset)
            k.rec = []
            for _ in g():
                pass
            lists.append(k.rec)
            k.rec = None
            assert sorted(cur["free"]) == sorted(bset), "mixer leaked PSUM banks"
        cur["free"] = freeq
        pos = [0] * len(lists)
        while True:
            best, bf = -1, 2.0
            for j, lst in enumerate(lists):
                if pos[j] < len(lst):
                    f = pos[j] / len(lst)
                    if f < bf:
                        best, bf = j, f
            if best < 0:
                break
            k.replay(lists[best][pos[best]])
            pos[best] += 1

    stg = [sb(es_top, "stg%d" % i, [128, 1024]) for i in range(2)]
    stg_c = stg[0]
    identf, b_identf = sb(es_top, "identf", [128, 128])
    identb, b_identb = sb(es_top, "identb", [128, 128], BF16)
    cmask, b_cmask = sb(es_top, "cmask", [128, 128])
    ublk, b_ublk = sb(es_top, "ublk", [128, 128])
    oblk, b_oblk = sb(es_top, "oblk", [128, 128])
    cind, b_cind = sb(es_top, "cind", [128, 2])
    sel, b_sel = sb(es_top, "sel", [4, 512])
    negAb, b_negAb = sb(es_top, "negAb", [128, 512], BF16)
    negTb, b_negTb = sb(es_top, "negTb", [128, 512], BF16)
    ones4, b_ones4 = sb(es_top, "ones4", [4, 128])
    ones64, b_ones64 = sb(es_top, "ones64", [64, 64])
    gC, b_gC = sb(es_top, "gC", [64, 256])
    CONSTS = []
    k.dma(identf, dr["c_identf"], W=[b_identf])
    k.dma(cmask, dr["c_cmask"], W=[b_cmask])
    k.dma(ublk, dr["c_ublk"], W=[b_ublk])
    k.dma(oblk, dr["c_oblk"], W=[b_oblk])
    k.dma(cind, dr["c_cind"], W=[b_cind])
    k.dma(sel, dr["c_sel"], W=[b_sel])
    k.dma(ones4, dr["c_ones4"], W=[b_ones4])
    k.dma(ones64, dr["c_ones64"], W=[b_ones64])
    k.dma(gC, dr["c_gC"], W=[b_gC])
    cp(identb, identf, [b_identf], [b_identb])
    for (dst_, b_dst_, nm_) in [(negAb, b_negAb, "c_negA4"), (negTb, b_negTb, "c_negT4")]:
        s_ap, s_b = stg_c
        k.dma(s_ap[:, 0:512], dr[nm_], W=[s_b])
        cp(dst_, s_ap[:, 0:512], [s_b], [b_dst_])
    sel3 = v3(sel, 128)
    try:
        ckpt(1)
    except StopBuild:
        k.barrier()
        return nc, k

    xt, b_xt = sb(es_top, "xt", [128, D])
    xo, b_xo = sb(es_top, "xo", [128, D])
    xn, b_xn = sb(es_top, "xn", [128, D], BF16)
    hT, b_hT = sb(es_top, "hT", [128, 8, 128], BF16)
    ss, b_ss = sb(es_top, "ss", [128, 1])
    nw, b_nw = sb(es_top, "nw", [128, 8])
    stg_i = {"i": 0}

    def load_w(dst, b_dst, src, KC, N, scale):
        for kc in range(KC):
            for n0 in range(0, N, 1024):
                n1 = min(N, n0 + 1024)
                s_ap, s_b = stg[stg_i["i"] % 2]
                eng = "dve" if stg_i["i"] % 2 == 0 else "pool"
                stg_i["i"] += 1
                k.dma(s_ap[:, :n1 - n0], src[kc * 128:(kc + 1) * 128, n0:n1], W=[s_b])
                if scale is not None:
                    ts(dst[:, kc, n0:n1], s_ap[:, :n1 - n0], scale[:, kc:kc + 1], ALU.mult,
                       [s_b, b_nw], [b_dst], eng=eng)
                else:
                    cp(dst[:, kc, n0:n1], s_ap[:, :n1 - n0], [s_b], [b_dst], eng=eng)

    def norm_to_hT(src_xt):
        act(xn, src_xt, AF.Square, [b_xt], [b_xn, b_ss], accum=ss)
        act(ss, ss, AF.Sqrt, [b_ss], [b_ss], bias=1e-6, scale=1.0 / D)
        rcp(ss, ss, [b_ss], [b_ss])
        ts(xn, src_xt, ss, ALU.mult, [b_xt, b_ss], [b_xn])
        pb, b_pb = psum()
        pbv = v3(pb.bitcast(BF16), 128)
        for kc in range(8):
            tr(pbv[:, kc, :], xn[:, kc * 128:(kc + 1) * 128], identb, [b_xn, b_identb], [b_pb])
        cp(hT, pbv, [b_pb], [b_hT], eng="act")
        pfree(b_pb)

    def pass_a(l):
        es = ExitStack()
        Wi, b_Wi = sb(es, "Wi", [128, 8, DIN], BF16)
        Wo, b_Wo = sb(es, "Wo", [128, 8, D], BF16)
        Dc, b_Dc = sb(es, "Dc", [128, 62, 128], BF16)
        ccw, b_ccw = sb(es, "ccw", [128, 2, 31])
        ccb, b_ccb = sb(es, "ccb", [128, 2])
        gcw, b_gcw = sb(es, "gcw", [64, 12, 4])
        rnw, b_rnw = sb(es, "rnw", [128, 256])
        clnw, b_clnw = sb(es, "clnw", [128, 256])
        clnb, b_clnb = sb(es, "clnb", [128, 256])
        gnw, b_gnw = sb(es, "gnw", [128, 64])
        hnw, b_hnw = sb(es, "hnw", [128, 64])
        negA, b_negA = sb(es, "negA", [128, 4])
        dtb, b_dtb = sb(es, "dtb", [128, 4])
        lbl, b_lbl = sb(es, "lbl", [128, 2, 256])
        lbt, b_lbt = sb(es, "lbt", [128, 256])
        oml, b_oml = sb(es, "oml", [128, 256])
        with nc.allow_non_contiguous_dma(reason="tiny param loads"):
            k.dma(nw, dr["norm_mix_w"][l].rearrange("(c p) -> p c", p=128), W=[b_nw])
            for ct in range(2):
                k.dma(ccw[:, ct, :], dr["conf_conv_w"][l][:, ct * 128:(ct + 1) * 128].rearrange("k c -> c k"), W=[b_ccw])
            k.dma(ccb, dr["conf_conv_b"][l].rearrange("(ct c) -> c ct", c=128), W=[b_ccb])
            for s_ in range(12):
                k.dma(gcw[:, s_, :], dr["gdn_conv_w"][l][:, s_ * 64:(s_ + 1) * 64].rearrange("k d -> d k"), W=[b_gcw])
            k.dma(rnw, dr["ret_norm_w"][l].partition_broadcast(128), W=[b_rnw])
            k.dma(clnw, dr["conf_ln_w"][l].partition_broadcast(128), W=[b_clnw])
            k.dma(clnb, dr["conf_ln_b"][l].partition_broadcast(128), W=[b_clnb])
            k.dma(gnw, dr["gdn_norm_w"][l].partition_broadcast(128), W=[b_gnw])
            k.dma(hnw, dr["hgrn_norm_w"][l].partition_broadcast(128), W=[b_hnw])
            k.dma(negA, dr["gdn_A_log"][l].partition_broadcast(128), W=[b_negA])
            k.dma(dtb, dr["gdn_dt_bias"][l].partition_broadcast(128), W=[b_dtb])
            k.dma(lbl[:, 0, :], dr["hgrn_lb_logits"][0].partition_broadcast(128), W=[b_lbl])
            k.dma(lbl[:, 1, :], dr["hgrn_lb_logits"][1].partition_broadcast(128), W=[b_lbl])
        ckpt(2)
        act(negA, negA, AF.Exp, [b_negA], [b_negA])
        ts(negA, negA, -1.0, ALU.mult, [b_negA], [b_negA])
        if l == 0:
            mset(lbt, 0.0, [b_lbt])
            mset(oml, 1.0, [b_oml])
        else:
            act(lbl, lbl, AF.Exp, [b_lbl], [b_lbl])
            tt(oml, lbl[:, 0, :], lbl[:, 1, :], ALU.add, [b_lbl], [b_oml])
            rcp(oml, oml, [b_oml], [b_oml])
            tt(lbt, lbl[:, 1, :], oml, ALU.mult, [b_lbl, b_oml], [b_lbt])
            ts(oml, lbt, -1.0, ALU.mult, [b_lbt], [b_oml], s2=1.0, op1=ALU.add)
        for kk in range(31):
            for ct in range(2):
                ts(Dc[:, kk * 2 + ct, :], identf, ccw[:, ct, kk:kk + 1], ALU.mult, [b_identf, b_ccw], [b_Dc])
        ckpt(3)
        load_w(Wi, b_Wi, dr["w_in"][l], 8, DIN, nw)
        load_w(Wo, b_Wo, dr["w_out"][l], 8, D, None)

        ckpt(4)
        rope_t, b_rope = sb(es, "rope_t", [128, 512])
        rqk, b_rqk = sb(es, "rqk", [128, 512])
        ctsg, b_ctsg = sb(es, "ctsg", [128, 256])
        glt, b_glt = sb(es, "glt", [128, 256], BF16)
        gpt, b_gpt = sb(es, "gpt", [128, 768], BF16)
        Gblk, b_Gblk = sb(es, "Gblk", [4, 512])
        nGblk, b_nGblk = sb(es, "nGblk", [4, 512])
        hqf, b_hqf = sb(es, "hqf", [128, 256])
        t1, b_t1 = sb(es, "t1", [128, 4, 32])
        t2, b_t2 = sb(es, "t2", [128, 4, 32])
        qrt, b_qrt = sb(es, "qrt", [128, 4, 64], BF16)
        krt, b_krt = sb(es, "krt", [128, 4, 64], BF16)
        qkT, b_qkT = sb(es, "qkT", [64, 8, 128], BF16)
        PTr, b_PTr = sb(es, "PTr", [128, 4, 128], BF16)
        rvb, b_rvb = sb(es, "rvb", [128, 256], BF16)
        Sr32, b_Sr32 = sb(es, "Sr32", [64, 256])
        Srb, b_Srb = sb(es, "Srb", [64, 4, 64], BF16)
        silr, b_silr = sb(es, "silr", [128, 256])
        silg, b_silg = sb(es, "silg", [128, 256])
        silh, b_silh = sb(es, "silh", [128, 256])
        HN = {}
        for nm_ in ("ret", "hgrn", "gdn"):
            HN[nm_] = dict(hsq=sb(es, "hsq", [128, 256]), hy=sb(es, "hy", [128, 256]), s1=sb(es, "s1", [128, 4]),
                           s2=sb(es, "s2", [128, 4]), mean=sb(es, "mean", [128, 4]), msq=sb(es, "msq", [128, 4]),
                           var=sb(es, "var", [128, 4]))
        hsq, b_hsq = HN["ret"]["hsq"]
        hf, b_hf = sb(es, "hf", [128, 256])
        logf, b_logf = sb(es, "logf", [128, 256])
        hkk, b_hkk = sb(es, "hkk", [128, 256])
        Gsb, b_Gsb = sb(es, "Gsb", [128, 256])
        eG, b_eG = sb(es, "eG", [128, 256])
        enG, b_enG = eG, b_eG
        hD, b_hD = Gsb, b_Gsb
        qh, b_qh = sb(es, "qh", [128, 256], BF16)
        kht, b_kht = sb(es, "kht", [128, 256], BF16)
        khat, b_khat = sb(es, "khat", [128, 256], BF16)
        hvb, b_hvb = sb(es, "hvb", [128, 256], BF16)
        hq = [sb(es, "hq%d" % c, [64, 4, 128], BF16) for c in range(2)]
        hkT, b_hkT = sb(es, "hkT", [64, 4, 128], BF16)
        PTh, b_PTh = sb(es, "PTh", [128, 4, 128], BF16)
        Sh32, b_Sh32 = sb(es, "Sh32", [64, 4, 64])
        Shb = [sb(es, "Shb%d" % c, [64, 4, 64], BF16) for c in range(3)]
        eGl, b_eGl = sb(es, "eGl", [64, 8])
        gl, b_gl = sb(es, "gl", [128, 2, 158], BF16)
        yT, b_yT = sb(es, "yT", [128, 2, 128])
        yc, b_yc = sb(es, "yc", [128, 256])
        c1, b_c1 = sb(es, "c1", [128, 1])
        c2, b_c2 = sb(es, "c2", [128, 1])
        c3, b_c3 = sb(es, "c3", [128, 1])
        gp, b_gp = sb(es, "gp", [64, 12, 131], BF16)
        gx, b_gx = sb(es, "gx", [64, 12, 128])
        b_gxs = [Buf("gxs%d" % i_) for i_ in range(12)]
        gsq, b_gsq = sb(es, "gsq", [64, 4, 128])
        grs, b_grs = sb(es, "grs", [64, 4, 128])
        gqT, b_gqT = sb(es, "gqT", [64, 4, 128], BF16)
        gkT, b_gkT = sb(es, "gkT", [64, 4, 128], BF16)
        gvT, b_gvT = sb(es, "gvT", [64, 4, 128], BF16)
        grk, b_grk = sb(es, "grk", [128, 4, 64], BF16)
        gkh, b_gkh = sb(es, "gkh", [128, 4, 64], BF16)
        gbv, b_gbv = sb(es, "gbv", [128, 4, 64], BF16)
        EA, b_EA = sb(es, "EA", [128, 4, 128], BF16)
        ET, b_ET = sb(es, "ET", [128, 4, 128], BF16)
        eGr, b_eGr = sb(es, "eGr", [64, 4, 128])
        gq = [sb(es, "gq%d" % c, [64, 4, 128], BF16) for c in range(2)]
        gw = [sb(es, "gw%d" % c, [64, 4, 128], BF16) for c in range(2)]
        Pm, b_Pm = sb(es, "Pm", [128, 4, 128], BF16)
        PmT, b_PmT = sb(es, "PmT", [128, 4, 128], BF16)
        Wt, b_Wt = sb(es, "Wt", [128, 4, 128], BF16)
        attT, b_attT = sb(es, "attT", [128, 4, 128], BF16)
        usb, b_usb = sb(es, "usb", [128, 256])
        vn, b_vn = sb(es, "vn", [128, 256], BF16)
        Sg32, b_Sg32 = sb(es, "Sg32", [64, 4, 64])
        Sgb = [sb(es, "Sgb%d" % c, [64, 4, 64], BF16) for c in range(3)]
        beta, b_beta = sb(es, "beta", [128, 4])
        gba, b_gba = sb(es, "gba", [128, 8])
        lnb, b_lnb = sb(es, "lnb", [128, 4])
        gsp, b_gsp = sb(es, "gsp", [128, 4])
        gg_, b_gg = sb(es, "gg_", [128, 4])
        Gs, b_Gs = sb(es, "Gs", [128, 4])
        nGs, b_nGs = sb(es, "nGs", [128, 4])
        Bs, b_Bs = sb(es, "Bs", [128, 4])
        eB, b_eB = sb(es, "eB", [128, 4])
        eDl, b_eDl = sb(es, "eDl", [128, 4])
        GTs, b_GTs = sb(es, "GTs", [4, 128])
        mix, b_mix = sb(es, "mix", [128, D], BF16)
        mT, b_mT = sb(es, "mT", [128, 8, 128], BF16)
        mixf, b_mixf = (xo, b_xo) if dbg else (None, None)

        mset(Sr32, 0.0, [b_Sr32]); mset(Srb, 0.0, [b_Srb])
        mset(Sh32, 0.0, [b_Sh32]); mset(Sg32, 0.0, [b_Sg32])
        for c in range(3):
            mset(Shb[c][0], 0.0, [Shb[c][1]]); mset(Sgb[c][0], 0.0, [Sgb[c][1]])
        for c in range(2):
            mset(hq[c][0], 0.0, [hq[c][1]]); mset(gq[c][0], 0.0, [gq[c][1]]); mset(gw[c][0], 0.0, [gw[c][1]])
        mset(gl, 0.0, [b_gl]); mset(gp, 0.0, [b_gp])
        mset(mix, 0.0, [b_mix])

        ckpt(5)

        def head_norm(o, b_o, sil, b_sil, w3, b_w, center, eps, dst3, who):
            t_ = HN[who]
            (hsq, b_hsq), (hy, b_hy), (s1, b_s1), (s2, b_s2) = t_["hsq"], t_["hy"], t_["s1"], t_["s2"]
            (mean, b_mean), (msq, b_msq), (var, b_var) = t_["mean"], t_["msq"], t_["var"]
            o3 = v3(o, 64)
            red(s1, o3, [b_o], [b_s1])
            act(hsq, o, AF.Square, [b_o], [b_hsq])
            red(s2, v3(hsq, 64), [b_hsq], [b_s2])
            if center:
                ts(mean, s1, 1.0 / 64, ALU.mult, [b_s1], [b_mean])
                tt(msq, mean, mean, ALU.mult, [b_mean], [b_msq])
                stt(var, s2, 1.0 / 64, msq, ALU.mult, ALU.subtract, [b_s2, b_msq], [b_var])
            else:
                ts(var, s2, 1.0 / 64, ALU.mult, [b_s2], [b_var])
            act(var, var, AF.Sqrt, [b_var], [b_var], bias=eps)
            rcp(var, var, [b_var], [b_var])
            hy3 = v3(hy, 64)
            if center:
                tt(hy3, o3, mean.unsqueeze(2).broadcast_to([128, 4, 64]), ALU.subtract, [b_o, b_mean], [b_hy])
                tt(hy3, hy3, var.unsqueeze(2).broadcast_to([128, 4, 64]), ALU.mult, [b_hy, b_var], [b_hy])
            else:
                tt(hy3, o3, var.unsqueeze(2).broadcast_to([128, 4, 64]), ALU.mult, [b_o, b_var], [b_hy])
            tt(hy3, hy3, w3, ALU.mult, [b_hy, b_w], [b_hy])
            tt(dst3, hy3, v3(sil, 64), ALU.mult, [b_hy, b_sil], [b_mix])

        for i in range(NT):
            pv, mid, end = (2 * i) % 3, (2 * i + 1) % 3, (2 * i + 2) % 3
            r0, r1 = slice(0, 64), slice(64, 128)
            src = dr["x"] if l == 0 else xs
            k.dma(xt, src[i * 128:(i + 1) * 128, :], R=([] if l == 0 else [xsb[i]]), W=[b_xt])
            k.dma(rope_t, dr["c_rope"][i * 128:(i + 1) * 128, :], W=[b_rope])
            ckpt(6)
            norm_to_hT(xt)
            ckpt(7)
            ckpt(8)

            def g_ret():
                pg, b_pg = yield from galloc()
                for kc in range(8):
                    mm(pg[:, 0:512], hT[:, kc, :], Wi[:, kc, 0:512], [b_hT, b_Wi], [b_pg], kc == 0, kc == 7)
                cp(rqk, pg, [b_pg], [b_rqk], eng="act")
                pfree(b_pg)
                pg, b_pg = yield from galloc()
                for kc in range(8):
                    mm(pg[:, 0:512], hT[:, kc, :], Wi[:, kc, 512:1024], [b_hT, b_Wi], [b_pg], kc == 0, kc == 7)
                silu(silr, pg[:, 256:512], [b_pg], [b_silr], HN["ret"]["hsq"][0], HN["ret"]["hsq"][1])
                cp(rvb, pg[:, 0:256], [b_pg], [b_rvb])
                pfree(b_pg)
                rt4 = rope_t.rearrange("p (a h d) -> p a h d", a=4, h=4)
                for (off, ci, si, dst, b_dst) in [(0, 0, 1, qrt, b_qrt), (256, 2, 3, krt, b_krt)]:
                    X = v3(rqk[:, off:off + 256], 64)
                    x1, x2 = X[:, :, 0:32], X[:, :, 32:64]
                    cs, sn = rt4[:, ci], rt4[:, si]
                    tt(t1, x1, cs, ALU.mult, [b_rqk, b_rope], [b_t1])
                    tt(t2, x2, sn, ALU.mult, [b_rqk, b_rope], [b_t2])
                    tt(dst[:, :, 0:32], t1, t2, ALU.subtract, [b_t1, b_t2], [b_dst])
                    tt(t1, x1, sn, ALU.mult, [b_rqk, b_rope], [b_t1])
                    tt(t2, x2, cs, ALU.mult, [b_rqk, b_rope], [b_t2])
                    tt(dst[:, :, 32:64], t1, t2, ALU.add, [b_t1, b_t2], [b_dst])
                pb, b_pb = yield from galloc()
                pbv = v3(pb.bitcast(BF16), 128)
                for h in range(4):
                    tr(pbv[0:64, h, :], qrt[:, h, :], identb, [b_qrt, b_identb], [b_pb])
                    tr(pbv[0:64, 4 + h, :], krt[:, h, :], identb, [b_krt, b_identb], [b_pb])
                cp(qkT, pbv[0:64], [b_pb], [b_qkT], eng="act")
                pfree(b_pb)
                yield
                ps, b_ps = yield from galloc()
                ps3 = v3(ps, 128)
                for h in range(4):
                    mm(ps3[:, h, :], qkT[:, 4 + h, :], qkT[:, h, :], [b_qkT], [b_ps])
                tt(PTr, ps3, cmask.unsqueeze(1).broadcast_to([128, 4, 128]), ALU.mult, [b_ps, b_cmask], [b_PTr])
                pfree(b_ps)
                yield
                po, b_po = yield from galloc()
                for h in range(4):
                    hs = slice(h * 64, (h + 1) * 64)
                    mm(po[:, hs], PTr[:, h, :], rvb[:, hs], [b_PTr, b_rvb], [b_po], True, False)
                    mm(po[:, hs], qkT[:, h, :], Srb[:, h, :], [b_qkT, b_Srb], [b_po], False, True)
                pd, b_pd = yield from galloc()
                for h in range(4):
                    hs = slice(h * 64, (h + 1) * 64)
                    mm(pd[0:64, hs], krt[:, h, :], rvb[:, hs], [b_krt, b_rvb], [b_pd])
                tt(Sr32, Sr32, pd[0:64, 0:256], ALU.add, [b_Sr32, b_pd], [b_Sr32])
                tt(Sr32, Sr32, gC, ALU.mult, [b_Sr32, b_gC], [b_Sr32])
                cp(Srb, v3(Sr32, 64), [b_Sr32], [b_Srb], eng="pool")
                pfree(b_pd)
                yield
                head_norm(po[:, 0:256], b_po, silr, b_silr, v3(rnw, 64), b_rnw, True, 1e-5, v3(mix[:, 0:256], 64), "ret")
                pfree(b_po)

            def g_hgrn():
                pg, b_pg = yield from galloc()
                for kc in range(8):
                    mm(pg[:, 0:512], hT[:, kc, :], Wi[:, kc, 2568:3080], [b_hT, b_Wi], [b_pg], kc == 0, kc == 7)
                cp(hqf, pg[:, 0:256], [b_pg], [b_hqf])
                act(hf, pg[:, 256:512], AF.Sigmoid, [b_pg], [b_hf])
                pfree(b_pg)
                pg, b_pg = yield from galloc()
                for kc in range(8):
                    mm(pg[:, 0:512], hT[:, kc, :], Wi[:, kc, 3080:3592], [b_hT, b_Wi], [b_pg], kc == 0, kc == 7)
                silu(silh, pg[:, 256:512], [b_pg], [b_silh], HN["hgrn"]["hsq"][0], HN["hgrn"]["hsq"][1])
                cp(hvb, pg[:, 0:256], [b_pg], [b_hvb])
                pfree(b_pg)
                tt(hf, hf, oml, ALU.mult, [b_hf, b_oml], [b_hf])
                tt(hf, hf, lbt, ALU.add, [b_hf, b_lbt], [b_hf])
                act(logf, hf, AF.Ln, [b_hf], [b_logf])
                ts(hkk, hf, -1.0, ALU.mult, [b_hf], [b_hkk], s2=1.0, op1=ALU.add)
                pG, b_pG = yield from galloc()
                mm(pG[:, 0:256], ublk, logf, [b_ublk, b_logf], [b_pG])
                mm(pG[:, 256:512], oblk, logf, [b_oblk, b_logf], [b_pG])
                pL, b_pL = yield from galloc()
                for h in range(4):
                    mm(pL[0:64, h * 2:(h + 1) * 2], logf[:, h * 64:(h + 1) * 64], cind, [b_logf, b_cind], [b_pL])
                act(eGl, pL[0:64, 0:8], AF.Exp, [b_pL], [b_eGl])
                pfree(b_pL)
                cp(Gsb, pG[:, 0:256], [b_pG], [b_Gsb], eng="act")
                act(eG, pG[:, 0:256], AF.Exp, [b_pG], [b_eG])
                tt(qh, hqf, eG, ALU.mult, [b_hqf, b_eG], [b_qh])
                act(enG, pG[:, 0:256], AF.Exp, [b_pG], [b_enG], scale=-1.0)
                tt(kht, hkk, enG, ALU.mult, [b_hkk, b_enG], [b_kht])
                tt(hD, pG[:, 256:512], Gsb, ALU.subtract, [b_pG, b_Gsb], [b_hD])
                pfree(b_pG)
                yield
                act(hD, hD, AF.Exp, [b_hD], [b_hD])
                tt(khat, hkk, hD, ALU.mult, [b_hkk, b_hD], [b_khat])
                pb, b_pb = yield from galloc()
                pbv = v3(pb.bitcast(BF16), 128)
                for h in range(4):
                    tr(pbv[0:64, h, :], qh[:, h * 64:(h + 1) * 64], identb, [b_qh, b_identb], [b_pb])
                    tr(pbv[0:64, 4 + h, :], kht[:, h * 64:(h + 1) * 64], identb, [b_kht, b_identb], [b_pb])
                cp(hq[0][0][:, :, 0:64], pbv[0:64, 0:4, 0:64], [b_pb], [hq[0][1]], eng="act")
                cp(hq[1][0][:, :, 64:128], pbv[0:64, 0:4, 64:128], [b_pb], [hq[1][1]], eng="act")
                cp(hkT, pbv[0:64, 4:8, :], [b_pb], [b_hkT])
                pfree(b_pb)
                yield
                ps, b_ps = yield from galloc()
                ps3 = v3(ps, 128)
                for h in range(4):
                    for c in range(2):
                        cs_ = slice(c * 64, (c + 1) * 64)
                        mm(ps3[:, h, cs_], hkT[:, h, :], hq[c][0][:, h, cs_], [b_hkT, hq[c][1]], [b_ps])
                tt(PTh, ps3, ublk.unsqueeze(1).broadcast_to([128, 4, 128]), ALU.mult, [b_ps, b_ublk], [b_PTh])
                pfree(b_ps)
                yield
                for c, (rr, sdst) in enumerate([(r0, mid), (r1, end)]):
                    pd, b_pd = yield from galloc()
                    for h in range(4):
                        hs = slice(h * 64, (h + 1) * 64)
                        mm(pd[0:64, hs], khat[rr, hs], hvb[rr, hs], [b_khat, b_hvb], [b_pd])
                    for h in range(4):
                        hs = slice(h * 64, (h + 1) * 64)
                        stt(Sh32[:, h, :], Sh32[:, h, :], eGl[:, h * 2 + c:h * 2 + c + 1], pd[0:64, hs],
                            ALU.mult, ALU.add, [b_Sh32, b_eGl, b_pd], [b_Sh32])
                    cp(Shb[sdst][0], Sh32, [b_Sh32], [Shb[sdst][1]], eng="pool")
                    pfree(b_pd)
                    yield
                po, b_po = yield from galloc()
                for h in range(4):
                    hs = slice(h * 64, (h + 1) * 64)
                    mm(po[:, hs], PTh[:, h, :], hvb[:, hs], [b_PTh, b_hvb], [b_po], True, False)
                    mm(po[:, hs], hq[0][0][:, h, :], Shb[pv][0][:, h, :], [hq[0][1], Shb[pv][1]], [b_po], False, False)
                    mm(po[:, hs], hq[1][0][:, h, :], Shb[mid][0][:, h, :], [hq[1][1], Shb[mid][1]], [b_po], False, True)
                head_norm(po[:, 0:256], b_po, silh, b_silh, hnw.unsqueeze(1).broadcast_to([128, 4, 64]), b_hnw,
                          False, 1e-6, v3(mix[:, 768:1024], 64), "hgrn")
                pfree(b_po)

            def g_conf():
                pg, b_pg = yield from galloc()
                for kc in range(8):
                    mm(pg[:, 0:512], hT[:, kc, :], Wi[:, kc, 1024:1536], [b_hT, b_Wi], [b_pg], kc == 0, kc == 7)
                act(ctsg, pg[:, 256:512], AF.Sigmoid, [b_pg], [b_ctsg])
                tt(glt, pg[:, 0:256], ctsg, ALU.mult, [b_pg, b_ctsg], [b_glt])
                pfree(b_pg)
                pc, b_pc = yield from galloc()
                pc3 = v3(pc.bitcast(BF16)[:, 0:256], 128)
                for ct in range(2):
                    tr(pc3[:, ct, :], glt[:, ct * 128:(ct + 1) * 128], identb, [b_glt, b_identb], [b_pc])
                cp(gl[:, :, 30:158], pc3, [b_pc], [b_gl], eng="act")
                pfree(b_pc)
                yield
                ckpt(21)
                py, b_py = yield from galloc()
                py3 = v3(py, 128)
                for ct in range(2):
                    for kk in range(31):
                        mm(py3[:, ct, :], Dc[:, kk * 2 + ct, :], gl[:, ct, kk:kk + 128], [b_Dc, b_gl], [b_py],
                           kk == 0, kk == 30)
                ckpt(22)
                for ct in range(2):
                    act(yT[:, ct, :], py3[:, ct, :], AF.Identity, [b_py, b_ccb], [b_yT], bias=ccb[:, ct:ct + 1])
                pfree(b_py)
                yield
                cp(gl[:, :, 0:30], gl[:, :, 128:158], [b_gl], [b_gl], eng="pool")
                ckpt(24)
                pt, b_pt = yield from galloc()
                for ct in range(2):
                    tr(pt[:, ct * 128:(ct + 1) * 128], yT[:, ct, :], identf, [b_yT, b_identf], [b_pt])
                ckpt(25)
                red(c1, pt[:, 0:256].unsqueeze(1), [b_pt], [b_c1])
                act(yc, pt[:, 0:256], AF.Square, [b_pt], [b_yc])
                red(c2, yc.unsqueeze(1), [b_yc], [b_c2])
                ckpt(27)
                ts(c1, c1, 1.0 / 256, ALU.mult, [b_c1], [b_c1])
                tt(c3, c1, c1, ALU.mult, [b_c1], [b_c3])
                stt(c2, c2, 1.0 / 256, c3, ALU.mult, ALU.subtract, [b_c2, b_c3], [b_c2])
                ckpt(28)
                act(c2, c2, AF.Sqrt, [b_c2], [b_c2], bias=1e-5)
                rcp(c2, c2, [b_c2], [b_c2])
                ckpt(29)
                ts(yc, pt[:, 0:256], c1, ALU.subtract, [b_pt, b_c1, b_c2], [b_yc], s2=c2, op1=ALU.mult)
                pfree(b_pt)
                ckpt(30)
                tt(yc, yc, clnw, ALU.mult, [b_yc, b_clnw], [b_yc])
                tt(yc, yc, clnb, ALU.add, [b_yc, b_clnb], [b_yc])
                silu(mix[:, 256:512], yc, [b_yc], [b_mix], ctsg, b_ctsg)

            def g_gdn():
                pg, b_pg = yield from galloc()
                for kc in range(8):
                    mm(pg[:, 0:264], hT[:, kc, :], Wi[:, kc, 2304:2568], [b_hT, b_Wi], [b_pg], kc == 0, kc == 7)
                silu(silg, pg[:, 0:256], [b_pg], [b_silg], HN["gdn"]["hsq"][0], HN["gdn"]["hsq"][1])
                cp(gba, pg[:, 256:264], [b_pg], [b_gba])
                pfree(b_pg)
                pg, b_pg = yield from galloc()
                for kc in range(8):
                    mm(pg[:, 0:512], hT[:, kc, :], Wi[:, kc, 1536:2048], [b_hT, b_Wi], [b_pg], kc == 0, kc == 7)
                cp(gpt[:, 0:512], pg, [b_pg], [b_gpt], eng="act")
                pfree(b_pg)
                pg, b_pg = yield from galloc()
                for kc in range(8):
                    mm(pg[:, 0:256], hT[:, kc, :], Wi[:, kc, 2048:2304], [b_hT, b_Wi], [b_pg], kc == 0, kc == 7)
                cp(gpt[:, 512:768], pg[:, 0:256], [b_pg], [b_gpt])
                pfree(b_pg)
                pb1, b_pb1 = yield from galloc()
                pb2, b_pb2 = yield from galloc()
                v1 = v3(pb1.bitcast(BF16), 128)
                v2 = v3(pb2.bitcast(BF16)[:, 0:512], 128)
                for s in range(12):
                    dst_ = v1[0:64, s, :] if s < 8 else v2[0:64, s - 8, :]
                    tr(dst_, gpt[:, s * 64:(s + 1) * 64], identb, [b_gpt, b_identb], [b_pb1 if s < 8 else b_pb2])
                cp(gp[:, 0:8, 3:131], v1[0:64], [b_pb1], [b_gp], eng="act")
                cp(gp[:, 8:12, 3:131], v2[0:64], [b_pb2], [b_gp])
                pfree(b_pb1, b_pb2)
                yield
                for s in range(12):
                    ts(gx[:, s, :], gp[:, s, 0:128], gcw[:, s, 0:1], ALU.mult, [b_gp, b_gcw],
                       [b_gxs[s]] + ([b_gx] if s == 0 else []))
                for kk in range(1, 4):
                    for s in range(12):
                        stt(gx[:, s, :], gp[:, s, kk:kk + 128], gcw[:, s, kk:kk + 1], gx[:, s, :],
                            ALU.mult, ALU.add, [b_gp, b_gcw, b_gxs[s]], [b_gxs[s]])
                for b in range(3):
                    silu(gx[:, b * 4:(b + 1) * 4, :], gx[:, b * 4:(b + 1) * 4, :], b_gxs[b * 4:(b + 1) * 4],
                         [b_gx] + b_gxs[b * 4:(b + 1) * 4], gsq, b_gsq)
                yield
                cp(gp[:, :, 0:3], gp[:, :, 128:131], [b_gp], [b_gp], eng="pool")
                for b in range(2):
                    xb = gx[:, b * 4:(b + 1) * 4, :]
                    tt(gsq, xb, xb, ALU.mult, [b_gx], [b_gsq])
                    pn, b_pn = yield from galloc()
                    pn3 = v3(pn, 128)
                    for h in range(4):
                        mm(pn3[0:64, h, :], ones64, gsq[:, h, :], [b_ones64, b_gsq], [b_pn])
                    act(grs, pn3[0:64], AF.Sqrt, [b_pn], [b_grs], bias=1e-6)
                    pfree(b_pn)
                    rcp(grs, grs, [b_grs], [b_grs])
                    if b == 0:
                        stt(gqT, xb, 0.125, grs, ALU.mult, ALU.mult, [b_gx, b_grs], [b_gqT])
                    else:
                        tt(gkT, xb, grs, ALU.mult, [b_gx, b_grs], [b_gkT])
                cp(gvT, gx[:, 8:12, :], [b_gx], [b_gvT], eng="act")
                act(beta, gba[:, 0:4], AF.Sigmoid, [b_gba], [b_beta])
                act(lnb, beta, AF.Ln, [b_beta], [b_lnb])
                tt(gsp, gba[:, 4:8], dtb, ALU.add, [b_gba, b_dtb], [b_gsp])
                act(gsp, gsp, AF.Exp, [b_gsp], [b_gsp])
                act(gsp, gsp, AF.Ln, [b_gsp], [b_gsp], bias=1.0)
                tt(gg_, gsp, negA, ALU.mult, [b_gsp, b_negA], [b_gg])
                pq, b_pq = yield from galloc()
                mm(pq[:, 0:4], ublk, gg_, [b_ublk, b_gg], [b_pq])
                mm(pq[:, 4:8], oblk, gg_, [b_oblk, b_gg], [b_pq])
                mm(pq[0:4, 128:256], gg_, ublk, [b_gg, b_ublk], [b_pq])
                cp(Gs, pq[:, 0:4], [b_pq], [b_Gs])
                ts(nGs, Gs, -1.0, ALU.mult, [b_Gs], [b_nGs])
                tt(Bs, Gs, lnb, ALU.add, [b_Gs, b_lnb], [b_Bs])
                act(eB, Bs, AF.Exp, [b_Bs], [b_eB])
                tt(eDl, pq[:, 4:8], Gs, ALU.subtract, [b_pq, b_Gs], [b_eDl])
                act(eDl, eDl, AF.Exp, [b_eDl], [b_eDl])
                cp(GTs, pq[0:4, 128:256], [b_pq], [b_GTs])
                pfree(b_pq)
                yield
                pb, b_pb = yield from galloc()
                pbv = v3(pb.bitcast(BF16)[:, 0:512], 64)
                for h in range(4):
                    tr(pbv[:, h, :], gkT[:, h, :], identb[0:64, 0:64], [b_gkT, b_identb], [b_pb])
                    tr(pbv[:, 4 + h, :], gvT[:, h, :], identb[0:64, 0:64], [b_gvT, b_identb], [b_pb])
                tt(grk, pbv[:, 0:4, :], eB.unsqueeze(2).broadcast_to([128, 4, 64]), ALU.mult, [b_pb, b_eB], [b_grk])
                tt(gkh, pbv[:, 0:4, :], eDl.unsqueeze(2).broadcast_to([128, 4, 64]), ALU.mult, [b_pb, b_eDl], [b_gkh])
                tt(gbv, pbv[:, 4:8, :], beta.unsqueeze(2).broadcast_to([128, 4, 64]), ALU.mult, [b_pb, b_beta], [b_gbv])
                pfree(b_pb)
                yield
                tt(v3(Gblk, 128), GTs.unsqueeze(1).broadcast_to([4, 4, 128]), sel3, ALU.mult, [b_GTs, b_sel], [b_Gblk])
                ts(nGblk, Gblk, -1.0, ALU.mult, [b_Gblk], [b_nGblk])
                pea, b_pea = yield from galloc()
                pea3 = v3(pea, 128)
                mm(pea, ones4, nGblk, [b_ones4, b_nGblk], [b_pea], True, False)
                mm(pea, identb, negAb, [b_identb, b_negAb], [b_pea], False, True)
                for h in range(4):
                    act(EA[:, h, :], pea3[:, h, :], AF.Exp, [b_pea, b_Bs], [b_EA], bias=Bs[:, h:h + 1])
                pfree(b_pea)
                pet, b_pet = yield from galloc()
                pet3 = v3(pet, 128)
                mm(pet, ones4, Gblk, [b_ones4, b_Gblk], [b_pet], True, False)
                mm(pet, identb, negTb, [b_identb, b_negTb], [b_pet], False, True)
                for h in range(4):
                    act(ET[:, h, :], pet3[:, h, :], AF.Exp, [b_pet, b_nGs], [b_ET], bias=nGs[:, h:h + 1])
                pfree(b_pet)
                peg, b_peg = yield from galloc()
                peg3 = v3(peg, 128)
                mm(peg[0:64, :], ones4[:, 0:64], Gblk, [b_ones4, b_Gblk], [b_peg])
                act(eGr, peg3[0:64], AF.Exp, [b_peg], [b_eGr])
                pfree(b_peg)
                yield
                tt(gq[0][0][:, :, 0:64], gqT[:, :, 0:64], eGr[:, :, 0:64], ALU.mult, [b_gqT, b_eGr], [gq[0][1]])
                tt(gq[1][0][:, :, 64:128], gqT[:, :, 64:128], eGr[:, :, 64:128], ALU.mult, [b_gqT, b_eGr], [gq[1][1]])
                pkk, b_pkk = yield from galloc()
                pkq, b_pkq = yield from galloc()
                for h in range(4):
                    mm(v3(pkk, 128)[:, h, :], gkT[:, h, :], gkT[:, h, :], [b_gkT], [b_pkk])
                    mm(v3(pkq, 128)[:, h, :], gkT[:, h, :], gqT[:, h, :], [b_gkT, b_gqT], [b_pkq])
                stt(Pm, v3(pkk, 128), -1.0, EA, ALU.mult, ALU.mult, [b_pkk, b_EA], [b_Pm])
                tt(attT, v3(pkq, 128), ET, ALU.mult, [b_pkq, b_ET], [b_attT])
                pfree(b_pkk, b_pkq)
                yield
                pb, b_pb = yield from galloc()
                pbv = v3(pb.bitcast(BF16)[:, 0:512], 128)
                for h in range(4):
                    tr(pbv[:, h, :], Pm[:, h, :], identb, [b_Pm, b_identb], [b_pb])
                cp(PmT, pbv, [b_pb], [b_PmT], eng="act")
                tt(Wt, pbv, identb.unsqueeze(1).broadcast_to([128, 4, 128]), ALU.add, [b_pb, b_identb], [b_Wt])
                pfree(b_pb)
                yield
                for m in range(5):
                    pa1, b_pa1 = yield from galloc()
                    for h in range(4):
                        mm(v3(pa1, 128)[:, h, :], PmT[:, h, :], Pm[:, h, :], [b_PmT, b_Pm], [b_pa1])
                    if m < 4:
                        pa2, b_pa2 = yield from galloc()
                        for h in range(4):
                            mm(v3(pa2, 128)[:, h, :], Pm[:, h, :], PmT[:, h, :], [b_PmT, b_Pm], [b_pa2])
                    cp(Pm, v3(pa1, 128), [b_pa1], [b_Pm], eng="act")
                    pfree(b_pa1)
                    if m < 4:
                        cp(PmT, v3(pa2, 128), [b_pa2], [b_PmT])
                        pfree(b_pa2)
                    yield
                    pa3, b_pa3 = yield from galloc()
                    for h in range(4):
                        mm(v3(pa3, 128)[:, h, :], Pm[:, h, :], Wt[:, h, :], [b_Pm, b_Wt], [b_pa3])
                    tt(Wt, Wt, v3(pa3, 128), ALU.add, [b_Wt, b_pa3], [b_Wt])
                    pfree(b_pa3)
                    yield
                pu, b_pu = yield from galloc()
                pw, b_pw = yield from galloc()
                for h in range(4):
                    hs = slice(h * 64, (h + 1) * 64)
                    mm(pu[:, hs], Wt[:, h, :], gbv[:, h, :], [b_Wt, b_gbv], [b_pu])
                    mm(v3(pw, 128)[0:64, h, :], grk[:, h, :], Wt[:, h, :], [b_grk, b_Wt], [b_pw])
                cp(usb, pu[:, 0:256], [b_pu], [b_usb])
                cp(gw[0][0][:, :, 0:64], v3(pw, 128)[0:64, :, 0:64], [b_pw], [gw[0][1]], eng="act")
                cp(gw[1][0][:, :, 64:128], v3(pw, 128)[0:64, :, 64:128], [b_pw], [gw[1][1]], eng="act")
                pfree(b_pu, b_pw)
                yield
                for c, (rr, ssrc, sdst) in enumerate([(r0, pv, mid), (r1, mid, end)]):
                    pa, b_pa = yield from galloc()
                    for h in range(4):
                        hs = slice(h * 64, (h + 1) * 64)
                        mm(pa[:, hs], gw[c][0][:, h, :], Sgb[ssrc][0][:, h, :], [gw[c][1], Sgb[ssrc][1]], [b_pa])
                    tt(vn[rr, :], usb[rr, :], pa[rr, 0:256], ALU.subtract, [b_usb, b_pa], [b_vn])
                    pfree(b_pa)
                    pd, b_pd = yield from galloc()
                    for h in range(4):
                        hs = slice(h * 64, (h + 1) * 64)
                        mm(pd[0:64, hs], gkh[rr, h, :], vn[rr, hs], [b_gkh, b_vn], [b_pd])
                    col = 63 + 64 * c
                    for h in range(4):
                        hs = slice(h * 64, (h + 1) * 64)
                        stt(Sg32[:, h, :], Sg32[:, h, :], eGr[:, h, col:col + 1], pd[0:64, hs],
                            ALU.mult, ALU.add, [b_Sg32, b_eGr, b_pd], [b_Sg32])
                    cp(Sgb[sdst][0], Sg32, [b_Sg32], [Sgb[sdst][1]], eng="pool")
                    pfree(b_pd)
                    yield
                po, b_po = yield from galloc()
                for h in range(4):
                    hs = slice(h * 64, (h + 1) * 64)
                    mm(po[:, hs], attT[:, h, :], vn[:, hs], [b_attT, b_vn], [b_po], True, False)
                    mm(po[:, hs], gq[0][0][:, h, :], Sgb[pv][0][:, h, :], [gq[0][1], Sgb[pv][1]], [b_po], False, False)
                    mm(po[:, hs], gq[1][0][:, h, :], Sgb[mid][0][:, h, :], [gq[1][1], Sgb[mid][1]], [b_po], False, True)
                head_norm(po[:, 0:256], b_po, silg, b_silg, gnw.unsqueeze(1).broadcast_to([128, 4, 64]), b_gnw,
                          False, 1e-6, v3(mix[:, 512:768], 64), "gdn")
                pfree(b_po)

            gl_ = [("gdn", g_gdn, [0, 1, 2]), ("hgrn", g_hgrn, [3, 4]), ("conf", g_conf, [5]), ("ret", g_ret, [6, 7])]
            assert len(freeq) == 8
            run_interleaved([(g, bs) for n_, g, bs in gl_ if n_ not in skip])
            if True:
                if dbg:
                    cp(mixf, mix, [b_mix], [b_mixf], eng="pool")
                    k.dma(dmix[l, i * 128:(i + 1) * 128, :], mixf, R=[b_mixf])
                pb, b_pb = psum()
                pbv = v3(pb.bitcast(BF16), 128)
                for kc in range(8):
                    tr(pbv[:, kc, :], mix[:, kc * 128:(kc + 1) * 128], identb, [b_mix, b_identb], [b_pb])
                cp(mT, pbv, [b_pb], [b_mT], eng="act")
                pfree(b_pb)
                for n in range(2):
                    pso, b_pso = psum()
                    ns = slice(n * 512, (n + 1) * 512)
                    for kc in range(8):
                        mm(pso, mT[:, kc, :], Wo[:, kc, ns], [b_mT, b_Wo], [b_pso], kc == 0, kc == 7)
                    tt(xo[:, ns], xt[:, ns], pso, ALU.add, [b_xt, b_pso], [b_xo])
                    pfree(b_pso)
                k.dma(xs[i * 128:(i + 1) * 128, :], xo, R=[b_xo], W=[xsb[i]])
        k.barrier()
        es.close()

    def pass_b(l, last):
        es = ExitStack()
        Wg, b_Wg = sb(es, "Wg", [128, 8, DFF], BF16)
        Wu, b_Wu = sb(es, "Wu", [128, 8, DFF], BF16)
        Wd, b_Wd = sb(es, "Wd", [128, NFT, D], BF16)
        aT, b_aT = sb(es, "aT", [128, NFT, 128], BF16)
        fsg, b_fsg = sb(es, "fsg", [128, 512])
        fsc, b_fsc = sb(es, "fsc", [128, 512])
        fnw, b_fnw = sb(es, "fnw", [128, D])
        with nc.allow_non_contiguous_dma(reason="tiny param loads"):
            k.dma(nw, dr["norm_ffn_w"][l].rearrange("(c p) -> p c", p=128), W=[b_nw])
            k.dma(fnw, dr["final_norm_w"].partition_broadcast(128), W=[b_fnw])
        load_w(Wg, b_Wg, dr["ffn_w_gate"][l], 8, DFF, nw)
        load_w(Wu, b_Wu, dr["ffn_w_up"][l], 8, DFF, nw)
        load_w(Wd, b_Wd, dr["ffn_w_down"][l], NFT, D, None)
        for i in range(NT):
            k.dma(xt, xs[i * 128:(i + 1) * 128, :], R=[xsb[i]], W=[b_xt])
            norm_to_hT(xt)
            for grp in range(6):
                nf = 4 if grp < 5 else 2
                psg, b_psg = psum()
                psu, b_psu = psum()
                for j in range(nf):
                    ft = grp * 4 + j
                    fs = slice(ft * 128, (ft + 1) * 128)
                    for kc in range(8):
                        mm(psg[:, j * 128:(j + 1) * 128], Wg[:, kc, fs], hT[:, kc, :], [b_Wg, b_hT], [b_psg], kc == 0, kc == 7)
                    for kc in range(8):
                        mm(psu[:, j * 128:(j + 1) * 128], Wu[:, kc, fs], hT[:, kc, :], [b_Wu, b_hT], [b_psu], kc == 0, kc == 7)
                silu(fsg[:, :nf * 128], psg[:, :nf * 128], [b_psg], [b_fsg], fsc[:, :nf * 128], b_fsc)
                tt(aT[:, grp * 4:grp * 4 + nf, :], v3(psu[:, :nf * 128], 128), v3(fsg[:, :nf * 128], 128), ALU.mult,
                   [b_psu, b_fsg], [b_aT])
                pfree(b_psg, b_psu)
            for n in range(2):
                pso, b_pso = psum()
                ns = slice(n * 512, (n + 1) * 512)
                for ft in range(NFT):
                    mm(pso, aT[:, ft, :], Wd[:, ft, ns], [b_aT, b_Wd], [b_pso], ft == 0, ft == NFT - 1)
                tt(xo[:, ns], xt[:, ns], pso, ALU.add, [b_xt, b_pso], [b_xo])
                pfree(b_pso)
            if last:
                act(xn, xo, AF.Square, [b_xo], [b_xn, b_ss], accum=ss)
                act(ss, ss, AF.Sqrt, [b_ss], [b_ss], bias=1e-6, scale=1.0 / D)
                rcp(ss, ss, [b_ss], [b_ss])
                ts(xo, xo, ss, ALU.mult, [b_xo, b_ss], [b_xo])
                tt(xo, xo, fnw, ALU.mult, [b_xo, b_fnw], [b_xo])
                k.dma(out[i * 128:(i + 1) * 128, :], xo, R=[b_xo])
            else:
                k.dma(xs[i * 128:(i + 1) * 128, :], xo, R=[b_xo], W=[xsb[i]])
        k.barrier()
        es.close()

    for l in range(NL):
        try:
            pass_a(l)
        except StopBuild:
            k.barrier()
            return nc, k
        if "b" in skip:
            for i in range(NT):
                k.dma(xt, xs[i * 128:(i + 1) * 128, :], R=[xsb[i]], W=[b_xt])
                k.dma(out[i * 128:(i + 1) * 128, :], xt, R=[b_xt])
        else:
            pass_b(l, l == NL - 1)
    k.barrier()
    es_top.close()
    return nc, k


_CACHE = {}


def kernel(**inputs):
    x = np.asarray(inputs["x"], dtype=np.float32)
    B, T, _ = x.shape
    key = T
    if key not in _CACHE:
        _CACHE[key] = (build(T)[0], host_consts(T))
    nc, consts = _CACHE[key]
    params = {n: np.ascontiguousarray(np.asarray(inputs[n], dtype=np.float32)) for n in PARAM_SHAPES}
    in_maps = []
    for c in range(8):
        m = {"x": np.ascontiguousarray(x[c % B])}
        m.update(params)
        m.update(consts)
        in_maps.append(m)
    res = run_bass_kernel_spmd(nc, in_maps, core_ids=list(range(8)))
    outs = [np.asarray(res.results[c]["out"], dtype=np.float32) for c in range(B)]
    return np.stack(outs, axis=0)
```

```python
from contextlib import ExitStack
import numpy as np
import ml_dtypes
import concourse.bass as bass
import concourse.mybir as mybir
from concourse.bass_utils import run_bass_kernel_spmd

F32 = mybir.dt.float32
BF16 = mybir.dt.bfloat16
AF = mybir.ActivationFunctionType
ALU = mybir.AluOpType
AX = mybir.AxisListType

D = 1024
DIN = 3592
DFF = 2816
NFT = DFF // 128
NEG = -30000.0


class Buf:
    __slots__ = ("name", "w", "r", "excl")

    def __init__(self, name="", excl=False):
        self.name = name
        self.w = None
        self.r = {}
        self.excl = excl


class K:
    NDSEM = 40

    def __init__(self, nc):
        self.nc = nc
        self.eng = {"pe": nc.tensor, "act": nc.scalar, "dve": nc.vector,
                    "pool": nc.gpsimd, "sp": nc.sync}
        self.sem = {e: nc.alloc_semaphore(name="s_" + e) for e in ("pe", "act", "dve", "pool")}
        self.cnt = {e: 0 for e in self.sem}
        self.dsem = [nc.alloc_semaphore(name="d%d" % i) for i in range(self.NDSEM)]
        self.dcnt = [0] * self.NDSEM
        self.dnext = 0
        self.waited = {e: {} for e in self.eng}
        self.nwaits = 0
        self.ninst = 0
        self.rec = None

    def replay(self, item):
        if item[0] == "op":
            self.op(*item[1:])
        else:
            self.dma(*item[1:3], **item[3])

    def _wait(self, eng, ev):
        if ev is None:
            return
        if ev[0] == "e":
            key, val = ("e", ev[1]), ev[2]
            if ev[1] == "pe" and eng == "pe":
                return
            sem = self.sem[ev[1]]
        else:
            key, val = ("d", ev[1]), ev[2]
            sem = self.dsem[ev[1]]
        if self.waited[eng].get(key, 0) >= val:
            return
        self.eng[eng].wait_ge(sem, val)
        self.waited[eng][key] = val
        self.nwaits += 1

    def _deps(self, eng, R, W):
        for b in R:
            self._wait(eng, b.w)
            if b.excl:
                for ev in b.r.values():
                    if not (ev[0] == "e" and ev[1] == eng):
                        self._wait(eng, ev)
        for b in W:
            self._wait(eng, b.w)
            for ev in b.r.values():
                self._wait(eng, ev)

    def _post(self, ev, R, W):
        for b in R:
            b.r[(ev[0], ev[1])] = ev
        for b in W:
            b.w = ev
            b.r = {}

    def op(self, eng, fn, R=(), W=()):
        if self.rec is not None:
            self.rec.append(("op", eng, fn, tuple(R), tuple(W)))
            return None
        self._deps(eng, R, W)
        inst = fn(self.eng[eng])
        self.cnt[eng] += 1
        inst.then_inc(self.sem[eng], 1)
        self._post(("e", eng, self.cnt[eng]), R, W)
        self.ninst += 1
        return inst

    def dma(self, out, in_, R=(), W=(), q="sp", **kw):
        if self.rec is not None:
            self.rec.append(("dma", out, in_, dict(R=tuple(R), W=tuple(W), q=q, **kw)))
            return None
        idx = self.dnext
        self.dnext = (self.dnext + 1) % self.NDSEM
        if self.dcnt[idx] > 0:
            self._wait(q, ("d", idx, self.dcnt[idx]))
        self._deps(q, R, W)
        inst = self.eng[q].dma_start(out=out, in_=in_, **kw)
        self.dcnt[idx] += 16
        inst.then_inc(self.dsem[idx], 16)
        ev = ("d", idx, self.dcnt[idx])
        self._post(ev, R, W)
        self.ninst += 1
        return ev

    def barrier(self):
        for e in self.eng:
            for e2 in self.sem:
                if self.cnt[e2] > 0:
                    self._wait(e, ("e", e2, self.cnt[e2])) if not (e == e2 == "pe") else None
            for i in range(self.NDSEM):
                if self.dcnt[i] > 0:
                    self._wait(e, ("d", i, self.dcnt[i]))


def host_consts(T):
    c = {}
    c["c_identf"] = np.eye(128, dtype=np.float32)
    j = np.arange(128)[:, None]
    i = np.arange(128)[None, :]
    same = (j // 64) == (i // 64)
    c["c_cmask"] = (j <= i).astype(np.float32)
    c["c_ublk"] = ((j <= i) & same).astype(np.float32)
    c["c_oblk"] = same.astype(np.float32)
    cind = np.zeros((128, 2), np.float32)
    cind[:64, 0] = 1.0
    cind[64:, 1] = 1.0
    c["c_cind"] = cind
    sel = np.zeros((4, 4, 128), np.float32)
    for h in range(4):
        sel[h, h, :] = 1.0
    c["c_sel"] = sel.reshape(4, 512)
    c["c_negA"] = np.where(same & (j > i), 0.0, NEG).astype(np.float32)
    c["c_negT"] = np.where(same & (i >= j), 0.0, NEG).astype(np.float32)
    c["c_ones64"] = np.ones((64, 64), np.float32)
    c["c_ones4"] = np.ones((4, 128), np.float32)
    c["c_negA4"] = np.tile(c["c_negA"], (1, 4))
    c["c_negT4"] = np.tile(c["c_negT"], (1, 4))
    t = np.arange(T, dtype=np.float64)
    inv = 10000.0 ** (-np.arange(32, dtype=np.float64) / 32.0)
    ang = (t[:, None].astype(np.float32) * inv[None, :].astype(np.float32)).astype(np.float64)
    cos, sin = np.cos(ang), np.sin(ang)
    gam = 1.0 - 2.0 ** (-5.0 - np.arange(4, dtype=np.float64))
    pos = (np.arange(T) % 128 + 1).astype(np.float64)
    dq = gam[None, :] ** pos[:, None]
    dk = gam[None, :] ** (-pos[:, None]) * 0.125
    rope = np.zeros((T, 4, 4, 32), np.float64)
    rope[:, 0] = cos[:, None, :] * dq[:, :, None]
    rope[:, 1] = sin[:, None, :] * dq[:, :, None]
    rope[:, 2] = cos[:, None, :] * dk[:, :, None]
    rope[:, 3] = sin[:, None, :] * dk[:, :, None]
    c["c_rope"] = rope.reshape(T, 512).astype(np.float32)
    gC = np.zeros((64, 4, 64), np.float64)
    gC[:] = (gam ** 128.0)[None, :, None]
    c["c_gC"] = gC.reshape(64, 256).astype(np.float32)
    return c


CONST_SHAPES = {"c_identf": [128, 128], "c_cmask": [128, 128], "c_ublk": [128, 128],
                "c_oblk": [128, 128], "c_cind": [128, 2], "c_sel": [4, 512],
                "c_negA": [128, 128], "c_negT": [128, 128], "c_ones64": [64, 64],
                "c_ones4": [4, 128], "c_negA4": [128, 512], "c_negT4": [128, 512],
                "c_gC": [64, 256]}

PARAM_SHAPES = {
    "norm_mix_w": [2, 1024], "w_in": [2, 1024, DIN], "ret_norm_w": [2, 256],
    "conf_conv_w": [2, 31, 256], "conf_conv_b": [2, 256], "conf_ln_w": [2, 256],
    "conf_ln_b": [2, 256], "gdn_conv_w": [2, 4, 768], "gdn_A_log": [2, 4],
    "gdn_dt_bias": [2, 4], "gdn_norm_w": [2, 64], "hgrn_lb_logits": [2, 256],
    "hgrn_norm_w": [2, 64], "w_out": [2, 1024, 1024], "norm_ffn_w": [2, 1024],
    "ffn_w_gate": [2, 1024, DFF], "ffn_w_up": [2, 1024, DFF], "ffn_w_down": [2, DFF, 1024],
    "final_norm_w": [1024],
}


class StopBuild(Exception):
    pass


def build(T, NL=2, dbg=False, skip=(), stop=99):
    nc = bass.Bass("TRN2", target_bir_lowering=False)

    def ckpt(n):
        if n >= stop:
            raise StopBuild()

    NT = T // 128
    k = K(nc)
    dr = {}
    dr["x"] = nc.dram_tensor("x", [T, D], F32, kind="ExternalInput").ap()
    for n, s in PARAM_SHAPES.items():
        dr[n] = nc.dram_tensor(n, s, F32, kind="ExternalInput").ap()
    for n, s in CONST_SHAPES.items():
        dr[n] = nc.dram_tensor(n, s, F32, kind="ExternalInput").ap()
    dr["c_rope"] = nc.dram_tensor("c_rope", [T, 512], F32, kind="ExternalInput").ap()
    out = nc.dram_tensor("out", [T, D], F32, kind="ExternalOutput").ap()
    xs = nc.dram_tensor("xs", [T, D], F32, kind="Internal").ap()
    if dbg:
        dmix = nc.dram_tensor("dmix", [NL, T, D], F32, kind="ExternalOutput").ap()
    xsb = [Buf("xs%d" % i) for i in range(NT)]

    es_top = ExitStack()

    uniq = {"n": 0}

    def sb(es, name, shape, dt=F32):
        uniq["n"] += 1
        t = es.enter_context(nc.sbuf_tensor("%s_%d" % (name, uniq["n"]), shape, dt))
        return t.ap(), Buf(name)

    def mm(out_, lhsT, rhs, R, W, start=True, stop=True):
        k.op("pe", lambda e: e.matmul(out_, lhsT=lhsT, rhs=rhs, start=start, stop=stop), R=R, W=W)

    def tr(out_, in_, ident, R, W):
        k.op("pe", lambda e: e.transpose(out=out_, in_=in_, identity=ident), R=R, W=W)

    def act(out_, in_, func, R, W, bias=None, scale=None, accum=None):
        kw = {}
        if bias is not None:
            kw["bias"] = bias
        if scale is not None:
            kw["scale"] = scale
        if accum is not None:
            kw["accum_out"] = accum
        k.op("act", lambda e: e.activation(out=out_, in_=in_, func=func, **kw), R=R, W=W)

    def silu(out_, in_, R, W, scr, b_scr):
        act(scr, in_, AF.Sigmoid, R, [b_scr])
        tt(out_, in_, scr, ALU.mult, list(R) + [b_scr], W)

    def tt(out_, a, b, op, R, W, eng="dve"):
        k.op(eng, lambda e: e.tensor_tensor(out=out_, in0=a, in1=b, op=op), R=R, W=W)

    def ts(out_, a, s1, op0, R, W, s2=None, op1=None, eng="dve"):
        if op1 is None:
            k.op(eng, lambda e: e.tensor_scalar(out=out_, in0=a, scalar1=s1, scalar2=None, op0=op0), R=R, W=W)
        else:
            k.op(eng, lambda e: e.tensor_scalar(out=out_, in0=a, scalar1=s1, scalar2=None, op0=op0), R=R, W=W)
            k.op(eng, lambda e: e.tensor_scalar(out=out_, in0=out_, scalar1=s2, scalar2=None, op0=op1), R=list(R) + list(W), W=W)

    def stt(out_, in0, scalar, in1, op0, op1, R, W):
        k.op("dve", lambda e: e.scalar_tensor_tensor(out=out_, in0=in0, scalar=scalar, in1=in1, op0=op0, op1=op1), R=R, W=W)

    def cp(out_, in_, R, W, eng="dve"):
        if eng == "act":
            k.op("act", lambda e: e.copy(out=out_, in_=in_), R=R, W=W)
        else:
            k.op(eng, lambda e: e.tensor_copy(out=out_, in_=in_), R=R, W=W)

    def red(out_, in_, R, W):
        k.op("dve", lambda e: e.tensor_reduce(out=out_, in_=in_, axis=AX.X, op=ALU.add), R=R, W=W)

    def rcp(out_, in_, R, W):
        k.op("dve", lambda e: e.reciprocal(out=out_, in_=in_), R=R, W=W)

    def mset(ap, val, W, eng="pool"):
        k.op(eng, lambda e: e.memset(ap, val), W=W)

    def v3(ap, inner):
        return ap.rearrange("p (a b) -> p a b", b=inner)

    banks = []
    for i in range(8):
        t = es_top.enter_context(nc.psum_tensor("bank%d" % i, [128, 512], F32))
        banks.append((t.ap(), Buf("bank%d" % i, excl=True)))
    freeq = list(range(8))
    bank_idx = {id(b[1]): i for i, b in enumerate(banks)}
    cur = {"free": freeq}

    def psum():
        assert cur["free"], "no free PSUM bank"
        return banks[cur["free"].pop(0)]

    def galloc():
        if False:
            yield
        return psum()

    def pfree(*bufs):
        for b in bufs:
            i = bank_idx[id(b)]
            assert i not in cur["free"]
            cur["free"].append(i)

    def run_interleaved(gens):
        lists = []
        for g, bset in gens:
            cur["free"] = list(bset)
            k.rec = []
            for _ in g():
                pass
            lists.append(k.rec)
            k.rec = None
            assert sorted(cur["free"]) == sorted(bset), "mixer leaked PSUM banks"
        cur["free"] = freeq
        pos = [0] * len(lists)
        while True:
            best, bf = -1, 2.0
            for j, lst in enumerate(lists):
                if pos[j] < len(lst):
                    f = pos[j] / len(lst)
                    if f < bf:
                        best, bf = j, f
            if best < 0:
                break
            k.replay(lists[best][pos[best]])
            pos[best] += 1

    stg = [sb(es_top, "stg%d" % i, [128, 1024]) for i in range(2)]
    stg_c = stg[0]
    identf, b_identf = sb(es_top, "identf", [128, 128])
    identb, b_identb = sb(es_top, "identb", [128, 128], BF16)
    cmask, b_cmask = sb(es_top, "cmask", [128, 128])
    ublk, b_ublk = sb(es_top, "ublk", [128, 128])
    oblk, b_oblk = sb(es_top, "oblk", [128, 128])
    cind, b_cind = sb(es_top, "cind", [128, 2])
    sel, b_sel = sb(es_top, "sel", [4, 512])
    negAb, b_negAb = sb(es_top, "negAb", [128, 512], BF16)
    negTb, b_negTb = sb(es_top, "negTb", [128, 512], BF16)
    ones4, b_ones4 = sb(es_top, "ones4", [4, 128])
    ones64, b_ones64 = sb(es_top, "ones64", [64, 64])
    gC, b_gC = sb(es_top, "gC", [64, 256])
    CONSTS = []
    k.dma(identf, dr["c_identf"], W=[b_identf])
    k.dma(cmask, dr["c_cmask"], W=[b_cmask])
    k.dma(ublk, dr["c_ublk"], W=[b_ublk])
    k.dma(oblk, dr["c_oblk"], W=[b_oblk])
    k.dma(cind, dr["c_cind"], W=[b_cind])
    k.dma(sel, dr["c_sel"], W=[b_sel])
    k.dma(ones4, dr["c_ones4"], W=[b_ones4])
    k.dma(ones64, dr["c_ones64"], W=[b_ones64])
    k.dma(gC, dr["c_gC"], W=[b_gC])
    cp(identb, identf, [b_identf], [b_identb])
    for (dst_, b_dst_, nm_) in [(negAb, b_negAb, "c_negA4"), (negTb, b_negTb, "c_negT4")]:
        s_ap, s_b = stg_c
        k.dma(s_ap[:, 0:512], dr[nm_], W=[s_b])
        cp(dst_, s_ap[:, 0:512], [s_b], [b_dst_])
    sel3 = v3(sel, 128)
    try:
        ckpt(1)
    except StopBuild:
        k.barrier()
        return nc, k

    xt, b_xt = sb(es_top, "xt", [128, D])
    xo, b_xo = sb(es_top, "xo", [128, D])
    xn, b_xn = sb(es_top, "xn", [128, D], BF16)
    hT, b_hT = sb(es_top, "hT", [128, 8, 128], BF16)
    ss, b_ss = sb(es_top, "ss", [128, 1])
    nw, b_nw = sb(es_top, "nw", [128, 8])
    stg_i = {"i": 0}

    def load_w(dst, b_dst, src, KC, N, scale, stgs=None):
        stgs = stgs or stg
        engs = ("dve", "pool") if len(stgs) == 2 else ("dve", "pool", "act", "dve")
        for kc in range(KC):
            for n0 in range(0, N, 1024):
                n1 = min(N, n0 + 1024)
                s_ap, s_b = stgs[stg_i["i"] % len(stgs)]
                eng = engs[stg_i["i"] % len(stgs)]
                stg_i["i"] += 1
                k.dma(s_ap[:, :n1 - n0], src[kc * 128:(kc + 1) * 128, n0:n1], W=[s_b])
                if eng == "act":
                    if scale is not None:
                        act(dst[:, kc, n0:n1], s_ap[:, :n1 - n0], AF.Identity, [s_b, b_nw], [b_dst],
                            scale=scale[:, kc:kc + 1])
                    else:
                        cp(dst[:, kc, n0:n1], s_ap[:, :n1 - n0], [s_b], [b_dst], eng="act")
                elif scale is not None:
                    ts(dst[:, kc, n0:n1], s_ap[:, :n1 - n0], scale[:, kc:kc + 1], ALU.mult,
                       [s_b, b_nw], [b_dst], eng=eng)
                else:
                    cp(dst[:, kc, n0:n1], s_ap[:, :n1 - n0], [s_b], [b_dst], eng=eng)

    def norm_to_hT(src_xt):
        act(xn, src_xt, AF.Square, [b_xt], [b_xn, b_ss], accum=ss)
        act(ss, ss, AF.Sqrt, [b_ss], [b_ss], bias=1e-6, scale=1.0 / D)
        rcp(ss, ss, [b_ss], [b_ss])
        ts(xn, src_xt, ss, ALU.mult, [b_xt, b_ss], [b_xn])
        pb, b_pb = psum()
        pbv = v3(pb.bitcast(BF16), 128)
        for kc in range(8):
            tr(pbv[:, kc, :], xn[:, kc * 128:(kc + 1) * 128], identb, [b_xn, b_identb], [b_pb])
        cp(hT, pbv, [b_pb], [b_hT], eng="act")
        pfree(b_pb)

    def pass_a(l):
        es = ExitStack()
        Wi, b_Wi = sb(es, "Wi", [128, 8, DIN], BF16)
        Wo, b_Wo = sb(es, "Wo", [128, 8, D], BF16)
        Dc, b_Dc = sb(es, "Dc", [128, 62, 128], BF16)
        ccw, b_ccw = sb(es, "ccw", [128, 2, 31])
        ccb, b_ccb = sb(es, "ccb", [128, 2])
        gcw, b_gcw = sb(es, "gcw", [64, 12, 4])
        rnw, b_rnw = sb(es, "rnw", [128, 256])
        clnw, b_clnw = sb(es, "clnw", [128, 256])
        clnb, b_clnb = sb(es, "clnb", [128, 256])
        gnw, b_gnw = sb(es, "gnw", [128, 64])
        hnw, b_hnw = sb(es, "hnw", [128, 64])
        negA, b_negA = sb(es, "negA", [128, 4])
        dtb, b_dtb = sb(es, "dtb", [128, 4])
        lbl, b_lbl = sb(es, "lbl", [128, 2, 256])
        lbt, b_lbt = sb(es, "lbt", [128, 256])
        oml, b_oml = sb(es, "oml", [128, 256])
        with nc.allow_non_contiguous_dma(reason="tiny param loads"):
            k.dma(nw, dr["norm_mix_w"][l].rearrange("(c p) -> p c", p=128), W=[b_nw])
            for ct in range(2):
                k.dma(ccw[:, ct, :], dr["conf_conv_w"][l][:, ct * 128:(ct + 1) * 128].rearrange("k c -> c k"), W=[b_ccw])
            k.dma(ccb, dr["conf_conv_b"][l].rearrange("(ct c) -> c ct", c=128), W=[b_ccb])
            for s_ in range(12):
                k.dma(gcw[:, s_, :], dr["gdn_conv_w"][l][:, s_ * 64:(s_ + 1) * 64].rearrange("k d -> d k"), W=[b_gcw])
            k.dma(rnw, dr["ret_norm_w"][l].partition_broadcast(128), W=[b_rnw])
            k.dma(clnw, dr["conf_ln_w"][l].partition_broadcast(128), W=[b_clnw])
            k.dma(clnb, dr["conf_ln_b"][l].partition_broadcast(128), W=[b_clnb])
            k.dma(gnw, dr["gdn_norm_w"][l].partition_broadcast(128), W=[b_gnw])
            k.dma(hnw, dr["hgrn_norm_w"][l].partition_broadcast(128), W=[b_hnw])
            k.dma(negA, dr["gdn_A_log"][l].partition_broadcast(128), W=[b_negA])
            k.dma(dtb, dr["gdn_dt_bias"][l].partition_broadcast(128), W=[b_dtb])
            k.dma(lbl[:, 0, :], dr["hgrn_lb_logits"][0].partition_broadcast(128), W=[b_lbl])
            k.dma(lbl[:, 1, :], dr["hgrn_lb_logits"][1].partition_broadcast(128), W=[b_lbl])
        ckpt(2)
        act(negA, negA, AF.Exp, [b_negA], [b_negA])
        ts(negA, negA, -1.0, ALU.mult, [b_negA], [b_negA])
        if l == 0:
            mset(lbt, 0.0, [b_lbt])
            mset(oml, 1.0, [b_oml])
        else:
            act(lbl, lbl, AF.Exp, [b_lbl], [b_lbl])
            tt(oml, lbl[:, 0, :], lbl[:, 1, :], ALU.add, [b_lbl], [b_oml])
            rcp(oml, oml, [b_oml], [b_oml])
            tt(lbt, lbl[:, 1, :], oml, ALU.mult, [b_lbl, b_oml], [b_lbt])
            ts(oml, lbt, -1.0, ALU.mult, [b_lbt], [b_oml], s2=1.0, op1=ALU.add)
        for kk in range(31):
            for ct in range(2):
                ts(Dc[:, kk * 2 + ct, :], identf, ccw[:, ct, kk:kk + 1], ALU.mult, [b_identf, b_ccw], [b_Dc])
        ckpt(3)
        load_w(Wi, b_Wi, dr["w_in"][l], 8, DIN, nw)
        load_w(Wo, b_Wo, dr["w_out"][l], 8, D, None)

        ckpt(4)
        rope_t, b_rope = sb(es, "rope_t", [128, 512])
        rqk, b_rqk = sb(es, "rqk", [128, 512])
        ctsg, b_ctsg = sb(es, "ctsg", [128, 256])
        glt, b_glt = sb(es, "glt", [128, 256], BF16)
        gpt, b_gpt = sb(es, "gpt", [128, 768], BF16)
        Gblk, b_Gblk = sb(es, "Gblk", [4, 512])
        nGblk, b_nGblk = sb(es, "nGblk", [4, 512])
        hqf, b_hqf = sb(es, "hqf", [128, 256])
        t1, b_t1 = sb(es, "t1", [128, 4, 32])
        t2, b_t2 = sb(es, "t2", [128, 4, 32])
        qrt, b_qrt = sb(es, "qrt", [128, 4, 64], BF16)
        krt, b_krt = sb(es, "krt", [128, 4, 64], BF16)
        qkT, b_qkT = sb(es, "qkT", [64, 8, 128], BF16)
        PTr, b_PTr = sb(es, "PTr", [128, 4, 128], BF16)
        rvb, b_rvb = sb(es, "rvb", [128, 256], BF16)
        Sr32, b_Sr32 = sb(es, "Sr32", [64, 256])
        Srb, b_Srb = sb(es, "Srb", [64, 4, 64], BF16)
        silr, b_silr = sb(es, "silr", [128, 256])
        silg, b_silg = sb(es, "silg", [128, 256])
        silh, b_silh = sb(es, "silh", [128, 256])
        HN = {}
        for nm_ in ("ret", "hgrn", "gdn"):
            HN[nm_] = dict(hsq=sb(es, "hsq", [128, 256]), hy=sb(es, "hy", [128, 256]), s1=sb(es, "s1", [128, 4]),
                           s2=sb(es, "s2", [128, 4]), mean=sb(es, "mean", [128, 4]), msq=sb(es, "msq", [128, 4]),
                           var=sb(es, "var", [128, 4]))
        hsq, b_hsq = HN["ret"]["hsq"]
        hf, b_hf = sb(es, "hf", [128, 256])
        logf, b_logf = sb(es, "logf", [128, 256])
        hkk, b_hkk = sb(es, "hkk", [128, 256])
        Gsb, b_Gsb = sb(es, "Gsb", [128, 256])
        eG, b_eG = sb(es, "eG", [128, 256])
        enG, b_enG = eG, b_eG
        hD, b_hD = Gsb, b_Gsb
        qh, b_qh = sb(es, "qh", [128, 256], BF16)
        kht, b_kht = sb(es, "kht", [128, 256], BF16)
        khat, b_khat = sb(es, "khat", [128, 256], BF16)
        hvb, b_hvb = sb(es, "hvb", [128, 256], BF16)
        hq = [sb(es, "hq%d" % c, [64, 4, 128], BF16) for c in range(2)]
        hkT, b_hkT = sb(es, "hkT", [64, 4, 128], BF16)
        PTh, b_PTh = sb(es, "PTh", [128, 4, 128], BF16)
        Sh32, b_Sh32 = sb(es, "Sh32", [64, 4, 64])
        Shb = [sb(es, "Shb%d" % c, [64, 4, 64], BF16) for c in range(3)]
        eGl, b_eGl = sb(es, "eGl", [64, 8])
        gl, b_gl = sb(es, "gl", [128, 2, 158], BF16)
        yT, b_yT = sb(es, "yT", [128, 2, 128])
        yc, b_yc = sb(es, "yc", [128, 256])
        c1, b_c1 = sb(es, "c1", [128, 1])
        c2, b_c2 = sb(es, "c2", [128, 1])
        c3, b_c3 = sb(es, "c3", [128, 1])
        gp, b_gp = sb(es, "gp", [64, 12, 131], BF16)
        gx, b_gx = sb(es, "gx", [64, 12, 128])
        b_gxs = [Buf("gxs%d" % i_) for i_ in range(12)]
        gsq, b_gsq = sb(es, "gsq", [64, 4, 128])
        grs, b_grs = sb(es, "grs", [64, 4, 128])
        gqT, b_gqT = sb(es, "gqT", [64, 4, 128], BF16)
        gkT, b_gkT = sb(es, "gkT", [64, 4, 128], BF16)
        gvT, b_gvT = sb(es, "gvT", [64, 4, 128], BF16)
        grk, b_grk = sb(es, "grk", [128, 4, 64], BF16)
        gkh, b_gkh = sb(es, "gkh", [128, 4, 64], BF16)
        gbv, b_gbv = sb(es, "gbv", [128, 4, 64], BF16)
        EA, b_EA = sb(es, "EA", [128, 4, 128], BF16)
        ET, b_ET = sb(es, "ET", [128, 4, 128], BF16)
        eGr, b_eGr = sb(es, "eGr", [64, 4, 128])
        gq = [sb(es, "gq%d" % c, [64, 4, 128], BF16) for c in range(2)]
        gw = [sb(es, "gw%d" % c, [64, 4, 128], BF16) for c in range(2)]
        Pm, b_Pm = sb(es, "Pm", [128, 4, 128], BF16)
        PmT, b_PmT = sb(es, "PmT", [128, 4, 128], BF16)
        Wt, b_Wt = sb(es, "Wt", [128, 4, 128], BF16)
        attT, b_attT = sb(es, "attT", [128, 4, 128], BF16)
        usb, b_usb = sb(es, "usb", [128, 256])
        vn, b_vn = sb(es, "vn", [128, 256], BF16)
        Sg32, b_Sg32 = sb(es, "Sg32", [64, 4, 64])
        Sgb = [sb(es, "Sgb%d" % c, [64, 4, 64], BF16) for c in range(3)]
        beta, b_beta = sb(es, "beta", [128, 4])
        gba, b_gba = sb(es, "gba", [128, 8])
        lnb, b_lnb = sb(es, "lnb", [128, 4])
        gsp, b_gsp = sb(es, "gsp", [128, 4])
        gg_, b_gg = sb(es, "gg_", [128, 4])
        Gs, b_Gs = sb(es, "Gs", [128, 4])
        nGs, b_nGs = sb(es, "nGs", [128, 4])
        Bs, b_Bs = sb(es, "Bs", [128, 4])
        eB, b_eB = sb(es, "eB", [128, 4])
        eDl, b_eDl = sb(es, "eDl", [128, 4])
        GTs, b_GTs = sb(es, "GTs", [4, 128])
        mix, b_mix = sb(es, "mix", [128, D], BF16)
        mT, b_mT = sb(es, "mT", [128, 8, 128], BF16)
        mixf, b_mixf = (xo, b_xo) if dbg else (None, None)

        mset(Sr32, 0.0, [b_Sr32]); mset(Srb, 0.0, [b_Srb])
        mset(Sh32, 0.0, [b_Sh32]); mset(Sg32, 0.0, [b_Sg32])
        for c in range(3):
            mset(Shb[c][0], 0.0, [Shb[c][1]]); mset(Sgb[c][0], 0.0, [Sgb[c][1]])
        for c in range(2):
            mset(hq[c][0], 0.0, [hq[c][1]]); mset(gq[c][0], 0.0, [gq[c][1]]); mset(gw[c][0], 0.0, [gw[c][1]])
        mset(gl, 0.0, [b_gl]); mset(gp, 0.0, [b_gp])
        mset(mix, 0.0, [b_mix])

        ckpt(5)

        def head_norm(o, b_o, sil, b_sil, w3, b_w, center, eps, dst3, who):
            t_ = HN[who]
            (hsq, b_hsq), (hy, b_hy), (s1, b_s1), (s2, b_s2) = t_["hsq"], t_["hy"], t_["s1"], t_["s2"]
            (mean, b_mean), (msq, b_msq), (var, b_var) = t_["mean"], t_["msq"], t_["var"]
            o3 = v3(o, 64)
            red(s1, o3, [b_o], [b_s1])
            act(hsq, o, AF.Square, [b_o], [b_hsq])
            red(s2, v3(hsq, 64), [b_hsq], [b_s2])
            if center:
                ts(mean, s1, 1.0 / 64, ALU.mult, [b_s1], [b_mean])
                tt(msq, mean, mean, ALU.mult, [b_mean], [b_msq])
                stt(var, s2, 1.0 / 64, msq, ALU.mult, ALU.subtract, [b_s2, b_msq], [b_var])
            else:
                ts(var, s2, 1.0 / 64, ALU.mult, [b_s2], [b_var])
            act(var, var, AF.Sqrt, [b_var], [b_var], bias=eps)
            rcp(var, var, [b_var], [b_var])
            hy3 = v3(hy, 64)
            if center:
                tt(hy3, o3, mean.unsqueeze(2).broadcast_to([128, 4, 64]), ALU.subtract, [b_o, b_mean], [b_hy])
                tt(hy3, hy3, var.unsqueeze(2).broadcast_to([128, 4, 64]), ALU.mult, [b_hy, b_var], [b_hy])
            else:
                tt(hy3, o3, var.unsqueeze(2).broadcast_to([128, 4, 64]), ALU.mult, [b_o, b_var], [b_hy])
            tt(hy3, hy3, w3, ALU.mult, [b_hy, b_w], [b_hy])
            tt(dst3, hy3, v3(sil, 64), ALU.mult, [b_hy, b_sil], [b_mix])

        for i in range(NT):
            pv, mid, end = (2 * i) % 3, (2 * i + 1) % 3, (2 * i + 2) % 3
            r0, r1 = slice(0, 64), slice(64, 128)
            src = dr["x"] if l == 0 else xs
            k.dma(xt, src[i * 128:(i + 1) * 128, :], R=([] if l == 0 else [xsb[i]]), W=[b_xt])
            k.dma(rope_t, dr["c_rope"][i * 128:(i + 1) * 128, :], W=[b_rope])
            ckpt(6)
            norm_to_hT(xt)
            ckpt(7)
            ckpt(8)

            def g_ret():
                pg, b_pg = yield from galloc()
                for kc in range(8):
                    mm(pg[:, 0:512], hT[:, kc, :], Wi[:, kc, 0:512], [b_hT, b_Wi], [b_pg], kc == 0, kc == 7)
                cp(rqk, pg, [b_pg], [b_rqk], eng="act")
                pfree(b_pg)
                pg, b_pg = yield from galloc()
                for kc in range(8):
                    mm(pg[:, 0:512], hT[:, kc, :], Wi[:, kc, 512:1024], [b_hT, b_Wi], [b_pg], kc == 0, kc == 7)
                silu(silr, pg[:, 256:512], [b_pg], [b_silr], HN["ret"]["hsq"][0], HN["ret"]["hsq"][1])
                cp(rvb, pg[:, 0:256], [b_pg], [b_rvb])
                pfree(b_pg)
                rt4 = rope_t.rearrange("p (a h d) -> p a h d", a=4, h=4)
                for (off, ci, si, dst, b_dst) in [(0, 0, 1, qrt, b_qrt), (256, 2, 3, krt, b_krt)]:
                    X = v3(rqk[:, off:off + 256], 64)
                    x1, x2 = X[:, :, 0:32], X[:, :, 32:64]
                    cs, sn = rt4[:, ci], rt4[:, si]
                    tt(t1, x1, cs, ALU.mult, [b_rqk, b_rope], [b_t1])
                    tt(t2, x2, sn, ALU.mult, [b_rqk, b_rope], [b_t2])
                    tt(dst[:, :, 0:32], t1, t2, ALU.subtract, [b_t1, b_t2], [b_dst])
                    tt(t1, x1, sn, ALU.mult, [b_rqk, b_rope], [b_t1])
                    tt(t2, x2, cs, ALU.mult, [b_rqk, b_rope], [b_t2])
                    tt(dst[:, :, 32:64], t1, t2, ALU.add, [b_t1, b_t2], [b_dst])
                pb, b_pb = yield from galloc()
                pbv = v3(pb.bitcast(BF16), 128)
                for h in range(4):
                    tr(pbv[0:64, h, :], qrt[:, h, :], identb, [b_qrt, b_identb], [b_pb])
                    tr(pbv[0:64, 4 + h, :], krt[:, h, :], identb, [b_krt, b_identb], [b_pb])
                cp(qkT, pbv[0:64], [b_pb], [b_qkT], eng="act")
                pfree(b_pb)
                yield
                ps, b_ps = yield from galloc()
                ps3 = v3(ps, 128)
                for h in range(4):
                    mm(ps3[:, h, :], qkT[:, 4 + h, :], qkT[:, h, :], [b_qkT], [b_ps])
                tt(PTr, ps3, cmask.unsqueeze(1).broadcast_to([128, 4, 128]), ALU.mult, [b_ps, b_cmask], [b_PTr])
                pfree(b_ps)
                yield
                po, b_po = yield from galloc()
                for h in range(4):
                    hs = slice(h * 64, (h + 1) * 64)
                    mm(po[:, hs], PTr[:, h, :], rvb[:, hs], [b_PTr, b_rvb], [b_po], True, False)
                    mm(po[:, hs], qkT[:, h, :], Srb[:, h, :], [b_qkT, b_Srb], [b_po], False, True)
                pd, b_pd = yield from galloc()
                for h in range(4):
                    hs = slice(h * 64, (h + 1) * 64)
                    mm(pd[0:64, hs], krt[:, h, :], rvb[:, hs], [b_krt, b_rvb], [b_pd])
                tt(Sr32, Sr32, pd[0:64, 0:256], ALU.add, [b_Sr32, b_pd], [b_Sr32])
                tt(Sr32, Sr32, gC, ALU.mult, [b_Sr32, b_gC], [b_Sr32])
                cp(Srb, v3(Sr32, 64), [b_Sr32], [b_Srb], eng="pool")
                pfree(b_pd)
                yield
                head_norm(po[:, 0:256], b_po, silr, b_silr, v3(rnw, 64), b_rnw, True, 1e-5, v3(mix[:, 0:256], 64), "ret")
                pfree(b_po)

            def g_hgrn():
                pg, b_pg = yield from galloc()
                for kc in range(8):
                    mm(pg[:, 0:512], hT[:, kc, :], Wi[:, kc, 2568:3080], [b_hT, b_Wi], [b_pg], kc == 0, kc == 7)
                cp(hqf, pg[:, 0:256], [b_pg], [b_hqf])
                act(hf, pg[:, 256:512], AF.Sigmoid, [b_pg], [b_hf])
                pfree(b_pg)
                pg, b_pg = yield from galloc()
                for kc in range(8):
                    mm(pg[:, 0:512], hT[:, kc, :], Wi[:, kc, 3080:3592], [b_hT, b_Wi], [b_pg], kc == 0, kc == 7)
                silu(silh, pg[:, 256:512], [b_pg], [b_silh], HN["hgrn"]["hsq"][0], HN["hgrn"]["hsq"][1])
                cp(hvb, pg[:, 0:256], [b_pg], [b_hvb])
                pfree(b_pg)
                tt(hf, hf, oml, ALU.mult, [b_hf, b_oml], [b_hf])
                tt(hf, hf, lbt, ALU.add, [b_hf, b_lbt], [b_hf])
                act(logf, hf, AF.Ln, [b_hf], [b_logf])
                ts(hkk, hf, -1.0, ALU.mult, [b_hf], [b_hkk], s2=1.0, op1=ALU.add)
                pG, b_pG = yield from galloc()
                mm(pG[:, 0:256], ublk, logf, [b_ublk, b_logf], [b_pG])
                mm(pG[:, 256:512], oblk, logf, [b_oblk, b_logf], [b_pG])
                pL, b_pL = yield from galloc()
                for h in range(4):
                    mm(pL[0:64, h * 2:(h + 1) * 2], logf[:, h * 64:(h + 1) * 64], cind, [b_logf, b_cind], [b_pL])
                act(eGl, pL[0:64, 0:8], AF.Exp, [b_pL], [b_eGl])
                pfree(b_pL)
                cp(Gsb, pG[:, 0:256], [b_pG], [b_Gsb], eng="act")
                act(eG, pG[:, 0:256], AF.Exp, [b_pG], [b_eG])
                tt(qh, hqf, eG, ALU.mult, [b_hqf, b_eG], [b_qh])
                act(enG, pG[:, 0:256], AF.Exp, [b_pG], [b_enG], scale=-1.0)
                tt(kht, hkk, enG, ALU.mult, [b_hkk, b_enG], [b_kht])
                tt(hD, pG[:, 256:512], Gsb, ALU.subtract, [b_pG, b_Gsb], [b_hD])
                pfree(b_pG)
                yield
                act(hD, hD, AF.Exp, [b_hD], [b_hD])
                tt(khat, hkk, hD, ALU.mult, [b_hkk, b_hD], [b_khat])
                pb, b_pb = yield from galloc()
                pbv = v3(pb.bitcast(BF16), 128)
                for h in range(4):
                    tr(pbv[0:64, h, :], qh[:, h * 64:(h + 1) * 64], identb, [b_qh, b_identb], [b_pb])
                    tr(pbv[0:64, 4 + h, :], kht[:, h * 64:(h + 1) * 64], identb, [b_kht, b_identb], [b_pb])
                cp(hq[0][0][:, :, 0:64], pbv[0:64, 0:4, 0:64], [b_pb], [hq[0][1]], eng="act")
                cp(hq[1][0][:, :, 64:128], pbv[0:64, 0:4, 64:128], [b_pb], [hq[1][1]], eng="act")
                cp(hkT, pbv[0:64, 4:8, :], [b_pb], [b_hkT])
                pfree(b_pb)
                yield
                ps, b_ps = yield from galloc()
                ps3 = v3(ps, 128)
                for h in range(4):
                    for c in range(2):
                        cs_ = slice(c * 64, (c + 1) * 64)
                        mm(ps3[:, h, cs_], hkT[:, h, :], hq[c][0][:, h, cs_], [b_hkT, hq[c][1]], [b_ps])
                tt(PTh, ps3, ublk.unsqueeze(1).broadcast_to([128, 4, 128]), ALU.mult, [b_ps, b_ublk], [b_PTh])
                pfree(b_ps)
                yield
                for c, (rr, sdst) in enumerate([(r0, mid), (r1, end)]):
                    pd, b_pd = yield from galloc()
                    for h in range(4):
                        hs = slice(h * 64, (h + 1) * 64)
                        mm(pd[0:64, hs], khat[rr, hs], hvb[rr, hs], [b_khat, b_hvb], [b_pd])
                    for h in range(4):
                        hs = slice(h * 64, (h + 1) * 64)
                        stt(Sh32[:, h, :], Sh32[:, h, :], eGl[:, h * 2 + c:h * 2 + c + 1], pd[0:64, hs],
                            ALU.mult, ALU.add, [b_Sh32, b_eGl, b_pd], [b_Sh32])
                    cp(Shb[sdst][0], Sh32, [b_Sh32], [Shb[sdst][1]], eng="pool")
                    pfree(b_pd)
                    yield
                po, b_po = yield from galloc()
                for h in range(4):
                    hs = slice(h * 64, (h + 1) * 64)
                    mm(po[:, hs], PTh[:, h, :], hvb[:, hs], [b_PTh, b_hvb], [b_po], True, False)
                    mm(po[:, hs], hq[0][0][:, h, :], Shb[pv][0][:, h, :], [hq[0][1], Shb[pv][1]], [b_po], False, False)
                    mm(po[:, hs], hq[1][0][:, h, :], Shb[mid][0][:, h, :], [hq[1][1], Shb[mid][1]], [b_po], False, True)
                head_norm(po[:, 0:256], b_po, silh, b_silh, hnw.unsqueeze(1).broadcast_to([128, 4, 64]), b_hnw,
                          False, 1e-6, v3(mix[:, 768:1024], 64), "hgrn")
                pfree(b_po)

            def g_conf():
                pg, b_pg = yield from galloc()
                for kc in range(8):
                    mm(pg[:, 0:512], hT[:, kc, :], Wi[:, kc, 1024:1536], [b_hT, b_Wi], [b_pg], kc == 0, kc == 7)
                act(ctsg, pg[:, 256:512], AF.Sigmoid, [b_pg], [b_ctsg])
                tt(glt, pg[:, 0:256], ctsg, ALU.mult, [b_pg, b_ctsg], [b_glt])
                pfree(b_pg)
                pc, b_pc = yield from galloc()
                pc3 = v3(pc.bitcast(BF16)[:, 0:256], 128)
                for ct in range(2):
                    tr(pc3[:, ct, :], glt[:, ct * 128:(ct + 1) * 128], identb, [b_glt, b_identb], [b_pc])
                cp(gl[:, :, 30:158], pc3, [b_pc], [b_gl], eng="act")
                pfree(b_pc)
                yield
                ckpt(21)
                py, b_py = yield from galloc()
                py3 = v3(py, 128)
                for ct in range(2):
                    for kk in range(31):
                        mm(py3[:, ct, :], Dc[:, kk * 2 + ct, :], gl[:, ct, kk:kk + 128], [b_Dc, b_gl], [b_py],
                           kk == 0, kk == 30)
                ckpt(22)
                for ct in range(2):
                    act(yT[:, ct, :], py3[:, ct, :], AF.Identity, [b_py, b_ccb], [b_yT], bias=ccb[:, ct:ct + 1])
                pfree(b_py)
                yield
                cp(gl[:, :, 0:30], gl[:, :, 128:158], [b_gl], [b_gl], eng="pool")
                ckpt(24)
                pt, b_pt = yield from galloc()
                for ct in range(2):
                    tr(pt[:, ct * 128:(ct + 1) * 128], yT[:, ct, :], identf, [b_yT, b_identf], [b_pt])
                ckpt(25)
                red(c1, pt[:, 0:256].unsqueeze(1), [b_pt], [b_c1])
                act(yc, pt[:, 0:256], AF.Square, [b_pt], [b_yc])
                red(c2, yc.unsqueeze(1), [b_yc], [b_c2])
                ckpt(27)
                ts(c1, c1, 1.0 / 256, ALU.mult, [b_c1], [b_c1])
                tt(c3, c1, c1, ALU.mult, [b_c1], [b_c3])
                stt(c2, c2, 1.0 / 256, c3, ALU.mult, ALU.subtract, [b_c2, b_c3], [b_c2])
                ckpt(28)
                act(c2, c2, AF.Sqrt, [b_c2], [b_c2], bias=1e-5)
                rcp(c2, c2, [b_c2], [b_c2])
                ckpt(29)
                ts(yc, pt[:, 0:256], c1, ALU.subtract, [b_pt, b_c1, b_c2], [b_yc], s2=c2, op1=ALU.mult)
                pfree(b_pt)
                ckpt(30)
                tt(yc, yc, clnw, ALU.mult, [b_yc, b_clnw], [b_yc])
                tt(yc, yc, clnb, ALU.add, [b_yc, b_clnb], [b_yc])
                silu(mix[:, 256:512], yc, [b_yc], [b_mix], ctsg, b_ctsg)

            def g_gdn():
                pg, b_pg = yield from galloc()
                for kc in range(8):
                    mm(pg[:, 0:264], hT[:, kc, :], Wi[:, kc, 2304:2568], [b_hT, b_Wi], [b_pg], kc == 0, kc == 7)
                silu(silg, pg[:, 0:256], [b_pg], [b_silg], HN["gdn"]["hsq"][0], HN["gdn"]["hsq"][1])
                cp(gba, pg[:, 256:264], [b_pg], [b_gba])
                pfree(b_pg)
                pg, b_pg = yield from galloc()
                for kc in range(8):
                    mm(pg[:, 0:512], hT[:, kc, :], Wi[:, kc, 1536:2048], [b_hT, b_Wi], [b_pg], kc == 0, kc == 7)
                cp(gpt[:, 0:512], pg, [b_pg], [b_gpt], eng="act")
                pfree(b_pg)
                pg, b_pg = yield from galloc()
                for kc in range(8):
                    mm(pg[:, 0:256], hT[:, kc, :], Wi[:, kc, 2048:2304], [b_hT, b_Wi], [b_pg], kc == 0, kc == 7)
                cp(gpt[:, 512:768], pg[:, 0:256], [b_pg], [b_gpt])
                pfree(b_pg)
                pb1, b_pb1 = yield from galloc()
                pb2, b_pb2 = yield from galloc()
                v1 = v3(pb1.bitcast(BF16), 128)
                v2 = v3(pb2.bitcast(BF16)[:, 0:512], 128)
                for s in range(12):
                    dst_ = v1[0:64, s, :] if s < 8 else v2[0:64, s - 8, :]
                    tr(dst_, gpt[:, s * 64:(s + 1) * 64], identb, [b_gpt, b_identb], [b_pb1 if s < 8 else b_pb2])
                cp(gp[:, 0:8, 3:131], v1[0:64], [b_pb1], [b_gp], eng="act")
                cp(gp[:, 8:12, 3:131], v2[0:64], [b_pb2], [b_gp])
                pfree(b_pb1, b_pb2)
                yield
                for s in range(12):
                    ts(gx[:, s, :], gp[:, s, 0:128], gcw[:, s, 0:1], ALU.mult, [b_gp, b_gcw],
                       [b_gxs[s]] + ([b_gx] if s == 0 else []))
                for kk in range(1, 4):
                    for s in range(12):
                        stt(gx[:, s, :], gp[:, s, kk:kk + 128], gcw[:, s, kk:kk + 1], gx[:, s, :],
                            ALU.mult, ALU.add, [b_gp, b_gcw, b_gxs[s]], [b_gxs[s]])
                for b in range(3):
                    silu(gx[:, b * 4:(b + 1) * 4, :], gx[:, b * 4:(b + 1) * 4, :], b_gxs[b * 4:(b + 1) * 4],
                         [b_gx] + b_gxs[b * 4:(b + 1) * 4], gsq, b_gsq)
                yield
                cp(gp[:, :, 0:3], gp[:, :, 128:131], [b_gp], [b_gp], eng="pool")
                for b in range(2):
                    xb = gx[:, b * 4:(b + 1) * 4, :]
                    tt(gsq, xb, xb, ALU.mult, [b_gx], [b_gsq])
                    pn, b_pn = yield from galloc()
                    pn3 = v3(pn, 128)
                    for h in range(4):
                        mm(pn3[0:64, h, :], ones64, gsq[:, h, :], [b_ones64, b_gsq], [b_pn])
                    act(grs, pn3[0:64], AF.Sqrt, [b_pn], [b_grs], bias=1e-6)
                    pfree(b_pn)
                    rcp(grs, grs, [b_grs], [b_grs])
                    if b == 0:
                        stt(gqT, xb, 0.125, grs, ALU.mult, ALU.mult, [b_gx, b_grs], [b_gqT])
                    else:
                        tt(gkT, xb, grs, ALU.mult, [b_gx, b_grs], [b_gkT])
                cp(gvT, gx[:, 8:12, :], [b_gx], [b_gvT], eng="act")
                act(beta, gba[:, 0:4], AF.Sigmoid, [b_gba], [b_beta])
                act(lnb, beta, AF.Ln, [b_beta], [b_lnb])
                tt(gsp, gba[:, 4:8], dtb, ALU.add, [b_gba, b_dtb], [b_gsp])
                act(gsp, gsp, AF.Exp, [b_gsp], [b_gsp])
                act(gsp, gsp, AF.Ln, [b_gsp], [b_gsp], bias=1.0)
                tt(gg_, gsp, negA, ALU.mult, [b_gsp, b_negA], [b_gg])
                pq, b_pq = yield from galloc()
                mm(pq[:, 0:4], ublk, gg_, [b_ublk, b_gg], [b_pq])
                mm(pq[:, 4:8], oblk, gg_, [b_oblk, b_gg], [b_pq])
                mm(pq[0:4, 128:256], gg_, ublk, [b_gg, b_ublk], [b_pq])
                cp(Gs, pq[:, 0:4], [b_pq], [b_Gs])
                ts(nGs, Gs, -1.0, ALU.mult, [b_Gs], [b_nGs])
                tt(Bs, Gs, lnb, ALU.add, [b_Gs, b_lnb], [b_Bs])
                act(eB, Bs, AF.Exp, [b_Bs], [b_eB])
                tt(eDl, pq[:, 4:8], Gs, ALU.subtract, [b_pq, b_Gs], [b_eDl])
                act(eDl, eDl, AF.Exp, [b_eDl], [b_eDl])
                cp(GTs, pq[0:4, 128:256], [b_pq], [b_GTs])
                pfree(b_pq)
                yield
                pb, b_pb = yield from galloc()
                pbv = v3(pb.bitcast(BF16)[:, 0:512], 64)
                for h in range(4):
                    tr(pbv[:, h, :], gkT[:, h, :], identb[0:64, 0:64], [b_gkT, b_identb], [b_pb])
                    tr(pbv[:, 4 + h, :], gvT[:, h, :], identb[0:64, 0:64], [b_gvT, b_identb], [b_pb])
                tt(grk, pbv[:, 0:4, :], eB.unsqueeze(2).broadcast_to([128, 4, 64]), ALU.mult, [b_pb, b_eB], [b_grk])
                tt(gkh, pbv[:, 0:4, :], eDl.unsqueeze(2).broadcast_to([128, 4, 64]), ALU.mult, [b_pb, b_eDl], [b_gkh])
                tt(gbv, pbv[:, 4:8, :], beta.unsqueeze(2).broadcast_to([128, 4, 64]), ALU.mult, [b_pb, b_beta], [b_gbv])
                pfree(b_pb)
                yield
                tt(v3(Gblk, 128), GTs.unsqueeze(1).broadcast_to([4, 4, 128]), sel3, ALU.mult, [b_GTs, b_sel], [b_Gblk])
                ts(nGblk, Gblk, -1.0, ALU.mult, [b_Gblk], [b_nGblk])
                pea, b_pea = yield from galloc()
                pea3 = v3(pea, 128)
                mm(pea, ones4, nGblk, [b_ones4, b_nGblk], [b_pea], True, False)
                mm(pea, identb, negAb, [b_identb, b_negAb], [b_pea], False, True)
                for h in range(4):
                    act(EA[:, h, :], pea3[:, h, :], AF.Exp, [b_pea, b_Bs], [b_EA], bias=Bs[:, h:h + 1])
                pfree(b_pea)
                pet, b_pet = yield from galloc()
                pet3 = v3(pet, 128)
                mm(pet, ones4, Gblk, [b_ones4, b_Gblk], [b_pet], True, False)
                mm(pet, identb, negTb, [b_identb, b_negTb], [b_pet], False, True)
                for h in range(4):
                    act(ET[:, h, :], pet3[:, h, :], AF.Exp, [b_pet, b_nGs], [b_ET], bias=nGs[:, h:h + 1])
                pfree(b_pet)
                peg, b_peg = yield from galloc()
                peg3 = v3(peg, 128)
                mm(peg[0:64, :], ones4[:, 0:64], Gblk, [b_ones4, b_Gblk], [b_peg])
                act(eGr, peg3[0:64], AF.Exp, [b_peg], [b_eGr])
                pfree(b_peg)
                yield
                tt(gq[0][0][:, :, 0:64], gqT[:, :, 0:64], eGr[:, :, 0:64], ALU.mult, [b_gqT, b_eGr], [gq[0][1]])
                tt(gq[1][0][:, :, 64:128], gqT[:, :, 64:128], eGr[:, :, 64:128], ALU.mult, [b_gqT, b_eGr], [gq[1][1]])
                pkk, b_pkk = yield from galloc()
                pkq, b_pkq = yield from galloc()
                for h in range(4):
                    mm(v3(pkk, 128)[:, h, :], gkT[:, h, :], gkT[:, h, :], [b_gkT], [b_pkk])
                    mm(v3(pkq, 128)[:, h, :], gkT[:, h, :], gqT[:, h, :], [b_gkT, b_gqT], [b_pkq])
                stt(Pm, v3(pkk, 128), -1.0, EA, ALU.mult, ALU.mult, [b_pkk, b_EA], [b_Pm])
                tt(attT, v3(pkq, 128), ET, ALU.mult, [b_pkq, b_ET], [b_attT])
                pfree(b_pkk, b_pkq)
                yield
                pb, b_pb = yield from galloc()
                pbv = v3(pb.bitcast(BF16)[:, 0:512], 128)
                for h in range(4):
                    tr(pbv[:, h, :], Pm[:, h, :], identb, [b_Pm, b_identb], [b_pb])
                cp(PmT, pbv, [b_pb], [b_PmT], eng="act")
                tt(Wt, pbv, identb.unsqueeze(1).broadcast_to([128, 4, 128]), ALU.add, [b_pb, b_identb], [b_Wt])
                pfree(b_pb)
                yield
                for m in range(5):
                    pa1, b_pa1 = yield from galloc()
                    for h in range(4):
                        mm(v3(pa1, 128)[:, h, :], PmT[:, h, :], Pm[:, h, :], [b_PmT, b_Pm], [b_pa1])
                    if m < 4:
                        pa2, b_pa2 = yield from galloc()
                        for h in range(4):
                            mm(v3(pa2, 128)[:, h, :], Pm[:, h, :], PmT[:, h, :], [b_PmT, b_Pm], [b_pa2])
                    cp(Pm, v3(pa1, 128), [b_pa1], [b_Pm], eng="act")
                    pfree(b_pa1)
                    if m < 4:
                        cp(PmT, v3(pa2, 128), [b_pa2], [b_PmT])
                        pfree(b_pa2)
                    yield
                    pa3, b_pa3 = yield from galloc()
                    for h in range(4):
                        mm(v3(pa3, 128)[:, h, :], Pm[:, h, :], Wt[:, h, :], [b_Pm, b_Wt], [b_pa3])
                    tt(Wt, Wt, v3(pa3, 128), ALU.add, [b_Wt, b_pa3], [b_Wt])
                    pfree(b_pa3)
                    yield
                pu, b_pu = yield from galloc()
                pw, b_pw = yield from galloc()
                for h in range(4):
                    hs = slice(h * 64, (h + 1) * 64)
                    mm(pu[:, hs], Wt[:, h, :], gbv[:, h, :], [b_Wt, b_gbv], [b_pu])
                    mm(v3(pw, 128)[0:64, h, :], grk[:, h, :], Wt[:, h, :], [b_grk, b_Wt], [b_pw])
                cp(usb, pu[:, 0:256], [b_pu], [b_usb])
                cp(gw[0][0][:, :, 0:64], v3(pw, 128)[0:64, :, 0:64], [b_pw], [gw[0][1]], eng="act")
                cp(gw[1][0][:, :, 64:128], v3(pw, 128)[0:64, :, 64:128], [b_pw], [gw[1][1]], eng="act")
                pfree(b_pu, b_pw)
                yield
                for c, (rr, ssrc, sdst) in enumerate([(r0, pv, mid), (r1, mid, end)]):
                    pa, b_pa = yield from galloc()
                    for h in range(4):
                        hs = slice(h * 64, (h + 1) * 64)
                        mm(pa[:, hs], gw[c][0][:, h, :], Sgb[ssrc][0][:, h, :], [gw[c][1], Sgb[ssrc][1]], [b_pa])
                    tt(vn[rr, :], usb[rr, :], pa[rr, 0:256], ALU.subtract, [b_usb, b_pa], [b_vn])
                    pfree(b_pa)
                    pd, b_pd = yield from galloc()
                    for h in range(4):
                        hs = slice(h * 64, (h + 1) * 64)
                        mm(pd[0:64, hs], gkh[rr, h, :], vn[rr, hs], [b_gkh, b_vn], [b_pd])
                    col = 63 + 64 * c
                    for h in range(4):
                        hs = slice(h * 64, (h + 1) * 64)
                        stt(Sg32[:, h, :], Sg32[:, h, :], eGr[:, h, col:col + 1], pd[0:64, hs],
                            ALU.mult, ALU.add, [b_Sg32, b_eGr, b_pd], [b_Sg32])
                    cp(Sgb[sdst][0], Sg32, [b_Sg32], [Sgb[sdst][1]], eng="pool")
                    pfree(b_pd)
                    yield
                po, b_po = yield from galloc()
                for h in range(4):
                    hs = slice(h * 64, (h + 1) * 64)
                    mm(po[:, hs], attT[:, h, :], vn[:, hs], [b_attT, b_vn], [b_po], True, False)
                    mm(po[:, hs], gq[0][0][:, h, :], Sgb[pv][0][:, h, :], [gq[0][1], Sgb[pv][1]], [b_po], False, False)
                    mm(po[:, hs], gq[1][0][:, h, :], Sgb[mid][0][:, h, :], [gq[1][1], Sgb[mid][1]], [b_po], False, True)
                head_norm(po[:, 0:256], b_po, silg, b_silg, gnw.unsqueeze(1).broadcast_to([128, 4, 64]), b_gnw,
                          False, 1e-6, v3(mix[:, 512:768], 64), "gdn")
                pfree(b_po)

            gl_ = [("gdn", g_gdn, [0, 1, 2]), ("hgrn", g_hgrn, [3, 4]), ("conf", g_conf, [5]), ("ret", g_ret, [6, 7])]
            assert len(freeq) == 8
            run_interleaved([(g, bs) for n_, g, bs in gl_ if n_ not in skip])
            if True:
                if dbg:
                    cp(mixf, mix, [b_mix], [b_mixf], eng="pool")
                    k.dma(dmix[l, i * 128:(i + 1) * 128, :], mixf, R=[b_mixf])
                pb, b_pb = psum()
                pbv = v3(pb.bitcast(BF16), 128)
                for kc in range(8):
                    tr(pbv[:, kc, :], mix[:, kc * 128:(kc + 1) * 128], identb, [b_mix, b_identb], [b_pb])
                cp(mT, pbv, [b_pb], [b_mT], eng="act")
                pfree(b_pb)
                for n in range(2):
                    pso, b_pso = psum()
                    ns = slice(n * 512, (n + 1) * 512)
                    for kc in range(8):
                        mm(pso, mT[:, kc, :], Wo[:, kc, ns], [b_mT, b_Wo], [b_pso], kc == 0, kc == 7)
                    tt(xo[:, ns], xt[:, ns], pso, ALU.add, [b_xt, b_pso], [b_xo])
                    pfree(b_pso)
                k.dma(xs[i * 128:(i + 1) * 128, :], xo, R=[b_xo], W=[xsb[i]])
        k.barrier()
        es.close()

    def pass_b(l, last):
        es = ExitStack()
        Wg, b_Wg = sb(es, "Wg", [128, 8, DFF], BF16)
        Wu, b_Wu = sb(es, "Wu", [128, 8, DFF], BF16)
        Wd, b_Wd = sb(es, "Wd", [128, NFT, D], BF16)
        aT, b_aT = sb(es, "aT", [128, NFT, 128], BF16)
        fsg, b_fsg = sb(es, "fsg", [128, 512])
        fsc, b_fsc = sb(es, "fsc", [128, 512])
        fnw, b_fnw = sb(es, "fnw", [128, D])
        with nc.allow_non_contiguous_dma(reason="tiny param loads"):
            k.dma(nw, dr["norm_ffn_w"][l].rearrange("(c p) -> p c", p=128), W=[b_nw])
            k.dma(fnw, dr["final_norm_w"].partition_broadcast(128), W=[b_fnw])
        stg_b = stg + [sb(es, "stgb%d" % i_, [128, 1024]) for i_ in range(2)]
        load_w(Wg, b_Wg, dr["ffn_w_gate"][l], 8, DFF, nw, stgs=stg_b)
        load_w(Wu, b_Wu, dr["ffn_w_up"][l], 8, DFF, nw, stgs=stg_b)
        load_w(Wd, b_Wd, dr["ffn_w_down"][l], NFT, D, None, stgs=stg_b)
        for i in range(NT):
            k.dma(xt, xs[i * 128:(i + 1) * 128, :], R=[xsb[i]], W=[b_xt])
            norm_to_hT(xt)
            for grp in range(6):
                nf = 4 if grp < 5 else 2
                psg, b_psg = psum()
                psu, b_psu = psum()
                for j in range(nf):
                    ft = grp * 4 + j
                    fs = slice(ft * 128, (ft + 1) * 128)
                    for kc in range(8):
                        mm(psg[:, j * 128:(j + 1) * 128], Wg[:, kc, fs], hT[:, kc, :], [b_Wg, b_hT], [b_psg], kc == 0, kc == 7)
                    for kc in range(8):
                        mm(psu[:, j * 128:(j + 1) * 128], Wu[:, kc, fs], hT[:, kc, :], [b_Wu, b_hT], [b_psu], kc == 0, kc == 7)
                silu(fsg[:, :nf * 128], psg[:, :nf * 128], [b_psg], [b_fsg], fsc[:, :nf * 128], b_fsc)
                tt(aT[:, grp * 4:grp * 4 + nf, :], v3(psu[:, :nf * 128], 128), v3(fsg[:, :nf * 128], 128), ALU.mult,
                   [b_psu, b_fsg], [b_aT])
                pfree(b_psg, b_psu)
            for n in range(2):
                pso, b_pso = psum()
                ns = slice(n * 512, (n + 1) * 512)
                for ft in range(NFT):
                    mm(pso, aT[:, ft, :], Wd[:, ft, ns], [b_aT, b_Wd], [b_pso], ft == 0, ft == NFT - 1)
                tt(xo[:, ns], xt[:, ns], pso, ALU.add, [b_xt, b_pso], [b_xo])
                pfree(b_pso)
            if last:
                act(xn, xo, AF.Square, [b_xo], [b_xn, b_ss], accum=ss)
                act(ss, ss, AF.Sqrt, [b_ss], [b_ss], bias=1e-6, scale=1.0 / D)
                rcp(ss, ss, [b_ss], [b_ss])
                ts(xo, xo, ss, ALU.mult, [b_xo, b_ss], [b_xo])
                tt(xo, xo, fnw, ALU.mult, [b_xo, b_fnw], [b_xo])
                k.dma(out[i * 128:(i + 1) * 128, :], xo, R=[b_xo])
            else:
                k.dma(xs[i * 128:(i + 1) * 128, :], xo, R=[b_xo], W=[xsb[i]])
        k.barrier()
        es.close()

    for l in range(NL):
        try:
            pass_a(l)
        except StopBuild:
            k.barrier()
            return nc, k
        if "b" in skip:
            for i in range(NT):
                k.dma(xt, xs[i * 128:(i + 1) * 128, :], R=[xsb[i]], W=[b_xt])
                k.dma(out[i * 128:(i + 1) * 128, :], xt, R=[b_xt])
        else:
            pass_b(l, l == NL - 1)
    k.barrier()
    es_top.close()
    return nc, k


_CACHE = {}


def kernel(**inputs):
    x = np.asarray(inputs["x"], dtype=np.float32)
    B, T, _ = x.shape
    key = T
    if key not in _CACHE:
        _CACHE[key] = (build(T)[0], host_consts(T))
    nc, consts = _CACHE[key]
    params = {n: np.ascontiguousarray(np.asarray(inputs[n], dtype=np.float32)) for n in PARAM_SHAPES}
    in_maps = []
    for c in range(8):
        m = {"x": np.ascontiguousarray(x[c % B])}
        m.update(params)
        m.update(consts)
        in_maps.append(m)
    res = run_bass_kernel_spmd(nc, in_maps, core_ids=list(range(8)))
    outs = [np.asarray(res.results[c]["out"], dtype=np.float32) for c in range(B)]
    return np.stack(outs, axis=0)
```

```python
from contextlib import ExitStack
import numpy as np
import ml_dtypes
import concourse.bass as bass
import concourse.mybir as mybir
from concourse.bass_utils import run_bass_kernel_spmd

F32 = mybir.dt.float32
BF16 = mybir.dt.bfloat16
AF = mybir.ActivationFunctionType
ALU = mybir.AluOpType
AX = mybir.AxisListType

D = 1024
DIN = 3592
DFF = 2816
NFT = DFF // 128
NEG = -30000.0


class Buf:
    __slots__ = ("name", "w", "r", "excl")

    def __init__(self, name="", excl=False):
        self.name = name
        self.w = None
        self.r = {}
        self.excl = excl


class K:
    NDSEM = 40

    def __init__(self, nc):
        self.nc = nc
        self.eng = {"pe": nc.tensor, "act": nc.scalar, "dve": nc.vector,
                    "pool": nc.gpsimd, "sp": nc.sync}
        self.sem = {e: nc.alloc_semaphore(name="s_" + e) for e in ("pe", "act", "dve", "pool")}
        self.cnt = {e: 0 for e in self.sem}
        self.dsem = [nc.alloc_semaphore(name="d%d" % i) for i in range(self.NDSEM)]
        self.dcnt = [0] * self.NDSEM
        self.dnext = 0
        self.waited = {e: {} for e in self.eng}
        self.nwaits = 0
        self.ninst = 0
        self.rec = None

    def replay(self, item):
        if item[0] == "op":
            self.op(*item[1:])
        else:
            self.dma(*item[1:3], **item[3])

    def _wait(self, eng, ev):
        if ev is None:
            return
        if ev[0] == "e":
            key, val = ("e", ev[1]), ev[2]
            if ev[1] == "pe" and eng == "pe":
                return
            sem = self.sem[ev[1]]
        else:
            key, val = ("d", ev[1]), ev[2]
            sem = self.dsem[ev[1]]
        if self.waited[eng].get(key, 0) >= val:
            return
        self.eng[eng].wait_ge(sem, val)
        self.waited[eng][key] = val
        self.nwaits += 1

    def _deps(self, eng, R, W):
        for b in R:
            self._wait(eng, b.w)
            if b.excl:
                for ev in b.r.values():
                    if not (ev[0] == "e" and ev[1] == eng):
                        self._wait(eng, ev)
        for b in W:
            self._wait(eng, b.w)
            for ev in b.r.values():
                self._wait(eng, ev)

    def _post(self, ev, R, W):
        for b in R:
            b.r[(ev[0], ev[1])] = ev
        for b in W:
            b.w = ev
            b.r = {}

    def op(self, eng, fn, R=(), W=()):
        if self.rec is not None:
            self.rec.append(("op", eng, fn, tuple(R), tuple(W)))
            return None
        self._deps(eng, R, W)
        inst = fn(self.eng[eng])
        self.cnt[eng] += 1
        inst.then_inc(self.sem[eng], 1)
        self._post(("e", eng, self.cnt[eng]), R, W)
        self.ninst += 1
        return inst

    def dma(self, out, in_, R=(), W=(), q="sp", **kw):
        if self.rec is not None:
            self.rec.append(("dma", out, in_, dict(R=tuple(R), W=tuple(W), q=q, **kw)))
            return None
        idx = self.dnext
        self.dnext = (self.dnext + 1) % self.NDSEM
        if self.dcnt[idx] > 0:
            self._wait(q, ("d", idx, self.dcnt[idx]))
        self._deps(q, R, W)
        inst = self.eng[q].dma_start(out=out, in_=in_, **kw)
        self.dcnt[idx] += 16
        inst.then_inc(self.dsem[idx], 16)
        ev = ("d", idx, self.dcnt[idx])
        self._post(ev, R, W)
        self.ninst += 1
        return ev

    def barrier(self):
        for e in self.eng:
            for e2 in self.sem:
                if self.cnt[e2] > 0:
                    self._wait(e, ("e", e2, self.cnt[e2])) if not (e == e2 == "pe") else None
            for i in range(self.NDSEM):
                if self.dcnt[i] > 0:
                    self._wait(e, ("d", i, self.dcnt[i]))


def host_consts(T):
    c = {}
    c["c_identf"] = np.eye(128, dtype=np.float32)
    j = np.arange(128)[:, None]
    i = np.arange(128)[None, :]
    same = (j // 64) == (i // 64)
    c["c_cmask"] = (j <= i).astype(np.float32)
    c["c_ublk"] = ((j <= i) & same).astype(np.float32)
    c["c_oblk"] = same.astype(np.float32)
    cind = np.zeros((128, 2), np.float32)
    cind[:64, 0] = 1.0
    cind[64:, 1] = 1.0
    c["c_cind"] = cind
    sel = np.zeros((4, 4, 128), np.float32)
    for h in range(4):
        sel[h, h, :] = 1.0
    c["c_sel"] = sel.reshape(4, 512)
    c["c_negA"] = np.where(same & (j > i), 0.0, NEG).astype(np.float32)
    c["c_negT"] = np.where(same & (i >= j), 0.0, NEG).astype(np.float32)
    c["c_ones64"] = np.ones((64, 64), np.float32)
    c["c_ones4"] = np.ones((4, 128), np.float32)
    c["c_negA4"] = np.tile(c["c_negA"], (1, 4))
    c["c_negT4"] = np.tile(c["c_negT"], (1, 4))
    t = np.arange(T, dtype=np.float64)
    inv = 10000.0 ** (-np.arange(32, dtype=np.float64) / 32.0)
    ang = (t[:, None].astype(np.float32) * inv[None, :].astype(np.float32)).astype(np.float64)
    cos, sin = np.cos(ang), np.sin(ang)
    gam = 1.0 - 2.0 ** (-5.0 - np.arange(4, dtype=np.float64))
    pos = (np.arange(T) % 128 + 1).astype(np.float64)
    dq = gam[None, :] ** pos[:, None]
    dk = gam[None, :] ** (-pos[:, None]) * 0.125
    rope = np.zeros((T, 4, 4, 32), np.float64)
    rope[:, 0] = cos[:, None, :] * dq[:, :, None]
    rope[:, 1] = sin[:, None, :] * dq[:, :, None]
    rope[:, 2] = cos[:, None, :] * dk[:, :, None]
    rope[:, 3] = sin[:, None, :] * dk[:, :, None]
    c["c_rope"] = rope.reshape(T, 512).astype(np.float32)
    gC = np.zeros((64, 4, 64), np.float64)
    gC[:] = (gam ** 128.0)[None, :, None]
    c["c_gC"] = gC.reshape(64, 256).astype(np.float32)
    return c


CONST_SHAPES = {"c_identf": [128, 128], "c_cmask": [128, 128], "c_ublk": [128, 128],
                "c_oblk": [128, 128], "c_cind": [128, 2], "c_sel": [4, 512],
                "c_negA": [128, 128], "c_negT": [128, 128], "c_ones64": [64, 64],
                "c_ones4": [4, 128], "c_negA4": [128, 512], "c_negT4": [128, 512],
                "c_gC": [64, 256]}

PARAM_SHAPES = {
    "norm_mix_w": [2, 1024], "w_in": [2, 1024, DIN], "ret_norm_w": [2, 256],
    "conf_conv_w": [2, 31, 256], "conf_conv_b": [2, 256], "conf_ln_w": [2, 256],
    "conf_ln_b": [2, 256], "gdn_conv_w": [2, 4, 768], "gdn_A_log": [2, 4],
    "gdn_dt_bias": [2, 4], "gdn_norm_w": [2, 64], "hgrn_lb_logits": [2, 256],
    "hgrn_norm_w": [2, 64], "w_out": [2, 1024, 1024], "norm_ffn_w": [2, 1024],
    "ffn_w_gate": [2, 1024, DFF], "ffn_w_up": [2, 1024, DFF], "ffn_w_down": [2, DFF, 1024],
    "final_norm_w": [1024],
}


class StopBuild(Exception):
    pass


def build(T, NL=2, dbg=False, skip=(), stop=99):
    nc = bass.Bass("TRN2", target_bir_lowering=False)

    def ckpt(n):
        if n >= stop:
            raise StopBuild()

    NT = T // 128
    k = K(nc)
    dr = {}
    dr["x"] = nc.dram_tensor("x", [T, D], F32, kind="ExternalInput").ap()
    for n, s in PARAM_SHAPES.items():
        dr[n] = nc.dram_tensor(n, s, F32, kind="ExternalInput").ap()
    for n, s in CONST_SHAPES.items():
        dr[n] = nc.dram_tensor(n, s, F32, kind="ExternalInput").ap()
    dr["c_rope"] = nc.dram_tensor("c_rope", [T, 512], F32, kind="ExternalInput").ap()
    out = nc.dram_tensor("out", [T, D], F32, kind="ExternalOutput").ap()
    xs = nc.dram_tensor("xs", [T, D], F32, kind="Internal").ap()
    if dbg:
        dmix = nc.dram_tensor("dmix", [NL, T, D], F32, kind="ExternalOutput").ap()
    xsb = [Buf("xs%d" % i) for i in range(NT)]

    es_top = ExitStack()

    uniq = {"n": 0}

    def sb(es, name, shape, dt=F32):
        uniq["n"] += 1
        t = es.enter_context(nc.sbuf_tensor("%s_%d" % (name, uniq["n"]), shape, dt))
        return t.ap(), Buf(name)

    def mm(out_, lhsT, rhs, R, W, start=True, stop=True):
        k.op("pe", lambda e: e.matmul(out_, lhsT=lhsT, rhs=rhs, start=start, stop=stop), R=R, W=W)

    def tr(out_, in_, ident, R, W):
        k.op("pe", lambda e: e.transpose(out=out_, in_=in_, identity=ident), R=R, W=W)

    def act(out_, in_, func, R, W, bias=None, scale=None, accum=None):
        kw = {}
        if bias is not None:
            kw["bias"] = bias
        if scale is not None:
            kw["scale"] = scale
        if accum is not None:
            kw["accum_out"] = accum
        k.op("act", lambda e: e.activation(out=out_, in_=in_, func=func, **kw), R=R, W=W)

    def silu(out_, in_, R, W, scr, b_scr):
        act(scr, in_, AF.Sigmoid, R, [b_scr])
        tt(out_, in_, scr, ALU.mult, list(R) + [b_scr], W)

    def tt(out_, a, b, op, R, W, eng="dve"):
        k.op(eng, lambda e: e.tensor_tensor(out=out_, in0=a, in1=b, op=op), R=R, W=W)

    def ts(out_, a, s1, op0, R, W, s2=None, op1=None, eng="dve"):
        if op1 is None:
            k.op(eng, lambda e: e.tensor_scalar(out=out_, in0=a, scalar1=s1, scalar2=None, op0=op0), R=R, W=W)
        else:
            k.op(eng, lambda e: e.tensor_scalar(out=out_, in0=a, scalar1=s1, scalar2=None, op0=op0), R=R, W=W)
            k.op(eng, lambda e: e.tensor_scalar(out=out_, in0=out_, scalar1=s2, scalar2=None, op0=op1), R=list(R) + list(W), W=W)

    def stt(out_, in0, scalar, in1, op0, op1, R, W):
        k.op("dve", lambda e: e.scalar_tensor_tensor(out=out_, in0=in0, scalar=scalar, in1=in1, op0=op0, op1=op1), R=R, W=W)

    def cp(out_, in_, R, W, eng="dve"):
        if eng == "act":
            k.op("act", lambda e: e.copy(out=out_, in_=in_), R=R, W=W)
        else:
            k.op(eng, lambda e: e.tensor_copy(out=out_, in_=in_), R=R, W=W)

    def red(out_, in_, R, W):
        k.op("dve", lambda e: e.tensor_reduce(out=out_, in_=in_, axis=AX.X, op=ALU.add), R=R, W=W)

    def rcp(out_, in_, R, W):
        k.op("dve", lambda e: e.reciprocal(out=out_, in_=in_), R=R, W=W)

    def mset(ap, val, W, eng="pool"):
        k.op(eng, lambda e: e.memset(ap, val), W=W)

    def v3(ap, inner):
        return ap.rearrange("p (a b) -> p a b", b=inner)

    banks = []
    for i in range(8):
        t = es_top.enter_context(nc.psum_tensor("bank%d" % i, [128, 512], F32))
        banks.append((t.ap(), Buf("bank%d" % i, excl=True)))
    freeq = list(range(8))
    bank_idx = {id(b[1]): i for i, b in enumerate(banks)}
    cur = {"free": freeq}

    def psum():
        assert cur["free"], "no free PSUM bank"
        return banks[cur["free"].pop(0)]

    def galloc():
        if False:
            yield
        return psum()

    def pfree(*bufs):
        for b in bufs:
            i = bank_idx[id(b)]
            assert i not in cur["free"]
            cur["free"].append(i)

    def run_interleaved(gens):
        lists = []
        for g, bset in gens:
            cur["free"] = list(bset)
            k.rec = []
            for _ in g():
                pass
            lists.append(k.rec)
            k.rec = None
            assert sorted(cur["free"]) == sorted(bset), "mixer leaked PSUM banks"
        cur["free"] = freeq
        pos = [0] * len(lists)
        while True:
            best, bf = -1, 2.0
            for j, lst in enumerate(lists):
                if pos[j] < len(lst):
                    f = pos[j] / len(lst)
                    if f < bf:
                        best, bf = j, f
            if best < 0:
                break
            k.replay(lists[best][pos[best]])
            pos[best] += 1

    stg = [sb(es_top, "stg%d" % i, [128, 1024]) for i in range(2)]
    stg_c = stg[0]
    identf, b_identf = sb(es_top, "identf", [128, 128])
    identb, b_identb = sb(es_top, "identb", [128, 128], BF16)
    cmask, b_cmask = sb(es_top, "cmask", [128, 128])
    ublk, b_ublk = sb(es_top, "ublk", [128, 128])
    oblk, b_oblk = sb(es_top, "oblk", [128, 128])
    cind, b_cind = sb(es_top, "cind", [128, 2])
    sel, b_sel = sb(es_top, "sel", [4, 512])
    negAb, b_negAb = sb(es_top, "negAb", [128, 512], BF16)
    negTb, b_negTb = sb(es_top, "negTb", [128, 512], BF16)
    ones4, b_ones4 = sb(es_top, "ones4", [4, 128])
    ones64, b_ones64 = sb(es_top, "ones64", [64, 64])
    gC, b_gC = sb(es_top, "gC", [64, 256])
    CONSTS = []
    k.dma(identf, dr["c_identf"], W=[b_identf])
    k.dma(cmask, dr["c_cmask"], W=[b_cmask])
    k.dma(ublk, dr["c_ublk"], W=[b_ublk])
    k.dma(oblk, dr["c_oblk"], W=[b_oblk])
    k.dma(cind, dr["c_cind"], W=[b_cind])
    k.dma(sel, dr["c_sel"], W=[b_sel])
    k.dma(ones4, dr["c_ones4"], W=[b_ones4])
    k.dma(ones64, dr["c_ones64"], W=[b_ones64])
    k.dma(gC, dr["c_gC"], W=[b_gC])
    cp(identb, identf, [b_identf], [b_identb])
    for (dst_, b_dst_, nm_) in [(negAb, b_negAb, "c_negA4"), (negTb, b_negTb, "c_negT4")]:
        s_ap, s_b = stg_c
        k.dma(s_ap[:, 0:512], dr[nm_], W=[s_b])
        cp(dst_, s_ap[:, 0:512], [s_b], [b_dst_])
    sel3 = v3(sel, 128)
    try:
        ckpt(1)
    except StopBuild:
        k.barrier()
        return nc, k

    xt, b_xt = sb(es_top, "xt", [128, D])
    xo, b_xo = sb(es_top, "xo", [128, D])
    xn, b_xn = sb(es_top, "xn", [128, D], BF16)
    hT, b_hT = sb(es_top, "hT", [128, 8, 128], BF16)
    b_xt0, hT0, b_hT0 = b_xt, hT, b_hT
    ss, b_ss = sb(es_top, "ss", [128, 1])
    nw, b_nw = sb(es_top, "nw", [128, 8])
    stg_i = {"i": 0}

    def load_w(dst, b_dst, src, KC, N, scale, stgs=None):
        stgs = stgs or stg
        engs = ("dve", "pool") if len(stgs) == 2 else ("dve", "pool", "act", "dve")
        for kc in range(KC):
            for n0 in range(0, N, 1024):
                n1 = min(N, n0 + 1024)
                s_ap, s_b = stgs[stg_i["i"] % len(stgs)]
                eng = engs[stg_i["i"] % len(stgs)]
                stg_i["i"] += 1
                k.dma(s_ap[:, :n1 - n0], src[kc * 128:(kc + 1) * 128, n0:n1], W=[s_b])
                if eng == "act":
                    if scale is not None:
                        act(dst[:, kc, n0:n1], s_ap[:, :n1 - n0], AF.Identity, [s_b, b_nw], [b_dst],
                            scale=scale[:, kc:kc + 1])
                    else:
                        cp(dst[:, kc, n0:n1], s_ap[:, :n1 - n0], [s_b], [b_dst], eng="act")
                elif scale is not None:
                    ts(dst[:, kc, n0:n1], s_ap[:, :n1 - n0], scale[:, kc:kc + 1], ALU.mult,
                       [s_b, b_nw], [b_dst], eng=eng)
                else:
                    cp(dst[:, kc, n0:n1], s_ap[:, :n1 - n0], [s_b], [b_dst], eng=eng)

    def norm_to_hT(src_xt, b_xt=None, hT=None, b_hT=None):
        b_xt = b_xt or b_xt0
        hT = hT if hT is not None else hT0
        b_hT = b_hT or b_hT0
        act(xn, src_xt, AF.Square, [b_xt], [b_xn, b_ss], accum=ss)
        act(ss, ss, AF.Sqrt, [b_ss], [b_ss], bias=1e-6, scale=1.0 / D)
        rcp(ss, ss, [b_ss], [b_ss])
        ts(xn, src_xt, ss, ALU.mult, [b_xt, b_ss], [b_xn])
        pb, b_pb = psum()
        pbv = v3(pb.bitcast(BF16), 128)
        for kc in range(8):
            tr(pbv[:, kc, :], xn[:, kc * 128:(kc + 1) * 128], identb, [b_xn, b_identb], [b_pb])
        cp(hT, pbv, [b_pb], [b_hT], eng="act")
        pfree(b_pb)

    def pass_a(l):
        es = ExitStack()
        Wi, b_Wi = sb(es, "Wi", [128, 8, DIN], BF16)
        Wo, b_Wo = sb(es, "Wo", [128, 8, D], BF16)
        Dc, b_Dc = sb(es, "Dc", [128, 62, 128], BF16)
        ccw, b_ccw = sb(es, "ccw", [128, 2, 31])
        ccb, b_ccb = sb(es, "ccb", [128, 2])
        gcw, b_gcw = sb(es, "gcw", [64, 12, 4])
        rnw, b_rnw = sb(es, "rnw", [128, 256])
        clnw, b_clnw = sb(es, "clnw", [128, 256])
        clnb, b_clnb = sb(es, "clnb", [128, 256])
        gnw, b_gnw = sb(es, "gnw", [128, 64])
        hnw, b_hnw = sb(es, "hnw", [128, 64])
        negA, b_negA = sb(es, "negA", [128, 4])
        dtb, b_dtb = sb(es, "dtb", [128, 4])
        lbl, b_lbl = sb(es, "lbl", [128, 2, 256])
        lbt, b_lbt = sb(es, "lbt", [128, 256])
        oml, b_oml = sb(es, "oml", [128, 256])
        with nc.allow_non_contiguous_dma(reason="tiny param loads"):
            k.dma(nw, dr["norm_mix_w"][l].rearrange("(c p) -> p c", p=128), W=[b_nw])
            for ct in range(2):
                k.dma(ccw[:, ct, :], dr["conf_conv_w"][l][:, ct * 128:(ct + 1) * 128].rearrange("k c -> c k"), W=[b_ccw])
            k.dma(ccb, dr["conf_conv_b"][l].rearrange("(ct c) -> c ct", c=128), W=[b_ccb])
            for s_ in range(12):
                k.dma(gcw[:, s_, :], dr["gdn_conv_w"][l][:, s_ * 64:(s_ + 1) * 64].rearrange("k d -> d k"), W=[b_gcw])
            k.dma(rnw, dr["ret_norm_w"][l].partition_broadcast(128), W=[b_rnw])
            k.dma(clnw, dr["conf_ln_w"][l].partition_broadcast(128), W=[b_clnw])
            k.dma(clnb, dr["conf_ln_b"][l].partition_broadcast(128), W=[b_clnb])
            k.dma(gnw, dr["gdn_norm_w"][l].partition_broadcast(128), W=[b_gnw])
            k.dma(hnw, dr["hgrn_norm_w"][l].partition_broadcast(128), W=[b_hnw])
            k.dma(negA, dr["gdn_A_log"][l].partition_broadcast(128), W=[b_negA])
            k.dma(dtb, dr["gdn_dt_bias"][l].partition_broadcast(128), W=[b_dtb])
            k.dma(lbl[:, 0, :], dr["hgrn_lb_logits"][0].partition_broadcast(128), W=[b_lbl])
            k.dma(lbl[:, 1, :], dr["hgrn_lb_logits"][1].partition_broadcast(128), W=[b_lbl])
        ckpt(2)
        act(negA, negA, AF.Exp, [b_negA], [b_negA])
        ts(negA, negA, -1.0, ALU.mult, [b_negA], [b_negA])
        if l == 0:
            mset(lbt, 0.0, [b_lbt])
            mset(oml, 1.0, [b_oml])
        else:
            act(lbl, lbl, AF.Exp, [b_lbl], [b_lbl])
            tt(oml, lbl[:, 0, :], lbl[:, 1, :], ALU.add, [b_lbl], [b_oml])
            rcp(oml, oml, [b_oml], [b_oml])
            tt(lbt, lbl[:, 1, :], oml, ALU.mult, [b_lbl, b_oml], [b_lbt])
            ts(oml, lbt, -1.0, ALU.mult, [b_lbt], [b_oml], s2=1.0, op1=ALU.add)
        for kk in range(31):
            for ct in range(2):
                ts(Dc[:, kk * 2 + ct, :], identf, ccw[:, ct, kk:kk + 1], ALU.mult, [b_identf, b_ccw], [b_Dc])
        ckpt(3)
        load_w(Wi, b_Wi, dr["w_in"][l], 8, DIN, nw)
        load_w(Wo, b_Wo, dr["w_out"][l], 8, D, None)

        ckpt(4)
        rope_t, b_rope = sb(es, "rope_t", [128, 512])
        rqk, b_rqk = sb(es, "rqk", [128, 512])
        ctsg, b_ctsg = sb(es, "ctsg", [128, 256])
        glt, b_glt = sb(es, "glt", [128, 256], BF16)
        gpt, b_gpt = sb(es, "gpt", [128, 768], BF16)
        Gblk, b_Gblk = sb(es, "Gblk", [4, 512])
        nGblk, b_nGblk = sb(es, "nGblk", [4, 512])
        hqf, b_hqf = sb(es, "hqf", [128, 256])
        t1, b_t1 = sb(es, "t1", [128, 4, 32])
        t2, b_t2 = sb(es, "t2", [128, 4, 32])
        qrt, b_qrt = sb(es, "qrt", [128, 4, 64], BF16)
        krt, b_krt = sb(es, "krt", [128, 4, 64], BF16)
        qkT, b_qkT = sb(es, "qkT", [64, 8, 128], BF16)
        PTr, b_PTr = sb(es, "PTr", [128, 4, 128], BF16)
        rvb, b_rvb = sb(es, "rvb", [128, 256], BF16)
        Sr32, b_Sr32 = sb(es, "Sr32", [64, 256])
        Srb, b_Srb = sb(es, "Srb", [64, 4, 64], BF16)
        silr, b_silr = sb(es, "silr", [128, 256])
        silg, b_silg = sb(es, "silg", [128, 256])
        silh, b_silh = sb(es, "silh", [128, 256])
        HN = {}
        for nm_ in ("ret", "hgrn", "gdn"):
            HN[nm_] = dict(hsq=sb(es, "hsq", [128, 256]), hy=sb(es, "hy", [128, 256]), s1=sb(es, "s1", [128, 4]),
                           s2=sb(es, "s2", [128, 4]), mean=sb(es, "mean", [128, 4]), msq=sb(es, "msq", [128, 4]),
                           var=sb(es, "var", [128, 4]))
        hsq, b_hsq = HN["ret"]["hsq"]
        hf, b_hf = sb(es, "hf", [128, 256])
        logf, b_logf = sb(es, "logf", [128, 256])
        hkk, b_hkk = sb(es, "hkk", [128, 256])
        Gsb, b_Gsb = sb(es, "Gsb", [128, 256])
        eG, b_eG = sb(es, "eG", [128, 256])
        enG, b_enG = eG, b_eG
        hD, b_hD = Gsb, b_Gsb
        qh, b_qh = sb(es, "qh", [128, 256], BF16)
        kht, b_kht = sb(es, "kht", [128, 256], BF16)
        khat, b_khat = sb(es, "khat", [128, 256], BF16)
        hvb, b_hvb = sb(es, "hvb", [128, 256], BF16)
        hq = [sb(es, "hq%d" % c, [64, 4, 128], BF16) for c in range(2)]
        hkT, b_hkT = sb(es, "hkT", [64, 4, 128], BF16)
        PTh, b_PTh = sb(es, "PTh", [128, 4, 128], BF16)
        Sh32, b_Sh32 = sb(es, "Sh32", [64, 4, 64])
        Shb = [sb(es, "Shb%d" % c, [64, 4, 64], BF16) for c in range(3)]
        eGl, b_eGl = sb(es, "eGl", [64, 8])
        gl, b_gl = sb(es, "gl", [128, 2, 158], BF16)
        yT, b_yT = sb(es, "yT", [128, 2, 128])
        yc, b_yc = sb(es, "yc", [128, 256])
        c1, b_c1 = sb(es, "c1", [128, 1])
        c2, b_c2 = sb(es, "c2", [128, 1])
        c3, b_c3 = sb(es, "c3", [128, 1])
        gp, b_gp = sb(es, "gp", [64, 12, 131], BF16)
        gx, b_gx = sb(es, "gx", [64, 12, 128])
        b_gxs = [Buf("gxs%d" % i_) for i_ in range(12)]
        gsq, b_gsq = sb(es, "gsq", [64, 4, 128])
        grs, b_grs = sb(es, "grs", [64, 4, 128])
        gqT, b_gqT = sb(es, "gqT", [64, 4, 128], BF16)
        gkT, b_gkT = sb(es, "gkT", [64, 4, 128], BF16)
        gvT, b_gvT = sb(es, "gvT", [64, 4, 128], BF16)
        grk, b_grk = sb(es, "grk", [128, 4, 64], BF16)
        gkh, b_gkh = sb(es, "gkh", [128, 4, 64], BF16)
        gbv, b_gbv = sb(es, "gbv", [128, 4, 64], BF16)
        EA, b_EA = sb(es, "EA", [128, 4, 128], BF16)
        ET, b_ET = sb(es, "ET", [128, 4, 128], BF16)
        eGr, b_eGr = sb(es, "eGr", [64, 4, 128])
        gq = [sb(es, "gq%d" % c, [64, 4, 128], BF16) for c in range(2)]
        gw = [sb(es, "gw%d" % c, [64, 4, 128], BF16) for c in range(2)]
        Pm, b_Pm = sb(es, "Pm", [128, 4, 128], BF16)
        PmT, b_PmT = sb(es, "PmT", [128, 4, 128], BF16)
        Wt, b_Wt = sb(es, "Wt", [128, 4, 128], BF16)
        attT, b_attT = sb(es, "attT", [128, 4, 128], BF16)
        usb, b_usb = sb(es, "usb", [128, 256])
        vn, b_vn = sb(es, "vn", [128, 256], BF16)
        Sg32, b_Sg32 = sb(es, "Sg32", [64, 4, 64])
        Sgb = [sb(es, "Sgb%d" % c, [64, 4, 64], BF16) for c in range(3)]
        beta, b_beta = sb(es, "beta", [128, 4])
        gba, b_gba = sb(es, "gba", [128, 8])
        lnb, b_lnb = sb(es, "lnb", [128, 4])
        gsp, b_gsp = sb(es, "gsp", [128, 4])
        gg_, b_gg = sb(es, "gg_", [128, 4])
        Gs, b_Gs = sb(es, "Gs", [128, 4])
        nGs, b_nGs = sb(es, "nGs", [128, 4])
        Bs, b_Bs = sb(es, "Bs", [128, 4])
        eB, b_eB = sb(es, "eB", [128, 4])
        eDl, b_eDl = sb(es, "eDl", [128, 4])
        GTs, b_GTs = sb(es, "GTs", [4, 128])
        mix, b_mix = sb(es, "mix", [128, D], BF16)
        mT, b_mT = sb(es, "mT", [128, 8, 128], BF16)
        mixf, b_mixf = (xo, b_xo) if dbg else (None, None)

        mset(Sr32, 0.0, [b_Sr32]); mset(Srb, 0.0, [b_Srb])
        mset(Sh32, 0.0, [b_Sh32]); mset(Sg32, 0.0, [b_Sg32])
        for c in range(3):
            mset(Shb[c][0], 0.0, [Shb[c][1]]); mset(Sgb[c][0], 0.0, [Sgb[c][1]])
        for c in range(2):
            mset(hq[c][0], 0.0, [hq[c][1]]); mset(gq[c][0], 0.0, [gq[c][1]]); mset(gw[c][0], 0.0, [gw[c][1]])
        mset(gl, 0.0, [b_gl]); mset(gp, 0.0, [b_gp])
        mset(mix, 0.0, [b_mix])

        ckpt(5)

        k.barrier()
        XT = [(xt, b_xt), (stg[0][0], Buf("xt1"))]
        HT = [(hT, b_hT), (v3(stg[1][0][:, 0:512].bitcast(BF16), 128), Buf("hT1"))]
        ROPE = [(rope_t, b_rope), (stg[1][0][:, 512:1024], Buf("rope1"))]
        src_x = dr["x"] if l == 0 else xs

        def front(j):
            (xt_n, b_xt_n), (hT_n, b_hT_n), (rope_n, b_rope_n) = XT[j % 2], HT[j % 2], ROPE[j % 2]
            k.dma(xt_n, src_x[j * 128:(j + 1) * 128, :], R=([] if l == 0 else [xsb[j]]), W=[b_xt_n])
            k.dma(rope_n, dr["c_rope"][j * 128:(j + 1) * 128, :], W=[b_rope_n])
            norm_to_hT(xt_n, b_xt_n, hT_n, b_hT_n)

        def tail_b(j):
            xt_p, b_xt_p = XT[j % 2]
            for n in range(2):
                pso, b_pso = psum()
                ns = slice(n * 512, (n + 1) * 512)
                for kc in range(8):
                    mm(pso, mT[:, kc, :], Wo[:, kc, ns], [b_mT, b_Wo], [b_pso], kc == 0, kc == 7)
                tt(xo[:, ns], xt_p[:, ns], pso, ALU.add, [b_xt_p, b_pso], [b_xo])
                pfree(b_pso)
            k.dma(xs[j * 128:(j + 1) * 128, :], xo, R=[b_xo], W=[xsb[j]])

        def head_norm(o, b_o, sil, b_sil, w3, b_w, center, eps, dst3, who):
            t_ = HN[who]
            (hsq, b_hsq), (hy, b_hy), (s1, b_s1), (s2, b_s2) = t_["hsq"], t_["hy"], t_["s1"], t_["s2"]
            (mean, b_mean), (msq, b_msq), (var, b_var) = t_["mean"], t_["msq"], t_["var"]
            o3 = v3(o, 64)
            red(s1, o3, [b_o], [b_s1])
            act(hsq, o, AF.Square, [b_o], [b_hsq])
            red(s2, v3(hsq, 64), [b_hsq], [b_s2])
            if center:
                ts(mean, s1, 1.0 / 64, ALU.mult, [b_s1], [b_mean])
                tt(msq, mean, mean, ALU.mult, [b_mean], [b_msq])
                stt(var, s2, 1.0 / 64, msq, ALU.mult, ALU.subtract, [b_s2, b_msq], [b_var])
            else:
                ts(var, s2, 1.0 / 64, ALU.mult, [b_s2], [b_var])
            act(var, var, AF.Sqrt, [b_var], [b_var], bias=eps)
            rcp(var, var, [b_var], [b_var])
            hy3 = v3(hy, 64)
            if center:
                tt(hy3, o3, mean.unsqueeze(2).broadcast_to([128, 4, 64]), ALU.subtract, [b_o, b_mean], [b_hy])
                tt(hy3, hy3, var.unsqueeze(2).broadcast_to([128, 4, 64]), ALU.mult, [b_hy, b_var], [b_hy])
            else:
                tt(hy3, o3, var.unsqueeze(2).broadcast_to([128, 4, 64]), ALU.mult, [b_o, b_var], [b_hy])
            tt(hy3, hy3, w3, ALU.mult, [b_hy, b_w], [b_hy])
            tt(dst3, hy3, v3(sil, 64), ALU.mult, [b_hy, b_sil], [b_mix])

        front(0)
        for i in range(NT):
            pv, mid, end = (2 * i) % 3, (2 * i + 1) % 3, (2 * i + 2) % 3
            r0, r1 = slice(0, 64), slice(64, 128)
            (xt_c, b_xt_c), (hT_c, b_hT_c), (rope_c, b_rope_c) = XT[i % 2], HT[i % 2], ROPE[i % 2]

            def g_ret():
                pg, b_pg = yield from galloc()
                for kc in range(8):
                    mm(pg[:, 0:512], hT_c[:, kc, :], Wi[:, kc, 0:512], [b_hT_c, b_Wi], [b_pg], kc == 0, kc == 7)
                cp(rqk, pg, [b_pg], [b_rqk], eng="act")
                pfree(b_pg)
                pg, b_pg = yield from galloc()
                for kc in range(8):
                    mm(pg[:, 0:512], hT_c[:, kc, :], Wi[:, kc, 512:1024], [b_hT_c, b_Wi], [b_pg], kc == 0, kc == 7)
                silu(silr, pg[:, 256:512], [b_pg], [b_silr], HN["ret"]["hsq"][0], HN["ret"]["hsq"][1])
                cp(rvb, pg[:, 0:256], [b_pg], [b_rvb])
                pfree(b_pg)
                rt4 = rope_c.rearrange("p (a h d) -> p a h d", a=4, h=4)
                for (off, ci, si, dst, b_dst) in [(0, 0, 1, qrt, b_qrt), (256, 2, 3, krt, b_krt)]:
                    X = v3(rqk[:, off:off + 256], 64)
                    x1, x2 = X[:, :, 0:32], X[:, :, 32:64]
                    cs, sn = rt4[:, ci], rt4[:, si]
                    tt(t1, x1, cs, ALU.mult, [b_rqk, b_rope_c], [b_t1])
                    tt(t2, x2, sn, ALU.mult, [b_rqk, b_rope_c], [b_t2])
                    tt(dst[:, :, 0:32], t1, t2, ALU.subtract, [b_t1, b_t2], [b_dst])
                    tt(t1, x1, sn, ALU.mult, [b_rqk, b_rope_c], [b_t1])
                    tt(t2, x2, cs, ALU.mult, [b_rqk, b_rope_c], [b_t2])
                    tt(dst[:, :, 32:64], t1, t2, ALU.add, [b_t1, b_t2], [b_dst])
                pb, b_pb = yield from galloc()
                pbv = v3(pb.bitcast(BF16), 128)
                for h in range(4):
                    tr(pbv[0:64, h, :], qrt[:, h, :], identb, [b_qrt, b_identb], [b_pb])
                    tr(pbv[0:64, 4 + h, :], krt[:, h, :], identb, [b_krt, b_identb], [b_pb])
                cp(qkT, pbv[0:64], [b_pb], [b_qkT], eng="act")
                pfree(b_pb)
                yield
                ps, b_ps = yield from galloc()
                ps3 = v3(ps, 128)
                for h in range(4):
                    mm(ps3[:, h, :], qkT[:, 4 + h, :], qkT[:, h, :], [b_qkT], [b_ps])
                tt(PTr, ps3, cmask.unsqueeze(1).broadcast_to([128, 4, 128]), ALU.mult, [b_ps, b_cmask], [b_PTr])
                pfree(b_ps)
                yield
                po, b_po = yield from galloc()
                for h in range(4):
                    hs = slice(h * 64, (h + 1) * 64)
                    mm(po[:, hs], PTr[:, h, :], rvb[:, hs], [b_PTr, b_rvb], [b_po], True, False)
                    mm(po[:, hs], qkT[:, h, :], Srb[:, h, :], [b_qkT, b_Srb], [b_po], False, True)
                head_norm(po[:, 0:256], b_po, silr, b_silr, v3(rnw, 64), b_rnw, True, 1e-5, v3(mix[:, 0:256], 64), "ret")
                pfree(b_po)
                pd, b_pd = yield from galloc()
                for h in range(4):
                    hs = slice(h * 64, (h + 1) * 64)
                    mm(pd[0:64, hs], krt[:, h, :], rvb[:, hs], [b_krt, b_rvb], [b_pd])
                tt(Sr32, Sr32, pd[0:64, 0:256], ALU.add, [b_Sr32, b_pd], [b_Sr32])
                tt(Sr32, Sr32, gC, ALU.mult, [b_Sr32, b_gC], [b_Sr32])
                cp(Srb, v3(Sr32, 64), [b_Sr32], [b_Srb], eng="pool")
                pfree(b_pd)

            def g_hgrn():
                pg, b_pg = yield from galloc()
                for kc in range(8):
                    mm(pg[:, 0:512], hT_c[:, kc, :], Wi[:, kc, 2568:3080], [b_hT_c, b_Wi], [b_pg], kc == 0, kc == 7)
                cp(hqf, pg[:, 0:256], [b_pg], [b_hqf])
                act(hf, pg[:, 256:512], AF.Sigmoid, [b_pg], [b_hf])
                pfree(b_pg)
                pg, b_pg = yield from galloc()
                for kc in range(8):
                    mm(pg[:, 0:512], hT_c[:, kc, :], Wi[:, kc, 3080:3592], [b_hT_c, b_Wi], [b_pg], kc == 0, kc == 7)
                silu(silh, pg[:, 256:512], [b_pg], [b_silh], HN["hgrn"]["hsq"][0], HN["hgrn"]["hsq"][1])
                cp(hvb, pg[:, 0:256], [b_pg], [b_hvb])
                pfree(b_pg)
                tt(hf, hf, oml, ALU.mult, [b_hf, b_oml], [b_hf])
                tt(hf, hf, lbt, ALU.add, [b_hf, b_lbt], [b_hf])
                act(logf, hf, AF.Ln, [b_hf], [b_logf])
                ts(hkk, hf, -1.0, ALU.mult, [b_hf], [b_hkk], s2=1.0, op1=ALU.add)
                pG, b_pG = yield from galloc()
                mm(pG[:, 0:256], ublk, logf, [b_ublk, b_logf], [b_pG])
                mm(pG[:, 256:512], oblk, logf, [b_oblk, b_logf], [b_pG])
                pL, b_pL = yield from galloc()
                for h in range(4):
                    mm(pL[0:64, h * 2:(h + 1) * 2], logf[:, h * 64:(h + 1) * 64], cind, [b_logf, b_cind], [b_pL])
                act(eGl, pL[0:64, 0:8], AF.Exp, [b_pL], [b_eGl])
                pfree(b_pL)
                cp(Gsb, pG[:, 0:256], [b_pG], [b_Gsb], eng="act")
                act(eG, pG[:, 0:256], AF.Exp, [b_pG], [b_eG])
                tt(qh, hqf, eG, ALU.mult, [b_hqf, b_eG], [b_qh])
                act(enG, pG[:, 0:256], AF.Exp, [b_pG], [b_enG], scale=-1.0)
                tt(kht, hkk, enG, ALU.mult, [b_hkk, b_enG], [b_kht])
                tt(hD, pG[:, 256:512], Gsb, ALU.subtract, [b_pG, b_Gsb], [b_hD])
                pfree(b_pG)
                yield
                act(hD, hD, AF.Exp, [b_hD], [b_hD])
                tt(khat, hkk, hD, ALU.mult, [b_hkk, b_hD], [b_khat])
                pb, b_pb = yield from galloc()
                pbv = v3(pb.bitcast(BF16), 128)
                for h in range(4):
                    tr(pbv[0:64, h, :], qh[:, h * 64:(h + 1) * 64], identb, [b_qh, b_identb], [b_pb])
                    tr(pbv[0:64, 4 + h, :], kht[:, h * 64:(h + 1) * 64], identb, [b_kht, b_identb], [b_pb])
                cp(hq[0][0][:, :, 0:64], pbv[0:64, 0:4, 0:64], [b_pb], [hq[0][1]], eng="act")
                cp(hq[1][0][:, :, 64:128], pbv[0:64, 0:4, 64:128], [b_pb], [hq[1][1]], eng="act")
                cp(hkT, pbv[0:64, 4:8, :], [b_pb], [b_hkT])
                pfree(b_pb)
                yield
                ps, b_ps = yield from galloc()
                ps3 = v3(ps, 128)
                for h in range(4):
                    for c in range(2):
                        cs_ = slice(c * 64, (c + 1) * 64)
                        mm(ps3[:, h, cs_], hkT[:, h, :], hq[c][0][:, h, cs_], [b_hkT, hq[c][1]], [b_ps])
                tt(PTh, ps3, ublk.unsqueeze(1).broadcast_to([128, 4, 128]), ALU.mult, [b_ps, b_ublk], [b_PTh])
                pfree(b_ps)
                yield
                for c, (rr, sdst) in enumerate([(r0, mid), (r1, end)]):
                    pd, b_pd = yield from galloc()
                    for h in range(4):
                        hs = slice(h * 64, (h + 1) * 64)
                        mm(pd[0:64, hs], khat[rr, hs], hvb[rr, hs], [b_khat, b_hvb], [b_pd])
                    for h in range(4):
                        hs = slice(h * 64, (h + 1) * 64)
                        stt(Sh32[:, h, :], Sh32[:, h, :], eGl[:, h * 2 + c:h * 2 + c + 1], pd[0:64, hs],
                            ALU.mult, ALU.add, [b_Sh32, b_eGl, b_pd], [b_Sh32])
                    cp(Shb[sdst][0], Sh32, [b_Sh32], [Shb[sdst][1]], eng="pool")
                    pfree(b_pd)
                    yield
                po, b_po = yield from galloc()
                for h in range(4):
                    hs = slice(h * 64, (h + 1) * 64)
                    mm(po[:, hs], PTh[:, h, :], hvb[:, hs], [b_PTh, b_hvb], [b_po], True, False)
                    mm(po[:, hs], hq[0][0][:, h, :], Shb[pv][0][:, h, :], [hq[0][1], Shb[pv][1]], [b_po], False, False)
                    mm(po[:, hs], hq[1][0][:, h, :], Shb[mid][0][:, h, :], [hq[1][1], Shb[mid][1]], [b_po], False, True)
                head_norm(po[:, 0:256], b_po, silh, b_silh, hnw.unsqueeze(1).broadcast_to([128, 4, 64]), b_hnw,
                          False, 1e-6, v3(mix[:, 768:1024], 64), "hgrn")
                pfree(b_po)

            def g_conf():
                pg, b_pg = yield from galloc()
                for kc in range(8):
                    mm(pg[:, 0:512], hT_c[:, kc, :], Wi[:, kc, 1024:1536], [b_hT_c, b_Wi], [b_pg], kc == 0, kc == 7)
                act(ctsg, pg[:, 256:512], AF.Sigmoid, [b_pg], [b_ctsg])
                tt(glt, pg[:, 0:256], ctsg, ALU.mult, [b_pg, b_ctsg], [b_glt])
                pfree(b_pg)
                pc, b_pc = yield from galloc()
                pc3 = v3(pc.bitcast(BF16)[:, 0:256], 128)
                for ct in range(2):
                    tr(pc3[:, ct, :], glt[:, ct * 128:(ct + 1) * 128], identb, [b_glt, b_identb], [b_pc])
                cp(gl[:, :, 30:158], pc3, [b_pc], [b_gl], eng="act")
                pfree(b_pc)
                yield
                ckpt(21)
                py, b_py = yield from galloc()
                py3 = v3(py, 128)
                for ct in range(2):
                    for kk in range(31):
                        mm(py3[:, ct, :], Dc[:, kk * 2 + ct, :], gl[:, ct, kk:kk + 128], [b_Dc, b_gl], [b_py],
                           kk == 0, kk == 30)
                ckpt(22)
                for ct in range(2):
                    act(yT[:, ct, :], py3[:, ct, :], AF.Identity, [b_py, b_ccb], [b_yT], bias=ccb[:, ct:ct + 1])
                pfree(b_py)
                yield
                cp(gl[:, :, 0:30], gl[:, :, 128:158], [b_gl], [b_gl], eng="pool")
                ckpt(24)
                pt, b_pt = yield from galloc()
                for ct in range(2):
                    tr(pt[:, ct * 128:(ct + 1) * 128], yT[:, ct, :], identf, [b_yT, b_identf], [b_pt])
                ckpt(25)
                red(c1, pt[:, 0:256].unsqueeze(1), [b_pt], [b_c1])
                act(yc, pt[:, 0:256], AF.Square, [b_pt], [b_yc])
                red(c2, yc.unsqueeze(1), [b_yc], [b_c2])
                ckpt(27)
                ts(c1, c1, 1.0 / 256, ALU.mult, [b_c1], [b_c1])
                tt(c3, c1, c1, ALU.mult, [b_c1], [b_c3])
                stt(c2, c2, 1.0 / 256, c3, ALU.mult, ALU.subtract, [b_c2, b_c3], [b_c2])
                ckpt(28)
                act(c2, c2, AF.Sqrt, [b_c2], [b_c2], bias=1e-5)
                rcp(c2, c2, [b_c2], [b_c2])
                ckpt(29)
                ts(yc, pt[:, 0:256], c1, ALU.subtract, [b_pt, b_c1, b_c2], [b_yc], s2=c2, op1=ALU.mult)
                pfree(b_pt)
                ckpt(30)
                tt(yc, yc, clnw, ALU.mult, [b_yc, b_clnw], [b_yc])
                tt(yc, yc, clnb, ALU.add, [b_yc, b_clnb], [b_yc])
                silu(mix[:, 256:512], yc, [b_yc], [b_mix], ctsg, b_ctsg)

            def g_gdn():
                pg, b_pg = yield from galloc()
                for kc in range(8):
                    mm(pg[:, 0:264], hT_c[:, kc, :], Wi[:, kc, 2304:2568], [b_hT_c, b_Wi], [b_pg], kc == 0, kc == 7)
                silu(silg, pg[:, 0:256], [b_pg], [b_silg], HN["gdn"]["hsq"][0], HN["gdn"]["hsq"][1])
                cp(gba, pg[:, 256:264], [b_pg], [b_gba])
                pfree(b_pg)
                pg, b_pg = yield from galloc()
                for kc in range(8):
                    mm(pg[:, 0:512], hT_c[:, kc, :], Wi[:, kc, 1536:2048], [b_hT_c, b_Wi], [b_pg], kc == 0, kc == 7)
                cp(gpt[:, 0:512], pg, [b_pg], [b_gpt], eng="act")
                pfree(b_pg)
                pg, b_pg = yield from galloc()
                for kc in range(8):
                    mm(pg[:, 0:256], hT_c[:, kc, :], Wi[:, kc, 2048:2304], [b_hT_c, b_Wi], [b_pg], kc == 0, kc == 7)
                cp(gpt[:, 512:768], pg[:, 0:256], [b_pg], [b_gpt])
                pfree(b_pg)
                pb1, b_pb1 = yield from galloc()
                pb2, b_pb2 = yield from galloc()
                v1 = v3(pb1.bitcast(BF16), 128)
                v2 = v3(pb2.bitcast(BF16)[:, 0:512], 128)
                for s in range(12):
                    dst_ = v1[0:64, s, :] if s < 8 else v2[0:64, s - 8, :]
                    tr(dst_, gpt[:, s * 64:(s + 1) * 64], identb, [b_gpt, b_identb], [b_pb1 if s < 8 else b_pb2])
                cp(gp[:, 0:8, 3:131], v1[0:64], [b_pb1], [b_gp], eng="act")
                cp(gp[:, 8:12, 3:131], v2[0:64], [b_pb2], [b_gp])
                pfree(b_pb1, b_pb2)
                yield
                for s in range(12):
                    ts(gx[:, s, :], gp[:, s, 0:128], gcw[:, s, 0:1], ALU.mult, [b_gp, b_gcw],
                       [b_gxs[s]] + ([b_gx] if s == 0 else []))
                for kk in range(1, 4):
                    for s in range(12):
                        stt(gx[:, s, :], gp[:, s, kk:kk + 128], gcw[:, s, kk:kk + 1], gx[:, s, :],
                            ALU.mult, ALU.add, [b_gp, b_gcw, b_gxs[s]], [b_gxs[s]])
                for b in range(3):
                    silu(gx[:, b * 4:(b + 1) * 4, :], gx[:, b * 4:(b + 1) * 4, :], b_gxs[b * 4:(b + 1) * 4],
                         [b_gx] + b_gxs[b * 4:(b + 1) * 4], gsq, b_gsq)
                yield
                cp(gp[:, :, 0:3], gp[:, :, 128:131], [b_gp], [b_gp], eng="pool")
                for b in range(2):
                    xb = gx[:, b * 4:(b + 1) * 4, :]
                    tt(gsq, xb, xb, ALU.mult, [b_gx], [b_gsq])
                    pn, b_pn = yield from galloc()
                    pn3 = v3(pn, 128)
                    for h in range(4):
                        mm(pn3[0:64, h, :], ones64, gsq[:, h, :], [b_ones64, b_gsq], [b_pn])
                    act(grs, pn3[0:64], AF.Sqrt, [b_pn], [b_grs], bias=1e-6)
                    pfree(b_pn)
                    rcp(grs, grs, [b_grs], [b_grs])
                    if b == 0:
                        stt(gqT, xb, 0.125, grs, ALU.mult, ALU.mult, [b_gx, b_grs], [b_gqT])
                    else:
                        tt(gkT, xb, grs, ALU.mult, [b_gx, b_grs], [b_gkT])
                cp(gvT, gx[:, 8:12, :], [b_gx], [b_gvT], eng="act")
                act(beta, gba[:, 0:4], AF.Sigmoid, [b_gba], [b_beta])
                act(lnb, beta, AF.Ln, [b_beta], [b_lnb])
                tt(gsp, gba[:, 4:8], dtb, ALU.add, [b_gba, b_dtb], [b_gsp])
                act(gsp, gsp, AF.Exp, [b_gsp], [b_gsp])
                act(gsp, gsp, AF.Ln, [b_gsp], [b_gsp], bias=1.0)
                tt(gg_, gsp, negA, ALU.mult, [b_gsp, b_negA], [b_gg])
                pq, b_pq = yield from galloc()
                mm(pq[:, 0:4], ublk, gg_, [b_ublk, b_gg], [b_pq])
                mm(pq[:, 4:8], oblk, gg_, [b_oblk, b_gg], [b_pq])
                mm(pq[0:4, 128:256], gg_, ublk, [b_gg, b_ublk], [b_pq])
                cp(Gs, pq[:, 0:4], [b_pq], [b_Gs])
                ts(nGs, Gs, -1.0, ALU.mult, [b_Gs], [b_nGs])
                tt(Bs, Gs, lnb, ALU.add, [b_Gs, b_lnb], [b_Bs])
                act(eB, Bs, AF.Exp, [b_Bs], [b_eB])
                tt(eDl, pq[:, 4:8], Gs, ALU.subtract, [b_pq, b_Gs], [b_eDl])
                act(eDl, eDl, AF.Exp, [b_eDl], [b_eDl])
                cp(GTs, pq[0:4, 128:256], [b_pq], [b_GTs])
                pfree(b_pq)
                yield
                pb, b_pb = yield from galloc()
                pbv = v3(pb.bitcast(BF16)[:, 0:512], 64)
                for h in range(4):
                    tr(pbv[:, h, :], gkT[:, h, :], identb[0:64, 0:64], [b_gkT, b_identb], [b_pb])
                    tr(pbv[:, 4 + h, :], gvT[:, h, :], identb[0:64, 0:64], [b_gvT, b_identb], [b_pb])
                tt(grk, pbv[:, 0:4, :], eB.unsqueeze(2).broadcast_to([128, 4, 64]), ALU.mult, [b_pb, b_eB], [b_grk])
                tt(gkh, pbv[:, 0:4, :], eDl.unsqueeze(2).broadcast_to([128, 4, 64]), ALU.mult, [b_pb, b_eDl], [b_gkh])
                tt(gbv, pbv[:, 4:8, :], beta.unsqueeze(2).broadcast_to([128, 4, 64]), ALU.mult, [b_pb, b_beta], [b_gbv])
                pfree(b_pb)
                yield
                tt(v3(Gblk, 128), GTs.unsqueeze(1).broadcast_to([4, 4, 128]), sel3, ALU.mult, [b_GTs, b_sel], [b_Gblk])
                ts(nGblk, Gblk, -1.0, ALU.mult, [b_Gblk], [b_nGblk])
                pea, b_pea = yield from galloc()
                pea3 = v3(pea, 128)
                mm(pea, ones4, nGblk, [b_ones4, b_nGblk], [b_pea], True, False)
                mm(pea, identb, negAb, [b_identb, b_negAb], [b_pea], False, True)
                for h in range(4):
                    act(EA[:, h, :], pea3[:, h, :], AF.Exp, [b_pea, b_Bs], [b_EA], bias=Bs[:, h:h + 1])
                pfree(b_pea)
                pet, b_pet = yield from galloc()
                pet3 = v3(pet, 128)
                mm(pet, ones4, Gblk, [b_ones4, b_Gblk], [b_pet], True, False)
                mm(pet, identb, negTb, [b_identb, b_negTb], [b_pet], False, True)
                for h in range(4):
                    act(ET[:, h, :], pet3[:, h, :], AF.Exp, [b_pet, b_nGs], [b_ET], bias=nGs[:, h:h + 1])
                pfree(b_pet)
                peg, b_peg = yield from galloc()
                peg3 = v3(peg, 128)
                mm(peg[0:64, :], ones4[:, 0:64], Gblk, [b_ones4, b_Gblk], [b_peg])
                act(eGr, peg3[0:64], AF.Exp, [b_peg], [b_eGr])
                pfree(b_peg)
                yield
                tt(gq[0][0][:, :, 0:64], gqT[:, :, 0:64], eGr[:, :, 0:64], ALU.mult, [b_gqT, b_eGr], [gq[0][1]])
                tt(gq[1][0][:, :, 64:128], gqT[:, :, 64:128], eGr[:, :, 64:128], ALU.mult, [b_gqT, b_eGr], [gq[1][1]])
                pkk, b_pkk = yield from galloc()
                pkq, b_pkq = yield from galloc()
                for h in range(4):
                    mm(v3(pkk, 128)[:, h, :], gkT[:, h, :], gkT[:, h, :], [b_gkT], [b_pkk])
                    mm(v3(pkq, 128)[:, h, :], gkT[:, h, :], gqT[:, h, :], [b_gkT, b_gqT], [b_pkq])
                stt(Pm, v3(pkk, 128), -1.0, EA, ALU.mult, ALU.mult, [b_pkk, b_EA], [b_Pm])
                tt(attT, v3(pkq, 128), ET, ALU.mult, [b_pkq, b_ET], [b_attT])
                pfree(b_pkk, b_pkq)
                yield
                pb, b_pb = yield from galloc()
                pbv = v3(pb.bitcast(BF16)[:, 0:512], 128)
                for h in range(4):
                    tr(pbv[:, h, :], Pm[:, h, :], identb, [b_Pm, b_identb], [b_pb])
                cp(PmT, pbv, [b_pb], [b_PmT], eng="act")
                tt(Wt, pbv, identb.unsqueeze(1).broadcast_to([128, 4, 128]), ALU.add, [b_pb, b_identb], [b_Wt])
                pfree(b_pb)
                yield
                for m in range(5):
                    pa1, b_pa1 = yield from galloc()
                    for h in range(4):
                        mm(v3(pa1, 128)[:, h, :], PmT[:, h, :], Pm[:, h, :], [b_PmT, b_Pm], [b_pa1])
                    if m < 4:
                        pa2, b_pa2 = yield from galloc()
                        for h in range(4):
                            mm(v3(pa2, 128)[:, h, :], Pm[:, h, :], PmT[:, h, :], [b_PmT, b_Pm], [b_pa2])
                    cp(Pm, v3(pa1, 128), [b_pa1], [b_Pm], eng="act")
                    pfree(b_pa1)
                    if m < 4:
                        cp(PmT, v3(pa2, 128), [b_pa2], [b_PmT])
                        pfree(b_pa2)
                    yield
                    pa3, b_pa3 = yield from galloc()
                    for h in range(4):
                        mm(v3(pa3, 128)[:, h, :], Pm[:, h, :], Wt[:, h, :], [b_Pm, b_Wt], [b_pa3])
                    tt(Wt, Wt, v3(pa3, 128), ALU.add, [b_Wt, b_pa3], [b_Wt])
                    pfree(b_pa3)
                    yield
                pu, b_pu = yield from galloc()
                pw, b_pw = yield from galloc()
                for h in range(4):
                    hs = slice(h * 64, (h + 1) * 64)
                    mm(pu[:, hs], Wt[:, h, :], gbv[:, h, :], [b_Wt, b_gbv], [b_pu])
                    mm(v3(pw, 128)[0:64, h, :], grk[:, h, :], Wt[:, h, :], [b_grk, b_Wt], [b_pw])
                cp(usb, pu[:, 0:256], [b_pu], [b_usb])
                cp(gw[0][0][:, :, 0:64], v3(pw, 128)[0:64, :, 0:64], [b_pw], [gw[0][1]], eng="act")
                cp(gw[1][0][:, :, 64:128], v3(pw, 128)[0:64, :, 64:128], [b_pw], [gw[1][1]], eng="act")
                pfree(b_pu, b_pw)
                yield
                for c, (rr, ssrc, sdst) in enumerate([(r0, pv, mid), (r1, mid, end)]):
                    pa, b_pa = yield from galloc()
                    for h in range(4):
                        hs = slice(h * 64, (h + 1) * 64)
                        mm(pa[:, hs], gw[c][0][:, h, :], Sgb[ssrc][0][:, h, :], [gw[c][1], Sgb[ssrc][1]], [b_pa])
                    tt(vn[rr, :], usb[rr, :], pa[rr, 0:256], ALU.subtract, [b_usb, b_pa], [b_vn])
                    pfree(b_pa)
                    pd, b_pd = yield from galloc()
                    for h in range(4):
                        hs = slice(h * 64, (h + 1) * 64)
                        mm(pd[0:64, hs], gkh[rr, h, :], vn[rr, hs], [b_gkh, b_vn], [b_pd])
                    col = 63 + 64 * c
                    for h in range(4):
                        hs = slice(h * 64, (h + 1) * 64)
                        stt(Sg32[:, h, :], Sg32[:, h, :], eGr[:, h, col:col + 1], pd[0:64, hs],
                            ALU.mult, ALU.add, [b_Sg32, b_eGr, b_pd], [b_Sg32])
                    cp(Sgb[sdst][0], Sg32, [b_Sg32], [Sgb[sdst][1]], eng="pool")
                    pfree(b_pd)
                    yield
                po, b_po = yield from galloc()
                for h in range(4):
                    hs = slice(h * 64, (h + 1) * 64)
                    mm(po[:, hs], attT[:, h, :], vn[:, hs], [b_attT, b_vn], [b_po], True, False)
                    mm(po[:, hs], gq[0][0][:, h, :], Sgb[pv][0][:, h, :], [gq[0][1], Sgb[pv][1]], [b_po], False, False)
                    mm(po[:, hs], gq[1][0][:, h, :], Sgb[mid][0][:, h, :], [gq[1][1], Sgb[mid][1]], [b_po], False, True)
                head_norm(po[:, 0:256], b_po, silg, b_silg, gnw.unsqueeze(1).broadcast_to([128, 4, 64]), b_gnw,
                          False, 1e-6, v3(mix[:, 512:768], 64), "gdn")
                pfree(b_po)

            def g_aux():
                if False:
                    yield
                if i > 0:
                    tail_b(i - 1)
                if i + 1 < NT:
                    front(i + 1)

            gl_ = [("gdn", g_gdn, [0, 1]), ("hgrn", g_hgrn, [2, 3]), ("conf", g_conf, [4]), ("ret", g_ret, [5])]
            assert len(freeq) == 8
            run_interleaved([(g, bs) for n_, g, bs in gl_ if n_ not in skip] + [(g_aux, [6, 7])])
            if dbg:
                cp(mixf, mix, [b_mix], [b_mixf], eng="pool")
                k.dma(dmix[l, i * 128:(i + 1) * 128, :], mixf, R=[b_mixf])
            pb, b_pb = psum()
            pbv = v3(pb.bitcast(BF16), 128)
            for kc in range(8):
                tr(pbv[:, kc, :], mix[:, kc * 128:(kc + 1) * 128], identb, [b_mix, b_identb], [b_pb])
            cp(mT, pbv, [b_pb], [b_mT], eng="act")
            pfree(b_pb)
        tail_b(NT - 1)
        k.barrier()
        es.close()

    def pass_b(l, last):
        es = ExitStack()
        Wg, b_Wg = sb(es, "Wg", [128, 8, DFF], BF16)
        Wu, b_Wu = sb(es, "Wu", [128, 8, DFF], BF16)
        Wd, b_Wd = sb(es, "Wd", [128, NFT, D], BF16)
        aT, b_aT = sb(es, "aT", [128, NFT, 128], BF16)
        fsg, b_fsg = sb(es, "fsg", [128, 512])
        fsc, b_fsc = sb(es, "fsc", [128, 512])
        fnw, b_fnw = sb(es, "fnw", [128, D])
        with nc.allow_non_contiguous_dma(reason="tiny param loads"):
            k.dma(nw, dr["norm_ffn_w"][l].rearrange("(c p) -> p c", p=128), W=[b_nw])
            k.dma(fnw, dr["final_norm_w"].partition_broadcast(128), W=[b_fnw])
        stg_b = stg + [sb(es, "stgb%d" % i_, [128, 1024]) for i_ in range(2)]
        load_w(Wg, b_Wg, dr["ffn_w_gate"][l], 8, DFF, nw, stgs=stg_b)
        load_w(Wu, b_Wu, dr["ffn_w_up"][l], 8, DFF, nw, stgs=stg_b)
        load_w(Wd, b_Wd, dr["ffn_w_down"][l], NFT, D, None, stgs=stg_b)
        for i in range(NT):
            k.dma(xt, xs[i * 128:(i + 1) * 128, :], R=[xsb[i]], W=[b_xt])
            norm_to_hT(xt)
            for grp in range(6):
                nf = 4 if grp < 5 else 2
                psg, b_psg = psum()
                psu, b_psu = psum()
                for j in range(nf):
                    ft = grp * 4 + j
                    fs = slice(ft * 128, (ft + 1) * 128)
                    for kc in range(8):
                        mm(psg[:, j * 128:(j + 1) * 128], Wg[:, kc, fs], hT[:, kc, :], [b_Wg, b_hT], [b_psg], kc == 0, kc == 7)
                    for kc in range(8):
                        mm(psu[:, j * 128:(j + 1) * 128], Wu[:, kc, fs], hT[:, kc, :], [b_Wu, b_hT], [b_psu], kc == 0, kc == 7)
                silu(fsg[:, :nf * 128], psg[:, :nf * 128], [b_psg], [b_fsg], fsc[:, :nf * 128], b_fsc)
                tt(aT[:, grp * 4:grp * 4 + nf, :], v3(psu[:, :nf * 128], 128), v3(fsg[:, :nf * 128], 128), ALU.mult,
                   [b_psu, b_fsg], [b_aT])
                pfree(b_psg, b_psu)
            for n in range(2):
                pso, b_pso = psum()
                ns = slice(n * 512, (n + 1) * 512)
                for ft in range(NFT):
                    mm(pso, aT[:, ft, :], Wd[:, ft, ns], [b_aT, b_Wd], [b_pso], ft == 0, ft == NFT - 1)
                tt(xo[:, ns], xt[:, ns], pso, ALU.add, [b_xt, b_pso], [b_xo])
                pfree(b_pso)
            if last:
                act(xn, xo, AF.Square, [b_xo], [b_xn, b_ss], accum=ss)
                act(ss, ss, AF.Sqrt, [b_ss], [b_ss], bias=1e-6, scale=1.0 / D)
                rcp(ss, ss, [b_ss], [b_ss])
                ts(xo, xo, ss, ALU.mult, [b_xo, b_ss], [b_xo])
                tt(xo, xo, fnw, ALU.mult, [b_xo, b_fnw], [b_xo])
                k.dma(out[i * 128:(i + 1) * 128, :], xo, R=[b_xo])
            else:
                k.dma(xs[i * 128:(i + 1) * 128, :], xo, R=[b_xo], W=[xsb[i]])
        k.barrier()
        es.close()

    for l in range(NL):
        try:
            pass_a(l)
        except StopBuild:
            k.barrier()
            return nc, k
        if "b" in skip:
            for i in range(NT):
                k.dma(xt, xs[i * 128:(i + 1) * 128, :], R=[xsb[i]], W=[b_xt])
                k.dma(out[i * 128:(i + 1) * 128, :], xt, R=[b_xt])
        else:
            pass_b(l, l == NL - 1)
    k.barrier()
    es_top.close()
    return nc, k


_CACHE = {}


def kernel(**inputs):
    x = np.asarray(inputs["x"], dtype=np.float32)
    B, T, _ = x.shape
    key = T
    if key not in _CACHE:
        _CACHE[key] = (build(T)[0], host_consts(T))
    nc, consts = _CACHE[key]
    params = {n: np.ascontiguousarray(np.asarray(inputs[n], dtype=np.float32)) for n in PARAM_SHAPES}
    in_maps = []
    for c in range(8):
        m = {"x": np.ascontiguousarray(x[c % B])}
        m.update(params)
        m.update(consts)
        in_maps.append(m)
    res = run_bass_kernel_spmd(nc, in_maps, core_ids=list(range(8)))
    outs = [np.asarray(res.results[c]["out"], dtype=np.float32) for c in range(B)]
    return np.stack(outs, axis=0)
```

```python
from contextlib import ExitStack
import numpy as np
import ml_dtypes
import concourse.bass as bass
import concourse.mybir as mybir
from concourse.bass_utils import run_bass_kernel_spmd

F32 = mybir.dt.float32
BF16 = mybir.dt.bfloat16
AF = mybir.ActivationFunctionType
ALU = mybir.AluOpType
AX = mybir.AxisListType

D = 1024
DIN = 3592
DFF = 2816
NFT = DFF // 128
NEG = -30000.0


class Buf:
    __slots__ = ("name", "w", "r", "excl")

    def __init__(self, name="", excl=False):
        self.name = name
        self.w = None
        self.r = {}
        self.excl = excl


class K:
    NDSEM = 40

    def __init__(self, nc):
        self.nc = nc
        self.eng = {"pe": nc.tensor, "act": nc.scalar, "dve": nc.vector,
                    "pool": nc.gpsimd, "sp": nc.sync}
        self.sem = {e: nc.alloc_semaphore(name="s_" + e) for e in ("pe", "act", "dve", "pool")}
        self.cnt = {e: 0 for e in self.sem}
        self.dsem = [nc.alloc_semaphore(name="d%d" % i) for i in range(self.NDSEM)]
        self.dcnt = [0] * self.NDSEM
        self.dnext = 0
        self.waited = {e: {} for e in self.eng}
        self.nwaits = 0
        self.ninst = 0
        self.rec = None

    def replay(self, item):
        if item[0] == "op":
            self.op(*item[1:])
        else:
            self.dma(*item[1:3], **item[3])

    def _wait(self, eng, ev):
        if ev is None:
            return
        if ev[0] == "e":
            key, val = ("e", ev[1]), ev[2]
            if ev[1] == "pe" and eng == "pe":
                return
            sem = self.sem[ev[1]]
        else:
            key, val = ("d", ev[1]), ev[2]
            sem = self.dsem[ev[1]]
        if self.waited[eng].get(key, 0) >= val:
            return
        self.eng[eng].wait_ge(sem, val)
        self.waited[eng][key] = val
        self.nwaits += 1

    def _deps(self, eng, R, W):
        for b in R:
            self._wait(eng, b.w)
            if b.excl:
                for ev in b.r.values():
                    if not (ev[0] == "e" and ev[1] == eng):
                        self._wait(eng, ev)
        for b in W:
            self._wait(eng, b.w)
            for ev in b.r.values():
                self._wait(eng, ev)

    def _post(self, ev, R, W):
        for b in R:
            b.r[(ev[0], ev[1])] = ev
        for b in W:
            b.w = ev
            b.r = {}

    def op(self, eng, fn, R=(), W=()):
        if self.rec is not None:
            self.rec.append(("op", eng, fn, tuple(R), tuple(W)))
            return None
        self._deps(eng, R, W)
        inst = fn(self.eng[eng])
        self.cnt[eng] += 1
        inst.then_inc(self.sem[eng], 1)
        self._post(("e", eng, self.cnt[eng]), R, W)
        self.ninst += 1
        return inst

    def dma(self, out, in_, R=(), W=(), q="sp", **kw):
        if self.rec is not None:
            self.rec.append(("dma", out, in_, dict(R=tuple(R), W=tuple(W), q=q, **kw)))
            return None
        idx = self.dnext
        self.dnext = (self.dnext + 1) % self.NDSEM
        if self.dcnt[idx] > 0:
            self._wait(q, ("d", idx, self.dcnt[idx]))
        self._deps(q, R, W)
        inst = self.eng[q].dma_start(out=out, in_=in_, **kw)
        self.dcnt[idx] += 16
        inst.then_inc(self.dsem[idx], 16)
        ev = ("d", idx, self.dcnt[idx])
        self._post(ev, R, W)
        self.ninst += 1
        return ev

    def barrier(self):
        for e in self.eng:
            for e2 in self.sem:
                if self.cnt[e2] > 0:
                    self._wait(e, ("e", e2, self.cnt[e2])) if not (e == e2 == "pe") else None
            for i in range(self.NDSEM):
                if self.dcnt[i] > 0:
                    self._wait(e, ("d", i, self.dcnt[i]))


def host_consts(T):
    c = {}
    c["c_identf"] = np.eye(128, dtype=np.float32)
    j = np.arange(128)[:, None]
    i = np.arange(128)[None, :]
    same = (j // 64) == (i // 64)
    c["c_cmask"] = (j <= i).astype(np.float32)
    c["c_ublk"] = ((j <= i) & same).astype(np.float32)
    c["c_oblk"] = same.astype(np.float32)
    cind = np.zeros((128, 2), np.float32)
    cind[:64, 0] = 1.0
    cind[64:, 1] = 1.0
    c["c_cind"] = cind
    sel = np.zeros((4, 4, 128), np.float32)
    for h in range(4):
        sel[h, h, :] = 1.0
    c["c_sel"] = sel.reshape(4, 512)
    c["c_negA"] = np.where(same & (j > i), 0.0, NEG).astype(np.float32)
    c["c_negT"] = np.where(same & (i >= j), 0.0, NEG).astype(np.float32)
    c["c_ones64"] = np.ones((64, 64), np.float32)
    c["c_ones4"] = np.ones((4, 128), np.float32)
    c["c_negA4"] = np.tile(c["c_negA"], (1, 4))
    c["c_negT4"] = np.tile(c["c_negT"], (1, 4))
    t = np.arange(T, dtype=np.float64)
    inv = 10000.0 ** (-np.arange(32, dtype=np.float64) / 32.0)
    ang = (t[:, None].astype(np.float32) * inv[None, :].astype(np.float32)).astype(np.float64)
    cos, sin = np.cos(ang), np.sin(ang)
    gam = 1.0 - 2.0 ** (-5.0 - np.arange(4, dtype=np.float64))
    pos = (np.arange(T) % 128 + 1).astype(np.float64)
    dq = gam[None, :] ** pos[:, None]
    dk = gam[None, :] ** (-pos[:, None]) * 0.125
    rope = np.zeros((T, 4, 4, 32), np.float64)
    rope[:, 0] = cos[:, None, :] * dq[:, :, None]
    rope[:, 1] = sin[:, None, :] * dq[:, :, None]
    rope[:, 2] = cos[:, None, :] * dk[:, :, None]
    rope[:, 3] = sin[:, None, :] * dk[:, :, None]
    c["c_rope"] = rope.reshape(T, 512).astype(np.float32)
    gC = np.zeros((64, 4, 64), np.float64)
    gC[:] = (gam ** 128.0)[None, :, None]
    c["c_gC"] = gC.reshape(64, 256).astype(np.float32)
    return c


CONST_SHAPES = {"c_identf": [128, 128], "c_cmask": [128, 128], "c_ublk": [128, 128],
                "c_oblk": [128, 128], "c_cind": [128, 2], "c_sel": [4, 512],
                "c_negA": [128, 128], "c_negT": [128, 128], "c_ones64": [64, 64],
                "c_ones4": [4, 128], "c_negA4": [128, 512], "c_negT4": [128, 512],
                "c_gC": [64, 256]}

PARAM_SHAPES = {
    "norm_mix_w": [2, 1024], "w_in": [2, 1024, DIN], "ret_norm_w": [2, 256],
    "conf_conv_w": [2, 31, 256], "conf_conv_b": [2, 256], "conf_ln_w": [2, 256],
    "conf_ln_b": [2, 256], "gdn_conv_w": [2, 4, 768], "gdn_A_log": [2, 4],
    "gdn_dt_bias": [2, 4], "gdn_norm_w": [2, 64], "hgrn_lb_logits": [2, 256],
    "hgrn_norm_w": [2, 64], "w_out": [2, 1024, 1024], "norm_ffn_w": [2, 1024],
    "ffn_w_gate": [2, 1024, DFF], "ffn_w_up": [2, 1024, DFF], "ffn_w_down": [2, DFF, 1024],
    "final_norm_w": [1024],
}


class StopBuild(Exception):
    pass


def build(T, NL=2, dbg=False, skip=(), stop=99):
    nc = bass.Bass("TRN2", target_bir_lowering=False)

    def ckpt(n):
        if n >= stop:
            raise StopBuild()

    NT = T // 128
    k = K(nc)
    dr = {}
    dr["x"] = nc.dram_tensor("x", [T, D], F32, kind="ExternalInput").ap()
    for n, s in PARAM_SHAPES.items():
        dr[n] = nc.dram_tensor(n, s, F32, kind="ExternalInput").ap()
    for n, s in CONST_SHAPES.items():
        dr[n] = nc.dram_tensor(n, s, F32, kind="ExternalInput").ap()
    dr["c_rope"] = nc.dram_tensor("c_rope", [T, 512], F32, kind="ExternalInput").ap()
    out = nc.dram_tensor("out", [T, D], F32, kind="ExternalOutput").ap()
    xs = nc.dram_tensor("xs", [T, D], F32, kind="Internal").ap()
    if dbg:
        dmix = nc.dram_tensor("dmix", [NL, T, D], F32, kind="ExternalOutput").ap()
    xsb = [Buf("xs%d" % i) for i in range(NT)]

    es_top = ExitStack()

    uniq = {"n": 0}

    def sb(es, name, shape, dt=F32):
        uniq["n"] += 1
        t = es.enter_context(nc.sbuf_tensor("%s_%d" % (name, uniq["n"]), shape, dt))
        return t.ap(), Buf(name)

    def mm(out_, lhsT, rhs, R, W, start=True, stop=True):
        k.op("pe", lambda e: e.matmul(out_, lhsT=lhsT, rhs=rhs, start=start, stop=stop), R=R, W=W)

    def tr(out_, in_, ident, R, W):
        k.op("pe", lambda e: e.transpose(out=out_, in_=in_, identity=ident), R=R, W=W)

    def act(out_, in_, func, R, W, bias=None, scale=None, accum=None):
        kw = {}
        if bias is not None:
            kw["bias"] = bias
        if scale is not None:
            kw["scale"] = scale
        if accum is not None:
            kw["accum_out"] = accum
        k.op("act", lambda e: e.activation(out=out_, in_=in_, func=func, **kw), R=R, W=W)

    def silu(out_, in_, R, W, scr, b_scr):
        act(scr, in_, AF.Sigmoid, R, [b_scr])
        tt(out_, in_, scr, ALU.mult, list(R) + [b_scr], W)

    def tt(out_, a, b, op, R, W, eng="dve"):
        k.op(eng, lambda e: e.tensor_tensor(out=out_, in0=a, in1=b, op=op), R=R, W=W)

    def ts(out_, a, s1, op0, R, W, s2=None, op1=None, eng="dve"):
        if op1 is None:
            k.op(eng, lambda e: e.tensor_scalar(out=out_, in0=a, scalar1=s1, scalar2=None, op0=op0), R=R, W=W)
        else:
            k.op(eng, lambda e: e.tensor_scalar(out=out_, in0=a, scalar1=s1, scalar2=None, op0=op0), R=R, W=W)
            k.op(eng, lambda e: e.tensor_scalar(out=out_, in0=out_, scalar1=s2, scalar2=None, op0=op1), R=list(R) + list(W), W=W)

    def stt(out_, in0, scalar, in1, op0, op1, R, W):
        k.op("dve", lambda e: e.scalar_tensor_tensor(out=out_, in0=in0, scalar=scalar, in1=in1, op0=op0, op1=op1), R=R, W=W)

    def cp(out_, in_, R, W, eng="dve"):
        if eng == "act":
            k.op("act", lambda e: e.copy(out=out_, in_=in_), R=R, W=W)
        else:
            k.op(eng, lambda e: e.tensor_copy(out=out_, in_=in_), R=R, W=W)

    def red(out_, in_, R, W):
        k.op("dve", lambda e: e.tensor_reduce(out=out_, in_=in_, axis=AX.X, op=ALU.add), R=R, W=W)

    def rcp(out_, in_, R, W):
        k.op("dve", lambda e: e.reciprocal(out=out_, in_=in_), R=R, W=W)

    def mset(ap, val, W, eng="pool"):
        k.op(eng, lambda e: e.memset(ap, val), W=W)

    def v3(ap, inner):
        return ap.rearrange("p (a b) -> p a b", b=inner)

    banks = []
    for i in range(8):
        t = es_top.enter_context(nc.psum_tensor("bank%d" % i, [128, 512], F32))
        banks.append((t.ap(), Buf("bank%d" % i, excl=True)))
    freeq = list(range(8))
    bank_idx = {id(b[1]): i for i, b in enumerate(banks)}
    cur = {"free": freeq}

    def psum():
        assert cur["free"], "no free PSUM bank"
        return banks[cur["free"].pop(0)]

    def galloc():
        if False:
            yield
        return psum()

    def pfree(*bufs):
        for b in bufs:
            i = bank_idx[id(b)]
            assert i not in cur["free"]
            cur["free"].append(i)

    def run_interleaved(gens):
        lists = []
        for g, bset in gens:
            cur["free"] = list(bset)
            k.rec = []
            for _ in g():
                pass
            lists.append(k.rec)
            k.rec = None
            assert sorted(cur["free"]) == sorted(bset), "mixer leaked PSUM banks"
        cur["free"] = freeq
        pos = [0] * len(lists)
        while True:
            best, bf = -1, 2.0
            for j, lst in enumerate(lists):
                if pos[j] < len(lst):
                    f = pos[j] / len(lst)
                    if f < bf:
                        best, bf = j, f
            if best < 0:
                break
            k.replay(lists[best][pos[best]])
            pos[best] += 1

    stg = [sb(es_top, "stg%d" % i, [128, 1024]) for i in range(2)]
    stg_c = stg[0]
    identf, b_identf = sb(es_top, "identf", [128, 128])
    identb, b_identb = sb(es_top, "identb", [128, 128], BF16)
    cmask, b_cmask = sb(es_top, "cmask", [128, 128])
    ublk, b_ublk = sb(es_top, "ublk", [128, 128])
    oblk, b_oblk = sb(es_top, "oblk", [128, 128])
    cind, b_cind = sb(es_top, "cind", [128, 2])
    sel, b_sel = sb(es_top, "sel", [4, 512])
    negAb, b_negAb = sb(es_top, "negAb", [128, 512], BF16)
    negTb, b_negTb = sb(es_top, "negTb", [128, 512], BF16)
    ones4, b_ones4 = sb(es_top, "ones4", [4, 128])
    ones64, b_ones64 = sb(es_top, "ones64", [64, 64])
    gC, b_gC = sb(es_top, "gC", [64, 256])
    CONSTS = []
    k.dma(identf, dr["c_identf"], W=[b_identf])
    k.dma(cmask, dr["c_cmask"], W=[b_cmask])
    k.dma(ublk, dr["c_ublk"], W=[b_ublk])
    k.dma(oblk, dr["c_oblk"], W=[b_oblk])
    k.dma(cind, dr["c_cind"], W=[b_cind])
    k.dma(sel, dr["c_sel"], W=[b_sel])
    k.dma(ones4, dr["c_ones4"], W=[b_ones4])
    k.dma(ones64, dr["c_ones64"], W=[b_ones64])
    k.dma(gC, dr["c_gC"], W=[b_gC])
    cp(identb, identf, [b_identf], [b_identb])
    for (dst_, b_dst_, nm_) in [(negAb, b_negAb, "c_negA4"), (negTb, b_negTb, "c_negT4")]:
        s_ap, s_b = stg_c
        k.dma(s_ap[:, 0:512], dr[nm_], W=[s_b])
        cp(dst_, s_ap[:, 0:512], [s_b], [b_dst_])
    sel3 = v3(sel, 128)
    try:
        ckpt(1)
    except StopBuild:
        k.barrier()
        return nc, k

    xt, b_xt = sb(es_top, "xt", [128, D])
    xo, b_xo = sb(es_top, "xo", [128, D])
    xn, b_xn = sb(es_top, "xn", [128, D], BF16)
    hT, b_hT = sb(es_top, "hT", [128, 8, 128], BF16)
    b_xt0, hT0, b_hT0 = b_xt, hT, b_hT
    ss, b_ss = sb(es_top, "ss", [128, 1])
    nw, b_nw = sb(es_top, "nw", [128, 8])
    stg_i = {"i": 0}

    def load_w(dst, b_dst, src, KC, N, scale, stgs=None):
        stgs = stgs or stg
        engs = ("dve", "pool") if len(stgs) == 2 else ("dve", "pool", "act", "dve")
        for kc in range(KC):
            for n0 in range(0, N, 1024):
                n1 = min(N, n0 + 1024)
                s_ap, s_b = stgs[stg_i["i"] % len(stgs)]
                eng = engs[stg_i["i"] % len(stgs)]
                stg_i["i"] += 1
                k.dma(s_ap[:, :n1 - n0], src[kc * 128:(kc + 1) * 128, n0:n1], W=[s_b])
                if eng == "act":
                    if scale is not None:
                        act(dst[:, kc, n0:n1], s_ap[:, :n1 - n0], AF.Identity, [s_b, b_nw], [b_dst],
                            scale=scale[:, kc:kc + 1])
                    else:
                        cp(dst[:, kc, n0:n1], s_ap[:, :n1 - n0], [s_b], [b_dst], eng="act")
                elif scale is not None:
                    ts(dst[:, kc, n0:n1], s_ap[:, :n1 - n0], scale[:, kc:kc + 1], ALU.mult,
                       [s_b, b_nw], [b_dst], eng=eng)
                else:
                    cp(dst[:, kc, n0:n1], s_ap[:, :n1 - n0], [s_b], [b_dst], eng=eng)

    def norm_to_hT(src_xt, b_xt=None, hT=None, b_hT=None):
        b_xt = b_xt or b_xt0
        hT = hT if hT is not None else hT0
        b_hT = b_hT or b_hT0
        act(xn, src_xt, AF.Square, [b_xt], [b_xn, b_ss], accum=ss)
        act(ss, ss, AF.Sqrt, [b_ss], [b_ss], bias=1e-6, scale=1.0 / D)
        rcp(ss, ss, [b_ss], [b_ss])
        ts(xn, src_xt, ss, ALU.mult, [b_xt, b_ss], [b_xn])
        pb, b_pb = psum()
        pbv = v3(pb.bitcast(BF16), 128)
        for kc in range(8):
            tr(pbv[:, kc, :], xn[:, kc * 128:(kc + 1) * 128], identb, [b_xn, b_identb], [b_pb])
        cp(hT, pbv, [b_pb], [b_hT], eng="act")
        pfree(b_pb)

    def pass_a(l):
        es = ExitStack()
        Wi, b_Wi = sb(es, "Wi", [128, 8, DIN], BF16)
        Wo, b_Wo = sb(es, "Wo", [128, 8, D], BF16)
        Dc, b_Dc = sb(es, "Dc", [128, 62, 128], BF16)
        ccw, b_ccw = sb(es, "ccw", [128, 2, 31])
        ccb, b_ccb = sb(es, "ccb", [128, 2])
        gcw, b_gcw = sb(es, "gcw", [64, 12, 4])
        rnw, b_rnw = sb(es, "rnw", [128, 256])
        clnw, b_clnw = sb(es, "clnw", [128, 256])
        clnb, b_clnb = sb(es, "clnb", [128, 256])
        gnw, b_gnw = sb(es, "gnw", [128, 64])
        hnw, b_hnw = sb(es, "hnw", [128, 64])
        negA, b_negA = sb(es, "negA", [128, 4])
        dtb, b_dtb = sb(es, "dtb", [128, 4])
        lbl, b_lbl = sb(es, "lbl", [128, 2, 256])
        lbt, b_lbt = sb(es, "lbt", [128, 256])
        oml, b_oml = sb(es, "oml", [128, 256])
        with nc.allow_non_contiguous_dma(reason="tiny param loads"):
            k.dma(nw, dr["norm_mix_w"][l].rearrange("(c p) -> p c", p=128), W=[b_nw])
            for ct in range(2):
                k.dma(ccw[:, ct, :], dr["conf_conv_w"][l][:, ct * 128:(ct + 1) * 128].rearrange("k c -> c k"), W=[b_ccw])
            k.dma(ccb, dr["conf_conv_b"][l].rearrange("(ct c) -> c ct", c=128), W=[b_ccb])
            for s_ in range(12):
                k.dma(gcw[:, s_, :], dr["gdn_conv_w"][l][:, s_ * 64:(s_ + 1) * 64].rearrange("k d -> d k"), W=[b_gcw])
            k.dma(rnw, dr["ret_norm_w"][l].partition_broadcast(128), W=[b_rnw])
            k.dma(clnw, dr["conf_ln_w"][l].partition_broadcast(128), W=[b_clnw])
            k.dma(clnb, dr["conf_ln_b"][l].partition_broadcast(128), W=[b_clnb])
            k.dma(gnw, dr["gdn_norm_w"][l].partition_broadcast(128), W=[b_gnw])
            k.dma(hnw, dr["hgrn_norm_w"][l].partition_broadcast(128), W=[b_hnw])
            k.dma(negA, dr["gdn_A_log"][l].partition_broadcast(128), W=[b_negA])
            k.dma(dtb, dr["gdn_dt_bias"][l].partition_broadcast(128), W=[b_dtb])
            k.dma(lbl[:, 0, :], dr["hgrn_lb_logits"][0].partition_broadcast(128), W=[b_lbl])
            k.dma(lbl[:, 1, :], dr["hgrn_lb_logits"][1].partition_broadcast(128), W=[b_lbl])
        ckpt(2)
        act(negA, negA, AF.Exp, [b_negA], [b_negA])
        ts(negA, negA, -1.0, ALU.mult, [b_negA], [b_negA])
        if l == 0:
            mset(lbt, 0.0, [b_lbt])
            mset(oml, 1.0, [b_oml])
        else:
            act(lbl, lbl, AF.Exp, [b_lbl], [b_lbl])
            tt(oml, lbl[:, 0, :], lbl[:, 1, :], ALU.add, [b_lbl], [b_oml])
            rcp(oml, oml, [b_oml], [b_oml])
            tt(lbt, lbl[:, 1, :], oml, ALU.mult, [b_lbl, b_oml], [b_lbt])
            ts(oml, lbt, -1.0, ALU.mult, [b_lbt], [b_oml], s2=1.0, op1=ALU.add)
        for kk in range(31):
            for ct in range(2):
                ts(Dc[:, kk * 2 + ct, :], identf, ccw[:, ct, kk:kk + 1], ALU.mult, [b_identf, b_ccw], [b_Dc])
        ckpt(3)
        load_w(Wi, b_Wi, dr["w_in"][l], 8, DIN, nw)
        load_w(Wo, b_Wo, dr["w_out"][l], 8, D, None)

        ckpt(4)
        rope_t, b_rope = sb(es, "rope_t", [128, 512])
        rqk, b_rqk = sb(es, "rqk", [128, 512])
        ctsg, b_ctsg = sb(es, "ctsg", [128, 256])
        glt, b_glt = sb(es, "glt", [128, 256], BF16)
        gpt, b_gpt = sb(es, "gpt", [128, 768], BF16)
        Gblk, b_Gblk = sb(es, "Gblk", [4, 512])
        nGblk, b_nGblk = sb(es, "nGblk", [4, 512])
        hqf, b_hqf = sb(es, "hqf", [128, 256])
        t1, b_t1 = sb(es, "t1", [128, 4, 32])
        t2, b_t2 = sb(es, "t2", [128, 4, 32])
        qrt, b_qrt = sb(es, "qrt", [128, 4, 64], BF16)
        krt, b_krt = sb(es, "krt", [128, 4, 64], BF16)
        qkT, b_qkT = sb(es, "qkT", [64, 8, 128], BF16)
        PTr, b_PTr = sb(es, "PTr", [128, 4, 128], BF16)
        rvb, b_rvb = sb(es, "rvb", [128, 256], BF16)
        Sr32, b_Sr32 = sb(es, "Sr32", [64, 256])
        Srb, b_Srb = sb(es, "Srb", [64, 4, 64], BF16)
        silr, b_silr = sb(es, "silr", [128, 256])
        silg, b_silg = sb(es, "silg", [128, 256])
        silh, b_silh = sb(es, "silh", [128, 256])
        HN = {}
        for nm_ in ("ret", "hgrn", "gdn"):
            HN[nm_] = dict(hsq=sb(es, "hsq", [128, 256]), hy=sb(es, "hy", [128, 256]), s1=sb(es, "s1", [128, 4]),
                           s2=sb(es, "s2", [128, 4]), mean=sb(es, "mean", [128, 4]), msq=sb(es, "msq", [128, 4]),
                           var=sb(es, "var", [128, 4]))
        hsq, b_hsq = HN["ret"]["hsq"]
        hf, b_hf = sb(es, "hf", [128, 256])
        logf, b_logf = sb(es, "logf", [128, 256])
        hkk, b_hkk = sb(es, "hkk", [128, 256])
        Gsb, b_Gsb = sb(es, "Gsb", [128, 256])
        eG, b_eG = sb(es, "eG", [128, 256])
        enG, b_enG = eG, b_eG
        hD, b_hD = Gsb, b_Gsb
        qh, b_qh = sb(es, "qh", [128, 256], BF16)
        kht, b_kht = sb(es, "kht", [128, 256], BF16)
        khat, b_khat = sb(es, "khat", [128, 256], BF16)
        hvb, b_hvb = sb(es, "hvb", [128, 256], BF16)
        hq = [sb(es, "hq%d" % c, [64, 4, 128], BF16) for c in range(2)]
        hkT, b_hkT = sb(es, "hkT", [64, 4, 128], BF16)
        PTh, b_PTh = sb(es, "PTh", [128, 4, 128], BF16)
        Sh32, b_Sh32 = sb(es, "Sh32", [64, 4, 64])
        Shb = [sb(es, "Shb%d" % c, [64, 4, 64], BF16) for c in range(3)]
        eGl, b_eGl = sb(es, "eGl", [64, 8])
        gl, b_gl = sb(es, "gl", [128, 2, 158], BF16)
        yT, b_yT = sb(es, "yT", [128, 2, 128])
        yc, b_yc = sb(es, "yc", [128, 256])
        c1, b_c1 = sb(es, "c1", [128, 1])
        c2, b_c2 = sb(es, "c2", [128, 1])
        c3, b_c3 = sb(es, "c3", [128, 1])
        gp, b_gp = sb(es, "gp", [64, 12, 131], BF16)
        gx, b_gx = sb(es, "gx", [64, 12, 128])
        b_gxs = [Buf("gxs%d" % i_) for i_ in range(12)]
        gsq, b_gsq = sb(es, "gsq", [64, 4, 128])
        grs, b_grs = sb(es, "grs", [64, 4, 128])
        gqT, b_gqT = sb(es, "gqT", [64, 4, 128], BF16)
        gkT, b_gkT = sb(es, "gkT", [64, 4, 128], BF16)
        gvT, b_gvT = sb(es, "gvT", [64, 4, 128], BF16)
        grk, b_grk = sb(es, "grk", [128, 4, 64], BF16)
        gkh, b_gkh = sb(es, "gkh", [128, 4, 64], BF16)
        gbv, b_gbv = sb(es, "gbv", [128, 4, 64], BF16)
        EA, b_EA = sb(es, "EA", [128, 4, 128], BF16)
        ET, b_ET = sb(es, "ET", [128, 4, 128], BF16)
        eGr, b_eGr = sb(es, "eGr", [64, 4, 128])
        gq = [sb(es, "gq%d" % c, [64, 4, 128], BF16) for c in range(2)]
        gw = [sb(es, "gw%d" % c, [64, 4, 128], BF16) for c in range(2)]
        Pm, b_Pm = sb(es, "Pm", [128, 4, 128], BF16)
        PmT, b_PmT = sb(es, "PmT", [128, 4, 128], BF16)
        Wt, b_Wt = sb(es, "Wt", [128, 4, 128], BF16)
        attT, b_attT = sb(es, "attT", [128, 4, 128], BF16)
        usb, b_usb = sb(es, "usb", [128, 256])
        vn, b_vn = sb(es, "vn", [128, 256], BF16)
        Sg32, b_Sg32 = sb(es, "Sg32", [64, 4, 64])
        Sgb = [sb(es, "Sgb%d" % c, [64, 4, 64], BF16) for c in range(3)]
        beta, b_beta = sb(es, "beta", [128, 4])
        gba, b_gba = sb(es, "gba", [128, 8])
        lnb, b_lnb = sb(es, "lnb", [128, 4])
        gsp, b_gsp = sb(es, "gsp", [128, 4])
        gg_, b_gg = sb(es, "gg_", [128, 4])
        Gs, b_Gs = sb(es, "Gs", [128, 4])
        nGs, b_nGs = sb(es, "nGs", [128, 4])
        Bs, b_Bs = sb(es, "Bs", [128, 4])
        eB, b_eB = sb(es, "eB", [128, 4])
        eDl, b_eDl = sb(es, "eDl", [128, 4])
        GTs, b_GTs = sb(es, "GTs", [4, 128])
        mix, b_mix = sb(es, "mix", [128, D], BF16)
        mT, b_mT = sb(es, "mT", [128, 8, 128], BF16)
        mixf, b_mixf = (xo, b_xo) if dbg else (None, None)

        mset(Sr32, 0.0, [b_Sr32]); mset(Srb, 0.0, [b_Srb])
        mset(Sh32, 0.0, [b_Sh32]); mset(Sg32, 0.0, [b_Sg32])
        for c in range(3):
            mset(Shb[c][0], 0.0, [Shb[c][1]]); mset(Sgb[c][0], 0.0, [Sgb[c][1]])
        for c in range(2):
            mset(hq[c][0], 0.0, [hq[c][1]]); mset(gq[c][0], 0.0, [gq[c][1]]); mset(gw[c][0], 0.0, [gw[c][1]])
        mset(gl, 0.0, [b_gl]); mset(gp, 0.0, [b_gp])
        mset(mix, 0.0, [b_mix])

        ckpt(5)

        k.barrier()
        XT = [(xt, b_xt), (stg[0][0], Buf("xt1"))]
        HT = [(hT, b_hT), (v3(stg[1][0][:, 0:512].bitcast(BF16), 128), Buf("hT1"))]
        ROPE = [(rope_t, b_rope), (stg[1][0][:, 512:1024], Buf("rope1"))]
        src_x = dr["x"] if l == 0 else xs

        def front(j):
            (xt_n, b_xt_n), (hT_n, b_hT_n), (rope_n, b_rope_n) = XT[j % 2], HT[j % 2], ROPE[j % 2]
            k.dma(xt_n, src_x[j * 128:(j + 1) * 128, :], R=([] if l == 0 else [xsb[j]]), W=[b_xt_n])
            k.dma(rope_n, dr["c_rope"][j * 128:(j + 1) * 128, :], W=[b_rope_n])
            norm_to_hT(xt_n, b_xt_n, hT_n, b_hT_n)

        def tail_b(j):
            xt_p, b_xt_p = XT[j % 2]
            for n in range(2):
                pso, b_pso = psum()
                ns = slice(n * 512, (n + 1) * 512)
                for kc in range(8):
                    mm(pso, mT[:, kc, :], Wo[:, kc, ns], [b_mT, b_Wo], [b_pso], kc == 0, kc == 7)
                tt(xo[:, ns], xt_p[:, ns], pso, ALU.add, [b_xt_p, b_pso], [b_xo])
                pfree(b_pso)
            k.dma(xs[j * 128:(j + 1) * 128, :], xo, R=[b_xo], W=[xsb[j]])

        def head_norm(o, b_o, sil, b_sil, w3, b_w, center, eps, dst3, who):
            t_ = HN[who]
            (hsq, b_hsq), (hy, b_hy), (s1, b_s1), (s2, b_s2) = t_["hsq"], t_["hy"], t_["s1"], t_["s2"]
            (mean, b_mean), (msq, b_msq), (var, b_var) = t_["mean"], t_["msq"], t_["var"]
            o3 = v3(o, 64)
            red(s1, o3, [b_o], [b_s1])
            act(hsq, o, AF.Square, [b_o], [b_hsq])
            red(s2, v3(hsq, 64), [b_hsq], [b_s2])
            if center:
                ts(mean, s1, 1.0 / 64, ALU.mult, [b_s1], [b_mean])
                tt(msq, mean, mean, ALU.mult, [b_mean], [b_msq])
                stt(var, s2, 1.0 / 64, msq, ALU.mult, ALU.subtract, [b_s2, b_msq], [b_var])
            else:
                ts(var, s2, 1.0 / 64, ALU.mult, [b_s2], [b_var])
            act(var, var, AF.Sqrt, [b_var], [b_var], bias=eps)
            rcp(var, var, [b_var], [b_var])
            hy3 = v3(hy, 64)
            if center:
                tt(hy3, o3, mean.unsqueeze(2).broadcast_to([128, 4, 64]), ALU.subtract, [b_o, b_mean], [b_hy])
                tt(hy3, hy3, var.unsqueeze(2).broadcast_to([128, 4, 64]), ALU.mult, [b_hy, b_var], [b_hy])
            else:
                tt(hy3, o3, var.unsqueeze(2).broadcast_to([128, 4, 64]), ALU.mult, [b_o, b_var], [b_hy])
            tt(hy3, hy3, w3, ALU.mult, [b_hy, b_w], [b_hy])
            tt(dst3, hy3, v3(sil, 64), ALU.mult, [b_hy, b_sil], [b_mix])

        front(0)
        for i in range(NT):
            pv, mid, end = (2 * i) % 3, (2 * i + 1) % 3, (2 * i + 2) % 3
            r0, r1 = slice(0, 64), slice(64, 128)
            (xt_c, b_xt_c), (hT_c, b_hT_c), (rope_c, b_rope_c) = XT[i % 2], HT[i % 2], ROPE[i % 2]

            def g_ret():
                pg, b_pg = yield from galloc()
                for kc in range(8):
                    mm(pg[:, 0:512], hT_c[:, kc, :], Wi[:, kc, 0:512], [b_hT_c, b_Wi], [b_pg], kc == 0, kc == 7)
                cp(rqk, pg, [b_pg], [b_rqk], eng="act")
                pfree(b_pg)
                pg, b_pg = yield from galloc()
                for kc in range(8):
                    mm(pg[:, 0:512], hT_c[:, kc, :], Wi[:, kc, 512:1024], [b_hT_c, b_Wi], [b_pg], kc == 0, kc == 7)
                silu(silr, pg[:, 256:512], [b_pg], [b_silr], HN["ret"]["hsq"][0], HN["ret"]["hsq"][1])
                cp(rvb, pg[:, 0:256], [b_pg], [b_rvb])
                pfree(b_pg)
                rt4 = rope_c.rearrange("p (a h d) -> p a h d", a=4, h=4)
                for (off, ci, si, dst, b_dst) in [(0, 0, 1, qrt, b_qrt), (256, 2, 3, krt, b_krt)]:
                    X = v3(rqk[:, off:off + 256], 64)
                    x1, x2 = X[:, :, 0:32], X[:, :, 32:64]
                    cs, sn = rt4[:, ci], rt4[:, si]
                    tt(t1, x1, cs, ALU.mult, [b_rqk, b_rope_c], [b_t1])
                    tt(t2, x2, sn, ALU.mult, [b_rqk, b_rope_c], [b_t2])
                    tt(dst[:, :, 0:32], t1, t2, ALU.subtract, [b_t1, b_t2], [b_dst])
                    tt(t1, x1, sn, ALU.mult, [b_rqk, b_rope_c], [b_t1])
                    tt(t2, x2, cs, ALU.mult, [b_rqk, b_rope_c], [b_t2])
                    tt(dst[:, :, 32:64], t1, t2, ALU.add, [b_t1, b_t2], [b_dst])
                pb, b_pb = yield from galloc()
                pbv = v3(pb.bitcast(BF16), 128)
                for h in range(4):
                    tr(pbv[0:64, h, :], qrt[:, h, :], identb, [b_qrt, b_identb], [b_pb])
                    tr(pbv[0:64, 4 + h, :], krt[:, h, :], identb, [b_krt, b_identb], [b_pb])
                cp(qkT, pbv[0:64], [b_pb], [b_qkT], eng="act")
                pfree(b_pb)
                yield
                ps, b_ps = yield from galloc()
                ps3 = v3(ps, 128)
                for h in range(4):
                    mm(ps3[:, h, :], qkT[:, 4 + h, :], qkT[:, h, :], [b_qkT], [b_ps])
                tt(PTr, ps3, cmask.unsqueeze(1).broadcast_to([128, 4, 128]), ALU.mult, [b_ps, b_cmask], [b_PTr])
                pfree(b_ps)
                yield
                po, b_po = yield from galloc()
                for h in range(4):
                    hs = slice(h * 64, (h + 1) * 64)
                    mm(po[:, hs], PTr[:, h, :], rvb[:, hs], [b_PTr, b_rvb], [b_po], True, False)
                    mm(po[:, hs], qkT[:, h, :], Srb[:, h, :], [b_qkT, b_Srb], [b_po], False, True)
                head_norm(po[:, 0:256], b_po, silr, b_silr, v3(rnw, 64), b_rnw, True, 1e-5, v3(mix[:, 0:256], 64), "ret")
                pfree(b_po)
                pd, b_pd = yield from galloc()
                for h in range(4):
                    hs = slice(h * 64, (h + 1) * 64)
                    mm(pd[0:64, hs], krt[:, h, :], rvb[:, hs], [b_krt, b_rvb], [b_pd])
                tt(Sr32, Sr32, pd[0:64, 0:256], ALU.add, [b_Sr32, b_pd], [b_Sr32])
                tt(Sr32, Sr32, gC, ALU.mult, [b_Sr32, b_gC], [b_Sr32])
                cp(Srb, v3(Sr32, 64), [b_Sr32], [b_Srb], eng="pool")
                pfree(b_pd)

            def g_hgrn():
                pg, b_pg = yield from galloc()
                for kc in range(8):
                    mm(pg[:, 0:512], hT_c[:, kc, :], Wi[:, kc, 2568:3080], [b_hT_c, b_Wi], [b_pg], kc == 0, kc == 7)
                cp(hqf, pg[:, 0:256], [b_pg], [b_hqf])
                act(hf, pg[:, 256:512], AF.Sigmoid, [b_pg], [b_hf])
                pfree(b_pg)
                pg, b_pg = yield from galloc()
                for kc in range(8):
                    mm(pg[:, 0:512], hT_c[:, kc, :], Wi[:, kc, 3080:3592], [b_hT_c, b_Wi], [b_pg], kc == 0, kc == 7)
                silu(silh, pg[:, 256:512], [b_pg], [b_silh], HN["hgrn"]["hsq"][0], HN["hgrn"]["hsq"][1])
                cp(hvb, pg[:, 0:256], [b_pg], [b_hvb])
                pfree(b_pg)
                tt(hf, hf, oml, ALU.mult, [b_hf, b_oml], [b_hf])
                tt(hf, hf, lbt, ALU.add, [b_hf, b_lbt], [b_hf])
                act(logf, hf, AF.Ln, [b_hf], [b_logf])
                ts(hkk, hf, -1.0, ALU.mult, [b_hf], [b_hkk], s2=1.0, op1=ALU.add)
                pG, b_pG = yield from galloc()
                mm(pG[:, 0:256], ublk, logf, [b_ublk, b_logf], [b_pG])
                mm(pG[:, 256:512], oblk, logf, [b_oblk, b_logf], [b_pG])
                pL, b_pL = yield from galloc()
                for h in range(4):
                    mm(pL[0:64, h * 2:(h + 1) * 2], logf[:, h * 64:(h + 1) * 64], cind, [b_logf, b_cind], [b_pL])
                act(eGl, pL[0:64, 0:8], AF.Exp, [b_pL], [b_eGl])
                pfree(b_pL)
                cp(Gsb, pG[:, 0:256], [b_pG], [b_Gsb], eng="act")
                act(eG, pG[:, 0:256], AF.Exp, [b_pG], [b_eG])
                tt(qh, hqf, eG, ALU.mult, [b_hqf, b_eG], [b_qh])
                act(enG, pG[:, 0:256], AF.Exp, [b_pG], [b_enG], scale=-1.0)
                tt(kht, hkk, enG, ALU.mult, [b_hkk, b_enG], [b_kht])
                tt(hD, pG[:, 256:512], Gsb, ALU.subtract, [b_pG, b_Gsb], [b_hD])
                pfree(b_pG)
                yield
                act(hD, hD, AF.Exp, [b_hD], [b_hD])
                tt(khat, hkk, hD, ALU.mult, [b_hkk, b_hD], [b_khat])
                pb, b_pb = yield from galloc()
                pbv = v3(pb.bitcast(BF16), 128)
                for h in range(4):
                    tr(pbv[0:64, h, :], qh[:, h * 64:(h + 1) * 64], identb, [b_qh, b_identb], [b_pb])
                    tr(pbv[0:64, 4 + h, :], kht[:, h * 64:(h + 1) * 64], identb, [b_kht, b_identb], [b_pb])
                cp(hq[0][0][:, :, 0:64], pbv[0:64, 0:4, 0:64], [b_pb], [hq[0][1]], eng="act")
                cp(hq[1][0][:, :, 64:128], pbv[0:64, 0:4, 64:128], [b_pb], [hq[1][1]], eng="act")
                cp(hkT, pbv[0:64, 4:8, :], [b_pb], [b_hkT])
                pfree(b_pb)
                yield
                ps, b_ps = yield from galloc()
                ps3 = v3(ps, 128)
                for h in range(4):
                    for c in range(2):
                        cs_ = slice(c * 64, (c + 1) * 64)
                        mm(ps3[:, h, cs_], hkT[:, h, :], hq[c][0][:, h, cs_], [b_hkT, hq[c][1]], [b_ps])
                tt(PTh, ps3, ublk.unsqueeze(1).broadcast_to([128, 4, 128]), ALU.mult, [b_ps, b_ublk], [b_PTh])
                pfree(b_ps)
                yield
                for c, (rr, sdst) in enumerate([(r0, mid), (r1, end)]):
                    pd, b_pd = yield from galloc()
                    for h in range(4):
                        hs = slice(h * 64, (h + 1) * 64)
                        mm(pd[0:64, hs], khat[rr, hs], hvb[rr, hs], [b_khat, b_hvb], [b_pd])
                    for h in range(4):
                        hs = slice(h * 64, (h + 1) * 64)
                        stt(Sh32[:, h, :], Sh32[:, h, :], eGl[:, h * 2 + c:h * 2 + c + 1], pd[0:64, hs],
                            ALU.mult, ALU.add, [b_Sh32, b_eGl, b_pd], [b_Sh32])
                    cp(Shb[sdst][0], Sh32, [b_Sh32], [Shb[sdst][1]], eng="pool")
                    pfree(b_pd)
                    yield
                po, b_po = yield from galloc()
                for h in range(4):
                    hs = slice(h * 64, (h + 1) * 64)
                    mm(po[:, hs], PTh[:, h, :], hvb[:, hs], [b_PTh, b_hvb], [b_po], True, False)
                    mm(po[:, hs], hq[0][0][:, h, :], Shb[pv][0][:, h, :], [hq[0][1], Shb[pv][1]], [b_po], False, False)
                    mm(po[:, hs], hq[1][0][:, h, :], Shb[mid][0][:, h, :], [hq[1][1], Shb[mid][1]], [b_po], False, True)
                head_norm(po[:, 0:256], b_po, silh, b_silh, hnw.unsqueeze(1).broadcast_to([128, 4, 64]), b_hnw,
                          False, 1e-6, v3(mix[:, 768:1024], 64), "hgrn")
                pfree(b_po)

            def g_conf():
                pg, b_pg = yield from galloc()
                for kc in range(8):
                    mm(pg[:, 0:512], hT_c[:, kc, :], Wi[:, kc, 1024:1536], [b_hT_c, b_Wi], [b_pg], kc == 0, kc == 7)
                act(ctsg, pg[:, 256:512], AF.Sigmoid, [b_pg], [b_ctsg])
                tt(glt, pg[:, 0:256], ctsg, ALU.mult, [b_pg, b_ctsg], [b_glt])
                pfree(b_pg)
                pc, b_pc = yield from galloc()
                pc3 = v3(pc.bitcast(BF16)[:, 0:256], 128)
                for ct in range(2):
                    tr(pc3[:, ct, :], glt[:, ct * 128:(ct + 1) * 128], identb, [b_glt, b_identb], [b_pc])
                cp(gl[:, :, 30:158], pc3, [b_pc], [b_gl], eng="act")
                pfree(b_pc)
                yield
                ckpt(21)
                py, b_py = yield from galloc()
                py3 = v3(py, 128)
                for ct in range(2):
                    for kk in range(31):
                        mm(py3[:, ct, :], Dc[:, kk * 2 + ct, :], gl[:, ct, kk:kk + 128], [b_Dc, b_gl], [b_py],
                           kk == 0, kk == 30)
                ckpt(22)
                for ct in range(2):
                    act(yT[:, ct, :], py3[:, ct, :], AF.Identity, [b_py, b_ccb], [b_yT], bias=ccb[:, ct:ct + 1])
                pfree(b_py)
                yield
                cp(gl[:, :, 0:30], gl[:, :, 128:158], [b_gl], [b_gl], eng="pool")
                ckpt(24)
                pt, b_pt = yield from galloc()
                for ct in range(2):
                    tr(pt[:, ct * 128:(ct + 1) * 128], yT[:, ct, :], identf, [b_yT, b_identf], [b_pt])
                ckpt(25)
                red(c1, pt[:, 0:256].unsqueeze(1), [b_pt], [b_c1])
                act(yc, pt[:, 0:256], AF.Square, [b_pt], [b_yc])
                red(c2, yc.unsqueeze(1), [b_yc], [b_c2])
                ckpt(27)
                ts(c1, c1, 1.0 / 256, ALU.mult, [b_c1], [b_c1])
                tt(c3, c1, c1, ALU.mult, [b_c1], [b_c3])
                stt(c2, c2, 1.0 / 256, c3, ALU.mult, ALU.subtract, [b_c2, b_c3], [b_c2])
                ckpt(28)
                act(c2, c2, AF.Sqrt, [b_c2], [b_c2], bias=1e-5)
                rcp(c2, c2, [b_c2], [b_c2])
                ckpt(29)
                ts(yc, pt[:, 0:256], c1, ALU.subtract, [b_pt, b_c1, b_c2], [b_yc], s2=c2, op1=ALU.mult)
                pfree(b_pt)
                ckpt(30)
                tt(yc, yc, clnw, ALU.mult, [b_yc, b_clnw], [b_yc])
                tt(yc, yc, clnb, ALU.add, [b_yc, b_clnb], [b_yc])
                silu(mix[:, 256:512], yc, [b_yc], [b_mix], ctsg, b_ctsg)

            def g_gdn():
                pg, b_pg = yield from galloc()
                for kc in range(8):
                    mm(pg[:, 0:264], hT_c[:, kc, :], Wi[:, kc, 2304:2568], [b_hT_c, b_Wi], [b_pg], kc == 0, kc == 7)
                silu(silg, pg[:, 0:256], [b_pg], [b_silg], HN["gdn"]["hsq"][0], HN["gdn"]["hsq"][1])
                cp(gba, pg[:, 256:264], [b_pg], [b_gba])
                pfree(b_pg)
                pg, b_pg = yield from galloc()
                for kc in range(8):
                    mm(pg[:, 0:512], hT_c[:, kc, :], Wi[:, kc, 1536:2048], [b_hT_c, b_Wi], [b_pg], kc == 0, kc == 7)
                cp(gpt[:, 0:512], pg, [b_pg], [b_gpt], eng="act")
                pfree(b_pg)
                pg, b_pg = yield from galloc()
                for kc in range(8):
                    mm(pg[:, 0:256], hT_c[:, kc, :], Wi[:, kc, 2048:2304], [b_hT_c, b_Wi], [b_pg], kc == 0, kc == 7)
                cp(gpt[:, 512:768], pg[:, 0:256], [b_pg], [b_gpt])
                pfree(b_pg)
                pb1, b_pb1 = yield from galloc()
                pb2, b_pb2 = yield from galloc()
                v1 = v3(pb1.bitcast(BF16), 128)
                v2 = v3(pb2.bitcast(BF16)[:, 0:512], 128)
                for s in range(12):
                    dst_ = v1[0:64, s, :] if s < 8 else v2[0:64, s - 8, :]
                    tr(dst_, gpt[:, s * 64:(s + 1) * 64], identb, [b_gpt, b_identb], [b_pb1 if s < 8 else b_pb2])
                cp(gp[:, 0:8, 3:131], v1[0:64], [b_pb1], [b_gp], eng="act")
                cp(gp[:, 8:12, 3:131], v2[0:64], [b_pb2], [b_gp])
                pfree(b_pb1, b_pb2)
                yield
                for s in range(12):
                    ts(gx[:, s, :], gp[:, s, 0:128], gcw[:, s, 0:1], ALU.mult, [b_gp, b_gcw],
                       [b_gxs[s]] + ([b_gx] if s == 0 else []))
                for kk in range(1, 4):
                    for s in range(12):
                        stt(gx[:, s, :], gp[:, s, kk:kk + 128], gcw[:, s, kk:kk + 1], gx[:, s, :],
                            ALU.mult, ALU.add, [b_gp, b_gcw, b_gxs[s]], [b_gxs[s]])
                for b in range(3):
                    silu(gx[:, b * 4:(b + 1) * 4, :], gx[:, b * 4:(b + 1) * 4, :], b_gxs[b * 4:(b + 1) * 4],
                         [b_gx] + b_gxs[b * 4:(b + 1) * 4], gsq, b_gsq)
                yield
                cp(gp[:, :, 0:3], gp[:, :, 128:131], [b_gp], [b_gp], eng="pool")
                for b in range(2):
                    xb = gx[:, b * 4:(b + 1) * 4, :]
                    tt(gsq, xb, xb, ALU.mult, [b_gx], [b_gsq])
                    pn, b_pn = yield from galloc()
                    pn3 = v3(pn, 128)
                    for h in range(4):
                        mm(pn3[0:64, h, :], ones64, gsq[:, h, :], [b_ones64, b_gsq], [b_pn])
                    act(grs, pn3[0:64], AF.Sqrt, [b_pn], [b_grs], bias=1e-6)
                    pfree(b_pn)
                    rcp(grs, grs, [b_grs], [b_grs])
                    if b == 0:
                        stt(gqT, xb, 0.125, grs, ALU.mult, ALU.mult, [b_gx, b_grs], [b_gqT])
                    else:
                        tt(gkT, xb, grs, ALU.mult, [b_gx, b_grs], [b_gkT])
                cp(gvT, gx[:, 8:12, :], [b_gx], [b_gvT], eng="act")
                act(beta, gba[:, 0:4], AF.Sigmoid, [b_gba], [b_beta])
                act(lnb, beta, AF.Ln, [b_beta], [b_lnb])
                tt(gsp, gba[:, 4:8], dtb, ALU.add, [b_gba, b_dtb], [b_gsp])
                act(gsp, gsp, AF.Exp, [b_gsp], [b_gsp])
                act(gsp, gsp, AF.Ln, [b_gsp], [b_gsp], bias=1.0)
                tt(gg_, gsp, negA, ALU.mult, [b_gsp, b_negA], [b_gg])
                pq, b_pq = yield from galloc()
                mm(pq[:, 0:4], ublk, gg_, [b_ublk, b_gg], [b_pq])
                mm(pq[:, 4:8], oblk, gg_, [b_oblk, b_gg], [b_pq])
                mm(pq[0:4, 128:256], gg_, ublk, [b_gg, b_ublk], [b_pq])
                cp(Gs, pq[:, 0:4], [b_pq], [b_Gs])
                ts(nGs, Gs, -1.0, ALU.mult, [b_Gs], [b_nGs])
                tt(Bs, Gs, lnb, ALU.add, [b_Gs, b_lnb], [b_Bs])
                act(eB, Bs, AF.Exp, [b_Bs], [b_eB])
                tt(eDl, pq[:, 4:8], Gs, ALU.subtract, [b_pq, b_Gs], [b_eDl])
                act(eDl, eDl, AF.Exp, [b_eDl], [b_eDl])
                cp(GTs, pq[0:4, 128:256], [b_pq], [b_GTs])
                pfree(b_pq)
                yield
                pb, b_pb = yield from galloc()
                pbv = v3(pb.bitcast(BF16)[:, 0:512], 64)
                for h in range(4):
                    tr(pbv[:, h, :], gkT[:, h, :], identb[0:64, 0:64], [b_gkT, b_identb], [b_pb])
                    tr(pbv[:, 4 + h, :], gvT[:, h, :], identb[0:64, 0:64], [b_gvT, b_identb], [b_pb])
                tt(grk, pbv[:, 0:4, :], eB.unsqueeze(2).broadcast_to([128, 4, 64]), ALU.mult, [b_pb, b_eB], [b_grk])
                tt(gkh, pbv[:, 0:4, :], eDl.unsqueeze(2).broadcast_to([128, 4, 64]), ALU.mult, [b_pb, b_eDl], [b_gkh])
                tt(gbv, pbv[:, 4:8, :], beta.unsqueeze(2).broadcast_to([128, 4, 64]), ALU.mult, [b_pb, b_beta], [b_gbv])
                pfree(b_pb)
                yield
                tt(v3(Gblk, 128), GTs.unsqueeze(1).broadcast_to([4, 4, 128]), sel3, ALU.mult, [b_GTs, b_sel], [b_Gblk])
                ts(nGblk, Gblk, -1.0, ALU.mult, [b_Gblk], [b_nGblk])
                pea, b_pea = yield from galloc()
                pea3 = v3(pea, 128)
                mm(pea, ones4, nGblk, [b_ones4, b_nGblk], [b_pea], True, False)
                mm(pea, identb, negAb, [b_identb, b_negAb], [b_pea], False, True)
                for h in range(4):
                    act(EA[:, h, :], pea3[:, h, :], AF.Exp, [b_pea, b_Bs], [b_EA], bias=Bs[:, h:h + 1])
                pfree(b_pea)
                pet, b_pet = yield from galloc()
                pet3 = v3(pet, 128)
                mm(pet, ones4, Gblk, [b_ones4, b_Gblk], [b_pet], True, False)
                mm(pet, identb, negTb, [b_identb, b_negTb], [b_pet], False, True)
                for h in range(4):
                    act(ET[:, h, :], pet3[:, h, :], AF.Exp, [b_pet, b_nGs], [b_ET], bias=nGs[:, h:h + 1])
                pfree(b_pet)
                peg, b_peg = yield from galloc()
                peg3 = v3(peg, 128)
                mm(peg[0:64, :], ones4[:, 0:64], Gblk, [b_ones4, b_Gblk], [b_peg])
                act(eGr, peg3[0:64], AF.Exp, [b_peg], [b_eGr])
                pfree(b_peg)
                yield
                tt(gq[0][0][:, :, 0:64], gqT[:, :, 0:64], eGr[:, :, 0:64], ALU.mult, [b_gqT, b_eGr], [gq[0][1]])
                tt(gq[1][0][:, :, 64:128], gqT[:, :, 64:128], eGr[:, :, 64:128], ALU.mult, [b_gqT, b_eGr], [gq[1][1]])
                pkk, b_pkk = yield from galloc()
                pkq, b_pkq = yield from galloc()
                for h in range(4):
                    mm(v3(pkk, 128)[:, h, :], gkT[:, h, :], gkT[:, h, :], [b_gkT], [b_pkk])
                    mm(v3(pkq, 128)[:, h, :], gkT[:, h, :], gqT[:, h, :], [b_gkT, b_gqT], [b_pkq])
                stt(Pm, v3(pkk, 128), -1.0, EA, ALU.mult, ALU.mult, [b_pkk, b_EA], [b_Pm])
                tt(attT, v3(pkq, 128), ET, ALU.mult, [b_pkq, b_ET], [b_attT])
                pfree(b_pkk, b_pkq)
                yield
                pb, b_pb = yield from galloc()
                pbv = v3(pb.bitcast(BF16)[:, 0:512], 128)
                for h in range(4):
                    tr(pbv[:, h, :], Pm[:, h, :], identb, [b_Pm, b_identb], [b_pb])
                cp(PmT, pbv, [b_pb], [b_PmT], eng="act")
                tt(Wt, pbv, identb.unsqueeze(1).broadcast_to([128, 4, 128]), ALU.add, [b_pb, b_identb], [b_Wt])
                pfree(b_pb)
                yield
                for m in range(5):
                    pa1, b_pa1 = yield from galloc()
                    for h in range(4):
                        mm(v3(pa1, 128)[:, h, :], PmT[:, h, :], Pm[:, h, :], [b_PmT, b_Pm], [b_pa1])
                    if m < 4:
                        pa2, b_pa2 = yield from galloc()
                        for h in range(4):
                            mm(v3(pa2, 128)[:, h, :], Pm[:, h, :], PmT[:, h, :], [b_PmT, b_Pm], [b_pa2])
                    cp(Pm, v3(pa1, 128), [b_pa1], [b_Pm], eng="act")
                    pfree(b_pa1)
                    if m < 4:
                        cp(PmT, v3(pa2, 128), [b_pa2], [b_PmT])
                        pfree(b_pa2)
                    yield
                    pa3, b_pa3 = yield from galloc()
                    for h in range(4):
                        mm(v3(pa3, 128)[:, h, :], Pm[:, h, :], Wt[:, h, :], [b_Pm, b_Wt], [b_pa3])
                    tt(Wt, Wt, v3(pa3, 128), ALU.add, [b_Wt, b_pa3], [b_Wt])
                    pfree(b_pa3)
                    yield
                pu, b_pu = yield from galloc()
                pw, b_pw = yield from galloc()
                for h in range(4):
                    hs = slice(h * 64, (h + 1) * 64)
                    mm(pu[:, hs], Wt[:, h, :], gbv[:, h, :], [b_Wt, b_gbv], [b_pu])
                    mm(v3(pw, 128)[0:64, h, :], grk[:, h, :], Wt[:, h, :], [b_grk, b_Wt], [b_pw])
                cp(usb, pu[:, 0:256], [b_pu], [b_usb])
                cp(gw[0][0][:, :, 0:64], v3(pw, 128)[0:64, :, 0:64], [b_pw], [gw[0][1]], eng="act")
                cp(gw[1][0][:, :, 64:128], v3(pw, 128)[0:64, :, 64:128], [b_pw], [gw[1][1]], eng="act")
                pfree(b_pu, b_pw)
                yield
                for c, (rr, ssrc, sdst) in enumerate([(r0, pv, mid), (r1, mid, end)]):
                    pa, b_pa = yield from galloc()
                    for h in range(4):
                        hs = slice(h * 64, (h + 1) * 64)
                        mm(pa[:, hs], gw[c][0][:, h, :], Sgb[ssrc][0][:, h, :], [gw[c][1], Sgb[ssrc][1]], [b_pa])
                    tt(vn[rr, :], usb[rr, :], pa[rr, 0:256], ALU.subtract, [b_usb, b_pa], [b_vn])
                    pfree(b_pa)
                    pd, b_pd = yield from galloc()
                    for h in range(4):
                        hs = slice(h * 64, (h + 1) * 64)
                        mm(pd[0:64, hs], gkh[rr, h, :], vn[rr, hs], [b_gkh, b_vn], [b_pd])
                    col = 63 + 64 * c
                    for h in range(4):
                        hs = slice(h * 64, (h + 1) * 64)
                        stt(Sg32[:, h, :], Sg32[:, h, :], eGr[:, h, col:col + 1], pd[0:64, hs],
                            ALU.mult, ALU.add, [b_Sg32, b_eGr, b_pd], [b_Sg32])
                    cp(Sgb[sdst][0], Sg32, [b_Sg32], [Sgb[sdst][1]], eng="pool")
                    pfree(b_pd)
                    yield
                po, b_po = yield from galloc()
                for h in range(4):
                    hs = slice(h * 64, (h + 1) * 64)
                    mm(po[:, hs], attT[:, h, :], vn[:, hs], [b_attT, b_vn], [b_po], True, False)
                    mm(po[:, hs], gq[0][0][:, h, :], Sgb[pv][0][:, h, :], [gq[0][1], Sgb[pv][1]], [b_po], False, False)
                    mm(po[:, hs], gq[1][0][:, h, :], Sgb[mid][0][:, h, :], [gq[1][1], Sgb[mid][1]], [b_po], False, True)
                head_norm(po[:, 0:256], b_po, silg, b_silg, gnw.unsqueeze(1).broadcast_to([128, 4, 64]), b_gnw,
                          False, 1e-6, v3(mix[:, 512:768], 64), "gdn")
                pfree(b_po)

            def g_aux():
                if False:
                    yield
                if i > 0:
                    tail_b(i - 1)
                if i + 1 < NT:
                    front(i + 1)

            gl_ = [("gdn", g_gdn, [0, 1]), ("hgrn", g_hgrn, [2, 3]), ("conf", g_conf, [4]), ("ret", g_ret, [5])]
            assert len(freeq) == 8
            run_interleaved([(g, bs) for n_, g, bs in gl_ if n_ not in skip] + [(g_aux, [6, 7])])
            if dbg:
                cp(mixf, mix, [b_mix], [b_mixf], eng="pool")
                k.dma(dmix[l, i * 128:(i + 1) * 128, :], mixf, R=[b_mixf])
            pb, b_pb = psum()
            pbv = v3(pb.bitcast(BF16), 128)
            for kc in range(8):
                tr(pbv[:, kc, :], mix[:, kc * 128:(kc + 1) * 128], identb, [b_mix, b_identb], [b_pb])
            cp(mT, pbv, [b_pb], [b_mT], eng="act")
            pfree(b_pb)
        tail_b(NT - 1)
        k.barrier()
        es.close()

    def pass_b(l, last):
        es = ExitStack()
        Wg, b_Wg = sb(es, "Wg", [128, 8, DFF], BF16)
        Wu, b_Wu = sb(es, "Wu", [128, 8, DFF], BF16)
        Wd, b_Wd = sb(es, "Wd", [128, NFT, D], BF16)
        aT, b_aT = sb(es, "aT", [128, NFT, 128], BF16)
        fsg, b_fsg = sb(es, "fsg", [128, 512])
        fsc, b_fsc = sb(es, "fsc", [128, 512])
        fnw, b_fnw = sb(es, "fnw", [128, D])
        with nc.allow_non_contiguous_dma(reason="tiny param loads"):
            k.dma(nw, dr["norm_ffn_w"][l].rearrange("(c p) -> p c", p=128), W=[b_nw])
            k.dma(fnw, dr["final_norm_w"].partition_broadcast(128), W=[b_fnw])
        stg_b = stg + [sb(es, "stgb%d" % i_, [128, 1024]) for i_ in range(2)]
        load_w(Wg, b_Wg, dr["ffn_w_gate"][l], 8, DFF, nw, stgs=stg_b)
        load_w(Wu, b_Wu, dr["ffn_w_up"][l], 8, DFF, nw, stgs=stg_b)
        load_w(Wd, b_Wd, dr["ffn_w_down"][l], NFT, D, None, stgs=stg_b)
        xtb2, b_xtb2 = sb(es, "xtb2", [128, D])
        hTb2, b_hTb2 = sb(es, "hTb2", [128, 8, 128], BF16)
        xn2, b_xn2 = sb(es, "xn2", [128, D], BF16)
        ss2, b_ss2 = sb(es, "ss2", [128, 1])
        XTB = [(xt, b_xt), (xtb2, b_xtb2)]
        HTB = [(hT, b_hT), (hTb2, b_hTb2)]

        def front_b(j):
            (xt_n, b_xt_n), (hT_n, b_hT_n) = XTB[j % 2], HTB[j % 2]
            k.dma(xt_n, xs[j * 128:(j + 1) * 128, :], R=[xsb[j]], W=[b_xt_n])
            norm_to_hT(xt_n, b_xt_n, hT_n, b_hT_n)

        front_b(0)
        for i in range(NT):
            (xt_c, b_xt_c), (hT_c, b_hT_c) = XTB[i % 2], HTB[i % 2]

            def g_main():
                if False:
                    yield
                for grp in range(6):
                    nf = 4 if grp < 5 else 2
                    psg, b_psg = psum()
                    psu, b_psu = psum()
                    for j in range(nf):
                        ft = grp * 4 + j
                        fs = slice(ft * 128, (ft + 1) * 128)
                        for kc in range(8):
                            mm(psg[:, j * 128:(j + 1) * 128], Wg[:, kc, fs], hT_c[:, kc, :], [b_Wg, b_hT_c], [b_psg], kc == 0, kc == 7)
                        for kc in range(8):
                            mm(psu[:, j * 128:(j + 1) * 128], Wu[:, kc, fs], hT_c[:, kc, :], [b_Wu, b_hT_c], [b_psu], kc == 0, kc == 7)
                    silu(fsg[:, :nf * 128], psg[:, :nf * 128], [b_psg], [b_fsg], fsc[:, :nf * 128], b_fsc)
                    tt(aT[:, grp * 4:grp * 4 + nf, :], v3(psu[:, :nf * 128], 128), v3(fsg[:, :nf * 128], 128), ALU.mult,
                       [b_psu, b_fsg], [b_aT])
                    pfree(b_psg, b_psu)
                for n in range(2):
                    pso, b_pso = psum()
                    ns = slice(n * 512, (n + 1) * 512)
                    for ft in range(NFT):
                        mm(pso, aT[:, ft, :], Wd[:, ft, ns], [b_aT, b_Wd], [b_pso], ft == 0, ft == NFT - 1)
                    tt(xo[:, ns], xt_c[:, ns], pso, ALU.add, [b_xt_c, b_pso], [b_xo])
                    pfree(b_pso)
                if last:
                    act(xn2, xo, AF.Square, [b_xo], [b_xn2, b_ss2], accum=ss2)
                    act(ss2, ss2, AF.Sqrt, [b_ss2], [b_ss2], bias=1e-6, scale=1.0 / D)
                    rcp(ss2, ss2, [b_ss2], [b_ss2])
                    ts(xo, xo, ss2, ALU.mult, [b_xo, b_ss2], [b_xo])
                    tt(xo, xo, fnw, ALU.mult, [b_xo, b_fnw], [b_xo])
                    k.dma(out[i * 128:(i + 1) * 128, :], xo, R=[b_xo])
                else:
                    k.dma(xs[i * 128:(i + 1) * 128, :], xo, R=[b_xo], W=[xsb[i]])

            def g_front():
                if False:
                    yield
                if i + 1 < NT:
                    front_b(i + 1)

            assert len(freeq) == 8
            run_interleaved([(g_main, [0, 1, 2, 3, 4, 5]), (g_front, [6, 7])])
        k.barrier()
        es.close()

    for l in range(NL):
        try:
            pass_a(l)
        except StopBuild:
            k.barrier()
            return nc, k
        if "b" in skip:
            for i in range(NT):
                k.dma(xt, xs[i * 128:(i + 1) * 128, :], R=[xsb[i]], W=[b_xt])
                k.dma(out[i * 128:(i + 1) * 128, :], xt, R=[b_xt])
        else:
            pass_b(l, l == NL - 1)
    k.barrier()
    es_top.close()
    return nc, k


_CACHE = {}


def kernel(**inputs):
    x = np.asarray(inputs["x"], dtype=np.float32)
    B, T, _ = x.shape
    key = T
    if key not in _CACHE:
        _CACHE[key] = (build(T)[0], host_consts(T))
    nc, consts = _CACHE[key]
    params = {n: np.ascontiguousarray(np.asarray(inputs[n], dtype=np.float32)) for n in PARAM_SHAPES}
    in_maps = []
    for c in range(8):
        m = {"x": np.ascontiguousarray(x[c % B])}
        m.update(params)
        m.update(consts)
        in_maps.append(m)
    res = run_bass_kernel_spmd(nc, in_maps, core_ids=list(range(8)))
    outs = [np.asarray(res.results[c]["out"], dtype=np.float32) for c in range(B)]
    return np.stack(outs, axis=0)
```

```python
from contextlib import ExitStack
import numpy as np
import ml_dtypes
import concourse.bass as bass
import concourse.mybir as mybir
from concourse.bass_utils import run_bass_kernel_spmd

F32 = mybir.dt.float32
BF16 = mybir.dt.bfloat16
AF = mybir.ActivationFunctionType
ALU = mybir.AluOpType
AX = mybir.AxisListType

D = 1024
DIN = 3592
DFF = 2816
NFT = DFF // 128
NEG = -30000.0


class Buf:
    __slots__ = ("name", "w", "r", "excl")

    def __init__(self, name="", excl=False):
        self.name = name
        self.w = None
        self.r = {}
        self.excl = excl


class K:
    NDSEM = 40

    def __init__(self, nc):
        self.nc = nc
        self.eng = {"pe": nc.tensor, "act": nc.scalar, "dve": nc.vector,
                    "pool": nc.gpsimd, "sp": nc.sync}
        self.sem = {e: nc.alloc_semaphore(name="s_" + e) for e in ("pe", "act", "dve", "pool")}
        self.cnt = {e: 0 for e in self.sem}
        self.dsem = [nc.alloc_semaphore(name="d%d" % i) for i in range(self.NDSEM)]
        self.dcnt = [0] * self.NDSEM
        self.dnext = 0
        self.waited = {e: {} for e in self.eng}
        self.nwaits = 0
        self.ninst = 0
        self.rec = None

    def replay(self, item):
        if item[0] == "op":
            self.op(*item[1:])
        else:
            self.dma(*item[1:3], **item[3])

    def _wait(self, eng, ev):
        if ev is None:
            return
        if ev[0] == "e":
            key, val = ("e", ev[1]), ev[2]
            if ev[1] == "pe" and eng == "pe":
                return
            sem = self.sem[ev[1]]
        else:
            key, val = ("d", ev[1]), ev[2]
            sem = self.dsem[ev[1]]
        if self.waited[eng].get(key, 0) >= val:
            return
        self.eng[eng].wait_ge(sem, val)
        self.waited[eng][key] = val
        self.nwaits += 1

    def _deps(self, eng, R, W):
        for b in R:
            self._wait(eng, b.w)
            if b.excl:
                for ev in b.r.values():
                    if not (ev[0] == "e" and ev[1] == eng):
                        self._wait(eng, ev)
        for b in W:
            self._wait(eng, b.w)
            for ev in b.r.values():
                self._wait(eng, ev)

    def _post(self, ev, R, W):
        for b in R:
            b.r[(ev[0], ev[1])] = ev
        for b in W:
            b.w = ev
            b.r = {}

    def op(self, eng, fn, R=(), W=()):
        if self.rec is not None:
            self.rec.append(("op", eng, fn, tuple(R), tuple(W)))
            return None
        self._deps(eng, R, W)
        inst = fn(self.eng[eng])
        self.cnt[eng] += 1
        inst.then_inc(self.sem[eng], 1)
        self._post(("e", eng, self.cnt[eng]), R, W)
        self.ninst += 1
        return inst

    def dma(self, out, in_, R=(), W=(), q="sp", **kw):
        if self.rec is not None:
            self.rec.append(("dma", out, in_, dict(R=tuple(R), W=tuple(W), q=q, **kw)))
            return None
        idx = self.dnext
        self.dnext = (self.dnext + 1) % self.NDSEM
        if self.dcnt[idx] > 0:
            self._wait(q, ("d", idx, self.dcnt[idx]))
        self._deps(q, R, W)
        inst = self.eng[q].dma_start(out=out, in_=in_, **kw)
        self.dcnt[idx] += 16
        inst.then_inc(self.dsem[idx], 16)
        ev = ("d", idx, self.dcnt[idx])
        self._post(ev, R, W)
        self.ninst += 1
        return ev

    def barrier(self):
        for e in self.eng:
            for e2 in self.sem:
                if self.cnt[e2] > 0:
                    self._wait(e, ("e", e2, self.cnt[e2])) if not (e == e2 == "pe") else None
            for i in range(self.NDSEM):
                if self.dcnt[i] > 0:
                    self._wait(e, ("d", i, self.dcnt[i]))


def host_consts(T):
    c = {}
    c["c_identf"] = np.eye(128, dtype=np.float32)
    j = np.arange(128)[:, None]
    i = np.arange(128)[None, :]
    same = (j // 64) == (i // 64)
    c["c_cmask"] = (j <= i).astype(np.float32)
    c["c_ublk"] = ((j <= i) & same).astype(np.float32)
    c["c_oblk"] = same.astype(np.float32)
    cind = np.zeros((128, 2), np.float32)
    cind[:64, 0] = 1.0
    cind[64:, 1] = 1.0
    c["c_cind"] = cind
    sel = np.zeros((4, 4, 128), np.float32)
    for h in range(4):
        sel[h, h, :] = 1.0
    c["c_sel"] = sel.reshape(4, 512)
    c["c_negA"] = np.where(same & (j > i), 0.0, NEG).astype(np.float32)
    c["c_negT"] = np.where(same & (i >= j), 0.0, NEG).astype(np.float32)
    c["c_ones64"] = np.ones((64, 64), np.float32)
    c["c_ones4"] = np.ones((4, 128), np.float32)
    c["c_negA4"] = np.tile(c["c_negA"], (1, 4))
    c["c_negT4"] = np.tile(c["c_negT"], (1, 4))
    t = np.arange(T, dtype=np.float64)
    inv = 10000.0 ** (-np.arange(32, dtype=np.float64) / 32.0)
    ang = (t[:, None].astype(np.float32) * inv[None, :].astype(np.float32)).astype(np.float64)
    cos, sin = np.cos(ang), np.sin(ang)
    gam = 1.0 - 2.0 ** (-5.0 - np.arange(4, dtype=np.float64))
    pos = (np.arange(T) % 128 + 1).astype(np.float64)
    dq = gam[None, :] ** pos[:, None]
    dk = gam[None, :] ** (-pos[:, None]) * 0.125
    rope = np.zeros((T, 4, 4, 32), np.float64)
    rope[:, 0] = cos[:, None, :] * dq[:, :, None]
    rope[:, 1] = sin[:, None, :] * dq[:, :, None]
    rope[:, 2] = cos[:, None, :] * dk[:, :, None]
    rope[:, 3] = sin[:, None, :] * dk[:, :, None]
    c["c_rope"] = rope.reshape(T, 512).astype(np.float32)
    gC = np.zeros((64, 4, 64), np.float64)
    gC[:] = (gam ** 128.0)[None, :, None]
    c["c_gC"] = gC.reshape(64, 256).astype(np.float32)
    return c


CONST_SHAPES = {"c_identf": [128, 128], "c_cmask": [128, 128], "c_ublk": [128, 128],
                "c_oblk": [128, 128], "c_cind": [128, 2], "c_sel": [4, 512],
                "c_negA": [128, 128], "c_negT": [128, 128], "c_ones64": [64, 64],
                "c_ones4": [4, 128], "c_negA4": [128, 512], "c_negT4": [128, 512],
                "c_gC": [64, 256]}

PARAM_SHAPES = {
    "norm_mix_w": [2, 1024], "w_in": [2, 1024, DIN], "ret_norm_w": [2, 256],
    "conf_conv_w": [2, 31, 256], "conf_conv_b": [2, 256], "conf_ln_w": [2, 256],
    "conf_ln_b": [2, 256], "gdn_conv_w": [2, 4, 768], "gdn_A_log": [2, 4],
    "gdn_dt_bias": [2, 4], "gdn_norm_w": [2, 64], "hgrn_lb_logits": [2, 256],
    "hgrn_norm_w": [2, 64], "w_out": [2, 1024, 1024], "norm_ffn_w": [2, 1024],
    "ffn_w_gate": [2, 1024, DFF], "ffn_w_up": [2, 1024, DFF], "ffn_w_down": [2, DFF, 1024],
    "final_norm_w": [1024],
}


class StopBuild(Exception):
    pass


def build(T, NL=2, dbg=False, skip=(), stop=99):
    nc = bass.Bass("TRN2", target_bir_lowering=False)

    def ckpt(n):
        if n >= stop:
            raise StopBuild()

    NT = T // 128
    k = K(nc)
    dr = {}
    dr["x"] = nc.dram_tensor("x", [T, D], F32, kind="ExternalInput").ap()
    for n, s in PARAM_SHAPES.items():
        dr[n] = nc.dram_tensor(n, s, F32, kind="ExternalInput").ap()
    for n, s in CONST_SHAPES.items():
        dr[n] = nc.dram_tensor(n, s, F32, kind="ExternalInput").ap()
    dr["c_rope"] = nc.dram_tensor("c_rope", [T, 512], F32, kind="ExternalInput").ap()
    out = nc.dram_tensor("out", [T, D], F32, kind="ExternalOutput").ap()
    xs = nc.dram_tensor("xs", [T, D], F32, kind="Internal").ap()
    if dbg:
        dmix = nc.dram_tensor("dmix", [NL, T, D], F32, kind="ExternalOutput").ap()
    xsb = [Buf("xs%d" % i) for i in range(NT)]

    es_top = ExitStack()

    uniq = {"n": 0}

    def sb(es, name, shape, dt=F32):
        uniq["n"] += 1
        t = es.enter_context(nc.sbuf_tensor("%s_%d" % (name, uniq["n"]), shape, dt))
        return t.ap(), Buf(name)

    def mm(out_, lhsT, rhs, R, W, start=True, stop=True):
        k.op("pe", lambda e: e.matmul(out_, lhsT=lhsT, rhs=rhs, start=start, stop=stop), R=R, W=W)

    def tr(out_, in_, ident, R, W):
        k.op("pe", lambda e: e.transpose(out=out_, in_=in_, identity=ident), R=R, W=W)

    def act(out_, in_, func, R, W, bias=None, scale=None, accum=None):
        kw = {}
        if bias is not None:
            kw["bias"] = bias
        if scale is not None:
            kw["scale"] = scale
        if accum is not None:
            kw["accum_out"] = accum
        k.op("act", lambda e: e.activation(out=out_, in_=in_, func=func, **kw), R=R, W=W)

    def silu(out_, in_, R, W, scr, b_scr):
        act(scr, in_, AF.Sigmoid, R, [b_scr])
        tt(out_, in_, scr, ALU.mult, list(R) + [b_scr], W)

    def tt(out_, a, b, op, R, W, eng="dve"):
        k.op(eng, lambda e: e.tensor_tensor(out=out_, in0=a, in1=b, op=op), R=R, W=W)

    def ts(out_, a, s1, op0, R, W, s2=None, op1=None, eng="dve"):
        if op1 is None:
            k.op(eng, lambda e: e.tensor_scalar(out=out_, in0=a, scalar1=s1, scalar2=None, op0=op0), R=R, W=W)
        else:
            k.op(eng, lambda e: e.tensor_scalar(out=out_, in0=a, scalar1=s1, scalar2=None, op0=op0), R=R, W=W)
            k.op(eng, lambda e: e.tensor_scalar(out=out_, in0=out_, scalar1=s2, scalar2=None, op0=op1), R=list(R) + list(W), W=W)

    def stt(out_, in0, scalar, in1, op0, op1, R, W):
        k.op("dve", lambda e: e.scalar_tensor_tensor(out=out_, in0=in0, scalar=scalar, in1=in1, op0=op0, op1=op1), R=R, W=W)

    def cp(out_, in_, R, W, eng="dve"):
        if eng == "act":
            k.op("act", lambda e: e.copy(out=out_, in_=in_), R=R, W=W)
        else:
            k.op(eng, lambda e: e.tensor_copy(out=out_, in_=in_), R=R, W=W)

    def red(out_, in_, R, W):
        k.op("dve", lambda e: e.tensor_reduce(out=out_, in_=in_, axis=AX.X, op=ALU.add), R=R, W=W)

    def rcp(out_, in_, R, W):
        k.op("dve", lambda e: e.reciprocal(out=out_, in_=in_), R=R, W=W)

    def mset(ap, val, W, eng="pool"):
        k.op(eng, lambda e: e.memset(ap, val), W=W)

    def v3(ap, inner):
        return ap.rearrange("p (a b) -> p a b", b=inner)

    banks = []
    for i in range(8):
        t = es_top.enter_context(nc.psum_tensor("bank%d" % i, [128, 512], F32))
        banks.append((t.ap(), Buf("bank%d" % i, excl=True)))
    freeq = list(range(8))
    bank_idx = {id(b[1]): i for i, b in enumerate(banks)}
    cur = {"free": freeq}

    def psum():
        assert cur["free"], "no free PSUM bank"
        return banks[cur["free"].pop(0)]

    def galloc():
        if False:
            yield
        return psum()

    def pfree(*bufs):
        for b in bufs:
            i = bank_idx[id(b)]
            assert i not in cur["free"]
            cur["free"].append(i)

    def run_interleaved(gens):
        lists = []
        for g, bset in gens:
            cur["free"] = list(bset)
            k.rec = []
            for _ in g():
                pass
            lists.append(k.rec)
            k.rec = None
            assert sorted(cur["free"]) == sorted(bset), "mixer leaked PSUM banks"
        cur["free"] = freeq
        pos = [0] * len(lists)
        while True:
            best, bf = -1, 2.0
            for j, lst in enumerate(lists):
                if pos[j] < len(lst):
                    f = pos[j] / len(lst)
                    if f < bf:
                        best, bf = j, f
            if best < 0:
                break
            k.replay(lists[best][pos[best]])
            pos[best] += 1

    stg = [sb(es_top, "stg%d" % i, [128, 1024]) for i in range(2)]
    stg_c = stg[0]
    identf, b_identf = sb(es_top, "identf", [128, 128])
    identb, b_identb = sb(es_top, "identb", [128, 128], BF16)
    cmask, b_cmask = sb(es_top, "cmask", [128, 128])
    ublk, b_ublk = sb(es_top, "ublk", [128, 128])
    oblk, b_oblk = sb(es_top, "oblk", [128, 128])
    cind, b_cind = sb(es_top, "cind", [128, 2])
    sel, b_sel = sb(es_top, "sel", [4, 512])
    negAb, b_negAb = sb(es_top, "negAb", [128, 512], BF16)
    negTb, b_negTb = sb(es_top, "negTb", [128, 512], BF16)
    ones4, b_ones4 = sb(es_top, "ones4", [4, 128])
    ones64, b_ones64 = sb(es_top, "ones64", [64, 64])
    gC, b_gC = sb(es_top, "gC", [64, 256])
    CONSTS = []
    k.dma(identf, dr["c_identf"], W=[b_identf])
    k.dma(cmask, dr["c_cmask"], W=[b_cmask])
    k.dma(ublk, dr["c_ublk"], W=[b_ublk])
    k.dma(oblk, dr["c_oblk"], W=[b_oblk])
    k.dma(cind, dr["c_cind"], W=[b_cind])
    k.dma(sel, dr["c_sel"], W=[b_sel])
    k.dma(ones4, dr["c_ones4"], W=[b_ones4])
    k.dma(ones64, dr["c_ones64"], W=[b_ones64])
    k.dma(gC, dr["c_gC"], W=[b_gC])
    cp(identb, identf, [b_identf], [b_identb])
    for (dst_, b_dst_, nm_) in [(negAb, b_negAb, "c_negA4"), (negTb, b_negTb, "c_negT4")]:
        s_ap, s_b = stg_c
        k.dma(s_ap[:, 0:512], dr[nm_], W=[s_b])
        cp(dst_, s_ap[:, 0:512], [s_b], [b_dst_])
    sel3 = v3(sel, 128)
    try:
        ckpt(1)
    except StopBuild:
        k.barrier()
        return nc, k

    xt, b_xt = sb(es_top, "xt", [128, D])
    xo, b_xo = sb(es_top, "xo", [128, D])
    xn, b_xn = sb(es_top, "xn", [128, D], BF16)
    hT, b_hT = sb(es_top, "hT", [128, 8, 128], BF16)
    b_xt0, hT0, b_hT0 = b_xt, hT, b_hT
    ss, b_ss = sb(es_top, "ss", [128, 1])
    nw, b_nw = sb(es_top, "nw", [128, 8])
    stg_i = {"i": 0}

    def load_w(dst, b_dst, src, KC, N, scale, stgs=None):
        stgs = stgs or stg
        engs = ("dve", "act") if len(stgs) == 2 else ("dve", "act", "dve", "act")
        for kc in range(KC):
            for n0 in range(0, N, 1024):
                n1 = min(N, n0 + 1024)
                s_ap, s_b = stgs[stg_i["i"] % len(stgs)]
                eng = engs[stg_i["i"] % len(stgs)]
                stg_i["i"] += 1
                k.dma(s_ap[:, :n1 - n0], src[kc * 128:(kc + 1) * 128, n0:n1], W=[s_b])
                if eng == "act":
                    if scale is not None:
                        act(dst[:, kc, n0:n1], s_ap[:, :n1 - n0], AF.Identity, [s_b, b_nw], [b_dst],
                            scale=scale[:, kc:kc + 1])
                    else:
                        cp(dst[:, kc, n0:n1], s_ap[:, :n1 - n0], [s_b], [b_dst], eng="act")
                elif scale is not None:
                    ts(dst[:, kc, n0:n1], s_ap[:, :n1 - n0], scale[:, kc:kc + 1], ALU.mult,
                       [s_b, b_nw], [b_dst], eng=eng)
                else:
                    cp(dst[:, kc, n0:n1], s_ap[:, :n1 - n0], [s_b], [b_dst], eng=eng)

    def norm_to_hT(src_xt, b_xt=None, hT=None, b_hT=None):
        b_xt = b_xt or b_xt0
        hT = hT if hT is not None else hT0
        b_hT = b_hT or b_hT0
        act(xn, src_xt, AF.Square, [b_xt], [b_xn, b_ss], accum=ss)
        act(ss, ss, AF.Sqrt, [b_ss], [b_ss], bias=1e-6, scale=1.0 / D)
        rcp(ss, ss, [b_ss], [b_ss])
        ts(xn, src_xt, ss, ALU.mult, [b_xt, b_ss], [b_xn])
        pb, b_pb = psum()
        pbv = v3(pb.bitcast(BF16), 128)
        for kc in range(8):
            tr(pbv[:, kc, :], xn[:, kc * 128:(kc + 1) * 128], identb, [b_xn, b_identb], [b_pb])
        cp(hT, pbv, [b_pb], [b_hT], eng="act")
        pfree(b_pb)

    def pass_a(l):
        es = ExitStack()
        Wi, b_Wi = sb(es, "Wi", [128, 8, DIN], BF16)
        Wo, b_Wo = sb(es, "Wo", [128, 8, D], BF16)
        Dc, b_Dc = sb(es, "Dc", [128, 62, 128], BF16)
        ccw, b_ccw = sb(es, "ccw", [128, 2, 31])
        ccb, b_ccb = sb(es, "ccb", [128, 2])
        gcw, b_gcw = sb(es, "gcw", [64, 12, 4])
        rnw, b_rnw = sb(es, "rnw", [128, 256])
        clnw, b_clnw = sb(es, "clnw", [128, 256])
        clnb, b_clnb = sb(es, "clnb", [128, 256])
        gnw, b_gnw = sb(es, "gnw", [128, 64])
        hnw, b_hnw = sb(es, "hnw", [128, 64])
        negA, b_negA = sb(es, "negA", [128, 4])
        dtb, b_dtb = sb(es, "dtb", [128, 4])
        lbl, b_lbl = sb(es, "lbl", [128, 2, 256])
        lbt, b_lbt = sb(es, "lbt", [128, 256])
        oml, b_oml = sb(es, "oml", [128, 256])
        with nc.allow_non_contiguous_dma(reason="tiny param loads"):
            k.dma(nw, dr["norm_mix_w"][l].rearrange("(c p) -> p c", p=128), W=[b_nw])
            for ct in range(2):
                k.dma(ccw[:, ct, :], dr["conf_conv_w"][l][:, ct * 128:(ct + 1) * 128].rearrange("k c -> c k"), W=[b_ccw])
            k.dma(ccb, dr["conf_conv_b"][l].rearrange("(ct c) -> c ct", c=128), W=[b_ccb])
            for s_ in range(12):
                k.dma(gcw[:, s_, :], dr["gdn_conv_w"][l][:, s_ * 64:(s_ + 1) * 64].rearrange("k d -> d k"), W=[b_gcw])
            k.dma(rnw, dr["ret_norm_w"][l].partition_broadcast(128), W=[b_rnw])
            k.dma(clnw, dr["conf_ln_w"][l].partition_broadcast(128), W=[b_clnw])
            k.dma(clnb, dr["conf_ln_b"][l].partition_broadcast(128), W=[b_clnb])
            k.dma(gnw, dr["gdn_norm_w"][l].partition_broadcast(128), W=[b_gnw])
            k.dma(hnw, dr["hgrn_norm_w"][l].partition_broadcast(128), W=[b_hnw])
            k.dma(negA, dr["gdn_A_log"][l].partition_broadcast(128), W=[b_negA])
            k.dma(dtb, dr["gdn_dt_bias"][l].partition_broadcast(128), W=[b_dtb])
            k.dma(lbl[:, 0, :], dr["hgrn_lb_logits"][0].partition_broadcast(128), W=[b_lbl])
            k.dma(lbl[:, 1, :], dr["hgrn_lb_logits"][1].partition_broadcast(128), W=[b_lbl])
        ckpt(2)
        act(negA, negA, AF.Exp, [b_negA], [b_negA])
        ts(negA, negA, -1.0, ALU.mult, [b_negA], [b_negA])
        if l == 0:
            mset(lbt, 0.0, [b_lbt])
            mset(oml, 1.0, [b_oml])
        else:
            act(lbl, lbl, AF.Exp, [b_lbl], [b_lbl])
            tt(oml, lbl[:, 0, :], lbl[:, 1, :], ALU.add, [b_lbl], [b_oml])
            rcp(oml, oml, [b_oml], [b_oml])
            tt(lbt, lbl[:, 1, :], oml, ALU.mult, [b_lbl, b_oml], [b_lbt])
            ts(oml, lbt, -1.0, ALU.mult, [b_lbt], [b_oml], s2=1.0, op1=ALU.add)
        for kk in range(31):
            for ct in range(2):
                ts(Dc[:, kk * 2 + ct, :], identf, ccw[:, ct, kk:kk + 1], ALU.mult, [b_identf, b_ccw], [b_Dc])
        ckpt(3)
        load_w(Wi, b_Wi, dr["w_in"][l], 8, DIN, nw)
        load_w(Wo, b_Wo, dr["w_out"][l], 8, D, None)

        ckpt(4)
        rope_t, b_rope = sb(es, "rope_t", [128, 512])
        rqk, b_rqk = sb(es, "rqk", [128, 512])
        ctsg, b_ctsg = sb(es, "ctsg", [128, 256])
        glt, b_glt = sb(es, "glt", [128, 256], BF16)
        gpt, b_gpt = sb(es, "gpt", [128, 768], BF16)
        Gblk, b_Gblk = sb(es, "Gblk", [4, 512])
        nGblk, b_nGblk = sb(es, "nGblk", [4, 512])
        hqf, b_hqf = sb(es, "hqf", [128, 256])
        t1, b_t1 = sb(es, "t1", [128, 4, 32])
        t2, b_t2 = sb(es, "t2", [128, 4, 32])
        qrt, b_qrt = sb(es, "qrt", [128, 4, 64], BF16)
        krt, b_krt = sb(es, "krt", [128, 4, 64], BF16)
        qkT, b_qkT = sb(es, "qkT", [64, 8, 128], BF16)
        PTr, b_PTr = sb(es, "PTr", [128, 4, 128], BF16)
        rvb, b_rvb = sb(es, "rvb", [128, 256], BF16)
        Sr32, b_Sr32 = sb(es, "Sr32", [64, 256])
        Srb, b_Srb = sb(es, "Srb", [64, 4, 64], BF16)
        silr, b_silr = sb(es, "silr", [128, 256])
        silg, b_silg = sb(es, "silg", [128, 256])
        silh, b_silh = sb(es, "silh", [128, 256])
        HN = {}
        for nm_ in ("ret", "hgrn", "gdn"):
            HN[nm_] = dict(hsq=sb(es, "hsq", [128, 256]), hy=sb(es, "hy", [128, 256]), s1=sb(es, "s1", [128, 4]),
                           s2=sb(es, "s2", [128, 4]), mean=sb(es, "mean", [128, 4]), msq=sb(es, "msq", [128, 4]),
                           var=sb(es, "var", [128, 4]))
        hsq, b_hsq = HN["ret"]["hsq"]
        hf, b_hf = sb(es, "hf", [128, 256])
        logf, b_logf = sb(es, "logf", [128, 256])
        hkk, b_hkk = sb(es, "hkk", [128, 256])
        Gsb, b_Gsb = sb(es, "Gsb", [128, 256])
        eG, b_eG = sb(es, "eG", [128, 256])
        enG, b_enG = eG, b_eG
        hD, b_hD = Gsb, b_Gsb
        qh, b_qh = sb(es, "qh", [128, 256], BF16)
        kht, b_kht = sb(es, "kht", [128, 256], BF16)
        khat, b_khat = sb(es, "khat", [128, 256], BF16)
        hvb, b_hvb = sb(es, "hvb", [128, 256], BF16)
        hq = [sb(es, "hq%d" % c, [64, 4, 128], BF16) for c in range(2)]
        hkT, b_hkT = sb(es, "hkT", [64, 4, 128], BF16)
        PTh, b_PTh = sb(es, "PTh", [128, 4, 128], BF16)
        Sh32, b_Sh32 = sb(es, "Sh32", [64, 4, 64])
        Shb = [sb(es, "Shb%d" % c, [64, 4, 64], BF16) for c in range(3)]
        eGl, b_eGl = sb(es, "eGl", [64, 8])
        gl, b_gl = sb(es, "gl", [128, 2, 158], BF16)
        yT, b_yT = sb(es, "yT", [128, 2, 128])
        yc, b_yc = sb(es, "yc", [128, 256])
        c1, b_c1 = sb(es, "c1", [128, 1])
        c2, b_c2 = sb(es, "c2", [128, 1])
        c3, b_c3 = sb(es, "c3", [128, 1])
        gp, b_gp = sb(es, "gp", [64, 12, 131], BF16)
        gx, b_gx = sb(es, "gx", [64, 12, 128])
        b_gxs = [Buf("gxs%d" % i_) for i_ in range(12)]
        gsq, b_gsq = sb(es, "gsq", [64, 4, 128])
        grs, b_grs = sb(es, "grs", [64, 4, 128])
        gqT, b_gqT = sb(es, "gqT", [64, 4, 128], BF16)
        gkT, b_gkT = sb(es, "gkT", [64, 4, 128], BF16)
        gvT, b_gvT = sb(es, "gvT", [64, 4, 128], BF16)
        grk, b_grk = sb(es, "grk", [128, 4, 64], BF16)
        gkh, b_gkh = sb(es, "gkh", [128, 4, 64], BF16)
        gbv, b_gbv = sb(es, "gbv", [128, 4, 64], BF16)
        EA, b_EA = sb(es, "EA", [128, 4, 128], BF16)
        ET, b_ET = sb(es, "ET", [128, 4, 128], BF16)
        eGr, b_eGr = sb(es, "eGr", [64, 4, 128])
        gq = [sb(es, "gq%d" % c, [64, 4, 128], BF16) for c in range(2)]
        gw = [sb(es, "gw%d" % c, [64, 4, 128], BF16) for c in range(2)]
        Pm, b_Pm = sb(es, "Pm", [128, 4, 128], BF16)
        PmT, b_PmT = sb(es, "PmT", [128, 4, 128], BF16)
        Wt, b_Wt = sb(es, "Wt", [128, 4, 128], BF16)
        attT, b_attT = sb(es, "attT", [128, 4, 128], BF16)
        usb, b_usb = sb(es, "usb", [128, 256])
        vn, b_vn = sb(es, "vn", [128, 256], BF16)
        Sg32, b_Sg32 = sb(es, "Sg32", [64, 4, 64])
        Sgb = [sb(es, "Sgb%d" % c, [64, 4, 64], BF16) for c in range(3)]
        beta, b_beta = sb(es, "beta", [128, 4])
        gba, b_gba = sb(es, "gba", [128, 8])
        lnb, b_lnb = sb(es, "lnb", [128, 4])
        gsp, b_gsp = sb(es, "gsp", [128, 4])
        gg_, b_gg = sb(es, "gg_", [128, 4])
        Gs, b_Gs = sb(es, "Gs", [128, 4])
        nGs, b_nGs = sb(es, "nGs", [128, 4])
        Bs, b_Bs = sb(es, "Bs", [128, 4])
        eB, b_eB = sb(es, "eB", [128, 4])
        eDl, b_eDl = sb(es, "eDl", [128, 4])
        GTs, b_GTs = sb(es, "GTs", [4, 128])
        mix, b_mix = sb(es, "mix", [128, D], BF16)
        mT, b_mT = sb(es, "mT", [128, 8, 128], BF16)
        mixf, b_mixf = (xo, b_xo) if dbg else (None, None)

        mset(Sr32, 0.0, [b_Sr32]); mset(Srb, 0.0, [b_Srb])
        mset(Sh32, 0.0, [b_Sh32]); mset(Sg32, 0.0, [b_Sg32])
        for c in range(3):
            mset(Shb[c][0], 0.0, [Shb[c][1]]); mset(Sgb[c][0], 0.0, [Sgb[c][1]])
        for c in range(2):
            mset(hq[c][0], 0.0, [hq[c][1]]); mset(gq[c][0], 0.0, [gq[c][1]]); mset(gw[c][0], 0.0, [gw[c][1]])
        mset(gl, 0.0, [b_gl]); mset(gp, 0.0, [b_gp])
        mset(mix, 0.0, [b_mix])

        ckpt(5)

        k.barrier()
        XT = [(xt, b_xt), (stg[0][0], Buf("xt1"))]
        HT = [(hT, b_hT), (v3(stg[1][0][:, 0:512].bitcast(BF16), 128), Buf("hT1"))]
        ROPE = [(rope_t, b_rope), (stg[1][0][:, 512:1024], Buf("rope1"))]
        src_x = dr["x"] if l == 0 else xs

        def front(j):
            (xt_n, b_xt_n), (hT_n, b_hT_n), (rope_n, b_rope_n) = XT[j % 2], HT[j % 2], ROPE[j % 2]
            k.dma(xt_n, src_x[j * 128:(j + 1) * 128, :], R=([] if l == 0 else [xsb[j]]), W=[b_xt_n])
            k.dma(rope_n, dr["c_rope"][j * 128:(j + 1) * 128, :], W=[b_rope_n])
            norm_to_hT(xt_n, b_xt_n, hT_n, b_hT_n)

        def tail_b(j):
            xt_p, b_xt_p = XT[j % 2]
            for n in range(2):
                pso, b_pso = psum()
                ns = slice(n * 512, (n + 1) * 512)
                for kc in range(8):
                    mm(pso, mT[:, kc, :], Wo[:, kc, ns], [b_mT, b_Wo], [b_pso], kc == 0, kc == 7)
                tt(xo[:, ns], xt_p[:, ns], pso, ALU.add, [b_xt_p, b_pso], [b_xo])
                pfree(b_pso)
            k.dma(xs[j * 128:(j + 1) * 128, :], xo, R=[b_xo], W=[xsb[j]])

        def head_norm(o, b_o, sil, b_sil, w3, b_w, center, eps, dst3, who):
            t_ = HN[who]
            (hsq, b_hsq), (hy, b_hy), (s1, b_s1), (s2, b_s2) = t_["hsq"], t_["hy"], t_["s1"], t_["s2"]
            (mean, b_mean), (msq, b_msq), (var, b_var) = t_["mean"], t_["msq"], t_["var"]
            o3 = v3(o, 64)
            red(s1, o3, [b_o], [b_s1])
            act(hsq, o, AF.Square, [b_o], [b_hsq])
            red(s2, v3(hsq, 64), [b_hsq], [b_s2])
            if center:
                ts(mean, s1, 1.0 / 64, ALU.mult, [b_s1], [b_mean])
                tt(msq, mean, mean, ALU.mult, [b_mean], [b_msq])
                stt(var, s2, 1.0 / 64, msq, ALU.mult, ALU.subtract, [b_s2, b_msq], [b_var])
            else:
                ts(var, s2, 1.0 / 64, ALU.mult, [b_s2], [b_var])
            act(var, var, AF.Sqrt, [b_var], [b_var], bias=eps)
            rcp(var, var, [b_var], [b_var])
            hy3 = v3(hy, 64)
            if center:
                tt(hy3, o3, mean.unsqueeze(2).broadcast_to([128, 4, 64]), ALU.subtract, [b_o, b_mean], [b_hy])
                tt(hy3, hy3, var.unsqueeze(2).broadcast_to([128, 4, 64]), ALU.mult, [b_hy, b_var], [b_hy])
            else:
                tt(hy3, o3, var.unsqueeze(2).broadcast_to([128, 4, 64]), ALU.mult, [b_o, b_var], [b_hy])
            tt(hy3, hy3, w3, ALU.mult, [b_hy, b_w], [b_hy])
            tt(dst3, hy3, v3(sil, 64), ALU.mult, [b_hy, b_sil], [b_mix])

        front(0)
        for i in range(NT):
            pv, mid, end = (2 * i) % 3, (2 * i + 1) % 3, (2 * i + 2) % 3
            r0, r1 = slice(0, 64), slice(64, 128)
            (xt_c, b_xt_c), (hT_c, b_hT_c), (rope_c, b_rope_c) = XT[i % 2], HT[i % 2], ROPE[i % 2]

            def g_ret():
                pg, b_pg = yield from galloc()
                for kc in range(8):
                    mm(pg[:, 0:512], hT_c[:, kc, :], Wi[:, kc, 0:512], [b_hT_c, b_Wi], [b_pg], kc == 0, kc == 7)
                cp(rqk, pg, [b_pg], [b_rqk], eng="act")
                pfree(b_pg)
                pg, b_pg = yield from galloc()
                for kc in range(8):
                    mm(pg[:, 0:512], hT_c[:, kc, :], Wi[:, kc, 512:1024], [b_hT_c, b_Wi], [b_pg], kc == 0, kc == 7)
                silu(silr, pg[:, 256:512], [b_pg], [b_silr], HN["ret"]["hsq"][0], HN["ret"]["hsq"][1])
                cp(rvb, pg[:, 0:256], [b_pg], [b_rvb])
                pfree(b_pg)
                rt4 = rope_c.rearrange("p (a h d) -> p a h d", a=4, h=4)
                for (off, ci, si, dst, b_dst) in [(0, 0, 1, qrt, b_qrt), (256, 2, 3, krt, b_krt)]:
                    X = v3(rqk[:, off:off + 256], 64)
                    x1, x2 = X[:, :, 0:32], X[:, :, 32:64]
                    cs, sn = rt4[:, ci], rt4[:, si]
                    tt(t1, x1, cs, ALU.mult, [b_rqk, b_rope_c], [b_t1])
                    tt(t2, x2, sn, ALU.mult, [b_rqk, b_rope_c], [b_t2])
                    tt(dst[:, :, 0:32], t1, t2, ALU.subtract, [b_t1, b_t2], [b_dst])
                    tt(t1, x1, sn, ALU.mult, [b_rqk, b_rope_c], [b_t1])
                    tt(t2, x2, cs, ALU.mult, [b_rqk, b_rope_c], [b_t2])
                    tt(dst[:, :, 32:64], t1, t2, ALU.add, [b_t1, b_t2], [b_dst])
                pb, b_pb = yield from galloc()
                pbv = v3(pb.bitcast(BF16), 128)
                for h in range(4):
                    tr(pbv[0:64, h, :], qrt[:, h, :], identb, [b_qrt, b_identb], [b_pb])
                    tr(pbv[0:64, 4 + h, :], krt[:, h, :], identb, [b_krt, b_identb], [b_pb])
                cp(qkT, pbv[0:64], [b_pb], [b_qkT], eng="act")
                pfree(b_pb)
                yield
                ps, b_ps = yield from galloc()
                ps3 = v3(ps, 128)
                for h in range(4):
                    mm(ps3[:, h, :], qkT[:, 4 + h, :], qkT[:, h, :], [b_qkT], [b_ps])
                tt(PTr, ps3, cmask.unsqueeze(1).broadcast_to([128, 4, 128]), ALU.mult, [b_ps, b_cmask], [b_PTr])
                pfree(b_ps)
                yield
                po, b_po = yield from galloc()
                for h in range(4):
                    hs = slice(h * 64, (h + 1) * 64)
                    mm(po[:, hs], PTr[:, h, :], rvb[:, hs], [b_PTr, b_rvb], [b_po], True, False)
                    mm(po[:, hs], qkT[:, h, :], Srb[:, h, :], [b_qkT, b_Srb], [b_po], False, True)
                head_norm(po[:, 0:256], b_po, silr, b_silr, v3(rnw, 64), b_rnw, True, 1e-5, v3(mix[:, 0:256], 64), "ret")
                pfree(b_po)
                pd, b_pd = yield from galloc()
                for h in range(4):
                    hs = slice(h * 64, (h + 1) * 64)
                    mm(pd[0:64, hs], krt[:, h, :], rvb[:, hs], [b_krt, b_rvb], [b_pd])
                tt(Sr32, Sr32, pd[0:64, 0:256], ALU.add, [b_Sr32, b_pd], [b_Sr32])
                tt(Sr32, Sr32, gC, ALU.mult, [b_Sr32, b_gC], [b_Sr32])
                cp(Srb, v3(Sr32, 64), [b_Sr32], [b_Srb], eng="pool")
                pfree(b_pd)

            def g_hgrn():
                pg, b_pg = yield from galloc()
                for kc in range(8):
                    mm(pg[:, 0:512], hT_c[:, kc, :], Wi[:, kc, 2568:3080], [b_hT_c, b_Wi], [b_pg], kc == 0, kc == 7)
                cp(hqf, pg[:, 0:256], [b_pg], [b_hqf])
                act(hf, pg[:, 256:512], AF.Sigmoid, [b_pg], [b_hf])
                pfree(b_pg)
                pg, b_pg = yield from galloc()
                for kc in range(8):
                    mm(pg[:, 0:512], hT_c[:, kc, :], Wi[:, kc, 3080:3592], [b_hT_c, b_Wi], [b_pg], kc == 0, kc == 7)
                silu(silh, pg[:, 256:512], [b_pg], [b_silh], HN["hgrn"]["hsq"][0], HN["hgrn"]["hsq"][1])
                cp(hvb, pg[:, 0:256], [b_pg], [b_hvb])
                pfree(b_pg)
                tt(hf, hf, oml, ALU.mult, [b_hf, b_oml], [b_hf])
                tt(hf, hf, lbt, ALU.add, [b_hf, b_lbt], [b_hf])
                act(logf, hf, AF.Ln, [b_hf], [b_logf])
                ts(hkk, hf, -1.0, ALU.mult, [b_hf], [b_hkk], s2=1.0, op1=ALU.add)
                pG, b_pG = yield from galloc()
                mm(pG[:, 0:256], ublk, logf, [b_ublk, b_logf], [b_pG])
                mm(pG[:, 256:512], oblk, logf, [b_oblk, b_logf], [b_pG])
                pL, b_pL = yield from galloc()
                for h in range(4):
                    mm(pL[0:64, h * 2:(h + 1) * 2], logf[:, h * 64:(h + 1) * 64], cind, [b_logf, b_cind], [b_pL])
                act(eGl, pL[0:64, 0:8], AF.Exp, [b_pL], [b_eGl])
                pfree(b_pL)
                cp(Gsb, pG[:, 0:256], [b_pG], [b_Gsb], eng="act")
                act(eG, pG[:, 0:256], AF.Exp, [b_pG], [b_eG])
                tt(qh, hqf, eG, ALU.mult, [b_hqf, b_eG], [b_qh])
                act(enG, pG[:, 0:256], AF.Exp, [b_pG], [b_enG], scale=-1.0)
                tt(kht, hkk, enG, ALU.mult, [b_hkk, b_enG], [b_kht])
                tt(hD, pG[:, 256:512], Gsb, ALU.subtract, [b_pG, b_Gsb], [b_hD])
                pfree(b_pG)
                yield
                act(hD, hD, AF.Exp, [b_hD], [b_hD])
                tt(khat, hkk, hD, ALU.mult, [b_hkk, b_hD], [b_khat])
                pb, b_pb = yield from galloc()
                pbv = v3(pb.bitcast(BF16), 128)
                for h in range(4):
                    tr(pbv[0:64, h, :], qh[:, h * 64:(h + 1) * 64], identb, [b_qh, b_identb], [b_pb])
                    tr(pbv[0:64, 4 + h, :], kht[:, h * 64:(h + 1) * 64], identb, [b_kht, b_identb], [b_pb])
                cp(hq[0][0][:, :, 0:64], pbv[0:64, 0:4, 0:64], [b_pb], [hq[0][1]], eng="act")
                cp(hq[1][0][:, :, 64:128], pbv[0:64, 0:4, 64:128], [b_pb], [hq[1][1]], eng="act")
                cp(hkT, pbv[0:64, 4:8, :], [b_pb], [b_hkT])
                pfree(b_pb)
                yield
                ps, b_ps = yield from galloc()
                ps3 = v3(ps, 128)
                for h in range(4):
                    for c in range(2):
                        cs_ = slice(c * 64, (c + 1) * 64)
                        mm(ps3[:, h, cs_], hkT[:, h, :], hq[c][0][:, h, cs_], [b_hkT, hq[c][1]], [b_ps])
                tt(PTh, ps3, ublk.unsqueeze(1).broadcast_to([128, 4, 128]), ALU.mult, [b_ps, b_ublk], [b_PTh])
                pfree(b_ps)
                yield
                for c, (rr, sdst) in enumerate([(r0, mid), (r1, end)]):
                    pd, b_pd = yield from galloc()
                    for h in range(4):
                        hs = slice(h * 64, (h + 1) * 64)
                        mm(pd[0:64, hs], khat[rr, hs], hvb[rr, hs], [b_khat, b_hvb], [b_pd])
                    for h in range(4):
                        hs = slice(h * 64, (h + 1) * 64)
                        stt(Sh32[:, h, :], Sh32[:, h, :], eGl[:, h * 2 + c:h * 2 + c + 1], pd[0:64, hs],
                            ALU.mult, ALU.add, [b_Sh32, b_eGl, b_pd], [b_Sh32])
                    cp(Shb[sdst][0], Sh32, [b_Sh32], [Shb[sdst][1]], eng="pool")
                    pfree(b_pd)
                    yield
                po, b_po = yield from galloc()
                for h in range(4):
                    hs = slice(h * 64, (h + 1) * 64)
                    mm(po[:, hs], PTh[:, h, :], hvb[:, hs], [b_PTh, b_hvb], [b_po], True, False)
                    mm(po[:, hs], hq[0][0][:, h, :], Shb[pv][0][:, h, :], [hq[0][1], Shb[pv][1]], [b_po], False, False)
                    mm(po[:, hs], hq[1][0][:, h, :], Shb[mid][0][:, h, :], [hq[1][1], Shb[mid][1]], [b_po], False, True)
                head_norm(po[:, 0:256], b_po, silh, b_silh, hnw.unsqueeze(1).broadcast_to([128, 4, 64]), b_hnw,
                          False, 1e-6, v3(mix[:, 768:1024], 64), "hgrn")
                pfree(b_po)

            def g_conf():
                pg, b_pg = yield from galloc()
                for kc in range(8):
                    mm(pg[:, 0:512], hT_c[:, kc, :], Wi[:, kc, 1024:1536], [b_hT_c, b_Wi], [b_pg], kc == 0, kc == 7)
                act(ctsg, pg[:, 256:512], AF.Sigmoid, [b_pg], [b_ctsg])
                tt(glt, pg[:, 0:256], ctsg, ALU.mult, [b_pg, b_ctsg], [b_glt])
                pfree(b_pg)
                pc, b_pc = yield from galloc()
                pc3 = v3(pc.bitcast(BF16)[:, 0:256], 128)
                for ct in range(2):
                    tr(pc3[:, ct, :], glt[:, ct * 128:(ct + 1) * 128], identb, [b_glt, b_identb], [b_pc])
                cp(gl[:, :, 30:158], pc3, [b_pc], [b_gl], eng="act")
                pfree(b_pc)
                yield
                ckpt(21)
                py, b_py = yield from galloc()
                py3 = v3(py, 128)
                for ct in range(2):
                    for kk in range(31):
                        mm(py3[:, ct, :], Dc[:, kk * 2 + ct, :], gl[:, ct, kk:kk + 128], [b_Dc, b_gl], [b_py],
                           kk == 0, kk == 30)
                ckpt(22)
                for ct in range(2):
                    act(yT[:, ct, :], py3[:, ct, :], AF.Identity, [b_py, b_ccb], [b_yT], bias=ccb[:, ct:ct + 1])
                pfree(b_py)
                yield
                cp(gl[:, :, 0:30], gl[:, :, 128:158], [b_gl], [b_gl], eng="pool")
                ckpt(24)
                pt, b_pt = yield from galloc()
                for ct in range(2):
                    tr(pt[:, ct * 128:(ct + 1) * 128], yT[:, ct, :], identf, [b_yT, b_identf], [b_pt])
                ckpt(25)
                red(c1, pt[:, 0:256].unsqueeze(1), [b_pt], [b_c1])
                act(yc, pt[:, 0:256], AF.Square, [b_pt], [b_yc])
                red(c2, yc.unsqueeze(1), [b_yc], [b_c2])
                ckpt(27)
                ts(c1, c1, 1.0 / 256, ALU.mult, [b_c1], [b_c1])
                tt(c3, c1, c1, ALU.mult, [b_c1], [b_c3])
                stt(c2, c2, 1.0 / 256, c3, ALU.mult, ALU.subtract, [b_c2, b_c3], [b_c2])
                ckpt(28)
                act(c2, c2, AF.Sqrt, [b_c2], [b_c2], bias=1e-5)
                rcp(c2, c2, [b_c2], [b_c2])
                ckpt(29)
                ts(yc, pt[:, 0:256], c1, ALU.subtract, [b_pt, b_c1, b_c2], [b_yc], s2=c2, op1=ALU.mult)
                pfree(b_pt)
                ckpt(30)
                tt(yc, yc, clnw, ALU.mult, [b_yc, b_clnw], [b_yc])
                tt(yc, yc, clnb, ALU.add, [b_yc, b_clnb], [b_yc])
                silu(mix[:, 256:512], yc, [b_yc], [b_mix], ctsg, b_ctsg)

            def g_gdn():
                pg, b_pg = yield from galloc()
                for kc in range(8):
                    mm(pg[:, 0:264], hT_c[:, kc, :], Wi[:, kc, 2304:2568], [b_hT_c, b_Wi], [b_pg], kc == 0, kc == 7)
                silu(silg, pg[:, 0:256], [b_pg], [b_silg], HN["gdn"]["hsq"][0], HN["gdn"]["hsq"][1])
                cp(gba, pg[:, 256:264], [b_pg], [b_gba])
                pfree(b_pg)
                pg, b_pg = yield from galloc()
                for kc in range(8):
                    mm(pg[:, 0:512], hT_c[:, kc, :], Wi[:, kc, 1536:2048], [b_hT_c, b_Wi], [b_pg], kc == 0, kc == 7)
                cp(gpt[:, 0:512], pg, [b_pg], [b_gpt], eng="act")
                pfree(b_pg)
                pg, b_pg = yield from galloc()
                for kc in range(8):
                    mm(pg[:, 0:256], hT_c[:, kc, :], Wi[:, kc, 2048:2304], [b_hT_c, b_Wi], [b_pg], kc == 0, kc == 7)
                cp(gpt[:, 512:768], pg[:, 0:256], [b_pg], [b_gpt])
                pfree(b_pg)
                pb1, b_pb1 = yield from galloc()
                pb2, b_pb2 = yield from galloc()
                v1 = v3(pb1.bitcast(BF16), 128)
                v2 = v3(pb2.bitcast(BF16)[:, 0:512], 128)
                for s in range(12):
                    dst_ = v1[0:64, s, :] if s < 8 else v2[0:64, s - 8, :]
                    tr(dst_, gpt[:, s * 64:(s + 1) * 64], identb, [b_gpt, b_identb], [b_pb1 if s < 8 else b_pb2])
                cp(gp[:, 0:8, 3:131], v1[0:64], [b_pb1], [b_gp], eng="act")
                cp(gp[:, 8:12, 3:131], v2[0:64], [b_pb2], [b_gp])
                pfree(b_pb1, b_pb2)
                yield
                for s in range(12):
                    ts(gx[:, s, :], gp[:, s, 0:128], gcw[:, s, 0:1], ALU.mult, [b_gp, b_gcw],
                       [b_gxs[s]] + ([b_gx] if s == 0 else []))
                for kk in range(1, 4):
                    for s in range(12):
                        stt(gx[:, s, :], gp[:, s, kk:kk + 128], gcw[:, s, kk:kk + 1], gx[:, s, :],
                            ALU.mult, ALU.add, [b_gp, b_gcw, b_gxs[s]], [b_gxs[s]])
                for b in range(3):
                    silu(gx[:, b * 4:(b + 1) * 4, :], gx[:, b * 4:(b + 1) * 4, :], b_gxs[b * 4:(b + 1) * 4],
                         [b_gx] + b_gxs[b * 4:(b + 1) * 4], gsq, b_gsq)
                yield
                cp(gp[:, :, 0:3], gp[:, :, 128:131], [b_gp], [b_gp], eng="pool")
                for b in range(2):
                    xb = gx[:, b * 4:(b + 1) * 4, :]
                    tt(gsq, xb, xb, ALU.mult, [b_gx], [b_gsq])
                    pn, b_pn = yield from galloc()
                    pn3 = v3(pn, 128)
                    for h in range(4):
                        mm(pn3[0:64, h, :], ones64, gsq[:, h, :], [b_ones64, b_gsq], [b_pn])
                    act(grs, pn3[0:64], AF.Sqrt, [b_pn], [b_grs], bias=1e-6)
                    pfree(b_pn)
                    rcp(grs, grs, [b_grs], [b_grs])
                    if b == 0:
                        stt(gqT, xb, 0.125, grs, ALU.mult, ALU.mult, [b_gx, b_grs], [b_gqT])
                    else:
                        tt(gkT, xb, grs, ALU.mult, [b_gx, b_grs], [b_gkT])
                cp(gvT, gx[:, 8:12, :], [b_gx], [b_gvT], eng="act")
                act(beta, gba[:, 0:4], AF.Sigmoid, [b_gba], [b_beta])
                act(lnb, beta, AF.Ln, [b_beta], [b_lnb])
                tt(gsp, gba[:, 4:8], dtb, ALU.add, [b_gba, b_dtb], [b_gsp])
                act(gsp, gsp, AF.Exp, [b_gsp], [b_gsp])
                act(gsp, gsp, AF.Ln, [b_gsp], [b_gsp], bias=1.0)
                tt(gg_, gsp, negA, ALU.mult, [b_gsp, b_negA], [b_gg])
                pq, b_pq = yield from galloc()
                mm(pq[:, 0:4], ublk, gg_, [b_ublk, b_gg], [b_pq])
                mm(pq[:, 4:8], oblk, gg_, [b_oblk, b_gg], [b_pq])
                mm(pq[0:4, 128:256], gg_, ublk, [b_gg, b_ublk], [b_pq])
                cp(Gs, pq[:, 0:4], [b_pq], [b_Gs])
                ts(nGs, Gs, -1.0, ALU.mult, [b_Gs], [b_nGs])
                tt(Bs, Gs, lnb, ALU.add, [b_Gs, b_lnb], [b_Bs])
                act(eB, Bs, AF.Exp, [b_Bs], [b_eB])
                tt(eDl, pq[:, 4:8], Gs, ALU.subtract, [b_pq, b_Gs], [b_eDl])
                act(eDl, eDl, AF.Exp, [b_eDl], [b_eDl])
                cp(GTs, pq[0:4, 128:256], [b_pq], [b_GTs])
                pfree(b_pq)
                yield
                pb, b_pb = yield from galloc()
                pbv = v3(pb.bitcast(BF16)[:, 0:512], 64)
                for h in range(4):
                    tr(pbv[:, h, :], gkT[:, h, :], identb[0:64, 0:64], [b_gkT, b_identb], [b_pb])
                    tr(pbv[:, 4 + h, :], gvT[:, h, :], identb[0:64, 0:64], [b_gvT, b_identb], [b_pb])
                tt(grk, pbv[:, 0:4, :], eB.unsqueeze(2).broadcast_to([128, 4, 64]), ALU.mult, [b_pb, b_eB], [b_grk])
                tt(gkh, pbv[:, 0:4, :], eDl.unsqueeze(2).broadcast_to([128, 4, 64]), ALU.mult, [b_pb, b_eDl], [b_gkh])
                tt(gbv, pbv[:, 4:8, :], beta.unsqueeze(2).broadcast_to([128, 4, 64]), ALU.mult, [b_pb, b_beta], [b_gbv])
                pfree(b_pb)
                yield
                tt(v3(Gblk, 128), GTs.unsqueeze(1).broadcast_to([4, 4, 128]), sel3, ALU.mult, [b_GTs, b_sel], [b_Gblk])
                ts(nGblk, Gblk, -1.0, ALU.mult, [b_Gblk], [b_nGblk])
                pea, b_pea = yield from galloc()
                pea3 = v3(pea, 128)
                mm(pea, ones4, nGblk, [b_ones4, b_nGblk], [b_pea], True, False)
                mm(pea, identb, negAb, [b_identb, b_negAb], [b_pea], False, True)
                for h in range(4):
                    act(EA[:, h, :], pea3[:, h, :], AF.Exp, [b_pea, b_Bs], [b_EA], bias=Bs[:, h:h + 1])
                pfree(b_pea)
                pet, b_pet = yield from galloc()
                pet3 = v3(pet, 128)
                mm(pet, ones4, Gblk, [b_ones4, b_Gblk], [b_pet], True, False)
                mm(pet, identb, negTb, [b_identb, b_negTb], [b_pet], False, True)
                for h in range(4):
                    act(ET[:, h, :], pet3[:, h, :], AF.Exp, [b_pet, b_nGs], [b_ET], bias=nGs[:, h:h + 1])
                pfree(b_pet)
                peg, b_peg = yield from galloc()
                peg3 = v3(peg, 128)
                mm(peg[0:64, :], ones4[:, 0:64], Gblk, [b_ones4, b_Gblk], [b_peg])
                act(eGr, peg3[0:64], AF.Exp, [b_peg], [b_eGr])
                pfree(b_peg)
                yield
                tt(gq[0][0][:, :, 0:64], gqT[:, :, 0:64], eGr[:, :, 0:64], ALU.mult, [b_gqT, b_eGr], [gq[0][1]])
                tt(gq[1][0][:, :, 64:128], gqT[:, :, 64:128], eGr[:, :, 64:128], ALU.mult, [b_gqT, b_eGr], [gq[1][1]])
                pkk, b_pkk = yield from galloc()
                pkq, b_pkq = yield from galloc()
                for h in range(4):
                    mm(v3(pkk, 128)[:, h, :], gkT[:, h, :], gkT[:, h, :], [b_gkT], [b_pkk])
                    mm(v3(pkq, 128)[:, h, :], gkT[:, h, :], gqT[:, h, :], [b_gkT, b_gqT], [b_pkq])
                stt(Pm, v3(pkk, 128), -1.0, EA, ALU.mult, ALU.mult, [b_pkk, b_EA], [b_Pm])
                tt(attT, v3(pkq, 128), ET, ALU.mult, [b_pkq, b_ET], [b_attT])
                pfree(b_pkk, b_pkq)
                yield
                pb, b_pb = yield from galloc()
                pbv = v3(pb.bitcast(BF16)[:, 0:512], 128)
                for h in range(4):
                    tr(pbv[:, h, :], Pm[:, h, :], identb, [b_Pm, b_identb], [b_pb])
                cp(PmT, pbv, [b_pb], [b_PmT], eng="act")
                tt(Wt, pbv, identb.unsqueeze(1).broadcast_to([128, 4, 128]), ALU.add, [b_pb, b_identb], [b_Wt])
                pfree(b_pb)
                yield
                for m in range(5):
                    pa1, b_pa1 = yield from galloc()
                    for h in range(4):
                        mm(v3(pa1, 128)[:, h, :], PmT[:, h, :], Pm[:, h, :], [b_PmT, b_Pm], [b_pa1])
                    if m < 4:
                        pa2, b_pa2 = yield from galloc()
                        for h in range(4):
                            mm(v3(pa2, 128)[:, h, :], Pm[:, h, :], PmT[:, h, :], [b_PmT, b_Pm], [b_pa2])
                    cp(Pm, v3(pa1, 128), [b_pa1], [b_Pm], eng="act")
                    pfree(b_pa1)
                    if m < 4:
                        cp(PmT, v3(pa2, 128), [b_pa2], [b_PmT])
                        pfree(b_pa2)
                    yield
                    pa3, b_pa3 = yield from galloc()
                    for h in range(4):
                        mm(v3(pa3, 128)[:, h, :], Pm[:, h, :], Wt[:, h, :], [b_Pm, b_Wt], [b_pa3])
                    tt(Wt, Wt, v3(pa3, 128), ALU.add, [b_Wt, b_pa3], [b_Wt])
                    pfree(b_pa3)
                    yield
                pu, b_pu = yield from galloc()
                pw, b_pw = yield from galloc()
                for h in range(4):
                    hs = slice(h * 64, (h + 1) * 64)
                    mm(pu[:, hs], Wt[:, h, :], gbv[:, h, :], [b_Wt, b_gbv], [b_pu])
                    mm(v3(pw, 128)[0:64, h, :], grk[:, h, :], Wt[:, h, :], [b_grk, b_Wt], [b_pw])
                cp(usb, pu[:, 0:256], [b_pu], [b_usb])
                cp(gw[0][0][:, :, 0:64], v3(pw, 128)[0:64, :, 0:64], [b_pw], [gw[0][1]], eng="act")
                cp(gw[1][0][:, :, 64:128], v3(pw, 128)[0:64, :, 64:128], [b_pw], [gw[1][1]], eng="act")
                pfree(b_pu, b_pw)
                yield
                for c, (rr, ssrc, sdst) in enumerate([(r0, pv, mid), (r1, mid, end)]):
                    pa, b_pa = yield from galloc()
                    for h in range(4):
                        hs = slice(h * 64, (h + 1) * 64)
                        mm(pa[:, hs], gw[c][0][:, h, :], Sgb[ssrc][0][:, h, :], [gw[c][1], Sgb[ssrc][1]], [b_pa])
                    tt(vn[rr, :], usb[rr, :], pa[rr, 0:256], ALU.subtract, [b_usb, b_pa], [b_vn])
                    pfree(b_pa)
                    pd, b_pd = yield from galloc()
                    for h in range(4):
                        hs = slice(h * 64, (h + 1) * 64)
                        mm(pd[0:64, hs], gkh[rr, h, :], vn[rr, hs], [b_gkh, b_vn], [b_pd])
                    col = 63 + 64 * c
                    for h in range(4):
                        hs = slice(h * 64, (h + 1) * 64)
                        stt(Sg32[:, h, :], Sg32[:, h, :], eGr[:, h, col:col + 1], pd[0:64, hs],
                            ALU.mult, ALU.add, [b_Sg32, b_eGr, b_pd], [b_Sg32])
                    cp(Sgb[sdst][0], Sg32, [b_Sg32], [Sgb[sdst][1]], eng="pool")
                    pfree(b_pd)
                    yield
                po, b_po = yield from galloc()
                for h in range(4):
                    hs = slice(h * 64, (h + 1) * 64)
                    mm(po[:, hs], attT[:, h, :], vn[:, hs], [b_attT, b_vn], [b_po], True, False)
                    mm(po[:, hs], gq[0][0][:, h, :], Sgb[pv][0][:, h, :], [gq[0][1], Sgb[pv][1]], [b_po], False, False)
                    mm(po[:, hs], gq[1][0][:, h, :], Sgb[mid][0][:, h, :], [gq[1][1], Sgb[mid][1]], [b_po], False, True)
                head_norm(po[:, 0:256], b_po, silg, b_silg, gnw.unsqueeze(1).broadcast_to([128, 4, 64]), b_gnw,
                          False, 1e-6, v3(mix[:, 512:768], 64), "gdn")
                pfree(b_po)

            def g_aux():
                if False:
                    yield
                if i > 0:
                    tail_b(i - 1)
                if i + 1 < NT:
                    front(i + 1)

            gl_ = [("gdn", g_gdn, [0, 1]), ("hgrn", g_hgrn, [2, 3]), ("conf", g_conf, [4]), ("ret", g_ret, [5])]
            assert len(freeq) == 8
            run_interleaved([(g, bs) for n_, g, bs in gl_ if n_ not in skip] + [(g_aux, [6, 7])])
            if dbg:
                cp(mixf, mix, [b_mix], [b_mixf], eng="pool")
                k.dma(dmix[l, i * 128:(i + 1) * 128, :], mixf, R=[b_mixf])
            pb, b_pb = psum()
            pbv = v3(pb.bitcast(BF16), 128)
            for kc in range(8):
                tr(pbv[:, kc, :], mix[:, kc * 128:(kc + 1) * 128], identb, [b_mix, b_identb], [b_pb])
            cp(mT, pbv, [b_pb], [b_mT], eng="act")
            pfree(b_pb)
        tail_b(NT - 1)
        k.barrier()
        es.close()

    def pass_b(l, last):
        es = ExitStack()
        Wg, b_Wg = sb(es, "Wg", [128, 8, DFF], BF16)
        Wu, b_Wu = sb(es, "Wu", [128, 8, DFF], BF16)
        Wd, b_Wd = sb(es, "Wd", [128, NFT, D], BF16)
        aT, b_aT = sb(es, "aT", [128, NFT, 128], BF16)
        fsg, b_fsg = sb(es, "fsg", [128, 512])
        fsc, b_fsc = sb(es, "fsc", [128, 512])
        fnw, b_fnw = sb(es, "fnw", [128, D])
        with nc.allow_non_contiguous_dma(reason="tiny param loads"):
            k.dma(nw, dr["norm_ffn_w"][l].rearrange("(c p) -> p c", p=128), W=[b_nw])
            k.dma(fnw, dr["final_norm_w"].partition_broadcast(128), W=[b_fnw])
        stg_b = stg + [sb(es, "stgb%d" % i_, [128, 1024]) for i_ in range(2)]
        load_w(Wg, b_Wg, dr["ffn_w_gate"][l], 8, DFF, nw, stgs=stg_b)
        load_w(Wu, b_Wu, dr["ffn_w_up"][l], 8, DFF, nw, stgs=stg_b)
        load_w(Wd, b_Wd, dr["ffn_w_down"][l], NFT, D, None, stgs=stg_b)
        xtb2, b_xtb2 = sb(es, "xtb2", [128, D])
        hTb2, b_hTb2 = sb(es, "hTb2", [128, 8, 128], BF16)
        xn2, b_xn2 = sb(es, "xn2", [128, D], BF16)
        ss2, b_ss2 = sb(es, "ss2", [128, 1])
        XTB = [(xt, b_xt), (xtb2, b_xtb2)]
        HTB = [(hT, b_hT), (hTb2, b_hTb2)]

        def front_b(j):
            (xt_n, b_xt_n), (hT_n, b_hT_n) = XTB[j % 2], HTB[j % 2]
            k.dma(xt_n, xs[j * 128:(j + 1) * 128, :], R=[xsb[j]], W=[b_xt_n])
            norm_to_hT(xt_n, b_xt_n, hT_n, b_hT_n)

        front_b(0)
        for i in range(NT):
            (xt_c, b_xt_c), (hT_c, b_hT_c) = XTB[i % 2], HTB[i % 2]

            def g_main():
                if False:
                    yield
                for grp in range(6):
                    nf = 4 if grp < 5 else 2
                    psg, b_psg = psum()
                    psu, b_psu = psum()
                    for j in range(nf):
                        ft = grp * 4 + j
                        fs = slice(ft * 128, (ft + 1) * 128)
                        for kc in range(8):
                            mm(psg[:, j * 128:(j + 1) * 128], Wg[:, kc, fs], hT_c[:, kc, :], [b_Wg, b_hT_c], [b_psg], kc == 0, kc == 7)
                        for kc in range(8):
                            mm(psu[:, j * 128:(j + 1) * 128], Wu[:, kc, fs], hT_c[:, kc, :], [b_Wu, b_hT_c], [b_psu], kc == 0, kc == 7)
                    silu(fsg[:, :nf * 128], psg[:, :nf * 128], [b_psg], [b_fsg], fsc[:, :nf * 128], b_fsc)
                    tt(aT[:, grp * 4:grp * 4 + nf, :], v3(psu[:, :nf * 128], 128), v3(fsg[:, :nf * 128], 128), ALU.mult,
                       [b_psu, b_fsg], [b_aT])
                    pfree(b_psg, b_psu)
                for n in range(2):
                    pso, b_pso = psum()
                    ns = slice(n * 512, (n + 1) * 512)
                    for ft in range(NFT):
                        mm(pso, aT[:, ft, :], Wd[:, ft, ns], [b_aT, b_Wd], [b_pso], ft == 0, ft == NFT - 1)
                    tt(xo[:, ns], xt_c[:, ns], pso, ALU.add, [b_xt_c, b_pso], [b_xo])
                    pfree(b_pso)
                if last:
                    act(xn2, xo, AF.Square, [b_xo], [b_xn2, b_ss2], accum=ss2)
                    act(ss2, ss2, AF.Sqrt, [b_ss2], [b_ss2], bias=1e-6, scale=1.0 / D)
                    rcp(ss2, ss2, [b_ss2], [b_ss2])
                    ts(xo, xo, ss2, ALU.mult, [b_xo, b_ss2], [b_xo])
                    tt(xo, xo, fnw, ALU.mult, [b_xo, b_fnw], [b_xo])
                    k.dma(out[i * 128:(i + 1) * 128, :], xo, R=[b_xo])
                else:
                    k.dma(xs[i * 128:(i + 1) * 128, :], xo, R=[b_xo], W=[xsb[i]])

            def g_front():
                if False:
                    yield
                if i + 1 < NT:
                    front_b(i + 1)

            assert len(freeq) == 8
            run_interleaved([(g_main, [0, 1, 2, 3, 4, 5]), (g_front, [6, 7])])
        k.barrier()
        es.close()

    for l in range(NL):
        try:
            pass_a(l)
        except StopBuild:
            k.barrier()
            return nc, k
        if "b" in skip:
            for i in range(NT):
                k.dma(xt, xs[i * 128:(i + 1) * 128, :], R=[xsb[i]], W=[b_xt])
                k.dma(out[i * 128:(i + 1) * 128, :], xt, R=[b_xt])
        else:
            pass_b(l, l == NL - 1)
    k.barrier()
    es_top.close()
    return nc, k


_CACHE = {}


def kernel(**inputs):
    x = np.asarray(inputs["x"], dtype=np.float32)
    B, T, _ = x.shape
    key = T
    if key not in _CACHE:
        _CACHE[key] = (build(T)[0], host_consts(T))
    nc, consts = _CACHE[key]
    params = {n: np.ascontiguousarray(np.asarray(inputs[n], dtype=np.float32)) for n in PARAM_SHAPES}
    in_maps = []
    for c in range(8):
        m = {"x": np.ascontiguousarray(x[c % B])}
        m.update(params)
        m.update(consts)
        in_maps.append(m)
    res = run_bass_kernel_spmd(nc, in_maps, core_ids=list(range(8)))
    outs = [np.asarray(res.results[c]["out"], dtype=np.float32) for c in range(B)]
    return np.stack(outs, axis=0)
```
